# Optimizing a Trainium2 kernel written in Bass

```python
import math
import jax, jax.numpy as jnp
from jax import lax
import numpy as np

D_MODEL = 4096
BATCH = 2
SEQ = 4096
DEPTH = 2

D_MIX = D_MODEL
ATT_W = D_MIX // 4
SSM_W = D_MIX // 4
RWKV_W = D_MIX // 4
MLP_W = D_MIX // 4
ATT_HEAD_DIM = 128
ATT_HEADS = ATT_W // ATT_HEAD_DIM
IDX_HEADS = 16
IDX_DIM = 64
IDX_TOPK_MAX = 256
Q_BLOCK = 128
SSM_HEAD_DIM = 64
SSM_HEADS = SSM_W // SSM_HEAD_DIM
SSM_GROUPS = 4
SSM_STATE = 128
SSM_CONV = 4
SSM_CHUNK = 128
SSM_CONV_CH = SSM_W + 2 * SSM_GROUPS * SSM_STATE
RWKV_HEAD_DIM = 64
RWKV_HEADS = RWKV_W // RWKV_HEAD_DIM
RWKV_LORA_W = 64
RWKV_LORA_A = 64
MLP_GROUPS = 8
MLP_CHUNK = 128
NORM_EPS = 1e-5
GN_EPS = 64e-5

ATT_SPLITS = (ATT_W, ATT_W, ATT_W, ATT_W, IDX_HEADS * IDX_DIM, IDX_DIM, IDX_HEADS)
SSM_SPLITS = (SSM_W, SSM_CONV_CH, SSM_HEADS)
XBC_SPLITS = (SSM_W, SSM_GROUPS * SSM_STATE, SSM_GROUPS * SSM_STATE)
RWKV_SPLITS = (RWKV_W, RWKV_W, RWKV_W, RWKV_W, RWKV_LORA_W, RWKV_LORA_A)
MLP_SPLITS = (MLP_W, MLP_W, MLP_W)
ATT_PROJ = 4 * ATT_W + IDX_HEADS * IDX_DIM + IDX_DIM + IDX_HEADS
SSM_PROJ = SSM_W + SSM_CONV_CH + SSM_HEADS
RWKV_PROJ = 4 * RWKV_W + RWKV_LORA_W + RWKV_LORA_A
MLP_PROJ = 3 * MLP_W
D_IN = ATT_PROJ + SSM_PROJ + RWKV_PROJ + MLP_PROJ
BRANCH_SPLITS = (ATT_PROJ, SSM_PROJ, RWKV_PROJ, MLP_PROJ)

kernel_name = 'hymba_style_dsa_ssd_rwkv7_gmlp_trunk'


def split_cols(p, sizes):
    cuts = [int(c) for c in np.cumsum(sizes)[:-1]]
    return jnp.split(p, cuts, axis=-1)


def rms_norm(u, g):
    uf = u.astype(jnp.float32)
    y = uf * lax.rsqrt(jnp.mean(uf * uf, axis=-1, keepdims=True) + NORM_EPS)
    return (y * g.astype(jnp.float32)).astype(u.dtype)


def dsa_attention(q, k, v, q_idx, k_idx, w_idx):
    bsz, seq = q.shape[0], q.shape[1]
    top_k = min(IDX_TOPK_MAX, seq // 4)
    key_pos = jnp.arange(seq)
    att_scale = ATT_HEAD_DIM ** -0.5
    idx_scale = IDX_DIM ** -0.5
    head_w_scale = IDX_HEADS ** -0.5
    gather = jax.vmap(lambda table, ids: table[ids])

    def one_block(blk):
        start = blk * Q_BLOCK
        qb = lax.dynamic_slice_in_dim(q, start, Q_BLOCK, axis=1)
        qib = lax.dynamic_slice_in_dim(q_idx, start, Q_BLOCK, axis=1)
        wib = lax.dynamic_slice_in_dim(w_idx, start, Q_BLOCK, axis=1).astype(jnp.float32)
        q_pos = start + jnp.arange(Q_BLOCK)
        causal = key_pos[None, :] <= q_pos[:, None]
        dots = jnp.einsum('bqhd,bsd->bqhs', qib, k_idx).astype(jnp.float32) * idx_scale
        score = jnp.einsum('bqh,bqhs->bqs', wib * head_w_scale, jax.nn.relu(dots))
        score = jnp.where(causal[None], score, -jnp.inf)
        _, sel = lax.top_k(score, top_k)
        k_sel = gather(k, sel)
        v_sel = gather(v, sel)
        logits = jnp.einsum('bqhd,bqjhd->bhqj', qb, k_sel).astype(jnp.float32) * att_scale
        valid = sel <= q_pos[None, :, None]
        logits = jnp.where(valid[:, None], logits, -jnp.inf)
        probs = jax.nn.softmax(logits, axis=-1).astype(v.dtype)
        return jnp.einsum('bhqj,bqjhd->bqhd', probs, v_sel)

    out = lax.map(one_block, jnp.arange(seq // Q_BLOCK))
    return jnp.swapaxes(out, 0, 1).reshape(bsz, seq, ATT_HEADS, ATT_HEAD_DIM)


def attention_branch(p):
    bsz, seq = p.shape[0], p.shape[1]
    q, k, v, z, qi, ki, wi = split_cols(p, ATT_SPLITS)
    heads = lambda t: t.reshape(bsz, seq, ATT_HEADS, ATT_HEAD_DIM)
    o = dsa_attention(heads(q), heads(k), heads(v), qi.reshape(bsz, seq, IDX_HEADS, IDX_DIM), ki, wi)
    return o.reshape(bsz, seq, ATT_W) * jax.nn.silu(z)


def causal_depthwise_conv(u, w, b):
    ch = u.shape[-1]
    out = lax.conv_general_dilated(u, w[:, None, :], window_strides=(1,), padding=[(SSM_CONV - 1, 0)],
                                   dimension_numbers=('NWC', 'WIO', 'NWC'), feature_group_count=ch)
    return out + b


def segsum(a):
    t = a.shape[-1]
    ar = jnp.broadcast_to(a[..., :, None], a.shape + (t,))
    strict = jnp.tril(jnp.ones((t, t), dtype=bool), -1)
    cs = jnp.cumsum(jnp.where(strict, ar, 0.0), axis=-2)
    return jnp.where(jnp.tril(jnp.ones((t, t), dtype=bool)), cs, -jnp.inf)


def ssd_scan(xh, dt, a_head, bg, cg):
    bsz, seq = xh.shape[0], xh.shape[1]
    nc = seq // SSM_CHUNK
    hpg = SSM_HEADS // SSM_GROUPS
    x = (xh * dt[..., None]).reshape(bsz, nc, SSM_CHUNK, SSM_HEADS, SSM_HEAD_DIM)
    a = jnp.transpose((dt * a_head).reshape(bsz, nc, SSM_CHUNK, SSM_HEADS), (0, 3, 1, 2))
    bc = bg.reshape(bsz, nc, SSM_CHUNK, SSM_GROUPS, SSM_STATE)
    cc = cg.reshape(bsz, nc, SSM_CHUNK, SSM_GROUPS, SSM_STATE)
    a_cum = jnp.cumsum(a, axis=-1)
    decay_in = jnp.exp(segsum(a))
    cb = jnp.repeat(jnp.einsum('bclgn,bcsgn->bgcls', cc, bc), hpg, axis=1)
    y_diag = jnp.einsum('bhcls,bcshp->bclhp', cb * decay_in, x)
    bh = jnp.repeat(bc, hpg, axis=3)
    ch = jnp.repeat(cc, hpg, axis=3)
    decay_states = jnp.exp(a_cum[..., -1:] - a_cum)
    states = jnp.einsum('bclhn,bhcl,bclhp->bchpn', bh, decay_states, x)
    chunk_decay = jnp.exp(a_cum[..., -1])

    def step(h, inp):
        s_c, d_c = inp
        return h * d_c[..., None, None] + s_c, h

    h0 = jnp.zeros((bsz, SSM_HEADS, SSM_HEAD_DIM, SSM_STATE), xh.dtype)
    _, init_states = lax.scan(step, h0, (jnp.transpose(states, (1, 0, 2, 3, 4)), jnp.transpose(chunk_decay, (2, 0, 1))))
    init_states = jnp.transpose(init_states, (1, 0, 2, 3, 4))
    y_off = jnp.einsum('bclhn,bchpn,bhcl->bclhp', ch, init_states, jnp.exp(a_cum))
    return (y_diag + y_off).reshape(bsz, seq, SSM_HEADS, SSM_HEAD_DIM)


def ssm_branch(p, conv_w, conv_b, dt_bias, a_log, d_skip, norm_g):
    bsz, seq = p.shape[0], p.shape[1]
    z, xbc, dt_raw = split_cols(p, SSM_SPLITS)
    xbc = jax.nn.silu(causal_depthwise_conv(xbc, conv_w, conv_b)).astype(jnp.float32)
    xs, bs, cs = split_cols(xbc, XBC_SPLITS)
    dt = jax.nn.softplus(dt_raw.astype(jnp.float32) + dt_bias.astype(jnp.float32))
    a_head = -jnp.exp(a_log.astype(jnp.float32))
    xh = xs.reshape(bsz, seq, SSM_HEADS, SSM_HEAD_DIM)
    y = ssd_scan(xh, dt, a_head,
                 bs.reshape(bsz, seq, SSM_GROUPS, SSM_STATE), cs.reshape(bsz, seq, SSM_GROUPS, SSM_STATE))
    y = y + d_skip.astype(jnp.float32)[:, None] * xh
    y = y.reshape(bsz, seq, SSM_W) * jax.nn.silu(z.astype(jnp.float32))
    yg = y.reshape(bsz, seq, SSM_GROUPS, SSM_W // SSM_GROUPS)
    yg = yg * lax.rsqrt(jnp.mean(yg * yg, axis=-1, keepdims=True) + NORM_EPS)
    return (yg.reshape(bsz, seq, SSM_W) * norm_g.astype(jnp.float32)).astype(p.dtype)


def rwkv7_scan(r, w, k, v, a, b):
    bsz, seq = r.shape[0], r.shape[1]
    tm = lambda t: jnp.moveaxis(t, 1, 0)

    def step(st, inp):
        r_t, w_t, k_t, v_t, a_t, b_t = inp
        sa = jnp.einsum('bhij,bhj->bhi', st, a_t)
        st = st * w_t[:, :, None, :] + sa[..., None] * b_t[:, :, None, :] + v_t[..., None] * k_t[:, :, None, :]
        return st, jnp.einsum('bhij,bhj->bhi', st, r_t)

    s0 = jnp.zeros((bsz, RWKV_HEADS, RWKV_HEAD_DIM, RWKV_HEAD_DIM), r.dtype)
    _, ys = lax.scan(step, s0, (tm(r), tm(w), tm(k), tm(v), tm(a), tm(b)))
    return jnp.moveaxis(ys, 0, 1)


def rwkv_branch(p, mu, w0, w2, a0, a2, k_k, k_a, r_k, gn_g, gn_b):
    bsz, seq = p.shape[0], p.shape[1]
    pf = p.astype(jnp.float32)
    prev = jnp.pad(pf[:, :-1], ((0, 0), (1, 0), (0, 0)))
    pf = pf + (prev - pf) * mu.astype(jnp.float32)
    r, k, v, z, wl, al = split_cols(pf, RWKV_SPLITS)
    w_raw = -jax.nn.softplus(-(w0 + jnp.tanh(wl) @ w2)) - 0.5
    decay = jnp.exp(-jnp.exp(w_raw))
    a = jax.nn.sigmoid(a0 + al @ a2)
    kk = k * k_k
    k = k * (1.0 + (a - 1.0) * k_a)
    heads = lambda t: t.reshape(bsz, seq, RWKV_HEADS, RWKV_HEAD_DIM)
    r, k, v, kk, a, decay = heads(r), heads(k), heads(v), heads(kk), heads(a), heads(decay)
    kk = kk * lax.rsqrt(jnp.sum(kk * kk, axis=-1, keepdims=True) + 1e-12)
    y = rwkv7_scan(r, decay, k, v, -kk, kk * a)
    mean = jnp.mean(y, axis=-1, keepdims=True)
    var = jnp.mean(jnp.square(y - mean), axis=-1, keepdims=True)
    yn = ((y - mean) * lax.rsqrt(var + GN_EPS)).reshape(bsz, seq, RWKV_W) * gn_g + gn_b
    bonus = (jnp.sum(r * k * r_k, axis=-1, keepdims=True) * v).reshape(bsz, seq, RWKV_W)
    return ((yn + bonus) * jax.nn.silu(z)).astype(p.dtype)


def mlp_branch(p, ln_g, ln_b, w_s, b_s):
    bsz, seq = p.shape[0], p.shape[1]
    u, v, z = split_cols(p, MLP_SPLITS)
    vf = v.astype(jnp.float32)
    mean = jnp.mean(vf, axis=-1, keepdims=True)
    var = jnp.mean(jnp.square(vf - mean), axis=-1, keepdims=True)
    vn = (vf - mean) * lax.rsqrt(var + NORM_EPS) * ln_g + ln_b
    vc = vn.reshape(bsz, seq // MLP_CHUNK, MLP_CHUNK, MLP_GROUPS, MLP_W // MLP_GROUPS)
    w_causal = w_s * jnp.tril(jnp.ones((MLP_CHUNK, MLP_CHUNK), dtype=w_s.dtype))
    vm = jnp.einsum('gts,bcsgd->bctgd', w_causal, vc.astype(w_s.dtype)) + jnp.transpose(b_s)[:, :, None]
    return u * vm.reshape(bsz, seq, MLP_W).astype(u.dtype) * jax.nn.silu(z)


def hybrid_layer(x, norm_g, w_in, w_out, conv_w, conv_b, dt_bias, a_log, d_skip, ssm_norm_g,
                 mu, w0, w2, a0, a2, k_k, k_a, r_k, gn_g, gn_b, ln_g, ln_b, w_s, b_s):
    h = rms_norm(x, norm_g)
    proj = jnp.einsum('bsd,de->bse', h, w_in)
    p_att, p_ssm, p_rwkv, p_mlp = split_cols(proj, BRANCH_SPLITS)
    y = jnp.concatenate([
        attention_branch(p_att),
        ssm_branch(p_ssm, conv_w, conv_b, dt_bias, a_log, d_skip, ssm_norm_g),
        rwkv_branch(p_rwkv, mu, w0, w2, a0, a2, k_k, k_a, r_k, gn_g, gn_b),
        mlp_branch(p_mlp, ln_g, ln_b, w_s, b_s),
    ], axis=-1).astype(x.dtype)
    return x + jnp.einsum('bse,ed->bsd', y, w_out)


def setup_inputs(seed: int = 0) -> dict:
    key = jax.random.key(seed)
    ks = jax.random.split(key, 26)
    f32 = jnp.float32
    nrm = lambda k, shape, s: s * jax.random.normal(k, shape, f32)
    x = jax.random.normal(ks[0], (BATCH, SEQ, D_MODEL), f32)
    norm_g = 1.0 + nrm(ks[1], (DEPTH, D_MODEL), 0.1)
    w_in = nrm(ks[2], (DEPTH, D_MODEL, D_IN), D_MODEL ** -0.5)
    w_out = nrm(ks[3], (DEPTH, D_MIX, D_MODEL), D_MIX ** -0.5)
    ssm_conv_w = nrm(ks[4], (DEPTH, SSM_CONV, SSM_CONV_CH), SSM_CONV ** -0.5)
    ssm_conv_b = nrm(ks[5], (DEPTH, SSM_CONV_CH), 0.02)
    dt = jnp.exp(jax.random.uniform(ks[6], (DEPTH, SSM_HEADS), f32, math.log(1e-3), math.log(1e-1)))
    ssm_dt_bias = dt + jnp.log(-jnp.expm1(-dt))
    ssm_A_log = jnp.log(jax.random.uniform(ks[7], (DEPTH, SSM_HEADS), f32, 1.0, 16.0))
    ssm_D = 1.0 + nrm(ks[8], (DEPTH, SSM_HEADS), 0.1)
    ssm_norm_g = 1.0 + nrm(ks[9], (DEPTH, SSM_W), 0.1)
    rwkv_mu = jax.random.uniform(ks[10], (DEPTH, RWKV_PROJ), f32, 0.0, 1.0)
    rwkv_w0 = jax.random.uniform(ks[11], (DEPTH, RWKV_W), f32, -6.0, 1.0)
    rwkv_w2 = nrm(ks[12], (DEPTH, RWKV_LORA_W, RWKV_W), 0.1)
    rwkv_a0 = nrm(ks[13], (DEPTH, RWKV_W), 0.5)
    rwkv_a2 = nrm(ks[14], (DEPTH, RWKV_LORA_A, RWKV_W), 0.1)
    rwkv_k_k = 0.85 + nrm(ks[15], (DEPTH, RWKV_W), 0.05)
    rwkv_k_a = 1.0 + nrm(ks[16], (DEPTH, RWKV_W), 0.05)
    rwkv_r_k = nrm(ks[17], (DEPTH, RWKV_HEADS, RWKV_HEAD_DIM), 0.1)
    rwkv_gn_g = 1.0 + nrm(ks[18], (DEPTH, RWKV_W), 0.1)
    rwkv_gn_b = nrm(ks[19], (DEPTH, RWKV_W), 0.02)
    mlp_ln_g = 1.0 + nrm(ks[20], (DEPTH, MLP_W), 0.1)
    mlp_ln_b = nrm(ks[21], (DEPTH, MLP_W), 0.02)
    mlp_w_s = nrm(ks[22], (DEPTH, MLP_GROUPS, MLP_CHUNK, MLP_CHUNK), MLP_CHUNK ** -0.5)
    mlp_b_s = 1.0 + nrm(ks[23], (DEPTH, MLP_GROUPS, MLP_CHUNK), 0.1)
    final_norm_g = 1.0 + nrm(ks[24], (D_MODEL,), 0.1)
    return {'x': x, 'norm_g': norm_g, 'w_in': w_in, 'w_out': w_out,
            'ssm_conv_w': ssm_conv_w, 'ssm_conv_b': ssm_conv_b, 'ssm_dt_bias': ssm_dt_bias,
            'ssm_A_log': ssm_A_log, 'ssm_D': ssm_D, 'ssm_norm_g': ssm_norm_g,
            'rwkv_mu': rwkv_mu, 'rwkv_w0': rwkv_w0, 'rwkv_w2': rwkv_w2, 'rwkv_a0': rwkv_a0,
            'rwkv_a2': rwkv_a2, 'rwkv_k_k': rwkv_k_k, 'rwkv_k_a': rwkv_k_a, 'rwkv_r_k': rwkv_r_k,
            'rwkv_gn_g': rwkv_gn_g, 'rwkv_gn_b': rwkv_gn_b,
            'mlp_ln_g': mlp_ln_g, 'mlp_ln_b': mlp_ln_b, 'mlp_w_s': mlp_w_s, 'mlp_b_s': mlp_b_s,
            'final_norm_g': final_norm_g}


def reference(x, norm_g, w_in, w_out, ssm_conv_w, ssm_conv_b, ssm_dt_bias, ssm_A_log, ssm_D, ssm_norm_g,
              rwkv_mu, rwkv_w0, rwkv_w2, rwkv_a0, rwkv_a2, rwkv_k_k, rwkv_k_a, rwkv_r_k, rwkv_gn_g, rwkv_gn_b,
              mlp_ln_g, mlp_ln_b, mlp_w_s, mlp_b_s, final_norm_g):
    h = x
    for l in range(DEPTH):
        h = hybrid_layer(h, norm_g[l], w_in[l], w_out[l],
                         ssm_conv_w[l], ssm_conv_b[l], ssm_dt_bias[l], ssm_A_log[l], ssm_D[l], ssm_norm_g[l],
                         rwkv_mu[l], rwkv_w0[l], rwkv_w2[l], rwkv_a0[l], rwkv_a2[l], rwkv_k_k[l], rwkv_k_a[l],
                         rwkv_r_k[l], rwkv_gn_g[l], rwkv_gn_b[l],
                         mlp_ln_g[l], mlp_ln_b[l], mlp_w_s[l], mlp_b_s[l])
    return rms_norm(h, final_norm_g)
```

```python
import contextlib
import numpy as np
import ml_dtypes
import concourse.bass as bass
import concourse.mybir as mybir
from concourse.bass_utils import run_bass_kernel_spmd

F32 = mybir.dt.float32
BF16 = mybir.dt.bfloat16
ALU = mybir.AluOpType
AF = mybir.ActivationFunctionType
AX = mybir.AxisListType

D_MODEL = 4096
SEQ = 4096
NCORE = 8
NQ = 4
OWN = SEQ // NQ
NORM_EPS = 1e-5
GN_EPS = 64e-5

COMPUTE = ("pe", "act", "dve", "pool")


SEM_ROLL = 30000


class SemState:
    def __init__(self, nc):
        self.nc = nc
        self.st = contextlib.ExitStack()
        self.sems = {}
        self.count = {e: 0 for e in COMPUTE}
        self.dma_slots = {}
        self.dma_rr = {}
        self.n_dma_slots = 8

    def handle(self, key):
        h = self.sems.get(key)
        if h is None:
            h = self.st.enter_context(self.nc.semaphore("s_" + "_".join(str(x) for x in key)))
            self.sems[key] = h
        return h


def semstate(nc):
    ss = getattr(nc, "_semstate", None)
    if ss is None:
        ss = SemState(nc)
        nc._semstate = ss
    return ss


class Prog:
    def __init__(self, nc):
        self.nc = nc
        self.ss = semstate(nc)
        self.streams = {e: [] for e in ("pe", "act", "dve", "pool", "sp")}
        self.last_writer = {}
        self.readers = {}
        self.waited = {}
        self.used_keys = []
        self.excl = set()

    def _semkey(self, key):
        if key not in self.used_keys:
            self.used_keys.append(key)
        return key

    def _deps_for(self, reads, writes, eng=None):
        deps = set()
        for b in reads:
            w = self.last_writer.get(b)
            if w is not None:
                deps.add(w)
        for b in writes:
            w = self.last_writer.get(b)
            if w is not None:
                deps.add(w)
            for r in self.readers.get(b, ()):
                if eng is not None and r[0][0] == eng:
                    continue
                deps.add(r)
        if eng == "pe":
            deps = {d for d in deps if d[0][0] != "pe"}
        return deps

    def _commit(self, tok, reads, writes):
        for b in reads:
            self.readers.setdefault(b, []).append(tok)
        for b in writes:
            self.last_writer[b] = tok
            self.readers[b] = []

    def _waits(self, eng, deps):
        waits = []
        for (k, v) in sorted(deps, key=lambda t: (str(t[0]), t[1])):
            if self.waited.get((eng, k), -1) >= v:
                continue
            self.waited[(eng, k)] = v
            self._semkey(k)
            waits.append((k, v))
        return waits

    def op(self, eng, fn, reads=(), writes=()):
        ex = [b for b in reads if (b[0] if isinstance(b, tuple) else b) in self.excl]
        if ex:
            writes = list(writes) + [b for b in ex if b not in writes]
        ss = self.ss
        idx = ss.count[eng]
        ss.count[eng] += 1
        ep = idx // SEM_ROLL
        key = self._semkey((eng, ep))
        deps = self._deps_for(reads, writes, eng=eng)
        waits = self._waits(eng, deps)
        self.streams[eng].append((waits, fn, (key, 1)))
        tok = (key, idx - ep * SEM_ROLL + 1)
        self._commit(tok, reads, writes)
        return tok

    def dma(self, eng, fn, reads=(), writes=()):
        ss = self.ss
        slots = ss.dma_slots.get(eng)
        if slots is None:
            slots = [[("dma", eng, i, 0), 0] for i in range(ss.n_dma_slots)]
            ss.dma_slots[eng] = slots
        rr = ss.dma_rr.get(eng, 0)
        ss.dma_rr[eng] = (rr + 1) % len(slots)
        slot = slots[rr]
        deps = self._deps_for(reads, writes)
        if slot[1] > 0:
            deps.add((slot[0], slot[1]))
        if slot[1] + 16 > SEM_ROLL:
            slot[0] = ("dma", eng, slot[0][2], slot[0][3] + 1)
            slot[1] = 0
        key = self._semkey(slot[0])
        waits = self._waits(eng, deps)
        slot[1] += 16
        self.streams[eng].append((waits, fn, (key, 16)))
        tok = (key, slot[1])
        self._commit(tok, reads, writes)
        return tok

    def finish(self, eng="sp"):
        toks = set()
        ss = self.ss
        for q, slots in ss.dma_slots.items():
            for key, cnt in slots:
                if cnt > 0:
                    toks.add((key, cnt))
        for e in COMPUTE:
            n = ss.count[e]
            if n > 0:
                ep = (n - 1) // SEM_ROLL
                toks.add(((e, ep), n - ep * SEM_ROLL))
        waits = self._waits(eng, toks)
        self.streams[eng].append((waits, None, None))

    def emit(self):
        nc = self.nc
        ss = self.ss
        sems = {k: ss.handle(k) for k in self.used_keys}
        with nc.Block() as block:

            def run(engobj, items):
                for waits, fn, inc in items:
                    for (k, v) in waits:
                        engobj.wait_ge(sems[k], v)
                    if fn is not None:
                        ins = fn(engobj)
                        ins.then_inc(sems[inc[0]], inc[1])

            @block.tensor
            def _(e):
                run(e, self.streams["pe"])

            @block.scalar
            def _(e):
                run(e, self.streams["act"])

            @block.vector
            def _(e):
                run(e, self.streams["dve"])

            @block.gpsimd
            def _(e):
                run(e, self.streams["pool"])

            @block.sync
            def _(e):
                run(e, self.streams["sp"])


class Phase:
    _count = [0]

    def __init__(self, nc):
        self.nc = nc
        self.st = contextlib.ExitStack()
        self.P = Prog(nc)
        self.n = 0
        Phase._count[0] += 1
        self.pid = Phase._count[0]

    def sb(self, shape, dt, name=None):
        self.n += 1
        return self.st.enter_context(self.nc.sbuf_tensor(f"{name or 't'}_{self.n}_{self.pid}", list(shape), dt))

    def ps(self, shape, dt, name=None):
        self.n += 1
        return self.st.enter_context(self.nc.psum_tensor(f"{name or 'p'}_{self.n}_{self.pid}", list(shape), dt))

    def dump(self, name, sb_ap, reads):
        dbg = getattr(self.nc, "_dbg", None)
        if not dbg or name not in dbg:
            return
        d = dbg[name]
        self.P.dma("sp", lambda e: e.dma_start(out=d, in_=sb_ap), reads=reads, writes=[("dbg", name)])

    def close(self):
        self.P.finish()
        self.P.emit()
        self.st.close()


ATT0, SSM0, RWKV0, MLP0 = 0, 5200, 8288, 12512


def col_groups(q):
    r = lambda a, n: np.arange(a, a + n)
    att_q, att_k, att_v, att_z = r(0, 1024), r(1024, 1024), r(2048, 1024), r(3072, 1024)
    att_qi, att_ki, att_wi = r(4096, 1024), r(5120, 64), r(5184, 16)
    ssm_z = r(SSM0 + 256 * q, 256)
    ssm_x = r(SSM0 + 1024 + 256 * q, 256)
    ssm_B = r(SSM0 + 2048 + 128 * q, 128)
    ssm_C = r(SSM0 + 2560 + 128 * q, 128)
    ssm_dt = r(SSM0 + 3072 + 4 * q, 4)
    rw = [r(RWKV0 + 1024 * i + 256 * q, 256) for i in range(4)]
    rw_wl, rw_al = r(RWKV0 + 4096, 64), r(RWKV0 + 4160, 64)
    mlp = [r(MLP0 + 1024 * i, 1024) for i in range(3)]
    cat = np.concatenate
    return {
        "fmb_all": cat([att_k, att_ki, att_ki]),
        "fmf_all": cat([ssm_x, ssm_B, ssm_C, rw_wl, rw_al]),
        "tmb_all": att_v,
        "tmf_all": cat([ssm_z, ssm_dt] + rw),
        "fmb_own": cat([att_q, att_qi]),
        "tmf_own": cat([att_z, att_wi] + mlp),
    }


GROUP_INFO = {
    "fmb_all": (1152, "fm", BF16, "all"),
    "fmf_all": (640, "fm", F32, "all"),
    "tmb_all": (1024, "tm", BF16, "all"),
    "tmf_all": (1284, "tm", F32, "all"),
    "fmb_own": (2048, "fm", BF16, "own"),
    "tmf_own": (4112, "tm", F32, "own"),
}


def phase_inproj(nc, x_all, x_own, g, ident_d, wts, scr, plan=None, ginfo=None):
    ph = Phase(nc)
    P = ph.P
    D = D_MODEL
    KT = D // 128
    T = 1024
    TT = T // 128
    KQ = 8
    NKQ = KT // KQ
    NB = 512
    ident = ph.sb([128, 128], BF16, "ident")
    hT = ph.sb([128, KT, T], BF16, "hT")
    xt = [ph.sb([128, D], F32, "xt") for _ in range(2)]
    hb = [ph.sb([128, D], BF16, "hb") for _ in range(2)]
    wb = [ph.sb([128, KT, NB], BF16, "wb") for _ in range(2)]
    stg = [ph.sb([128, NB], F32, "stg") for _ in range(4)]
    stgb = [ph.sb([128, NB], BF16, "stgb") for _ in range(4)]
    gb = ph.sb([128, D], F32, "gb")
    ss = ph.sb([128, 8 * 5], F32, "ss")
    rstd = ph.sb([128, 8 * 5], F32, "rstd")
    pT = [ph.ps([128, 1024], BF16, "pT") for _ in range(2)]
    pM = [ph.ps([128, 512], F32, "pM") for _ in range(6)]

    P.dma("sp", lambda e: e.dma_start(out=ident[:], in_=ident_d[:, :]), writes=["ident"])
    P.dma("sp", lambda e: e.dma_start(out=gb[:], in_=g[0:1, :].partition_broadcast(128)), writes=["gb"])
    cnt = {"t": 0, "m": 0, "w": 0, "x": 0}
    P.op("dve", lambda e: e.memset(ss[:], 0.0), writes=[("ss", i) for i in range(40)])

    def load_hT(x_ap, row0, pidx):
        for tt in range(TT):
            i = cnt["x"] % 2
            cnt["x"] += 1
            xb, hbb = xt[i], hb[i]
            sc = pidx * 8 + tt
            P.dma("sp", lambda e, xb=xb, tt=tt: e.dma_start(out=xb[:], in_=x_ap[row0 + tt * 128:row0 + (tt + 1) * 128, :]),
                  writes=[("xt", i)])
            P.op("act", lambda e, xb=xb, hbb=hbb, sc=sc: e.activation(out=hbb[:], in_=xb[:], func=AF.Square,
                                                                     accum_out=ss[:, sc:sc + 1]),
                 reads=[("xt", i)], writes=[("hb", i), ("ss", sc)])
            P.op("dve", lambda e, sc=sc: e.tensor_scalar(out=rstd[:, sc:sc + 1], in0=ss[:, sc:sc + 1],
                                                         scalar1=1.0 / D, scalar2=NORM_EPS, op0=ALU.mult, op1=ALU.add),
                 reads=[("ss", sc)], writes=[("rstd", sc)])
            P.op("act", lambda e, sc=sc: e.activation(out=rstd[:, sc:sc + 1], in_=rstd[:, sc:sc + 1], func=AF.Sqrt),
                 reads=[("rstd", sc)], writes=[("rstd", sc)])
            P.op("dve", lambda e, sc=sc: e.reciprocal(out=rstd[:, sc:sc + 1], in_=rstd[:, sc:sc + 1]),
                 reads=[("rstd", sc)], writes=[("rstd", sc)])
            P.op("dve", lambda e, xb=xb, hbb=hbb, sc=sc: e.scalar_tensor_tensor(
                out=hbb[:], in0=xb[:], scalar=rstd[:, sc:sc + 1], in1=gb[:], op0=ALU.mult, op1=ALU.mult),
                reads=[("xt", i), ("rstd", sc), "gb", ("hb", i)], writes=[("hb", i)])
            for kq in range(NKQ):
                pb = cnt["t"] % 2
                cnt["t"] += 1
                for k8 in range(KQ):
                    kt = kq * KQ + k8
                    P.op("pe", lambda e, pb=pb, k8=k8, kt=kt, hbb=hbb: e.transpose(
                        out=pT[pb][:, k8 * 128:(k8 + 1) * 128], in_=hbb[:, kt * 128:(kt + 1) * 128], identity=ident[:]),
                        reads=[("hb", i), "ident"], writes=[("pT", pb)])
                dst = hT[:, kq * KQ:(kq + 1) * KQ, tt * 128:(tt + 1) * 128]
                src = pT[pb][:].rearrange("p (k t) -> p k t", k=KQ)
                if kq % 2 == 0:
                    P.op("dve", lambda e, dst=dst, src=src: e.tensor_copy(out=dst, in_=src),
                         reads=[("pT", pb)], writes=[("hT", tt, kq)])
                else:
                    P.op("act", lambda e, dst=dst, src=src: e.copy(out=dst, in_=src),
                         reads=[("pT", pb)], writes=[("hT", tt, kq)])

    def evac_store(j, n_part, nfree, is_bf16, dst_ap, okey):
        s = cnt["m"] % 4
        use_dve = (cnt["m"] % 2 == 0)
        cnt["m"] += 1
        sbuf = (stgb if is_bf16 else stg)[s]
        skey = ("stgb" if is_bf16 else "stg", s)
        if use_dve:
            P.op("dve", lambda e: e.tensor_copy(out=sbuf[0:n_part, 0:nfree], in_=pM[j][0:n_part, 0:nfree]),
                 reads=[("pM", j)], writes=[skey])
        else:
            P.op("act", lambda e: e.copy(out=sbuf[0:n_part, 0:nfree], in_=pM[j][0:n_part, 0:nfree]),
                 reads=[("pM", j)], writes=[skey])
        P.dma("sp", lambda e: e.dma_start(out=dst_ap, in_=sbuf[0:n_part, 0:nfree]), reads=[skey], writes=[okey])

    def do_group(name, tok0):
        ncols, layout, dt, _ = (ginfo or GROUP_INFO)[name]
        w = wts[name]
        dst = scr[name]
        wv = w.rearrange("(kt p) n -> p kt n", p=128)
        is_bf = (dt == BF16)
        for c0 in range(0, ncols, NB):
            nb = min(NB, ncols - c0)
            wi = cnt["w"] % 2
            cnt["w"] += 1
            wbb = wb[wi]
            for kq in range(NKQ):
                P.dma("pool", lambda e, wbb=wbb, kq=kq, c0=c0, nb=nb: e.dma_start(
                    out=wbb[:, kq * KQ:(kq + 1) * KQ, 0:nb], in_=wv[:, kq * KQ:(kq + 1) * KQ, c0:c0 + nb]),
                    writes=[("wb", wi, kq)])
            if layout == "tm":
                for tt in range(TT):
                    j = cnt["m"] % 6
                    for kt in range(KT):
                        P.op("pe", lambda e, j=j, kt=kt, tt=tt, wbb=wbb, nb=nb: e.matmul(
                            pM[j][:, 0:nb], lhsT=hT[:, kt, tt * 128:(tt + 1) * 128], rhs=wbb[:, kt, 0:nb],
                            start=(kt == 0), stop=(kt == KT - 1)),
                            reads=[("hT", tt, kt // KQ), ("wb", wi, kt // KQ)], writes=[("pM", j)])
                    r0 = tok0 + tt * 128
                    evac_store(j, 128, nb, is_bf, dst[r0:r0 + 128, c0:c0 + nb], (name, "o", tok0, tt, c0))
            else:
                for ct in range(nb // 128):
                    for th in range(T // 512):
                        j = cnt["m"] % 6
                        for kt in range(KT):
                            P.op("pe", lambda e, j=j, kt=kt, th=th, ct=ct, wbb=wbb: e.matmul(
                                pM[j][:, 0:512], lhsT=wbb[:, kt, ct * 128:(ct + 1) * 128],
                                rhs=hT[:, kt, th * 512:(th + 1) * 512], start=(kt == 0), stop=(kt == KT - 1)),
                                reads=[("hT", 4 * th, kt // KQ), ("hT", 4 * th + 1, kt // KQ), ("hT", 4 * th + 2, kt // KQ),
                                       ("hT", 4 * th + 3, kt // KQ), ("wb", wi, kt // KQ)], writes=[("pM", j)])
                        cc = c0 + ct * 128
                        t0 = tok0 + th * 512
                        evac_store(j, 128, 512, is_bf, dst[cc:cc + 128, t0:t0 + 512], (name, "o", tok0, th, cc))

    if plan is None:
        plan = [(x_all, p * 1024, [(n, p * 1024) for n in ("fmb_all", "fmf_all", "tmb_all", "tmf_all")]) for p in range(4)]
        plan.append((x_own, 0, [("fmb_own", 0), ("tmf_own", 0)]))
    for pidx, (xsrc, row0, glist) in enumerate(plan):
        load_hT(xsrc, row0, pidx)
        for name, tok0 in glist:
            do_group(name, tok0)
    ph.close()


def make_scratch(nc, kind=None):
    scr = {}
    for name, (ncols, layout, dt, which) in GROUP_INFO.items():
        ntok = SEQ if which == "all" else OWN
        shape = [ntok, ncols] if layout == "tm" else [ncols, ntok]
        if kind:
            scr[name] = nc.dram_tensor("scr_" + name, shape, dt, kind=kind).ap()
        else:
            scr[name] = nc.dram_tensor("scr_" + name, shape, dt).ap()
    return scr


TMF_OWN_ATTZ, TMF_OWN_WI, TMF_OWN_U, TMF_OWN_V, TMF_OWN_Z = 0, 1024, 1040, 2064, 3088


def phase_mlp(nc, scr, prm, y_own, nchunks=OWN // 128, ycol0=1024):
    ph = Phase(nc)
    P = ph.P
    src = scr["tmf_own"]
    W = 1024
    gb = ph.sb([128, W], F32, "lng")
    bb = ph.sb([128, W], F32, "lnb")
    wsT = ph.sb([128, 8, 128], F32, "wsT")
    wcT = ph.sb([128, 8, 128], BF16, "wcT")
    tri = ph.sb([128, 128], F32, "tri")
    bsT = ph.sb([128, 8], F32, "bsT")
    bbc = ph.sb([128, 8, 128], F32, "bbc")
    zero = ph.sb([128, 128], F32, "zero")
    NBUF = 2
    ut = [ph.sb([128, W], F32, "u") for _ in range(NBUF)]
    vt = [ph.sb([128, W], F32, "v") for _ in range(NBUF)]
    zt = [ph.sb([128, W], F32, "z") for _ in range(NBUF)]
    vn = [ph.sb([128, W], F32, "vn") for _ in range(NBUF)]
    vnb = [ph.sb([128, W], BF16, "vnb") for _ in range(NBUF)]
    junk = ph.sb([128, W], BF16, "junk")
    t1 = [ph.sb([128, W], F32, "t1") for _ in range(NBUF)]
    st = ph.sb([128, 8 * nchunks], F32, "stats")
    pV = [ph.ps([128, 512], F32, "pV") for _ in range(4)]

    P.dma("sp", lambda e: e.dma_start(out=gb[:], in_=prm["mlp_ln_g"][0:1, :].partition_broadcast(128)), writes=["gb"])
    P.dma("sp", lambda e: e.dma_start(out=bb[:], in_=prm["mlp_ln_b"][0:1, :].partition_broadcast(128)), writes=["bb"])
    P.dma("sp", lambda e: e.dma_start(out=wsT[:], in_=prm["mlp_wsT"][:, :, :]), writes=["wsT"])
    P.dma("sp", lambda e: e.dma_start(out=tri[:], in_=prm["tri_le"][:, :]), writes=["tri"])
    P.dma("sp", lambda e: e.dma_start(out=bsT[:], in_=prm["mlp_bsT"][:, :]), writes=["bsT"])
    P.op("dve", lambda e: e.memset(zero[:], 0.0), writes=["zero"])
    P.op("dve", lambda e: e.memset(st[:], 0.0), writes=[("st", c) for c in range(nchunks)])
    for g in range(8):
        P.op("dve", lambda e, g=g: e.tensor_tensor(out=wcT[:, g, :], in0=wsT[:, g, :], in1=tri[:], op=ALU.mult),
             reads=["wsT", "tri"], writes=["wcT"])
        P.op("dve", lambda e, g=g: e.tensor_scalar(out=bbc[:, g, :], in0=zero[:], scalar1=bsT[:, g:g + 1], scalar2=None,
                                                   op0=ALU.add), reads=["zero", "bsT"], writes=["bbc"])
    for c in range(nchunks):
        i = c % NBUF
        r0 = c * 128
        P.dma("sp", lambda e, i=i, r0=r0: e.dma_start(out=ut[i][:], in_=src[r0:r0 + 128, TMF_OWN_U:TMF_OWN_U + W]),
              writes=[("u", i)])
        P.dma("sp", lambda e, i=i, r0=r0: e.dma_start(out=vt[i][:], in_=src[r0:r0 + 128, TMF_OWN_V:TMF_OWN_V + W]),
              writes=[("v", i)])
        P.dma("sp", lambda e, i=i, r0=r0: e.dma_start(out=zt[i][:], in_=src[r0:r0 + 128, TMF_OWN_Z:TMF_OWN_Z + W]),
              writes=[("z", i)])
        s0 = c * 8
        P.op("act", lambda e, i=i, s0=s0: e.activation(out=junk[:], in_=vt[i][:], func=AF.Square,
                                                       accum_out=st[:, s0 + 1:s0 + 2]),
             reads=[("v", i)], writes=["junk", ("st", c)])
        P.op("dve", lambda e, i=i, s0=s0: e.reduce_sum(out=st[:, s0:s0 + 1], in_=vt[i][:], axis=AX.X),
             reads=[("v", i), ("st", c)], writes=[("st", c)])
        P.op("dve", lambda e, s0=s0: e.tensor_scalar(out=st[:, s0:s0 + 2], in0=st[:, s0:s0 + 2], scalar1=1.0 / W,
                                                     scalar2=None, op0=ALU.mult), reads=[("st", c)], writes=[("st", c)])
        P.op("dve", lambda e, s0=s0: e.tensor_tensor(out=st[:, s0 + 2:s0 + 3], in0=st[:, s0:s0 + 1], in1=st[:, s0:s0 + 1],
                                                     op=ALU.mult), reads=[("st", c)], writes=[("st", c)])
        P.op("dve", lambda e, s0=s0: e.tensor_tensor(out=st[:, s0 + 3:s0 + 4], in0=st[:, s0 + 1:s0 + 2],
                                                     in1=st[:, s0 + 2:s0 + 3], op=ALU.subtract),
             reads=[("st", c)], writes=[("st", c)])
        P.op("dve", lambda e, s0=s0: e.tensor_scalar(out=st[:, s0 + 3:s0 + 4], in0=st[:, s0 + 3:s0 + 4], scalar1=NORM_EPS,
                                                     scalar2=None, op0=ALU.add), reads=[("st", c)], writes=[("st", c)])
        P.op("act", lambda e, s0=s0: e.activation(out=st[:, s0 + 3:s0 + 4], in_=st[:, s0 + 3:s0 + 4], func=AF.Sqrt),
             reads=[("st", c)], writes=[("st", c)])
        P.op("dve", lambda e, s0=s0: e.reciprocal(out=st[:, s0 + 3:s0 + 4], in_=st[:, s0 + 3:s0 + 4]),
             reads=[("st", c)], writes=[("st", c)])
        P.op("dve", lambda e, i=i, s0=s0: e.tensor_scalar(out=vn[i][:], in0=vt[i][:], scalar1=st[:, s0:s0 + 1],
                                                          scalar2=st[:, s0 + 3:s0 + 4], op0=ALU.subtract, op1=ALU.mult),
             reads=[("v", i), ("st", c)], writes=[("vn", i)])
        P.op("pool", lambda e, i=i: e.tensor_tensor(out=vn[i][:], in0=vn[i][:], in1=gb[:], op=ALU.mult),
             reads=[("vn", i), "gb"], writes=[("vn", i)])
        P.op("dve", lambda e, i=i: e.tensor_tensor(out=vnb[i][:], in0=vn[i][:], in1=bb[:], op=ALU.add),
             reads=[("vn", i), "bb"], writes=[("vnb", i)])
        if c == 0:
            ph.dump("mlp_st", st[:, 0:8], [("st", c)])
            ph.dump("mlp_vn", vn[i][:], [("vn", i)])
            ph.dump("mlp_vnb", vnb[i][:], [("vnb", i)])
        for hf in range(2):
            pj = (2 * c + hf) % 4
            for g4 in range(4):
                g = hf * 4 + g4
                P.op("pe", lambda e, pj=pj, g=g, g4=g4, i=i: e.matmul(
                    pV[pj][:, g4 * 128:(g4 + 1) * 128], lhsT=wcT[:, g, :], rhs=vnb[i][:, g * 128:(g + 1) * 128],
                    start=True, stop=True), reads=["wcT", ("vnb", i)], writes=[("pV", pj)])
            P.op("dve", lambda e, pj=pj, hf=hf, i=i: e.tensor_tensor(
                out=t1[i][:, hf * 512:(hf + 1) * 512], in0=pV[pj][:],
                in1=bbc[:, hf * 4:(hf + 1) * 4, :].rearrange("p g d -> p (g d)"), op=ALU.add),
                reads=[("pV", pj), "bbc"], writes=[("t1", i, hf)])
        if c == 0:
            ph.dump("mlp_t1", t1[i][:], [("t1", i, 0), ("t1", i, 1)])
        P.op("pool", lambda e, i=i: e.tensor_tensor(out=t1[i][:], in0=t1[i][:], in1=ut[i][:], op=ALU.mult),
             reads=[("t1", i, 0), ("t1", i, 1), ("u", i)], writes=[("t1", i, 0), ("t1", i, 1)])
        P.op("act", lambda e, i=i: e.activation(out=zt[i][:], in_=zt[i][:], func=AF.Silu),
             reads=[("z", i)], writes=[("z", i)])
        P.op("dve", lambda e, i=i: e.tensor_tensor(out=t1[i][:], in0=t1[i][:], in1=zt[i][:], op=ALU.mult),
             reads=[("t1", i, 0), ("t1", i, 1), ("z", i)], writes=[("t1", i, 0), ("t1", i, 1)])
        P.dma("sp", lambda e, i=i, r0=r0: e.dma_start(out=y_own[r0:r0 + 128, ycol0:ycol0 + 1024], in_=t1[i][:]),
              reads=[("t1", i, 0), ("t1", i, 1)], writes=[("y_mlp", c)])
    ph.close()


def record_ops(P, rec):
    real_op, real_dma = P.op, P.dma
    P.op = lambda eng, fn, reads=(), writes=(): rec.append((real_op, eng, fn, list(reads), list(writes)))
    P.dma = lambda eng, fn, reads=(), writes=(): rec.append((real_dma, eng, fn, list(reads), list(writes)))

    def restore():
        P.op, P.dma = real_op, real_dma
    return restore


def pipeline_merge(recs, nstage):
    allp = []
    for ops, marks in recs:
        m = [0] + list(marks[:nstage - 1])
        while len(m) < nstage:
            m.append(len(ops))
        m.append(len(ops))
        allp.append([ops[m[k]:m[k + 1]] for k in range(nstage)])
    n = len(recs)
    for t in range(n + nstage - 1):
        lists = []
        for k in range(nstage):
            bi = t - k
            if 0 <= bi < n and allp[bi][k]:
                lists.append(allp[bi][k])
        merged = []
        for li, L in enumerate(lists):
            for pi, item in enumerate(L):
                merged.append(((pi + 0.5) / len(L), li, pi, item))
        merged.sort(key=lambda t_: (t_[0], t_[1]))
        for _, _, _, (f, eng, fn, reads, writes) in merged:
            f(eng, fn, reads=reads, writes=writes)


FMF_X, FMF_B, FMF_C, FMF_WL, FMF_AL = 0, 256, 384, 512, 576
TMF_SSMZ, TMF_DT, TMF_R, TMF_K, TMF_V, TMF_Z = 0, 256, 260, 516, 772, 1028
NEG_BIG = -30000.0


def phase_ssm(nc, scr, prm, y_all, y_dst=None):
    ph = Phase(nc)
    P = ph.P
    fm = scr["fmf_all"]
    tm = scr["tmf_all"]
    if y_dst is None:
        y_dst = y_all[:, 0:256]
    NCH = SEQ // 128
    SC = 512
    identf = ph.sb([128, 128], F32, "identf")
    identb = ph.sb([128, 128], BF16, "identb")
    tri = ph.sb([128, 128], F32, "tri")
    onesf = ph.sb([128, 128], F32, "ones")
    sel4 = ph.sb([4, 4, 128], F32, "sel4")
    negb = ph.sb([128, 128], F32, "negb")
    cw = ph.sb([128, 4, 4], F32, "cw")
    cb = ph.sb([128, 4], F32, "cb")
    dtb = ph.sb([128, 128], F32, "dtb")
    alog = ph.sb([128, 128], F32, "alog")
    Dbc = ph.sb([128, 4], F32, "Dbc")
    ngb = ph.sb([128, 256], F32, "ngb")
    dt = ph.sb([128, NCH, 4], F32, "dt")
    aa = ph.sb([128, NCH, 4], F32, "aa")
    cum = ph.sb([128, NCH * 4], F32, "cum")
    cumL = ph.sb([128, NCH * 4], F32, "cumL")
    ecum = ph.sb([128, NCH * 4], F32, "ecum")
    ncum = ph.sb([128, NCH * 4], F32, "ncum")
    ecumL = ph.sb([128, NCH * 4], F32, "ecumL")
    dtd = ph.sb([128, NCH * 4], F32, "dtd")
    win = [ph.sb([128, SC + 3], F32, "win") for _ in range(4)]
    acc = [ph.sb([128, SC], F32, "acc") for _ in range(2)]
    xsT = [ph.sb([128, 2, SC], F32, "xsT") for _ in range(2)]
    BT = [ph.sb([128, SC], BF16, "BT") for _ in range(2)]
    CT = [ph.sb([128, SC], BF16, "CT") for _ in range(2)]
    xtok = [ph.sb([128, 256], F32, "xtok") for _ in range(2)]
    xdt = [ph.sb([128, 256], BF16, "xdt") for _ in range(2)]
    xdd = [ph.sb([128, 256], BF16, "xdd") for _ in range(2)]
    Btok = [ph.sb([128, 128], BF16, "Btok") for _ in range(2)]
    cumT = [ph.sb([4, 128], F32, "cumT") for _ in range(2)]
    LT = [ph.sb([128, 4, 128], F32, "LT") for _ in range(2)]
    MT = [ph.sb([128, 4, 128], BF16, "MT") for _ in range(2)]
    ydsb = [ph.sb([128, 256], F32, "ydsb") for _ in range(2)]
    dsk = [ph.sb([128, 256], F32, "dsk") for _ in range(2)]
    yt = [ph.sb([128, 256], F32, "yt") for _ in range(2)]
    zt = [ph.sb([128, 256], F32, "zt") for _ in range(2)]
    junk = ph.sb([128, 256], F32, "junk")
    nst = ph.sb([128, NCH], F32, "nst")
    hT = ph.sb([128, 256], F32, "hT")
    hTb = ph.sb([128, 256], BF16, "hTb")
    pTr = [ph.ps([128, 512], F32, "pTr") for _ in range(1)]
    pTb = [ph.ps([128, 1024], BF16, "pTb") for _ in range(1)]
    pSm = [ph.ps([128, 512], F32, "pSm") for _ in range(1)]
    pL = [ph.ps([128, 512], F32, "pL") for _ in range(1)]
    pCB = [ph.ps([128, 512], F32, "pCB") for _ in range(1)]
    pYd = [ph.ps([128, 512], F32, "pYd") for _ in range(1)]
    pYo = [ph.ps([128, 512], F32, "pYo") for _ in range(1)]
    pS = [ph.ps([128, 512], F32, "pS") for _ in range(1)]

    ld = lambda dst, src, key: P.dma("sp", lambda e: e.dma_start(out=dst, in_=src), writes=[key])
    ld(identf[:], prm["ident_f"][:, :], "identf")
    ld(identb[:], prm["ident_b"][:, :], "identb")
    ld(tri[:], prm["tri_le"][:, :], "tri")
    ld(onesf[:], prm["ones_f"][:, :], "ones")
    ld(sel4[:], prm["sel4"][:, :, :], "sel4")
    ld(negb[:], prm["negbig_lt"][:, :], "negb")
    ld(cw[:], prm["ssm_cw"][:, :, :], "cw")
    ld(cb[:], prm["ssm_cb"][:, :], "cb")
    ld(dtb[:], prm["ssm_dtb_t"][0:1, :].partition_broadcast(128), "dtb")
    ld(alog[:], prm["ssm_alog_t"][0:1, :].partition_broadcast(128), "alog")
    ld(Dbc[:], prm["ssm_D"][0:1, :].partition_broadcast(128), "Dbc")
    ld(ngb[:], prm["ssm_ng"][0:1, :].partition_broadcast(128), "ngb")
    ld(dt[:], tm[:, TMF_DT:TMF_DT + 4].rearrange("(c l) h -> l c h", l=128), "dt")
    dtf = dt[:].rearrange("p c h -> p (c h)")
    aaf = aa[:].rearrange("p c h -> p (c h)")
    P.op("dve", lambda e: e.tensor_tensor(out=dtf, in0=dtf, in1=dtb[:], op=ALU.add), reads=["dt", "dtb"], writes=["dt"])
    P.op("act", lambda e: e.activation(out=dtf, in_=dtf, func=AF.Exp), reads=["dt"], writes=["dt"])
    P.op("act", lambda e: e.activation(out=dtf, in_=dtf, func=AF.Ln, bias=1.0, scale=1.0), reads=["dt"], writes=["dt"])
    P.op("act", lambda e: e.activation(out=alog[:], in_=alog[:], func=AF.Exp), reads=["alog"], writes=["alog"])
    P.op("dve", lambda e: e.scalar_tensor_tensor(out=aaf, in0=dtf, scalar=-1.0, in1=alog[:], op0=ALU.mult, op1=ALU.mult),
         reads=["dt", "alog"], writes=["aa"])
    P.op("pe", lambda e: e.matmul(pSm[0][:, 0:128], lhsT=tri[:], rhs=aaf, start=True, stop=True),
         reads=["tri", "aa"], writes=["pSm"])
    P.op("dve", lambda e: e.tensor_copy(out=cum[:], in_=pSm[0][:, 0:128]), reads=["pSm"], writes=["cum"])
    P.op("pe", lambda e: e.matmul(pSm[0][:, 128:256], lhsT=onesf[:], rhs=aaf, start=True, stop=True),
         reads=["ones", "aa", "cum"], writes=["pSm"])
    P.op("dve", lambda e: e.tensor_copy(out=cumL[:], in_=pSm[0][:, 128:256]), reads=["pSm"], writes=["cumL"])
    P.op("act", lambda e: e.activation(out=ecum[:], in_=cum[:], func=AF.Exp), reads=["cum"], writes=["ecum"])
    P.op("pool", lambda e: e.tensor_scalar(out=ncum[:], in0=cum[:], scalar1=-1.0, scalar2=None, op0=ALU.mult), reads=["cum"], writes=["ncum"])
    P.op("act", lambda e: e.activation(out=ecumL[:], in_=cumL[:], func=AF.Exp), reads=["cumL"], writes=["ecumL"])
    P.op("dve", lambda e: e.tensor_tensor(out=dtd[:], in0=cumL[:], in1=cum[:], op=ALU.subtract),
         reads=["cumL", "cum"], writes=["dtd"])
    P.op("act", lambda e: e.activation(out=dtd[:], in_=dtd[:], func=AF.Exp), reads=["dtd"], writes=["dtd"])
    P.op("dve", lambda e: e.tensor_tensor(out=dtd[:], in0=dtd[:], in1=dtf, op=ALU.mult), reads=["dtd", "dt"], writes=["dtd"])
    P.op("dve", lambda e: e.memset(hT[:], 0.0), writes=["hT"])
    P.op("dve", lambda e: e.memset(hTb[:], 0.0), writes=["hTb"])
    P.op("dve", lambda e: e.memset(nst[:], 0.0), writes=["nst"])

    recs = []
    rec_cur = [None]

    def new_unit():
        rec = []
        recs.append([rec, []])
        rec_cur[0] = rec
        return record_ops(P, rec)

    for s in range(SEQ // SC):
        t0 = s * SC
        si = s % 2
        restore = new_unit()
        for j in range(4):
            wj = win[j]
            if s == 0:
                P.op("dve", lambda e, wj=wj: e.memset(wj[:, 0:3], 0.0), writes=[("win", j)])
                P.dma("sp", lambda e, wj=wj, j=j: e.dma_start(out=wj[:, 3:SC + 3], in_=fm[j * 128:(j + 1) * 128, 0:SC]),
                      writes=[("win", j)])
            else:
                P.dma("sp", lambda e, wj=wj, j=j, t0=t0: e.dma_start(out=wj[:], in_=fm[j * 128:(j + 1) * 128, t0 - 3:t0 + SC]),
                      writes=[("win", j)])
            ac = acc[j % 2]
            ak = ("acc", j % 2)
            P.op("dve", lambda e, ac=ac, wj=wj, j=j: e.tensor_scalar(out=ac[:], in0=wj[:, 0:SC], scalar1=cw[:, j, 0:1],
                                                                    scalar2=None, op0=ALU.mult),
                 reads=[("win", j), "cw"], writes=[ak])
            for tap in range(1, 4):
                P.op("dve", lambda e, ac=ac, wj=wj, j=j, tap=tap: e.scalar_tensor_tensor(
                    out=ac[:], in0=wj[:, tap:tap + SC], scalar=cw[:, j, tap:tap + 1], in1=ac[:], op0=ALU.mult, op1=ALU.add),
                    reads=[("win", j), "cw", ak], writes=[ak])
            if j < 2:
                dst, dk = xsT[si][:, j, :], ("xsT", si, j)
            elif j == 2:
                dst, dk = BT[si][:], ("BT", si)
            else:
                dst, dk = CT[si][:], ("CT", si)
            P.op("act", lambda e, ac=ac, dst=dst, j=j: e.activation(out=dst, in_=ac[:], func=AF.Silu, bias=cb[:, j:j + 1], scale=1.0),
                 reads=[ak, "cb"], writes=[dk])
        if s < 2:
            ph.dump(f"ssm_xsT{s}", xsT[si][:], [("xsT", si, 0), ("xsT", si, 1)])
            ph.dump(f"ssm_BT{s}", BT[si][:], [("BT", si)])
            ph.dump(f"ssm_CT{s}", CT[si][:], [("CT", si)])
        for cc in range(SC // 128):
            c = s * (SC // 128) + cc
            ci = c % 2
            lo = cc * 128
            c4 = c * 4
            if cc > 0:
                restore = new_unit()
            for j in range(2):
                P.op("pe", lambda e, si=si, j=j, lo=lo: e.transpose(out=pTr[0][:, j * 128:(j + 1) * 128], in_=xsT[si][:, j, lo:lo + 128],
                                                            identity=identf[:]),
                     reads=[("xsT", si, j), "identf"], writes=["pTr"])
            P.op("act", lambda e, ci=ci: e.copy(out=xtok[ci][:], in_=pTr[0][:, 0:256]), reads=["pTr"], writes=[("xtok", ci)])
            P.op("pe", lambda e, si=si, lo=lo: e.transpose(out=pTb[0][:, 0:128], in_=BT[si][:, lo:lo + 128], identity=identb[:]),
                 reads=[("BT", si), "identb"], writes=["pTb"])
            P.op("act", lambda e, ci=ci: e.copy(out=Btok[ci][:], in_=pTb[0][:, 0:128]), reads=["pTb"], writes=[("Btok", ci)])
            P.op("pe", lambda e, c=c: e.matmul(pSm[0][0:4, 256:384], lhsT=aa[:, c, :], rhs=tri[:], start=True, stop=True),
                 reads=["aa", "tri"], writes=["pSm"])
            P.op("dve", lambda e, ci=ci: e.tensor_copy(out=cumT[ci][:], in_=pSm[0][0:4, 256:384]),
                 reads=["pSm"], writes=[("cumT", ci)])
            for h in range(4):
                P.op("pe", lambda e, h=h, ci=ci: e.matmul(pL[0][:, h * 128:(h + 1) * 128], lhsT=sel4[:, h, :], rhs=cumT[ci][:],
                                                          start=True, stop=False),
                     reads=["sel4", ("cumT", ci)], writes=["pL"])
                P.op("pe", lambda e, h=h: e.matmul(pL[0][:, h * 128:(h + 1) * 128], lhsT=identf[:], rhs=negb[:],
                                                   start=False, stop=True),
                     reads=["identf", "negb"], writes=["pL"])
            for h in range(4):
                P.op("act", lambda e, h=h, ci=ci, c4=c4: e.activation(
                    out=LT[ci][:, h, :], in_=pL[0][:, h * 128:(h + 1) * 128], func=AF.Exp, bias=ncum[:, c4 + h:c4 + h + 1], scale=1.0),
                    reads=["pL", "ncum"], writes=[("LT", ci)])
            P.op("pe", lambda e, si=si, lo=lo: e.matmul(pCB[0][:, 0:128], lhsT=BT[si][:, lo:lo + 128], rhs=CT[si][:, lo:lo + 128],
                                                 start=True, stop=True), reads=[("BT", si), ("CT", si)], writes=["pCB"])
            P.op("dve", lambda e, ci=ci: e.tensor_tensor(out=MT[ci][:], in0=pCB[0][:, 0:128].unsqueeze(1).to_broadcast([128, 4, 128]),
                                                         in1=LT[ci][:], op=ALU.mult), reads=["pCB", ("LT", ci)], writes=[("MT", ci)])
            h4 = lambda ap: ap.rearrange("p (h d) -> p h d", h=4)
            P.op("dve", lambda e, ci=ci, c4=c4: e.tensor_tensor(out=h4(xdt[ci][:]), in0=h4(xtok[ci][:]),
                                                                in1=dtf[:, c4:c4 + 4].unsqueeze(2).to_broadcast([128, 4, 64]), op=ALU.mult),
                 reads=[("xtok", ci), "dt"], writes=[("xdt", ci)])
            P.op("pool", lambda e, ci=ci, c4=c4: e.tensor_tensor(out=h4(xdd[ci][:]), in0=h4(xtok[ci][:]),
                                                                 in1=dtd[:, c4:c4 + 4].unsqueeze(2).to_broadcast([128, 4, 64]), op=ALU.mult),
                 reads=[("xtok", ci), "dtd"], writes=[("xdd", ci)])
            for h in range(4):
                P.op("pe", lambda e, h=h, ci=ci: e.matmul(pYd[0][:, h * 64:(h + 1) * 64], lhsT=MT[ci][:, h, :],
                                                          rhs=xdt[ci][:, h * 64:(h + 1) * 64], start=True, stop=True),
                     reads=[("MT", ci), ("xdt", ci)], writes=["pYd"])
            recs[-1][1].append(len(rec_cur[0]))
            for h in range(4):
                P.op("pe", lambda e, si=si, h=h, lo=lo: e.matmul(pYo[0][:, h * 64:(h + 1) * 64], lhsT=CT[si][:, lo:lo + 128],
                                                          rhs=hTb[:, h * 64:(h + 1) * 64], start=True, stop=True),
                     reads=[("CT", si), "hTb"], writes=["pYo"])
            P.op("act", lambda e, ci=ci: e.copy(out=ydsb[ci][:], in_=pYd[0][:, 0:256]), reads=["pYd"], writes=[("ydsb", ci)])
            P.dma("sp", lambda e, ci=ci, c=c: e.dma_start(out=zt[ci][:], in_=tm[c * 128:(c + 1) * 128, TMF_SSMZ:TMF_SSMZ + 256]),
                  writes=[("zt", ci)])
            P.op("dve", lambda e, ci=ci, c4=c4: e.tensor_tensor(out=h4(yt[ci][:]), in0=pYo[0][:, 0:256].rearrange("p (h d) -> p h d", h=4),
                                                                in1=ecum[:, c4:c4 + 4].unsqueeze(2).to_broadcast([128, 4, 64]), op=ALU.mult),
                 reads=["pYo", "ecum"], writes=[("yt", ci)])
            P.op("pool", lambda e, ci=ci: e.tensor_tensor(out=h4(dsk[ci][:]), in0=h4(xtok[ci][:]),
                                                          in1=Dbc[:, 0:4].unsqueeze(2).to_broadcast([128, 4, 64]), op=ALU.mult),
                 reads=[("xtok", ci), "Dbc"], writes=[("dsk", ci)])
            P.op("dve", lambda e, ci=ci: e.tensor_tensor(out=yt[ci][:], in0=yt[ci][:], in1=ydsb[ci][:], op=ALU.add),
                 reads=[("yt", ci), ("ydsb", ci)], writes=[("yt", ci)])
            P.op("dve", lambda e, ci=ci: e.tensor_tensor(out=yt[ci][:], in0=yt[ci][:], in1=dsk[ci][:], op=ALU.add),
                 reads=[("yt", ci), ("dsk", ci)], writes=[("yt", ci)])
            if c in (0, 4):
                ph.dump(f"ssm_xtok{c}", xtok[ci][:], [("xtok", ci)])
                ph.dump(f"ssm_LT{c}", LT[ci][:], [("LT", ci)])
                ph.dump(f"ssm_MT{c}", MT[ci][:], [("MT", ci)])
                ph.dump(f"ssm_yt{c}", yt[ci][:], [("yt", ci)])
            for h in range(4):
                P.op("pe", lambda e, h=h, ci=ci: e.matmul(pS[0][:, h * 64:(h + 1) * 64], lhsT=Btok[ci][:],
                                                          rhs=xdd[ci][:, h * 64:(h + 1) * 64], start=True, stop=True),
                     reads=[("Btok", ci), ("xdd", ci)], writes=["pS"])
            P.op("pool", lambda e, c4=c4: e.tensor_tensor(out=h4(hT[:]), in0=h4(hT[:]),
                                                          in1=ecumL[:, c4:c4 + 4].unsqueeze(2).to_broadcast([128, 4, 64]), op=ALU.mult),
                 reads=["hT", "ecumL"], writes=["hT"])
            P.op("dve", lambda e: e.tensor_tensor(out=hT[:], in0=hT[:], in1=pS[0][:, 0:256], op=ALU.add), reads=["hT", "pS"], writes=["hT"])
            P.op("act", lambda e: e.copy(out=hTb[:], in_=hT[:]), reads=["hT"], writes=["hTb"])
            P.op("act", lambda e, ci=ci: e.activation(out=zt[ci][:], in_=zt[ci][:], func=AF.Silu),
                 reads=[("zt", ci)], writes=[("zt", ci)])
            P.op("pool", lambda e, ci=ci: e.tensor_tensor(out=yt[ci][:], in0=yt[ci][:], in1=zt[ci][:], op=ALU.mult),
                 reads=[("yt", ci), ("zt", ci)], writes=[("yt", ci)])
            P.op("act", lambda e, ci=ci, c=c: e.activation(out=junk[:], in_=yt[ci][:], func=AF.Square, accum_out=nst[:, c:c + 1]),
                 reads=[("yt", ci), "nst"], writes=["junk", ("nst", c)])
            P.op("dve", lambda e, c=c: e.tensor_scalar(out=nst[:, c:c + 1], in0=nst[:, c:c + 1], scalar1=1.0 / 256, scalar2=NORM_EPS,
                                                       op0=ALU.mult, op1=ALU.add), reads=[("nst", c)], writes=[("nst", c)])
            P.op("act", lambda e, c=c: e.activation(out=nst[:, c:c + 1], in_=nst[:, c:c + 1], func=AF.Sqrt),
                 reads=[("nst", c)], writes=[("nst", c)])
            P.op("dve", lambda e, c=c: e.reciprocal(out=nst[:, c:c + 1], in_=nst[:, c:c + 1]), reads=[("nst", c)], writes=[("nst", c)])
            P.op("dve", lambda e, ci=ci, c=c: e.scalar_tensor_tensor(out=yt[ci][:], in0=yt[ci][:], scalar=nst[:, c:c + 1], in1=ngb[:],
                                                                     op0=ALU.mult, op1=ALU.mult),
                 reads=[("yt", ci), ("nst", c), "ngb"], writes=[("yt", ci)])
            P.dma("sp", lambda e, ci=ci, c=c: e.dma_start(out=y_dst[c * 128:(c + 1) * 128, :], in_=yt[ci][:]),
                  reads=[("yt", ci)], writes=[("y_ssm", c)])
            restore()
    pipeline_merge([(r, m) for r, m in recs], 2)
    ph.close()


NPAIR = 144
TOPK = 256
NBIS = 17
FILLER = False


def pair_off(i):
    return 2 * i * (i + 1)


def default_blocks():
    return [(i + 1, 0) for i in range(8)]


def pair_offsets(blocks):
    offs, o = [], 0
    for nch, _ in blocks:
        offs.append(o)
        o += 4 * nch
    return offs, o


def phase_indexer(nc, scr, prm, maskT_d, blocks=None):
    blocks = blocks or default_blocks()
    nblk = len(blocks)
    assert nblk % 2 == 0
    NT = nblk * 128
    ncb = max(cb for _, cb in blocks) + 1
    poffs, _ = pair_offsets(blocks)
    ph = Phase(nc)
    P = ph.P
    fmo = scr["fmb_own"]
    fma = scr["fmb_all"]
    tmo = scr["tmf_own"]
    NB4 = 4
    identb = ph.sb([128, 128], BF16, "identb")
    kiT = ph.sb([128, SEQ], BF16, "kiT")
    qiT = [ph.sb([128, 8, 128], BF16, "qiT") for _ in range(NB4)]
    wi = ph.sb([128, nblk, 16], F32, "wi")
    iota = ph.sb([128, 512], F32, "iota")
    qrel = ph.sb([128, ncb], F32, "qrel")
    cbias = ph.sb([128, ncb, 512], F32, "cbias")
    pow2 = ph.sb([128, NBIS], F32, "pow2")
    wdiag = [ph.sb([128, 16, 128], BF16, "wdiag") for _ in range(NB4)]
    R = [ph.sb([128, 512], BF16, "R") for _ in range(4)]
    sc = [ph.sb([128, SEQ], F32, "sc") for _ in range(NB4)]
    junk = [ph.sb([128, SEQ], BF16, "junk") for _ in range(2)]
    mk = [ph.sb([128, SEQ], BF16, "mk") for _ in range(2)]
    mT = [ph.sb([128, 8, 128], BF16, "mT") for _ in range(3)]
    mx = [ph.sb([128, 8], F32, "mx") for _ in range(NB4)]
    bs = [ph.sb([128, 8], F32, "bs") for _ in range(NB4)]
    wk = [ph.sb([128, NBIS], F32, "wk") for _ in range(NB4)]
    cnt = [ph.sb([128, NBIS], F32, "cnt") for _ in range(NB4)]
    pD = [ph.ps([128, 512], F32, "pD") for _ in range(4)]
    pSc = [ph.ps([128, 512], F32, "pSc") for _ in range(2)]
    pT = [ph.ps([128, 1024], BF16, "pT") for _ in range(1)]
    pJ = ph.ps([128, 512], F32, "pJ")

    ld = lambda dst, src, key: P.dma("sp", lambda e: e.dma_start(out=dst, in_=src), writes=[key])
    ld(identb[:], prm["ident_b"][:, :], "identb")
    ld(kiT[:], fma[1024:1152, :], "kiT")
    ld(wi[:], tmo[0:NT, TMF_OWN_WI:TMF_OWN_WI + 16].rearrange("(i p) h -> p i h", p=128), "wi")
    ld(iota[:], prm["iota512"][0:1, :].partition_broadcast(128), "iota")
    ld(qrel[:], prm["qrel"][:, :], "qrel")
    ld(pow2[:], prm["pow2"][0:1, :].partition_broadcast(128), "pow2")
    P.op("dve", lambda e: e.tensor_scalar(out=wi[:], in0=wi[:], scalar1=0.03125, scalar2=None, op0=ALU.mult), reads=["wi"], writes=["wi"])
    for cb in range(ncb):
        P.op("dve", lambda e, o=cbias[:, cb, :], s=qrel[:, cb:cb + 1]: e.tensor_scalar(out=o, in0=iota[:], scalar1=s, scalar2=-1e30,
                                                                                  op0=ALU.is_gt, op1=ALU.mult),
             reads=["iota", "qrel"], writes=["cbias"])
    k = {"d": 0, "r": 0, "s": 0, "t": 0, "m": 0}
    qv = fmo[1024:2048, :].rearrange("(p r) t -> r p t", r=128)

    def scores(i):
        nch, cbi = blocks[i]
        b4 = i % NB4
        scb, mxb, wdb, qb = sc[b4], mx[b4], wdiag[b4], qiT[b4]
        ld(qb[:], qv[:, :, i * 128:(i + 1) * 128], ("qiT", b4))
        P.op("pool", lambda e, o=wdb[:], w_=wi[:, i, :].unsqueeze(2).to_broadcast([128, 16, 128]),
             d_=identb[:].unsqueeze(1).to_broadcast([128, 16, 128]): e.tensor_tensor(out=o, in0=d_, in1=w_, op=ALU.mult),
             reads=["identb", "wi"], writes=[("wdiag", b4)])
        P.op("dve", lambda e, o=mxb[:]: e.memset(o, 0.0), writes=[("mx", b4)])
        P.op("dve", lambda e, o=cnt[b4][:]: e.memset(o, 0.0), writes=[("cnt", b4)])
        for ch in range(nch):
            js = k["s"] % 2
            k["s"] += 1
            jds = {}

            def dots(h):
                jd = k["d"] % 4
                k["d"] += 1
                jds[h] = jd
                r0 = (h % 2) * 64
                P.op("pe", lambda e, o=pD[jd][:], l=qb[r0:r0 + 64, h // 2, :],
                     r=kiT[r0:r0 + 64, ch * 512:(ch + 1) * 512]: e.matmul(o, lhsT=l, rhs=r, start=True, stop=True),
                     reads=[("qiT", b4), "kiT"], writes=[("pD", jd)])

            dots(0)
            dots(1)
            for h in range(16):
                if h + 2 < 16:
                    dots(h + 2)
                jd = jds[h]
                jr = k["r"] % 4
                k["r"] += 1
                P.op("act", lambda e, o=R[jr][:], s=pD[jd][:]: e.activation(out=o, in_=s, func=AF.Relu),
                     reads=[("pD", jd)], writes=[("R", jr)])
                P.op("pe", lambda e, o=pSc[js][:], l=wdb[:, h, :], r=R[jr][:], h=h: e.matmul(o, lhsT=l, rhs=r, start=(h == 0),
                                                                                           stop=(h == 15)),
                     reads=[("wdiag", b4), ("R", jr)], writes=[("pSc", js)])
                if FILLER:
                    P.op("pe", lambda e, l=identb[:], r=kiT[:, ch * 512:(ch + 1) * 512]: e.matmul(pJ[:], lhsT=l, rhs=r, start=True, stop=True),
                         reads=["identb", "kiT"], writes=["pJ"])
            P.op("dve", lambda e, o=mxb[:, ch:ch + 1], s=pSc[js][:]: e.tensor_reduce(out=o, in_=s, axis=AX.X, op=ALU.max,
                                                                                   apply_absolute_value=True),
                 reads=[("pSc", js)], writes=[("mx", b4)])
            if ch == nch - 1:
                P.op("dve", lambda e, o=scb[:, ch * 512:(ch + 1) * 512], s=pSc[js][:], c_=cbias[:, cbi, :]: e.tensor_tensor(
                    out=o, in0=s, in1=c_, op=ALU.add), reads=[("pSc", js), "cbias"], writes=[("sc", b4)])
            else:
                P.op("dve", lambda e, o=scb[:, ch * 512:(ch + 1) * 512], s=pSc[js][:]: e.tensor_copy(out=o, in_=s),
                     reads=[("pSc", js)], writes=[("sc", b4)])
            yield

    def bis_init(i):
        b4 = i % NB4
        bsb, wkb, mxb = bs[b4], wk[b4], mx[b4]
        bk = ("bs", b4)
        P.op("dve", lambda e, o=bsb[:, 0:1], s=mxb[:]: e.tensor_reduce(out=o, in_=s, axis=AX.X, op=ALU.max),
             reads=[("mx", b4)], writes=[bk])
        P.op("dve", lambda e, o=bsb[:, 0:1]: e.tensor_scalar(out=o, in0=o, scalar1=1.001, scalar2=1e-6, op0=ALU.mult, op1=ALU.add),
             reads=[bk], writes=[bk])
        P.op("dve", lambda e, o=wkb[:], s=bsb[:, 0:1]: e.tensor_scalar(out=o, in0=pow2[:], scalar1=s, scalar2=2.0, op0=ALU.mult,
                                                                      op1=ALU.mult), reads=[bk, "pow2"], writes=[("wk", b4)])
        P.op("dve", lambda e, o=bsb[:, 1:2]: e.memset(o, 0.0), reads=[bk], writes=[bk])

    def bis_count(i, it, jj):
        b4 = i % NB4
        n = 512 * blocks[i][0]
        P.op("dve", lambda e, o=junk[jj][:, 0:n], s=sc[b4][:, 0:n], m=bs[b4][:, 1:2], a=cnt[b4][:, it:it + 1]: e.tensor_scalar(
            out=o, in0=s, scalar1=m, scalar2=0.0, op0=ALU.is_ge, op1=ALU.add, accum_out=a),
            reads=[("sc", b4), ("bs", b4), ("cnt", b4)], writes=[("junk", jj), ("cnt", b4)])

    def bis_delta(i, it):
        b4 = i % NB4
        P.op("dve", lambda e, o=bs[b4][:, 2:3], c_=cnt[b4][:, it:it + 1], w_=wk[b4][:, it:it + 1]: e.tensor_scalar(
            out=o, in0=c_, scalar1=TOPK - 0.5, scalar2=w_, op0=ALU.is_ge, op1=ALU.mult),
            reads=[("cnt", b4), ("wk", b4), ("bs", b4)], writes=[("bs", b4)])

    def bis_mid(i, it):
        b4 = i % NB4
        nx = min(it + 1, NBIS - 1)
        P.op("dve", lambda e, o=bs[b4][:, 1:2], d_=bs[b4][:, 2:3], w_=wk[b4][:, nx:nx + 1]: e.scalar_tensor_tensor(
            out=o, in0=d_, scalar=w_, in1=o, op0=ALU.subtract, op1=ALU.add), reads=[("bs", b4), ("wk", b4)], writes=[("bs", b4)])

    def finish_block(i, jj):
        nch, _ = blocks[i]
        b4 = i % NB4
        n = 512 * nch
        mkb = mk[jj]
        P.op("dve", lambda e, o=mkb[:, 0:n], s=sc[b4][:, 0:n], t=bs[b4][:, 1:2]: e.tensor_scalar(
            out=o, in0=s, scalar1=t, scalar2=-1.0, op0=ALU.is_ge, op1=ALU.add), reads=[("sc", b4), ("bs", b4)], writes=[("mk", jj)])
        nkb = 4 * nch
        for g0 in range(0, nkb, 8):
            ng = min(8, nkb - g0)
            jt = 0
            jm = k["m"] % 3
            k["m"] += 1
            for kb in range(g0, g0 + ng):
                P.op("pe", lambda e, o=pT[jt][:, (kb - g0) * 128:(kb - g0 + 1) * 128], s=mkb[:, kb * 128:(kb + 1) * 128]: e.transpose(
                    out=o, in_=s, identity=identb[:]), reads=[("mk", jj), "identb"], writes=[("pT", jt)])
            P.op("act", lambda e, o=mT[jm][:, 0:ng, :], s=pT[jt][:, 0:ng * 128].rearrange("p (k t) -> p k t", k=ng): e.copy(out=o, in_=s),
                 reads=[("pT", jt)], writes=[("mT", jm)])
            po = poffs[i] + g0
            P.dma("sp", lambda e, o=maskT_d[:, po:po + ng, :], s=mT[jm][:, 0:ng, :]: e.dma_start(out=o, in_=s),
                  reads=[("mT", jm)], writes=[("maskT", i, g0)])

    import itertools
    for _ in itertools.chain(scores(0), scores(1)):
        pass
    for pr in range(nblk // 2):
        ia, ib = 2 * pr, 2 * pr + 1
        pending = iter(())
        if 2 * pr + 2 < nblk:
            pending = itertools.chain(scores(2 * pr + 2), scores(2 * pr + 3))
        bis_init(ia)
        bis_init(ib)
        for it in range(NBIS):
            bis_count(ia, it, 0)
            bis_count(ib, it, 1)
            bis_delta(ia, it)
            bis_delta(ib, it)
            bis_mid(ia, it)
            bis_mid(ib, it)
            next(pending, None)
        for _ in pending:
            pass
        finish_block(ia, 0)
        finish_block(ib, 1)
    ph.close()


def phase_attn(nc, scr, prm, maskT_d, y_own, blocks=None):
    blocks = blocks or default_blocks()
    nblk = len(blocks)
    poffs, _ = pair_offsets(blocks)
    ph = Phase(nc)
    P = ph.P
    fmo = scr["fmb_own"]
    fma = scr["fmb_all"]
    tmb = scr["tmb_all"]
    tmo = scr["tmf_own"]
    att_scale = 128 ** -0.5
    i30k = ph.sb([128, 128], BF16, "i30k")
    kT = ph.sb([128, 8, SEQ], BF16, "kT")
    V = ph.sb([128, 32, 8, 129], BF16, "V")
    qT = [ph.sb([128, 8, 128], BF16, "qT") for _ in range(2)]
    mT = [ph.sb([128, 32, 128], BF16, "mT") for _ in range(2)]
    PT = [ph.sb([128, 512], BF16, "PT") for _ in range(4)]
    ot = [ph.sb([128, 1024], F32, "ot") for _ in range(2)]
    zt = [ph.sb([128, 1024], F32, "zt") for _ in range(2)]
    rs = ph.sb([128, 8 * nblk], F32, "rs")
    pS = [ph.ps([128, 512], F32, "pS") for _ in range(4)]
    pO = [ph.ps([128, 512], F32, "pO") for _ in range(2)]

    ld = lambda dst, src, key: P.dma("sp", lambda e: e.dma_start(out=dst, in_=src), writes=[key])
    ld(i30k[:], prm["ident30k_b"][:, :], "i30k")
    for h in range(8):
        ld(kT[:, h, :], fma[h * 128:(h + 1) * 128, :], ("kT", h))
    vv = tmb.rearrange("(kb p) (h d) -> p kb h d", p=128, d=128)
    for kb in range(32):
        ld(V[:, kb, :, 0:128], vv[:, kb, :, :], ("V", kb))
    P.op("pool", lambda e: e.memset(V[:, :, :, 128:129], 1.0), writes=["Vones"])
    qv = fmo[0:1024, :].rearrange("(h d) t -> d h t", d=128)
    k = {"s": 0, "p": 0, "o": 0}
    def loads(i):
        b2_ = i % 2
        nkb_ = 4 * blocks[i][0]
        po_ = poffs[i]
        ld(qT[b2_][:], qv[:, :, i * 128:(i + 1) * 128], ("qT", b2_))
        ld(mT[b2_][:, 0:nkb_, :], maskT_d[:, po_:po_ + nkb_, :], ("mT", b2_))
        ld(zt[b2_][:], tmo[i * 128:(i + 1) * 128, TMF_OWN_ATTZ:TMF_OWN_ATTZ + 1024], ("zt", b2_))

    loads(0)
    for i, (nch, _) in enumerate(blocks):
        b2 = i % 2
        nkb = 4 * nch
        po = poffs[i]
        if i + 1 < nblk:
            loads(i + 1)
        P.op("act", lambda e, o=zt[b2][:]: e.activation(out=o, in_=o, func=AF.Silu), reads=[("zt", b2)], writes=[("zt", b2)])
        for h in range(8):
            jo = k["o"] % 2
            k["o"] += 1
            jss = {}

            def st_mm(ch):
                js = k["s"] % 4
                k["s"] += 1
                jss[ch] = js
                P.op("pe", lambda e, o=pS[js][:], r=mT[b2][:, ch * 4:(ch + 1) * 4, :].rearrange("p k t -> p (k t)"): e.matmul(
                    o, lhsT=i30k[:], rhs=r, start=True, stop=False), reads=["i30k", ("mT", b2)], writes=[("pS", js)])
                for k4 in range(4):
                    kb = ch * 4 + k4
                    P.op("pe", lambda e, o=pS[js][:, k4 * 128:(k4 + 1) * 128], l=kT[:, h, kb * 128:(kb + 1) * 128],
                         r=qT[b2][:, h, :], k4=k4: e.matmul(o, lhsT=l, rhs=r, start=False, stop=(k4 == 3)),
                         reads=[("kT", h), ("qT", b2)], writes=[("pS", js)])

            st_mm(0)
            for ch in range(nch):
                if ch + 1 < nch:
                    st_mm(ch + 1)
                js = jss[ch]
                jp = k["p"] % 4
                k["p"] += 1
                P.op("act", lambda e, o=PT[jp][:], s=pS[js][:]: e.activation(out=o, in_=s, func=AF.Exp, scale=att_scale),
                     reads=[("pS", js)], writes=[("PT", jp)])
                for k4 in range(4):
                    kb = ch * 4 + k4
                    P.op("pe", lambda e, o=pO[jo][:, 0:129], l=PT[jp][:, k4 * 128:(k4 + 1) * 128], r=V[:, kb, h, :], kb=kb, nkb=nkb:
                         e.matmul(o, lhsT=l, rhs=r, start=(kb == 0), stop=(kb == nkb - 1)),
                         reads=[("PT", jp), ("V", kb), "Vones"], writes=[("pO", jo)])
            c = i * 8 + h
            P.op("dve", lambda e, o=rs[:, c:c + 1], s=pO[jo][:, 128:129]: e.reciprocal(out=o, in_=s),
                 reads=[("pO", jo)], writes=[("rs", c)])
            P.op("dve", lambda e, o=ot[b2][:, h * 128:(h + 1) * 128], s=pO[jo][:, 0:128], r=rs[:, c:c + 1],
                 z=zt[b2][:, h * 128:(h + 1) * 128]: e.scalar_tensor_tensor(out=o, in0=s, scalar=r, in1=z, op0=ALU.mult, op1=ALU.mult),
                 reads=[("pO", jo), ("rs", c), ("zt", b2)], writes=[("ot", b2)])
        P.dma("sp", lambda e, o=y_own[i * 128:(i + 1) * 128, 0:1024], s=ot[b2][:]: e.dma_start(out=o, in_=s),
              reads=[("ot", b2)], writes=[("y_att", i)])
    ph.close()


RW_DECAY_C = -0.6065306597126334


class _Stop(Exception):
    pass


def phase_rwkv(nc, scr, prm, y_all, NBLK=SEQ // 128, stop_after=None, y_dst=None, stagger=True):
    ph = Phase(nc)
    P = ph.P
    tm = scr["tmf_all"]
    fm = scr["fmf_all"]
    if y_dst is None:
        y_dst = y_all[:, 256:512]
    cst = {}
    for nm in ("ident_f", "mask_sl", "mask_su", "mask_u", "ones_bd"):
        cst[nm] = ph.sb([128, 128], F32, nm)
    mu_tm = ph.sb([128, 1024], F32, "mu_tm")
    mu_fm = ph.sb([128, 1], F32, "mu_fm")
    w2a2 = ph.sb([128, 256], F32, "w2a2")
    vec = {}
    for nm in ("rw_w0", "rw_a0", "rw_kk", "rw_ka", "rw_rk", "rw_gng", "rw_gnb"):
        vec[nm] = ph.sb([128, 256], F32, nm)
    onecol = ph.sb([128, 1], F32, "onecol")
    NBUF3 = 3
    cur = [ph.sb([128, 1024], F32, "cur") for _ in range(NBUF3)]
    prv = [ph.sb([128, 1024], F32, "prv") for _ in range(NBUF3)]
    lcur = [ph.sb([128, 128], F32, "lcur") for _ in range(NBUF3)]
    lprv = [ph.sb([128, 128], F32, "lprv") for _ in range(NBUF3)]
    T = {}
    BFT = ("rt", "at", "bt", "kt", "bh", "kh", "vb")
    for nm in ("lw", "asig", "kkn", "kp", "aa", "bb", "cum", "cumL", "e1", "rt", "at", "bt", "kt", "bh", "kh", "vb", "tmp", "tmp2", "yb", "bon"):
        T[nm] = [ph.sb([128, 256], BF16 if nm in BFT else F32, nm) for _ in range(NBUF3)]
    st4 = [ph.sb([128, 16], F32, "st4") for _ in range(NBUF3)]
    gL = [ph.sb([128, 4], F32, "gL") for _ in range(NBUF3)]
    TR = {nm: [[ph.sb([128, 128], BF16, nm) for _ in range(2)] for _ in range(NBUF3)] for nm in ("atT", "btT", "ktT", "rtT")}
    H = {}
    for nm in ("N", "NT", "Pa", "PaT", "Pb", "PbT", "TTa", "TTb", "MakT"):
        H[nm] = ph.sb([128, 4, 128], BF16, nm)
    H["W2"] = ph.sb([128, 4, 64], BF16, "W2")
    mask2 = {nm: ph.sb([128, 2, 128], F32, nm + "2") for nm in ("mask_sl", "mask_su", "mask_u")}
    HS = {}
    for nm in ("P1T", "P2", "MrbT", "MrkT"):
        shp = {"P1T": [128, 2, 128], "P2": [128, 4, 64], "MrbT": [128, 4, 128], "MrkT": [128, 4, 128]}[nm]
        HS[nm] = [ph.sb(shp, F32 if nm == "P2" else BF16, nm) for _ in range(NBUF3)]
    Usb = ph.sb([128, 4, 64], BF16, "Usb")
    STb = [ph.sb([128, 64], BF16, "STb") for _ in range(4)]
    identb = ph.sb([128, 128], BF16, "identb")
    ST = [ph.sb([128, 64], F32, "ST") for _ in range(4)]
    pA = [ph.ps([128, 512], F32, "pA") for _ in range(3)]
    pP = [ph.ps([128, 512], F32, "pP") for _ in range(2)]
    pTb = ph.ps([128, 1024], BF16, "pTb")
    pQ = [ph.ps([128, 512], F32, "pQ") for _ in range(2)]

    ld = lambda dst, src, key: P.dma("sp", lambda e: e.dma_start(out=dst, in_=src), writes=[key])
    for nm in cst:
        ld(cst[nm][:], prm[nm][:, :], nm)
    ld(identb[:], prm["ident_b"][:, :], "identb")
    ld(mu_tm[:], prm["rw_mu_tm"][0:1, :].partition_broadcast(128), "mu_tm")
    ld(mu_fm[:], prm["rw_mu_fm"][:, :], "mu_fm")
    ld(w2a2[:], prm["rw_w2a2"][:, :], "w2a2")
    for nm in vec:
        ld(vec[nm][:], prm[nm][0:1, :].partition_broadcast(128), nm)
    P.op("dve", lambda e: e.memset(onecol[:], 1.0), writes=["onecol"])
    for h in range(4):
        P.op("dve", lambda e, o=ST[h][:]: e.memset(o, 0.0), writes=[("ST", h)])
        P.op("dve", lambda e, o=STb[h][:]: e.memset(o, 0.0), writes=[("STb", h)])
    P.op("dve", lambda e: e.memset(Usb[:], 0.0), writes=["Usb"])
    for nm in ("mask_sl", "mask_su", "mask_u"):
        for r_ in range(2):
            P.op("pool", lambda e, o=mask2[nm][:, r_, :], s=cst[nm][:]: e.tensor_copy(out=o, in_=s), reads=[nm], writes=[nm + "2"])
    kq = {"a": 0, "q": 0, "e": 0}
    P.excl.update(["pA", "pQ", "pTb", "pP"])

    def mm(out, lhsT, rhs, reads, writes, start=True, stop=True):
        P.op("pe", lambda e: e.matmul(out, lhsT=lhsT, rhs=rhs, start=start, stop=stop), reads=reads, writes=writes)

    def evac(out, in_, reads, writes, mask=None, mreads=()):
        kq["e"] += 1
        if mask is not None:
            P.op("dve", lambda e: e.tensor_tensor(out=out, in0=in_, in1=mask, op=ALU.mult), reads=list(reads) + list(mreads), writes=writes)
        elif kq["e"] % 4 == 0:
            P.op("dve", lambda e: e.tensor_copy(out=out, in_=in_), reads=reads, writes=writes)
        else:
            P.op("act", lambda e: e.copy(out=out, in_=in_), reads=reads, writes=writes)

    def nextA():
        j = kq["a"] % 3
        kq["a"] += 1
        return j

    def nextQ():
        j = kq["q"] % 2
        kq["q"] += 1
        return j

    def dv(fn, reads, writes, eng="dve"):
        P.op(eng, fn, reads=reads, writes=writes)

    def stage(n):
        if stop_after is not None and n > stop_after:
            raise _Stop()

    real_op, real_dma = P.op, P.dma
    recs = []
    for blk in range(NBLK):
      rec = []
      marks = []
      recs.append((rec, marks))
      P.op = lambda eng, fn, reads=(), writes=(), rec=rec: rec.append((real_op, eng, fn, list(reads), list(writes)))
      P.dma = lambda eng, fn, reads=(), writes=(), rec=rec: rec.append((real_dma, eng, fn, list(reads), list(writes)))
      try:
          b2 = blk % NBUF3
          t0 = blk * 128
          B = {nm: T[nm][b2] for nm in T}
          K2 = lambda nm: (nm, b2)
          cu, pv, lc, lp = cur[b2], prv[b2], lcur[b2], lprv[b2]
          stage(-1)
          ld(cu[:], tm[t0:t0 + 128, TMF_R:TMF_R + 1024], K2("cur"))
          ld(lc[:], fm[FMF_WL:FMF_WL + 128, t0:t0 + 128], K2("lcur"))
          if blk == 0:
              dv(lambda e, o=pv[:]: e.memset(o, 0.0), [], [K2("prv")])
              dv(lambda e, o=lp[:]: e.memset(o, 0.0), [], [K2("lprv")])
              ld(pv[1:128, :], tm[0:127, TMF_R:TMF_R + 1024], K2("prv"))
              ld(lp[:, 1:128], fm[FMF_WL:FMF_WL + 128, 0:127], K2("lprv"))
          else:
              ld(pv[:], tm[t0 - 1:t0 + 127, TMF_R:TMF_R + 1024], K2("prv"))
              ld(lp[:], fm[FMF_WL:FMF_WL + 128, t0 - 1:t0 + 127], K2("lprv"))
          stage(-0.5)
          dv(lambda e, o=pv[:], c=cu[:]: e.tensor_tensor(out=o, in0=o, in1=c, op=ALU.subtract), [K2("prv"), K2("cur")], [K2("prv")])
          dv(lambda e, o=pv[:]: e.tensor_tensor(out=o, in0=o, in1=mu_tm[:], op=ALU.mult), [K2("prv"), "mu_tm"], [K2("prv")], eng="pool")
          dv(lambda e, o=cu[:], d=pv[:]: e.tensor_tensor(out=o, in0=o, in1=d, op=ALU.add), [K2("prv"), K2("cur")], [K2("cur")])
          dv(lambda e, o=lp[:], c=lc[:]: e.tensor_tensor(out=o, in0=o, in1=c, op=ALU.subtract), [K2("lprv"), K2("lcur")], [K2("lprv")])
          dv(lambda e, o=lc[:], d=lp[:]: e.scalar_tensor_tensor(out=o, in0=d, scalar=mu_fm[:, 0:1], in1=o, op0=ALU.mult, op1=ALU.add),
             [K2("lprv"), K2("lcur"), "mu_fm"], [K2("lcur")])
          stage(-0.2)
          P.op("act", lambda e, o=lc[0:64, :]: e.activation(out=o, in_=o, func=AF.Tanh), reads=[K2("lcur")], writes=[K2("lcur")])
          r_, k_, v_, z_ = cu[:, 0:256], cu[:, 256:512], cu[:, 512:768], cu[:, 768:1024]
          P.op("act", lambda e, o=B["vb"][:], v_=v_: e.copy(out=o, in_=v_), reads=[K2("cur")], writes=[K2("vb")])
          stage(1)
          mm(pP[0][:, 0:256], lc[0:64, :], w2a2[0:64, :], [K2("lcur"), "w2a2"], [("pP", 0)])
          mm(pP[1][:, 0:256], lc[64:128, :], w2a2[64:128, :], [K2("lcur"), "w2a2"], [("pP", 1)])
          dv(lambda e, o=B["lw"][:], s=pP[0][:, 0:256]: e.tensor_tensor(out=o, in0=s, in1=vec["rw_w0"][:], op=ALU.add),
             [("pP", 0), "rw_w0"], [K2("lw")])
          dv(lambda e, o=B["asig"][:], s=pP[1][:, 0:256]: e.tensor_tensor(out=o, in0=s, in1=vec["rw_a0"][:], op=ALU.add),
             [("pP", 1), "rw_a0"], [K2("asig")])
          P.op("act", lambda e, o=B["lw"][:]: e.activation(out=o, in_=o, func=AF.Sigmoid), reads=[K2("lw")], writes=[K2("lw")])
          P.op("act", lambda e, o=B["asig"][:]: e.activation(out=o, in_=o, func=AF.Sigmoid), reads=[K2("asig")], writes=[K2("asig")])
          dv(lambda e, o=B["lw"][:]: e.tensor_scalar(out=o, in0=o, scalar1=RW_DECAY_C, scalar2=None, op0=ALU.mult), [K2("lw")], [K2("lw")])
          stage(2)
          s4 = st4[b2]
          v3 = lambda ap: ap.rearrange("p (h j) -> p h j", h=4)
          dv(lambda e, o=B["kkn"][:], k_=k_: e.tensor_tensor(out=o, in0=k_, in1=vec["rw_kk"][:], op=ALU.mult), [K2("cur"), "rw_kk"], [K2("kkn")])
          dv(lambda e, o=B["tmp"][:], s=B["kkn"][:]: e.tensor_tensor(out=o, in0=s, in1=s, op=ALU.mult), [K2("kkn")], [K2("tmp")], eng="pool")
          dv(lambda e, o=s4[:, 0:4], s=v3(B["tmp"][:]): e.tensor_reduce(out=o, in_=s, axis=AX.X, op=ALU.add), [K2("tmp")], [K2("st4")])
          dv(lambda e, o=s4[:, 0:4]: e.tensor_scalar(out=o, in0=o, scalar1=1e-12, scalar2=None, op0=ALU.add), [K2("st4")], [K2("st4")])
          P.op("act", lambda e, o=s4[:, 0:4]: e.activation(out=o, in_=o, func=AF.Sqrt), reads=[K2("st4")], writes=[K2("st4")])
          dv(lambda e, o=s4[:, 0:4]: e.reciprocal(out=o, in_=o), [K2("st4")], [K2("st4")])
          dv(lambda e, o=v3(B["kkn"][:]), s=s4[:, 0:4].unsqueeze(2).to_broadcast([128, 4, 64]): e.tensor_tensor(out=o, in0=o, in1=s, op=ALU.mult),
             [K2("kkn"), K2("st4")], [K2("kkn")])
          dv(lambda e, o=B["tmp"][:], s=B["asig"][:]: e.scalar_tensor_tensor(out=o, in0=s, scalar=-1.0, in1=vec["rw_ka"][:], op0=ALU.add,
                                                                            op1=ALU.mult), [K2("asig"), "rw_ka"], [K2("tmp")])
          dv(lambda e, o=B["kp"][:], s=B["tmp"][:], k_=k_: e.scalar_tensor_tensor(out=o, in0=s, scalar=1.0, in1=k_, op0=ALU.add, op1=ALU.mult),
             [K2("tmp"), K2("cur")], [K2("kp")])
          dv(lambda e, o=B["aa"][:], s=B["kkn"][:]: e.tensor_scalar(out=o, in0=s, scalar1=-1.0, scalar2=None, op0=ALU.mult),
             [K2("kkn")], [K2("aa")], eng="pool")
          dv(lambda e, o=B["bb"][:], s=B["kkn"][:], a=B["asig"][:]: e.tensor_tensor(out=o, in0=s, in1=a, op=ALU.mult),
             [K2("kkn"), K2("asig")], [K2("bb")], eng="pool")
          dv(lambda e, o=B["tmp2"][:], s=B["kp"][:], r_=r_: e.tensor_tensor(out=o, in0=r_, in1=s, op=ALU.mult), [K2("cur"), K2("kp")], [K2("tmp2")])
          dv(lambda e, o=B["tmp2"][:]: e.tensor_tensor(out=o, in0=o, in1=vec["rw_rk"][:], op=ALU.mult), [K2("tmp2"), "rw_rk"], [K2("tmp2")])
          dv(lambda e, o=s4[:, 4:8], s=v3(B["tmp2"][:]): e.tensor_reduce(out=o, in_=s, axis=AX.X, op=ALU.add), [K2("tmp2")], [K2("st4")])
          dv(lambda e, o=v3(B["bon"][:]), s=v3(cu[:, 512:768]), c=s4[:, 4:8].unsqueeze(2).to_broadcast([128, 4, 64]):
             e.tensor_tensor(out=o, in0=s, in1=c, op=ALU.mult), [K2("cur"), K2("st4")], [K2("bon")])
          stage(3)
          mm(pP[0][:, 0:256], cst["mask_u"][:], B["lw"][:], ["mask_u", K2("lw")], [("pP", 0)])
          mm(pP[0][:, 256:512], cst["ones_bd"][:], B["lw"][:], ["ones_bd", K2("lw")], [("pP", 0)])
          evac(B["cum"][:], pP[0][:, 0:256], [("pP", 0)], [K2("cum")])
          evac(B["cumL"][:], pP[0][:, 256:512], [("pP", 0)], [K2("cumL")])
          for p in range(2):
              for c2 in range(2):
                  mm(pP[1][:, c2 * 2 + p:c2 * 2 + p + 1], B["lw"][:, p * 128:(p + 1) * 128], cst["ones_bd"][:, c2 * 64:c2 * 64 + 1],
                     [K2("lw"), "ones_bd"], [("pP", 1)])
          P.op("act", lambda e, o=gL[b2][:], s=pP[1][:, 0:4]: e.activation(out=o, in_=s, func=AF.Exp), reads=[("pP", 1)], writes=[K2("gL")])
          P.op("act", lambda e, o=B["e1"][:], s=B["cum"][:]: e.activation(out=o, in_=s, func=AF.Exp), reads=[K2("cum")], writes=[K2("e1")])
          dv(lambda e, o=B["rt"][:], s=B["e1"][:], r_=r_: e.tensor_tensor(out=o, in0=r_, in1=s, op=ALU.mult), [K2("cur"), K2("e1")], [K2("rt")])
          dv(lambda e, o=B["tmp"][:], s=B["cum"][:], l=B["lw"][:]: e.tensor_tensor(out=o, in0=s, in1=l, op=ALU.subtract),
             [K2("cum"), K2("lw")], [K2("tmp")], eng="pool")
          P.op("act", lambda e, o=B["tmp"][:]: e.activation(out=o, in_=o, func=AF.Exp), reads=[K2("tmp")], writes=[K2("tmp")])
          dv(lambda e, o=B["at"][:], s=B["aa"][:], t=B["tmp"][:]: e.tensor_tensor(out=o, in0=s, in1=t, op=ALU.mult),
             [K2("aa"), K2("tmp")], [K2("at")])
          P.op("act", lambda e, o=B["e1"][:], s=B["cum"][:]: e.activation(out=o, in_=s, func=AF.Exp, scale=-1.0),
               reads=[K2("cum"), K2("rt")], writes=[K2("e1")])
          dv(lambda e, o=B["bt"][:], s=B["bb"][:], t=B["e1"][:]: e.tensor_tensor(out=o, in0=s, in1=t, op=ALU.mult),
             [K2("bb"), K2("e1")], [K2("bt")])
          dv(lambda e, o=B["kt"][:], s=B["kp"][:], t=B["e1"][:]: e.tensor_tensor(out=o, in0=s, in1=t, op=ALU.mult),
             [K2("kp"), K2("e1")], [K2("kt")], eng="pool")
          dv(lambda e, o=B["tmp2"][:], s=B["cumL"][:], c=B["cum"][:]: e.tensor_tensor(out=o, in0=s, in1=c, op=ALU.subtract),
             [K2("cumL"), K2("cum")], [K2("tmp2")], eng="pool")
          P.op("act", lambda e, o=B["tmp2"][:]: e.activation(out=o, in_=o, func=AF.Exp), reads=[K2("tmp2")], writes=[K2("tmp2")])
          dv(lambda e, o=B["bh"][:], s=B["bb"][:], t=B["tmp2"][:]: e.tensor_tensor(out=o, in0=s, in1=t, op=ALU.mult),
             [K2("bb"), K2("tmp2")], [K2("bh")])
          dv(lambda e, o=B["kh"][:], s=B["kp"][:], t=B["tmp2"][:]: e.tensor_tensor(out=o, in0=s, in1=t, op=ALU.mult),
             [K2("kp"), K2("tmp2")], [K2("kh")], eng="pool")
          stage(4)
          for qi_, (nm_src, nm_dst) in enumerate((("at", "atT"), ("bt", "btT"), ("kt", "ktT"), ("rt", "rtT"))):
              for p in range(2):
                  P.op("pe", lambda e, o=pTb[:, (qi_ * 2 + p) * 128:(qi_ * 2 + p + 1) * 128], s=B[nm_src][:, p * 128:(p + 1) * 128]: e.transpose(
                      out=o, in_=s, identity=identb[:]), reads=[K2(nm_src), "identb"], writes=["pTb"])
          for qi_, (nm_src, nm_dst) in enumerate((("at", "atT"), ("bt", "btT"), ("kt", "ktT"), ("rt", "rtT"))):
              for p in range(2):
                  evac(TR[nm_dst][b2][p][:], pTb[:, (qi_ * 2 + p) * 128:(qi_ * 2 + p + 1) * 128], ["pTb"], [(nm_dst, b2, p)])
          marks.append(len(rec))
          slot = lambda h: (h % 2) * 2 + h // 2
          rk = lambda nm, h: (nm, b2, h // 2)
          opd = {}
          for h in range(4):
              p, r0 = h // 2, (h % 2) * 64
              opd[h] = {nm: TR[nm2][b2][p][r0:r0 + 64, :] for nm, nm2 in (("aT", "atT"), ("bT", "btT"), ("kT", "ktT"), ("rT", "rtT"))}
          jx, jy2 = nextA(), nextA()
          for r_, jb in ((0, jx), (1, jy2)):
              for p in range(2):
                  h = 2 * p + r_
                  o_ = opd[h]
                  mm(pA[jb][:, p * 128:(p + 1) * 128], o_["aT"], o_["bT"], [rk("atT", h), rk("btT", h)], [("pA", jb)])
                  mm(pA[jb][:, 256 + p * 128:256 + (p + 1) * 128], o_["bT"], o_["aT"], [rk("atT", h), rk("btT", h)], [("pA", jb)])
          for r_, jb in ((0, jx), (1, jy2)):
              sl_ = slice(2 * r_, 2 * r_ + 2)
              dv(lambda e, o=H["N"][:, sl_, :], s=pA[jb][:, 0:256].rearrange("p (a t) -> p a t", a=2), m=mask2["mask_sl"][:]:
                 e.tensor_tensor(out=o, in0=s, in1=m, op=ALU.mult), [("pA", jb), "mask_sl2"], [("N", r_)])
              dv(lambda e, o=H["NT"][:, sl_, :], s=pA[jb][:, 256:512].rearrange("p (a t) -> p a t", a=2), m=mask2["mask_su"][:]:
                 e.tensor_tensor(out=o, in0=s, in1=m, op=ALU.mult), [("pA", jb), "mask_su2"], [("NT", r_)])
          jz = nextA()
          jx2 = nextA()
          for r_, jb in ((0, jx2), (1, jz)):
              for p in range(2):
                  h = 2 * p + r_
                  o_ = opd[h]
                  mm(pA[jb][:, p * 128:(p + 1) * 128], o_["kT"], o_["aT"], [rk("ktT", h), rk("atT", h)], [("pA", jb)])
                  mm(pA[jb][:, 256 + p * 128:256 + (p + 1) * 128], o_["bT"], o_["rT"], [rk("btT", h), rk("rtT", h)], [("pA", jb)])
          for r_, jb in ((0, jx2), (1, jz)):
              sl_ = slice(2 * r_, 2 * r_ + 2)
              dv(lambda e, o=H["MakT"][:, sl_, :], s=pA[jb][:, 0:256].rearrange("p (a t) -> p a t", a=2), m=mask2["mask_su"][:]:
                 e.tensor_tensor(out=o, in0=s, in1=m, op=ALU.mult), [("pA", jb), "mask_su2"], [("MakT", r_)])
              dv(lambda e, o=HS["MrbT"][b2][:, sl_, :], s=pA[jb][:, 256:512].rearrange("p (a t) -> p a t", a=2), m=mask2["mask_u"][:]:
                 e.tensor_tensor(out=o, in0=s, in1=m, op=ALU.mult), [("pA", jb), "mask_u2"], [("MrbT", b2, r_)])
          jk0, jk1 = nextA(), nextA()
          for r_, jb in ((0, jk0), (1, jk1)):
              for p in range(2):
                  h = 2 * p + r_
                  o_ = opd[h]
                  mm(pA[jb][:, p * 128:(p + 1) * 128], o_["kT"], o_["rT"], [rk("ktT", h), rk("rtT", h)], [("pA", jb)])
          for r_, jb in ((0, jk0), (1, jk1)):
              sl_ = slice(2 * r_, 2 * r_ + 2)
              dv(lambda e, o=HS["MrkT"][b2][:, sl_, :], s=pA[jb][:, 0:256].rearrange("p (a t) -> p a t", a=2), m=mask2["mask_u"][:]:
                 e.tensor_tensor(out=o, in0=s, in1=m, op=ALU.mult), [("pA", jb), "mask_u2"], [("MrkT", b2, r_)])
          dv(lambda e, o=H["TTa"][:], s=H["NT"][:], i_=cst["ident_f"][:].unsqueeze(1).to_broadcast([128, 4, 128]):
             e.tensor_tensor(out=o, in0=s, in1=i_, op=ALU.add), [("NT", 0), ("NT", 1), "ident_f"], ["TTa"], eng="pool")
          cur_, curT_, nxt_, nxtT_ = "N", "NT", "Pa", "PaT"
          tc_, tn_ = "TTa", "TTb"
          kn = lambda nm: [(nm, 0), (nm, 1)] if nm in ("N", "NT") else [nm]
          for lvl in range(1, 6):
              ja = nextA()
              for sl in range(4):
                  mm(pA[ja][:, sl * 128:(sl + 1) * 128], H[curT_][:, sl, :], H[cur_][:, sl, :], kn(cur_) + kn(curT_), [("pA", ja)])
              evac(H[nxt_][:], pA[ja][:].rearrange("p (a t) -> p a t", a=4), [("pA", ja)], [nxt_])
              if lvl < 5:
                  jb = nextA()
                  for sl in range(4):
                      mm(pA[jb][:, sl * 128:(sl + 1) * 128], H[cur_][:, sl, :], H[curT_][:, sl, :], kn(cur_) + kn(curT_), [("pA", jb)])
                  evac(H[nxtT_][:], pA[jb][:].rearrange("p (a t) -> p a t", a=4), [("pA", jb)], [nxtT_])
              jc = nextA()
              for sl in range(4):
                  mm(pA[jc][:, sl * 128:(sl + 1) * 128], H[nxt_][:, sl, :], H[tc_][:, sl, :], [nxt_, tc_], [("pA", jc)])
              dv(lambda e, o=H[tn_][:], s=pA[jc][:].rearrange("p (a t) -> p a t", a=4), t=H[tc_][:]: e.tensor_tensor(out=o, in0=s, in1=t, op=ALU.add),
                 [("pA", jc), tc_], [tn_])
              if lvl == 1:
                  cur_, curT_, nxt_, nxtT_ = "Pa", "PaT", "Pb", "PbT"
              else:
                  cur_, curT_, nxt_, nxtT_ = nxt_, nxtT_, cur_, curT_
              tc_, tn_ = tn_, tc_
          TTn = tc_
          ja = nextA()
          for h in range(4):
              p, r0 = h // 2, (h % 2) * 64
              mm(pA[ja][r0:r0 + 64, p * 128:(p + 1) * 128], B["at"][:, h * 64:(h + 1) * 64], H[TTn][:, slot(h), :], [K2("at"), TTn], [("pA", ja)])
              mm(pA[ja][:, 256 + h * 64:256 + (h + 1) * 64], H["MakT"][:, slot(h), :], B["vb"][:, h * 64:(h + 1) * 64],
                 [("MakT", h % 2), K2("vb")], [("pA", ja)])
          evac(HS["P1T"][b2][:], pA[ja][:, 0:256].rearrange("p (a t) -> p a t", a=2), [("pA", ja)], [("P1T", b2)])
          evac(H["W2"][:], pA[ja][:, 256:512].rearrange("p (h i) -> p h i", h=4), [("pA", ja)], ["W2"])
          jb = nextA()
          for h in range(4):
              mm(pA[jb][:, h * 64:(h + 1) * 64], H[TTn][:, slot(h), :], H["W2"][:, h, :], [TTn, "W2"], [("pA", jb)])
          evac(HS["P2"][b2][:], pA[jb][:, 0:256].rearrange("p (h i) -> p h i", h=4), [("pA", jb)], [("P2", b2)])
          stage(6)
          marks.append(len(rec))
          for c2 in range(2):
              cs = slice(c2 * 64, (c2 + 1) * 64)
              jq = nextQ()
              for h in range(4):
                  mm(pQ[jq][cs, h * 64:(h + 1) * 64], HS["P1T"][b2][:, h // 2, cs], STb[h][:, :], [("P1T", b2), ("STb", h)], [("pQ", jq)])
              dv(lambda e, o=Usb[cs, :, :], s=pQ[jq][cs, 0:256].rearrange("p (h i) -> p h i", h=4), t=HS["P2"][b2][cs, :, :]:
                 e.tensor_tensor(out=o, in0=s, in1=t, op=ALU.add), [("pQ", jq), ("P2", b2)], ["Usb"])
              jy = nextQ()
              for h in range(4):
                  p = h // 2
                  yo = pQ[jy][cs, h * 64:(h + 1) * 64]
                  mm(yo, TR["rtT"][b2][p][:, cs], STb[h][:, :], [("rtT", b2, p), ("STb", h)], [("pQ", jy)], start=True, stop=False)
                  mm(yo, HS["MrkT"][b2][:, slot(h), cs], B["vb"][:, h * 64:(h + 1) * 64], [("MrkT", b2, h % 2), K2("vb")], [("pQ", jy)],
                     start=False, stop=False)
                  mm(yo, HS["MrbT"][b2][:, slot(h), cs], Usb[:, h, :], [("MrbT", b2, h % 2), "Usb"], [("pQ", jy)], start=False, stop=True)
              evac(B["yb"][cs, :], pQ[jy][cs, 0:256], [("pQ", jy)], [K2("yb")])
              js = nextQ()
              for h in range(4):
                  r0 = (h % 2) * 64
                  rows = slice(r0, r0 + 64)
                  so = pQ[js][rows, h * 64:(h + 1) * 64]
                  mm(so, B["kh"][cs, h * 64:(h + 1) * 64], B["vb"][cs, h * 64:(h + 1) * 64], [K2("kh"), K2("vb")], [("pQ", js)],
                     start=True, stop=False)
                  mm(so, B["bh"][cs, h * 64:(h + 1) * 64], Usb[cs, h, :], [K2("bh"), "Usb"], [("pQ", js)], start=False, stop=True)
              for h in range(4):
                  p, r0 = h // 2, (h % 2) * 64
                  rows = slice(r0, r0 + 64)
                  dv(lambda e, o=ST[h][rows, :], s=pQ[js][rows, h * 64:(h + 1) * 64], g=gL[b2][rows, c2 * 2 + p:c2 * 2 + p + 1]:
                     e.scalar_tensor_tensor(out=o, in0=o, scalar=g, in1=s, op0=ALU.mult, op1=ALU.add),
                     [("ST", h), ("pQ", js), K2("gL")], [("ST", h)])
                  P.op("act", lambda e, o=STb[h][rows, :], s=ST[h][rows, :]: e.copy(out=o, in_=s), reads=[("ST", h)], writes=[("STb", h)])
          stage(7)
          yb = B["yb"]
          dv(lambda e, o=s4[:, 8:12], s=v3(yb[:]): e.tensor_reduce(out=o, in_=s, axis=AX.X, op=ALU.add), [K2("yb")], [K2("st4")])
          dv(lambda e, o=B["tmp"][:], s=yb[:]: e.tensor_tensor(out=o, in0=s, in1=s, op=ALU.mult), [K2("yb")], [K2("tmp")], eng="pool")
          dv(lambda e, o=s4[:, 12:16], s=v3(B["tmp"][:]): e.tensor_reduce(out=o, in_=s, axis=AX.X, op=ALU.add), [K2("tmp")], [K2("st4")])
          dv(lambda e, o=s4[:, 8:16]: e.tensor_scalar(out=o, in0=o, scalar1=1.0 / 64, scalar2=None, op0=ALU.mult), [K2("st4")], [K2("st4")])
          dv(lambda e, o=s4[:, 0:4], m=s4[:, 8:12]: e.tensor_tensor(out=o, in0=m, in1=m, op=ALU.mult), [K2("st4")], [K2("st4")])
          dv(lambda e, o=s4[:, 12:16], m2=s4[:, 0:4]: e.tensor_tensor(out=o, in0=o, in1=m2, op=ALU.subtract), [K2("st4")], [K2("st4")])
          dv(lambda e, o=s4[:, 12:16]: e.tensor_scalar(out=o, in0=o, scalar1=GN_EPS, scalar2=None, op0=ALU.add), [K2("st4")], [K2("st4")])
          P.op("act", lambda e, o=s4[:, 12:16]: e.activation(out=o, in_=o, func=AF.Sqrt), reads=[K2("st4")], writes=[K2("st4")])
          dv(lambda e, o=s4[:, 12:16]: e.reciprocal(out=o, in_=o), [K2("st4")], [K2("st4")])
          dv(lambda e, o=v3(yb[:]), m=s4[:, 8:12].unsqueeze(2).to_broadcast([128, 4, 64]): e.tensor_tensor(out=o, in0=o, in1=m, op=ALU.subtract),
             [K2("yb"), K2("st4")], [K2("yb")])
          dv(lambda e, o=v3(yb[:]), r=s4[:, 12:16].unsqueeze(2).to_broadcast([128, 4, 64]): e.tensor_tensor(out=o, in0=o, in1=r, op=ALU.mult),
             [K2("yb"), K2("st4")], [K2("yb")])
          dv(lambda e, o=yb[:]: e.tensor_tensor(out=o, in0=o, in1=vec["rw_gng"][:], op=ALU.mult), [K2("yb"), "rw_gng"], [K2("yb")], eng="pool")
          dv(lambda e, o=yb[:]: e.tensor_tensor(out=o, in0=o, in1=vec["rw_gnb"][:], op=ALU.add), [K2("yb"), "rw_gnb"], [K2("yb")])
          dv(lambda e, o=yb[:], b_=B["bon"][:]: e.tensor_tensor(out=o, in0=o, in1=b_, op=ALU.add), [K2("yb"), K2("bon")], [K2("yb")], eng="pool")
          P.op("act", lambda e, o=B["tmp2"][:], z_=z_: e.activation(out=o, in_=z_, func=AF.Silu), reads=[K2("cur")], writes=[K2("tmp2")])
          dv(lambda e, o=yb[:], z=B["tmp2"][:]: e.tensor_tensor(out=o, in0=o, in1=z, op=ALU.mult), [K2("yb"), K2("tmp2")], [K2("yb")])
          P.dma("sp", lambda e, o=y_dst[t0:t0 + 128, :], s=yb[:]: e.dma_start(out=o, in_=s), reads=[K2("yb")], writes=[("y_rwkv", blk)])
      except _Stop:
          pass
    P.op, P.dma = real_op, real_dma
    def parts(bi):
        rec, marks = recs[bi]
        m = marks[:2] if len(marks) >= 2 else (marks + [len(rec), len(rec)])[:2]
        return [rec[0:m[0]], rec[m[0]:m[1]], rec[m[1]:]]

    allp = [parts(bi) for bi in range(len(recs))]
    nb_ = len(recs)
    if stagger:
        for t in range(nb_ + 2):
            lists = []
            for st_ in range(3):
                bi = t - st_
                if 0 <= bi < nb_ and allp[bi][st_]:
                    lists.append(allp[bi][st_])
            merged = []
            for li, L in enumerate(lists):
                for pi, item in enumerate(L):
                    merged.append(((pi + 0.5) / len(L), li, pi, item))
            merged.sort(key=lambda t_: (t_[0], t_[1]))
            for _, _, _, (f, eng, fn, reads, writes) in merged:
                f(eng, fn, reads=reads, writes=writes)
    else:
        for rec, _ in recs:
            for (f, eng, fn, reads, writes) in rec:
                f(eng, fn, reads=reads, writes=writes)
    ph.close()


def phase_outproj(nc, y_tok, x_res, w_out, ident_d, x_new):
    ph = Phase(nc)
    P = ph.P
    D = D_MODEL
    KT = D // 128
    T = OWN
    TT = T // 128
    KQ = 8
    NKQ = KT // KQ
    NB = 512
    ident = ph.sb([128, 128], BF16, "ident")
    hT = ph.sb([128, KT, T], BF16, "hT")
    xt = [ph.sb([128, D], F32, "xt") for _ in range(2)]
    hb = [ph.sb([128, D], BF16, "hb") for _ in range(2)]
    wb = [ph.sb([128, KT, NB], BF16, "wb") for _ in range(2)]
    stg = [ph.sb([128, NB], F32, "stg") for _ in range(4)]
    rs = [ph.sb([128, NB], F32, "rs") for _ in range(4)]
    pT = [ph.ps([128, 1024], BF16, "pT") for _ in range(2)]
    pM = [ph.ps([128, 512], F32, "pM") for _ in range(6)]
    P.dma("sp", lambda e: e.dma_start(out=ident[:], in_=ident_d[:, :]), writes=["ident"])
    tc_ = 0
    for tt in range(TT):
        i = tt % 2
        P.dma("sp", lambda e, o=xt[i][:], s=y_tok[tt * 128:(tt + 1) * 128, :]: e.dma_start(out=o, in_=s), writes=[("xt", i)])
        P.op("act", lambda e, o=hb[i][:], s=xt[i][:]: e.copy(out=o, in_=s), reads=[("xt", i)], writes=[("hb", i)])
        for kq in range(NKQ):
            pb = tc_ % 2
            tc_ += 1
            for k8 in range(KQ):
                kt = kq * KQ + k8
                P.op("pe", lambda e, o=pT[pb][:, k8 * 128:(k8 + 1) * 128], s=hb[i][:, kt * 128:(kt + 1) * 128]: e.transpose(
                    out=o, in_=s, identity=ident[:]), reads=[("hb", i), "ident"], writes=[("pT", pb)])
            dst = hT[:, kq * KQ:(kq + 1) * KQ, tt * 128:(tt + 1) * 128]
            src = pT[pb][:].rearrange("p (k t) -> p k t", k=KQ)
            if kq % 2 == 0:
                P.op("dve", lambda e, dst=dst, src=src: e.tensor_copy(out=dst, in_=src), reads=[("pT", pb)], writes=[("hT", tt, kq)])
            else:
                P.op("act", lambda e, dst=dst, src=src: e.copy(out=dst, in_=src), reads=[("pT", pb)], writes=[("hT", tt, kq)])
    wv = w_out.rearrange("(kt p) n -> p kt n", p=128)
    mc = 0
    for cb in range(D // NB):
        c0 = cb * NB
        wi = cb % 2
        for kq in range(NKQ):
            P.dma("pool", lambda e, o=wb[wi][:, kq * KQ:(kq + 1) * KQ, :], s=wv[:, kq * KQ:(kq + 1) * KQ, c0:c0 + NB]: e.dma_start(
                out=o, in_=s), writes=[("wb", wi, kq)])
        for tt in range(TT):
            j = mc % 6
            s4 = mc % 4
            mc += 1
            for kt in range(KT):
                P.op("pe", lambda e, o=pM[j][:], l=hT[:, kt, tt * 128:(tt + 1) * 128], r=wb[wi][:, kt, :], kt=kt: e.matmul(
                    o, lhsT=l, rhs=r, start=(kt == 0), stop=(kt == KT - 1)),
                    reads=[("hT", tt, kt // KQ), ("wb", wi, kt // KQ)], writes=[("pM", j)])
            P.dma("sp", lambda e, o=rs[s4][:], s=x_res[tt * 128:(tt + 1) * 128, c0:c0 + NB]: e.dma_start(out=o, in_=s), writes=[("rs", s4)])
            P.op("dve", lambda e, o=stg[s4][:], a=pM[j][:], b_=rs[s4][:]: e.tensor_tensor(out=o, in0=a, in1=b_, op=ALU.add),
                 reads=[("pM", j), ("rs", s4)], writes=[("stg", s4)])
            P.dma("sp", lambda e, o=x_new[tt * 128:(tt + 1) * 128, c0:c0 + NB], s=stg[s4][:]: e.dma_start(out=o, in_=s),
                  reads=[("stg", s4)], writes=[("xn", tt, cb)])
    ph.close()


def phase_finalnorm(nc, x_in, g, out, ntiles=OWN // 128):
    ph = Phase(nc)
    P = ph.P
    D = D_MODEL
    gb = ph.sb([128, D], F32, "gb")
    xt = [ph.sb([128, D], F32, "xt") for _ in range(2)]
    junk = ph.sb([128, D], BF16, "junk")
    ss = ph.sb([128, ntiles], F32, "ss")
    P.dma("sp", lambda e: e.dma_start(out=gb[:], in_=g[0:1, :].partition_broadcast(128)), writes=["gb"])
    P.op("dve", lambda e: e.memset(ss[:], 0.0), writes=["ss"])
    for tt in range(ntiles):
        i = tt % 2
        P.dma("sp", lambda e, o=xt[i][:], s=x_in[tt * 128:(tt + 1) * 128, :]: e.dma_start(out=o, in_=s), writes=[("xt", i)])
        sc = ss[:, tt:tt + 1]
        P.op("act", lambda e, s=xt[i][:], sc=sc: e.activation(out=junk[:], in_=s, func=AF.Square, accum_out=sc),
             reads=[("xt", i), "ss"], writes=["junk", ("ssv", tt)])
        P.op("dve", lambda e, sc=sc: e.tensor_scalar(out=sc, in0=sc, scalar1=1.0 / D, scalar2=NORM_EPS, op0=ALU.mult, op1=ALU.add),
             reads=[("ssv", tt)], writes=[("ssv", tt)])
        P.op("act", lambda e, sc=sc: e.activation(out=sc, in_=sc, func=AF.Sqrt), reads=[("ssv", tt)], writes=[("ssv", tt)])
        P.op("dve", lambda e, sc=sc: e.reciprocal(out=sc, in_=sc), reads=[("ssv", tt)], writes=[("ssv", tt)])
        P.op("dve", lambda e, o=xt[i][:], sc=sc: e.scalar_tensor_tensor(out=o, in0=o, scalar=sc, in1=gb[:], op0=ALU.mult, op1=ALU.mult),
             reads=[("xt", i), ("ssv", tt), "gb"], writes=[("xt", i)])
        P.dma("sp", lambda e, o=out[tt * 128:(tt + 1) * 128, :], s=xt[i][:]: e.dma_start(out=o, in_=s), reads=[("xt", i)],
              writes=[("out", tt)])
    ph.close()


def own_tok(q):
    return ((4 * np.arange(8)[:, None] + q) * 128 + np.arange(128)[None, :]).reshape(-1)


def const_inputs():
    i = np.arange(128)
    sel4 = np.zeros((4, 4, 128), np.float32)
    for h in range(4):
        sel4[h, h, :] = 1.0
    same = (i[:, None] // 64) == (i[None, :] // 64)
    sl = ((i[None, :] < i[:, None]) & same).astype(np.float32)
    su = np.ascontiguousarray(sl.T)
    return {
        "ident": np.eye(128, dtype=ml_dtypes.bfloat16),
        "ident_f": np.eye(128, dtype=np.float32), "ident_b": np.eye(128, dtype=ml_dtypes.bfloat16),
        "tri_le": (i[:, None] <= i[None, :]).astype(np.float32), "ones_f": np.ones((128, 128), np.float32),
        "sel4": sel4, "negbig_lt": np.where(i[None, :] < i[:, None], NEG_BIG, 0.0).astype(np.float32),
        "ident30k_b": (30000.0 * np.eye(128)).astype(ml_dtypes.bfloat16),
        "iota512": np.arange(512, dtype=np.float32)[None, :].copy(),
        "pow2": (0.5 ** (np.arange(NBIS) + 1)).astype(np.float32)[None, :].copy(),
        "mask_sl": sl, "mask_su": su, "mask_u": su + np.eye(128, dtype=np.float32), "ones_bd": same.astype(np.float32),
    }


def layer_params(inp, l, q):
    c = np.ascontiguousarray
    p = {}
    p["qrel"] = (q * 128 + np.arange(128, dtype=np.float32))[:, None].copy()
    p["g"] = c(inp["norm_g"][l][None, :])
    p["mlp_ln_g"] = c(inp["mlp_ln_g"][l][None, :])
    p["mlp_ln_b"] = c(inp["mlp_ln_b"][l][None, :])
    p["mlp_wsT"] = c(inp["mlp_w_s"][l].transpose(2, 0, 1))
    p["mlp_bsT"] = c(inp["mlp_b_s"][l].T)
    cols = np.concatenate([np.arange(256 * q, 256 * (q + 1)), 1024 + np.arange(128 * q, 128 * (q + 1)),
                           1536 + np.arange(128 * q, 128 * (q + 1))])
    cw = inp["ssm_conv_w"][l][:, cols]
    cb = inp["ssm_conv_b"][l][cols]
    hs = slice(4 * q, 4 * q + 4)
    p["ssm_cw"] = c(cw.reshape(4, 4, 128).transpose(2, 1, 0))
    p["ssm_cb"] = c(cb.reshape(4, 128).T)
    p["ssm_dtb_t"] = c(np.tile(inp["ssm_dt_bias"][l][hs], 32)[None, :])
    p["ssm_alog_t"] = c(np.tile(inp["ssm_A_log"][l][hs], 32)[None, :])
    p["ssm_D"] = c(inp["ssm_D"][l][hs][None, :])
    p["ssm_ng"] = c(inp["ssm_norm_g"][l][256 * q:256 * (q + 1)][None, :])
    mu = inp["rwkv_mu"][l]
    hc = np.arange(256 * q, 256 * (q + 1))
    p["rw_mu_tm"] = c(np.concatenate([mu[1024 * k + hc] for k in range(4)])[None, :])
    p["rw_mu_fm"] = c(mu[4096:4224][:, None])
    p["rw_w2a2"] = c(np.concatenate([inp["rwkv_w2"][l][:, hc], inp["rwkv_a2"][l][:, hc]], axis=0))
    for nm, src in (("rw_w0", "rwkv_w0"), ("rw_a0", "rwkv_a0"), ("rw_kk", "rwkv_k_k"), ("rw_ka", "rwkv_k_a"), ("rw_rk", "rwkv_r_k"),
                    ("rw_gng", "rwkv_gn_g"), ("rw_gnb", "rwkv_gn_b")):
        p[nm] = c(inp[src][l].reshape(-1)[hc][None, :])
    return p


def _decl(nc, arrs):
    out = {}
    for n, a in arrs.items():
        dt = BF16 if a.dtype == ml_dtypes.bfloat16 else F32
        out[n] = nc.dram_tensor(n, list(a.shape), dt, kind="ExternalInput").ap()
    return out


def build_AB(sample_inputs):
    nc = bass.Bass("TRN2", target_bir_lowering=False)
    d = _decl(nc, sample_inputs)
    wts = {n: d["w_" + n] for n in GROUP_INFO}
    scr = make_scratch(nc)
    maskT = nc.dram_tensor("maskT", [128, NPAIR, 128], BF16).ap()
    y_own = nc.dram_tensor("y_own", [OWN, 2048], F32, kind="ExternalOutput").ap()
    y_all = nc.dram_tensor("y_all", [SEQ, 512], F32, kind="ExternalOutput").ap()
    phase_inproj(nc, d["x_all"], d["x_own"], d["g"], d["ident"], wts, scr)
    phase_indexer(nc, scr, d, maskT)
    phase_attn(nc, scr, d, maskT, y_own)
    phase_mlp(nc, scr, d, y_own)
    phase_ssm(nc, scr, d, y_all)
    phase_rwkv(nc, scr, d, y_all)
    return nc


def build_C(final):
    nc = bass.Bass("TRN2", target_bir_lowering=False)
    y_tok = nc.dram_tensor("y_tok", [OWN, D_MODEL], F32, kind="ExternalInput").ap()
    x_res = nc.dram_tensor("x_res", [OWN, D_MODEL], F32, kind="ExternalInput").ap()
    w_out = nc.dram_tensor("w_out", [D_MODEL, D_MODEL], F32, kind="ExternalInput").ap()
    ident = nc.dram_tensor("ident", [128, 128], BF16, kind="ExternalInput").ap()
    if final:
        g = nc.dram_tensor("gf", [1, D_MODEL], F32, kind="ExternalInput").ap()
        x_mid = nc.dram_tensor("x_mid", [OWN, D_MODEL], F32).ap()
        out = nc.dram_tensor("out", [OWN, D_MODEL], F32, kind="ExternalOutput").ap()
        phase_outproj(nc, y_tok, x_res, w_out, ident, x_mid)
        phase_finalnorm(nc, x_mid, g, out)
    else:
        out = nc.dram_tensor("out", [OWN, D_MODEL], F32, kind="ExternalOutput").ap()
        phase_outproj(nc, y_tok, x_res, w_out, ident, out)
    return nc


def kernel_unfused(**inp):
    inp = {k: np.asarray(v) for k, v in inp.items()}
    x = inp["x"]
    cst = const_inputs()
    otk = [own_tok(q) for q in range(NQ)]
    nc_ab = None
    for l in range(2):
        w_in = inp["w_in"][l]
        in_maps = []
        for c in range(NCORE):
            b, q = c // NQ, c % NQ
            m = dict(cst)
            m.update(layer_params(inp, l, q))
            m["x_all"] = np.ascontiguousarray(x[b])
            m["x_own"] = np.ascontiguousarray(x[b][otk[q]])
            for name, cols in col_groups(q).items():
                m["w_" + name] = np.ascontiguousarray(w_in[:, cols])
            in_maps.append(m)
        if nc_ab is None:
            nc_ab = build_AB(in_maps[0])
        res = run_bass_kernel_spmd(nc_ab, in_maps, core_ids=list(range(NCORE))).results
        in_maps_c = []
        for c in range(NCORE):
            b, q = c // NQ, c % NQ
            y_tok = np.empty((OWN, D_MODEL), np.float32)
            y_tok[:, 0:1024] = res[c]["y_own"][:, 0:1024]
            y_tok[:, 3072:4096] = res[c]["y_own"][:, 1024:2048]
            for q2 in range(NQ):
                ya = res[b * NQ + q2]["y_all"][otk[q]]
                y_tok[:, 1024 + 256 * q2:1024 + 256 * (q2 + 1)] = ya[:, 0:256]
                y_tok[:, 2048 + 256 * q2:2048 + 256 * (q2 + 1)] = ya[:, 256:512]
            m = {"y_tok": y_tok, "x_res": np.ascontiguousarray(x[b][otk[q]]), "w_out": np.ascontiguousarray(inp["w_out"][l]),
                 "ident": cst["ident"]}
            if l == 1:
                m["gf"] = np.ascontiguousarray(inp["final_norm_g"][None, :])
            in_maps_c.append(m)
        nc_c = build_C(final=(l == 1))
        resc = run_bass_kernel_spmd(nc_c, in_maps_c, core_ids=list(range(NCORE))).results
        xn = np.empty_like(x)
        for c in range(NCORE):
            b, q = c // NQ, c % NQ
            xn[b][otk[q]] = resc[c]["out"]
        x = xn
    return x


def fused_ginfo():
    gi = {"fmb_all": (1152, "fm", BF16, "all"), "tmb_all": (1024, "tm", BF16, "all"),
          "fmb_own": (2048, "fm", BF16, "all"), "tmf_own": (4112, "tm", F32, "all")}
    for q in range(NQ):
        gi[f"fmf_all{q}"] = (640, "fm", F32, "all")
        gi[f"tmf_all{q}"] = (1284, "tm", F32, "all")
    return gi


FUSED_BLOCKS = [(qb // 4 + 1, qb % 4) for qb in range(32)]
Q_KEYS = ("ssm_cw", "ssm_cb", "ssm_dtb_t", "ssm_alog_t", "ssm_D", "ssm_ng", "rw_mu_tm", "rw_mu_fm", "rw_w2a2", "rw_w0", "rw_a0",
          "rw_kk", "rw_ka", "rw_rk", "rw_gng", "rw_gnb")
L_KEYS = ("g", "mlp_ln_g", "mlp_ln_b", "mlp_wsT", "mlp_bsT")


def fused_inputs(inp, b):
    c = np.ascontiguousarray
    m = dict(const_inputs())
    m["qrel"] = c((np.arange(4, dtype=np.float32)[None, :] * 128 + np.arange(128, dtype=np.float32)[:, None]))
    m["x"] = c(inp["x"][b])
    m["gf"] = c(inp["final_norm_g"][None, :])
    for l in range(2):
        w_in = inp["w_in"][l]
        cg0 = col_groups(0)
        for name in ("fmb_all", "tmb_all", "fmb_own", "tmf_own"):
            m[f"w{l}_{name}"] = c(w_in[:, cg0[name]])
        for q in range(NQ):
            cg = col_groups(q)
            m[f"w{l}_fmf_all{q}"] = c(w_in[:, cg["fmf_all"]])
            m[f"w{l}_tmf_all{q}"] = c(w_in[:, cg["tmf_all"]])
            lp = layer_params(inp, l, q)
            for key in Q_KEYS:
                m[f"l{l}q{q}_{key}"] = lp[key]
            if q == 0:
                for key in L_KEYS:
                    m[f"l{l}_{key}"] = lp[key]
        m[f"wout{l}"] = c(inp["w_out"][l])
    return m


def build_fused(sample, upto=None, nlayers=2, skip=()):
    nc = bass.Bass("TRN2", target_bir_lowering=False)
    d = _decl(nc, sample)
    gi = fused_ginfo()
    scr = {}
    for name, (ncols, layout, dt, _) in gi.items():
        shape = [SEQ, ncols] if layout == "tm" else [ncols, SEQ]
        scr[name] = nc.dram_tensor("scr_" + name, shape, dt).ap()
    _, npairs = pair_offsets(FUSED_BLOCKS)
    maskT = nc.dram_tensor("maskT", [128, npairs, 128], BF16).ap()
    y_full = nc.dram_tensor("y_full", [SEQ, D_MODEL], F32).ap()
    xs = [d["x"], nc.dram_tensor("x1", [SEQ, D_MODEL], F32).ap(), nc.dram_tensor("x2", [SEQ, D_MODEL], F32).ap()]
    out = nc.dram_tensor("out", [SEQ, D_MODEL], F32, kind="ExternalOutput").ap()
    gnames = list(gi.keys())
    cnt = [0]

    def go():
        cnt[0] += 1
        return (upto is None or cnt[0] <= upto) and cnt[0] not in skip

    for l in range(nlayers):
        x_cur, x_nxt = xs[l], xs[l + 1]
        wts = {name: d[f"w{l}_{name}"] for name in gnames}
        plan = [(x_cur, p * 1024, [(n, p * 1024) for n in gnames]) for p in range(4)]
        if go():
            phase_inproj(nc, None, None, d[f"l{l}_g"], d["ident"], wts, scr, plan=plan, ginfo=gi)
        prm_l = dict(d)
        for key in L_KEYS:
            prm_l[key] = d[f"l{l}_{key}"]
        sc_own = {"fmb_own": scr["fmb_own"], "fmb_all": scr["fmb_all"], "tmf_own": scr["tmf_own"], "tmb_all": scr["tmb_all"]}
        if go():
            phase_indexer(nc, sc_own, prm_l, maskT, blocks=FUSED_BLOCKS)
        if go():
            phase_attn(nc, sc_own, prm_l, maskT, y_full, blocks=FUSED_BLOCKS)
        if go():
            phase_mlp(nc, sc_own, prm_l, y_full, nchunks=SEQ // 128, ycol0=3072)
        for q in range(NQ):
            prm_q = dict(d)
            for key in Q_KEYS:
                prm_q[key] = d[f"l{l}q{q}_{key}"]
            sc_q = {"fmf_all": scr[f"fmf_all{q}"], "tmf_all": scr[f"tmf_all{q}"]}
            if go():
                phase_ssm(nc, sc_q, prm_q, None, y_dst=y_full[:, 1024 + 256 * q:1024 + 256 * (q + 1)])
            if go():
                phase_rwkv(nc, sc_q, prm_q, None, y_dst=y_full[:, 2048 + 256 * q:2048 + 256 * (q + 1)])
        for p in range(4):
            rs_ = slice(p * 1024, (p + 1) * 1024)
            if go():
                phase_outproj(nc, y_full[rs_, :], x_cur[rs_, :], d[f"wout{l}"], d["ident"], x_nxt[rs_, :])
    if upto is None:
        phase_finalnorm(nc, xs[nlayers], d["gf"], out, ntiles=SEQ // 128)
    else:
        phase_finalnorm(nc, y_full, d["gf"], out, ntiles=SEQ // 128)
    return nc


def kernel_fused(**inp):
    inp = {k: np.asarray(v) for k, v in inp.items()}
    nb = inp["x"].shape[0]
    in_maps = [fused_inputs(inp, b) for b in range(nb)]
    nc = build_fused(in_maps[0])
    res = run_bass_kernel_spmd(nc, in_maps, core_ids=list(range(nb))).results
    return np.stack([res[b]["out"] for b in range(nb)], axis=0)


FUSED = True


def kernel(**inp):
    return kernel_fused(**inp) if FUSED else kernel_unfused(**inp)
```

```python
import contextlib
import numpy as np
import ml_dtypes
import concourse.bass as bass
import concourse.mybir as mybir
from concourse.bass_utils import run_bass_kernel_spmd

F32 = mybir.dt.float32
BF16 = mybir.dt.bfloat16
ALU = mybir.AluOpType
AF = mybir.ActivationFunctionType
AX = mybir.AxisListType

D_MODEL = 4096
SEQ = 4096
NCORE = 8
NQ = 4
OWN = SEQ // NQ
NORM_EPS = 1e-5
GN_EPS = 64e-5

COMPUTE = ("pe", "act", "dve", "pool")


SEM_ROLL = 30000


class SemState:
    def __init__(self, nc):
        self.nc = nc
        self.st = contextlib.ExitStack()
        self.sems = {}
        self.count = {e: 0 for e in COMPUTE}
        self.dma_slots = {}
        self.dma_rr = {}
        self.n_dma_slots = 8

    def handle(self, key):
        h = self.sems.get(key)
        if h is None:
            h = self.st.enter_context(self.nc.semaphore("s_" + "_".join(str(x) for x in key)))
            self.sems[key] = h
        return h


def semstate(nc):
    ss = getattr(nc, "_semstate", None)
    if ss is None:
        ss = SemState(nc)
        nc._semstate = ss
    return ss


class Prog:
    def __init__(self, nc):
        self.nc = nc
        self.ss = semstate(nc)
        self.streams = {e: [] for e in ("pe", "act", "dve", "pool", "sp")}
        self.last_writer = {}
        self.readers = {}
        self.waited = {}
        self.used_keys = []
        self.excl = set()

    def _semkey(self, key):
        if key not in self.used_keys:
            self.used_keys.append(key)
        return key

    def _deps_for(self, reads, writes, eng=None):
        deps = set()
        for b in reads:
            w = self.last_writer.get(b)
            if w is not None:
                deps.add(w)
        for b in writes:
            w = self.last_writer.get(b)
            if w is not None:
                deps.add(w)
            for r in self.readers.get(b, ()):
                if eng is not None and r[0][0] == eng:
                    continue
                deps.add(r)
        if eng == "pe":
            deps = {d for d in deps if d[0][0] != "pe"}
        return deps

    def _commit(self, tok, reads, writes):
        for b in reads:
            self.readers.setdefault(b, []).append(tok)
        for b in writes:
            self.last_writer[b] = tok
            self.readers[b] = []

    def _waits(self, eng, deps):
        waits = []
        for (k, v) in sorted(deps, key=lambda t: (str(t[0]), t[1])):
            if self.waited.get((eng, k), -1) >= v:
                continue
            self.waited[(eng, k)] = v
            self._semkey(k)
            waits.append((k, v))
        return waits

    def op(self, eng, fn, reads=(), writes=()):
        ex = [b for b in reads if (b[0] if isinstance(b, tuple) else b) in self.excl]
        if ex:
            writes = list(writes) + [b for b in ex if b not in writes]
        ss = self.ss
        idx = ss.count[eng]
        ss.count[eng] += 1
        ep = idx // SEM_ROLL
        key = self._semkey((eng, ep))
        deps = self._deps_for(reads, writes, eng=eng)
        waits = self._waits(eng, deps)
        self.streams[eng].append((waits, fn, (key, 1)))
        tok = (key, idx - ep * SEM_ROLL + 1)
        self._commit(tok, reads, writes)
        return tok

    def dma(self, eng, fn, reads=(), writes=()):
        ss = self.ss
        slots = ss.dma_slots.get(eng)
        if slots is None:
            slots = [[("dma", eng, i, 0), 0] for i in range(ss.n_dma_slots)]
            ss.dma_slots[eng] = slots
        rr = ss.dma_rr.get(eng, 0)
        ss.dma_rr[eng] = (rr + 1) % len(slots)
        slot = slots[rr]
        deps = self._deps_for(reads, writes)
        if slot[1] > 0:
            deps.add((slot[0], slot[1]))
        if slot[1] + 16 > SEM_ROLL:
            slot[0] = ("dma", eng, slot[0][2], slot[0][3] + 1)
            slot[1] = 0
        key = self._semkey(slot[0])
        waits = self._waits(eng, deps)
        slot[1] += 16
        self.streams[eng].append((waits, fn, (key, 16)))
        tok = (key, slot[1])
        self._commit(tok, reads, writes)
        return tok

    def finish(self, eng="sp"):
        toks = set()
        ss = self.ss
        for q, slots in ss.dma_slots.items():
            for key, cnt in slots:
                if cnt > 0:
                    toks.add((key, cnt))
        for e in COMPUTE:
            n = ss.count[e]
            if n > 0:
                ep = (n - 1) // SEM_ROLL
                toks.add(((e, ep), n - ep * SEM_ROLL))
        waits = self._waits(eng, toks)
        self.streams[eng].append((waits, None, None))

    def emit(self):
        nc = self.nc
        ss = self.ss
        sems = {k: ss.handle(k) for k in self.used_keys}
        with nc.Block() as block:

            def run(engobj, items):
                for waits, fn, inc in items:
                    for (k, v) in waits:
                        engobj.wait_ge(sems[k], v)
                    if fn is not None:
                        ins = fn(engobj)
                        ins.then_inc(sems[inc[0]], inc[1])

            @block.tensor
            def _(e):
                run(e, self.streams["pe"])

            @block.scalar
            def _(e):
                run(e, self.streams["act"])

            @block.vector
            def _(e):
                run(e, self.streams["dve"])

            @block.gpsimd
            def _(e):
                run(e, self.streams["pool"])

            @block.sync
            def _(e):
                run(e, self.streams["sp"])


class Phase:
    _count = [0]

    def __init__(self, nc):
        self.nc = nc
        self.st = contextlib.ExitStack()
        self.P = Prog(nc)
        self.n = 0
        Phase._count[0] += 1
        self.pid = Phase._count[0]

    def sb(self, shape, dt, name=None):
        self.n += 1
        return self.st.enter_context(self.nc.sbuf_tensor(f"{name or 't'}_{self.n}_{self.pid}", list(shape), dt))

    def ps(self, shape, dt, name=None):
        self.n += 1
        return self.st.enter_context(self.nc.psum_tensor(f"{name or 'p'}_{self.n}_{self.pid}", list(shape), dt))

    def dump(self, name, sb_ap, reads):
        dbg = getattr(self.nc, "_dbg", None)
        if not dbg or name not in dbg:
            return
        d = dbg[name]
        self.P.dma("sp", lambda e: e.dma_start(out=d, in_=sb_ap), reads=reads, writes=[("dbg", name)])

    def close(self):
        self.P.finish()
        self.P.emit()
        self.st.close()


ATT0, SSM0, RWKV0, MLP0 = 0, 5200, 8288, 12512


def col_groups(q):
    r = lambda a, n: np.arange(a, a + n)
    att_q, att_k, att_v, att_z = r(0, 1024), r(1024, 1024), r(2048, 1024), r(3072, 1024)
    att_qi, att_ki, att_wi = r(4096, 1024), r(5120, 64), r(5184, 16)
    ssm_z = r(SSM0 + 256 * q, 256)
    ssm_x = r(SSM0 + 1024 + 256 * q, 256)
    ssm_B = r(SSM0 + 2048 + 128 * q, 128)
    ssm_C = r(SSM0 + 2560 + 128 * q, 128)
    ssm_dt = r(SSM0 + 3072 + 4 * q, 4)
    rw = [r(RWKV0 + 1024 * i + 256 * q, 256) for i in range(4)]
    rw_wl, rw_al = r(RWKV0 + 4096, 64), r(RWKV0 + 4160, 64)
    mlp = [r(MLP0 + 1024 * i, 1024) for i in range(3)]
    cat = np.concatenate
    return {
        "fmb_all": cat([att_k, att_ki, att_ki]),
        "fmf_all": cat([ssm_x, ssm_B, ssm_C, rw_wl, rw_al]),
        "tmb_all": att_v,
        "tmf_all": cat([ssm_z, ssm_dt] + rw),
        "fmb_own": cat([att_q, att_qi]),
        "tmf_own": cat([att_z, att_wi] + mlp),
    }


GROUP_INFO = {
    "fmb_all": (1152, "fm", BF16, "all"),
    "fmf_all": (640, "fm", F32, "all"),
    "tmb_all": (1024, "tm", BF16, "all"),
    "tmf_all": (1284, "tm", F32, "all"),
    "fmb_own": (2048, "fm", BF16, "own"),
    "tmf_own": (4112, "tm", F32, "own"),
}


def phase_inproj(nc, x_all, x_own, g, ident_d, wts, scr, plan=None, ginfo=None):
    ph = Phase(nc)
    P = ph.P
    D = D_MODEL
    KT = D // 128
    T = 1024
    TT = T // 128
    KQ = 8
    NKQ = KT // KQ
    NB = 512
    ident = ph.sb([128, 128], BF16, "ident")
    hT = ph.sb([128, KT, T], BF16, "hT")
    xt = [ph.sb([128, D], F32, "xt") for _ in range(2)]
    hb = [ph.sb([128, D], BF16, "hb") for _ in range(2)]
    wb = [ph.sb([128, KT, NB], BF16, "wb") for _ in range(2)]
    stg = [ph.sb([128, NB], F32, "stg") for _ in range(4)]
    stgb = [ph.sb([128, NB], BF16, "stgb") for _ in range(4)]
    gb = ph.sb([128, D], F32, "gb")
    ss = ph.sb([128, 8 * 5], F32, "ss")
    rstd = ph.sb([128, 8 * 5], F32, "rstd")
    pT = [ph.ps([128, 1024], BF16, "pT") for _ in range(2)]
    pM = [ph.ps([128, 512], F32, "pM") for _ in range(6)]

    P.dma("sp", lambda e: e.dma_start(out=ident[:], in_=ident_d[:, :]), writes=["ident"])
    P.dma("sp", lambda e: e.dma_start(out=gb[:], in_=g[0:1, :].partition_broadcast(128)), writes=["gb"])
    cnt = {"t": 0, "m": 0, "w": 0, "x": 0}
    P.op("dve", lambda e: e.memset(ss[:], 0.0), writes=[("ss", i) for i in range(40)])

    def load_hT(x_ap, row0, pidx):
        for tt in range(TT):
            i = cnt["x"] % 2
            cnt["x"] += 1
            xb, hbb = xt[i], hb[i]
            sc = pidx * 8 + tt
            P.dma("sp", lambda e, xb=xb, tt=tt: e.dma_start(out=xb[:], in_=x_ap[row0 + tt * 128:row0 + (tt + 1) * 128, :]),
                  writes=[("xt", i)])
            P.op("act", lambda e, xb=xb, hbb=hbb, sc=sc: e.activation(out=hbb[:], in_=xb[:], func=AF.Square,
                                                                     accum_out=ss[:, sc:sc + 1]),
                 reads=[("xt", i)], writes=[("hb", i), ("ss", sc)])
            P.op("dve", lambda e, sc=sc: e.tensor_scalar(out=rstd[:, sc:sc + 1], in0=ss[:, sc:sc + 1],
                                                         scalar1=1.0 / D, scalar2=NORM_EPS, op0=ALU.mult, op1=ALU.add),
                 reads=[("ss", sc)], writes=[("rstd", sc)])
            P.op("act", lambda e, sc=sc: e.activation(out=rstd[:, sc:sc + 1], in_=rstd[:, sc:sc + 1], func=AF.Sqrt),
                 reads=[("rstd", sc)], writes=[("rstd", sc)])
            P.op("dve", lambda e, sc=sc: e.reciprocal(out=rstd[:, sc:sc + 1], in_=rstd[:, sc:sc + 1]),
                 reads=[("rstd", sc)], writes=[("rstd", sc)])
            P.op("dve", lambda e, xb=xb, hbb=hbb, sc=sc: e.scalar_tensor_tensor(
                out=hbb[:], in0=xb[:], scalar=rstd[:, sc:sc + 1], in1=gb[:], op0=ALU.mult, op1=ALU.mult),
                reads=[("xt", i), ("rstd", sc), "gb", ("hb", i)], writes=[("hb", i)])
            for kq in range(NKQ):
                pb = cnt["t"] % 2
                cnt["t"] += 1
                for k8 in range(KQ):
                    kt = kq * KQ + k8
                    P.op("pe", lambda e, pb=pb, k8=k8, kt=kt, hbb=hbb: e.transpose(
                        out=pT[pb][:, k8 * 128:(k8 + 1) * 128], in_=hbb[:, kt * 128:(kt + 1) * 128], identity=ident[:]),
                        reads=[("hb", i), "ident"], writes=[("pT", pb)])
                dst = hT[:, kq * KQ:(kq + 1) * KQ, tt * 128:(tt + 1) * 128]
                src = pT[pb][:].rearrange("p (k t) -> p k t", k=KQ)
                if kq % 2 == 0:
                    P.op("dve", lambda e, dst=dst, src=src: e.tensor_copy(out=dst, in_=src),
                         reads=[("pT", pb)], writes=[("hT", tt, kq)])
                else:
                    P.op("act", lambda e, dst=dst, src=src: e.copy(out=dst, in_=src),
                         reads=[("pT", pb)], writes=[("hT", tt, kq)])

    def evac_store(j, n_part, nfree, is_bf16, dst_ap, okey):
        s = cnt["m"] % 4
        use_dve = (cnt["m"] % 2 == 0)
        cnt["m"] += 1
        sbuf = (stgb if is_bf16 else stg)[s]
        skey = ("stgb" if is_bf16 else "stg", s)
        if use_dve:
            P.op("dve", lambda e: e.tensor_copy(out=sbuf[0:n_part, 0:nfree], in_=pM[j][0:n_part, 0:nfree]),
                 reads=[("pM", j)], writes=[skey])
        else:
            P.op("act", lambda e: e.copy(out=sbuf[0:n_part, 0:nfree], in_=pM[j][0:n_part, 0:nfree]),
                 reads=[("pM", j)], writes=[skey])
        P.dma("sp", lambda e: e.dma_start(out=dst_ap, in_=sbuf[0:n_part, 0:nfree]), reads=[skey], writes=[okey])

    def do_group(name, tok0):
        ncols, layout, dt, _ = (ginfo or GROUP_INFO)[name]
        w = wts[name]
        dst = scr[name]
        wv = w.rearrange("(kt p) n -> p kt n", p=128)
        is_bf = (dt == BF16)
        for c0 in range(0, ncols, NB):
            nb = min(NB, ncols - c0)
            wi = cnt["w"] % 2
            cnt["w"] += 1
            wbb = wb[wi]
            for kq in range(NKQ):
                P.dma("pool", lambda e, wbb=wbb, kq=kq, c0=c0, nb=nb: e.dma_start(
                    out=wbb[:, kq * KQ:(kq + 1) * KQ, 0:nb], in_=wv[:, kq * KQ:(kq + 1) * KQ, c0:c0 + nb]),
                    writes=[("wb", wi, kq)])
            if layout == "tm":
                for tt in range(TT):
                    j = cnt["m"] % 6
                    for kt in range(KT):
                        P.op("pe", lambda e, j=j, kt=kt, tt=tt, wbb=wbb, nb=nb: e.matmul(
                            pM[j][:, 0:nb], lhsT=hT[:, kt, tt * 128:(tt + 1) * 128], rhs=wbb[:, kt, 0:nb],
                            start=(kt == 0), stop=(kt == KT - 1)),
                            reads=[("hT", tt, kt // KQ), ("wb", wi, kt // KQ)], writes=[("pM", j)])
                    r0 = tok0 + tt * 128
                    evac_store(j, 128, nb, is_bf, dst[r0:r0 + 128, c0:c0 + nb], (name, "o", tok0, tt, c0))
            else:
                for ct in range(nb // 128):
                    for th in range(T // 512):
                        j = cnt["m"] % 6
                        for kt in range(KT):
                            P.op("pe", lambda e, j=j, kt=kt, th=th, ct=ct, wbb=wbb: e.matmul(
                                pM[j][:, 0:512], lhsT=wbb[:, kt, ct * 128:(ct + 1) * 128],
                                rhs=hT[:, kt, th * 512:(th + 1) * 512], start=(kt == 0), stop=(kt == KT - 1)),
                                reads=[("hT", 4 * th, kt // KQ), ("hT", 4 * th + 1, kt // KQ), ("hT", 4 * th + 2, kt // KQ),
                                       ("hT", 4 * th + 3, kt // KQ), ("wb", wi, kt // KQ)], writes=[("pM", j)])
                        cc = c0 + ct * 128
                        t0 = tok0 + th * 512
                        evac_store(j, 128, 512, is_bf, dst[cc:cc + 128, t0:t0 + 512], (name, "o", tok0, th, cc))

    if plan is None:
        plan = [(x_all, p * 1024, [(n, p * 1024) for n in ("fmb_all", "fmf_all", "tmb_all", "tmf_all")]) for p in range(4)]
        plan.append((x_own, 0, [("fmb_own", 0), ("tmf_own", 0)]))
    for pidx, (xsrc, row0, glist) in enumerate(plan):
        load_hT(xsrc, row0, pidx)
        for name, tok0 in glist:
            do_group(name, tok0)
    ph.close()


def make_scratch(nc, kind=None):
    scr = {}
    for name, (ncols, layout, dt, which) in GROUP_INFO.items():
        ntok = SEQ if which == "all" else OWN
        shape = [ntok, ncols] if layout == "tm" else [ncols, ntok]
        if kind:
            scr[name] = nc.dram_tensor("scr_" + name, shape, dt, kind=kind).ap()
        else:
            scr[name] = nc.dram_tensor("scr_" + name, shape, dt).ap()
    return scr


TMF_OWN_ATTZ, TMF_OWN_WI, TMF_OWN_U, TMF_OWN_V, TMF_OWN_Z = 0, 1024, 1040, 2064, 3088


def phase_mlp(nc, scr, prm, y_own, nchunks=OWN // 128, ycol0=1024):
    ph = Phase(nc)
    P = ph.P
    src = scr["tmf_own"]
    W = 1024
    gb = ph.sb([128, W], F32, "lng")
    bb = ph.sb([128, W], F32, "lnb")
    wsT = ph.sb([128, 8, 128], F32, "wsT")
    wcT = ph.sb([128, 8, 128], BF16, "wcT")
    tri = ph.sb([128, 128], F32, "tri")
    bsT = ph.sb([128, 8], F32, "bsT")
    bbc = ph.sb([128, 8, 128], F32, "bbc")
    zero = ph.sb([128, 128], F32, "zero")
    NBUF = 2
    ut = [ph.sb([128, W], F32, "u") for _ in range(NBUF)]
    vt = [ph.sb([128, W], F32, "v") for _ in range(NBUF)]
    zt = [ph.sb([128, W], F32, "z") for _ in range(NBUF)]
    vn = [ph.sb([128, W], F32, "vn") for _ in range(NBUF)]
    vnb = [ph.sb([128, W], BF16, "vnb") for _ in range(NBUF)]
    junk = ph.sb([128, W], BF16, "junk")
    t1 = [ph.sb([128, W], F32, "t1") for _ in range(NBUF)]
    st = ph.sb([128, 8 * nchunks], F32, "stats")
    pV = [ph.ps([128, 512], F32, "pV") for _ in range(4)]

    P.dma("sp", lambda e: e.dma_start(out=gb[:], in_=prm["mlp_ln_g"][0:1, :].partition_broadcast(128)), writes=["gb"])
    P.dma("sp", lambda e: e.dma_start(out=bb[:], in_=prm["mlp_ln_b"][0:1, :].partition_broadcast(128)), writes=["bb"])
    P.dma("sp", lambda e: e.dma_start(out=wsT[:], in_=prm["mlp_wsT"][:, :, :]), writes=["wsT"])
    P.dma("sp", lambda e: e.dma_start(out=tri[:], in_=prm["tri_le"][:, :]), writes=["tri"])
    P.dma("sp", lambda e: e.dma_start(out=bsT[:], in_=prm["mlp_bsT"][:, :]), writes=["bsT"])
    P.op("dve", lambda e: e.memset(zero[:], 0.0), writes=["zero"])
    P.op("dve", lambda e: e.memset(st[:], 0.0), writes=[("st", c) for c in range(nchunks)])
    for g in range(8):
        P.op("dve", lambda e, g=g: e.tensor_tensor(out=wcT[:, g, :], in0=wsT[:, g, :], in1=tri[:], op=ALU.mult),
             reads=["wsT", "tri"], writes=["wcT"])
        P.op("dve", lambda e, g=g: e.tensor_scalar(out=bbc[:, g, :], in0=zero[:], scalar1=bsT[:, g:g + 1], scalar2=None,
                                                   op0=ALU.add), reads=["zero", "bsT"], writes=["bbc"])
    for c in range(nchunks):
        i = c % NBUF
        r0 = c * 128
        P.dma("sp", lambda e, i=i, r0=r0: e.dma_start(out=ut[i][:], in_=src[r0:r0 + 128, TMF_OWN_U:TMF_OWN_U + W]),
              writes=[("u", i)])
        P.dma("sp", lambda e, i=i, r0=r0: e.dma_start(out=vt[i][:], in_=src[r0:r0 + 128, TMF_OWN_V:TMF_OWN_V + W]),
              writes=[("v", i)])
        P.dma("sp", lambda e, i=i, r0=r0: e.dma_start(out=zt[i][:], in_=src[r0:r0 + 128, TMF_OWN_Z:TMF_OWN_Z + W]),
              writes=[("z", i)])
        s0 = c * 8
        P.op("act", lambda e, i=i, s0=s0: e.activation(out=junk[:], in_=vt[i][:], func=AF.Square,
                                                       accum_out=st[:, s0 + 1:s0 + 2]),
             reads=[("v", i)], writes=["junk", ("st", c)])
        P.op("dve", lambda e, i=i, s0=s0: e.reduce_sum(out=st[:, s0:s0 + 1], in_=vt[i][:], axis=AX.X),
             reads=[("v", i), ("st", c)], writes=[("st", c)])
        P.op("dve", lambda e, s0=s0: e.tensor_scalar(out=st[:, s0:s0 + 2], in0=st[:, s0:s0 + 2], scalar1=1.0 / W,
                                                     scalar2=None, op0=ALU.mult), reads=[("st", c)], writes=[("st", c)])
        P.op("dve", lambda e, s0=s0: e.tensor_tensor(out=st[:, s0 + 2:s0 + 3], in0=st[:, s0:s0 + 1], in1=st[:, s0:s0 + 1],
                                                     op=ALU.mult), reads=[("st", c)], writes=[("st", c)])
        P.op("dve", lambda e, s0=s0: e.tensor_tensor(out=st[:, s0 + 3:s0 + 4], in0=st[:, s0 + 1:s0 + 2],
                                                     in1=st[:, s0 + 2:s0 + 3], op=ALU.subtract),
             reads=[("st", c)], writes=[("st", c)])
        P.op("dve", lambda e, s0=s0: e.tensor_scalar(out=st[:, s0 + 3:s0 + 4], in0=st[:, s0 + 3:s0 + 4], scalar1=NORM_EPS,
                                                     scalar2=None, op0=ALU.add), reads=[("st", c)], writes=[("st", c)])
        P.op("act", lambda e, s0=s0: e.activation(out=st[:, s0 + 3:s0 + 4], in_=st[:, s0 + 3:s0 + 4], func=AF.Sqrt),
             reads=[("st", c)], writes=[("st", c)])
        P.op("dve", lambda e, s0=s0: e.reciprocal(out=st[:, s0 + 3:s0 + 4], in_=st[:, s0 + 3:s0 + 4]),
             reads=[("st", c)], writes=[("st", c)])
        P.op("dve", lambda e, i=i, s0=s0: e.tensor_scalar(out=vn[i][:], in0=vt[i][:], scalar1=st[:, s0:s0 + 1],
                                                          scalar2=st[:, s0 + 3:s0 + 4], op0=ALU.subtract, op1=ALU.mult),
             reads=[("v", i), ("st", c)], writes=[("vn", i)])
        P.op("pool", lambda e, i=i: e.tensor_tensor(out=vn[i][:], in0=vn[i][:], in1=gb[:], op=ALU.mult),
             reads=[("vn", i), "gb"], writes=[("vn", i)])
        P.op("dve", lambda e, i=i: e.tensor_tensor(out=vnb[i][:], in0=vn[i][:], in1=bb[:], op=ALU.add),
             reads=[("vn", i), "bb"], writes=[("vnb", i)])
        if c == 0:
            ph.dump("mlp_st", st[:, 0:8], [("st", c)])
            ph.dump("mlp_vn", vn[i][:], [("vn", i)])
            ph.dump("mlp_vnb", vnb[i][:], [("vnb", i)])
        for hf in range(2):
            pj = (2 * c + hf) % 4
            for g4 in range(4):
                g = hf * 4 + g4
                P.op("pe", lambda e, pj=pj, g=g, g4=g4, i=i: e.matmul(
                    pV[pj][:, g4 * 128:(g4 + 1) * 128], lhsT=wcT[:, g, :], rhs=vnb[i][:, g * 128:(g + 1) * 128],
                    start=True, stop=True), reads=["wcT", ("vnb", i)], writes=[("pV", pj)])
            P.op("dve", lambda e, pj=pj, hf=hf, i=i: e.tensor_tensor(
                out=t1[i][:, hf * 512:(hf + 1) * 512], in0=pV[pj][:],
                in1=bbc[:, hf * 4:(hf + 1) * 4, :].rearrange("p g d -> p (g d)"), op=ALU.add),
                reads=[("pV", pj), "bbc"], writes=[("t1", i, hf)])
        if c == 0:
            ph.dump("mlp_t1", t1[i][:], [("t1", i, 0), ("t1", i, 1)])
        P.op("pool", lambda e, i=i: e.tensor_tensor(out=t1[i][:], in0=t1[i][:], in1=ut[i][:], op=ALU.mult),
             reads=[("t1", i, 0), ("t1", i, 1), ("u", i)], writes=[("t1", i, 0), ("t1", i, 1)])
        P.op("act", lambda e, i=i: e.activation(out=zt[i][:], in_=zt[i][:], func=AF.Silu),
             reads=[("z", i)], writes=[("z", i)])
        P.op("dve", lambda e, i=i: e.tensor_tensor(out=t1[i][:], in0=t1[i][:], in1=zt[i][:], op=ALU.mult),
             reads=[("t1", i, 0), ("t1", i, 1), ("z", i)], writes=[("t1", i, 0), ("t1", i, 1)])
        P.dma("sp", lambda e, i=i, r0=r0: e.dma_start(out=y_own[r0:r0 + 128, ycol0:ycol0 + 1024], in_=t1[i][:]),
              reads=[("t1", i, 0), ("t1", i, 1)], writes=[("y_mlp", c)])
    ph.close()


def record_ops(P, rec):
    real_op, real_dma = P.op, P.dma
    P.op = lambda eng, fn, reads=(), writes=(): rec.append((real_op, eng, fn, list(reads), list(writes)))
    P.dma = lambda eng, fn, reads=(), writes=(): rec.append((real_dma, eng, fn, list(reads), list(writes)))

    def restore():
        P.op, P.dma = real_op, real_dma
    return restore


def pipeline_merge(recs, nstage):
    allp = []
    for ops, marks in recs:
        m = [0] + list(marks[:nstage - 1])
        while len(m) < nstage:
            m.append(len(ops))
        m.append(len(ops))
        allp.append([ops[m[k]:m[k + 1]] for k in range(nstage)])
    n = len(recs)
    for t in range(n + nstage - 1):
        lists = []
        for k in range(nstage):
            bi = t - k
            if 0 <= bi < n and allp[bi][k]:
                lists.append(allp[bi][k])
        merged = []
        for li, L in enumerate(lists):
            for pi, item in enumerate(L):
                merged.append(((pi + 0.5) / len(L), li, pi, item))
        merged.sort(key=lambda t_: (t_[0], t_[1]))
        for _, _, _, (f, eng, fn, reads, writes) in merged:
            f(eng, fn, reads=reads, writes=writes)


FMF_X, FMF_B, FMF_C, FMF_WL, FMF_AL = 0, 256, 384, 512, 576
TMF_SSMZ, TMF_DT, TMF_R, TMF_K, TMF_V, TMF_Z = 0, 256, 260, 516, 772, 1028
NEG_BIG = -30000.0


def phase_ssm(nc, scr, prm, y_all, y_dst=None):
    ph = Phase(nc)
    P = ph.P
    fm = scr["fmf_all"]
    tm = scr["tmf_all"]
    if y_dst is None:
        y_dst = y_all[:, 0:256]
    NCH = SEQ // 128
    SC = 512
    identf = ph.sb([128, 128], F32, "identf")
    identb = ph.sb([128, 128], BF16, "identb")
    tri = ph.sb([128, 128], F32, "tri")
    onesf = ph.sb([128, 128], F32, "ones")
    sel4 = ph.sb([4, 4, 128], F32, "sel4")
    negb = ph.sb([128, 128], F32, "negb")
    cw = ph.sb([128, 4, 4], F32, "cw")
    cb = ph.sb([128, 4], F32, "cb")
    dtb = ph.sb([128, 128], F32, "dtb")
    alog = ph.sb([128, 128], F32, "alog")
    Dbc = ph.sb([128, 4], F32, "Dbc")
    ngb = ph.sb([128, 256], F32, "ngb")
    dt = ph.sb([128, NCH, 4], F32, "dt")
    aa = ph.sb([128, NCH, 4], F32, "aa")
    cum = ph.sb([128, NCH * 4], F32, "cum")
    cumL = ph.sb([128, NCH * 4], F32, "cumL")
    ecum = ph.sb([128, NCH * 4], F32, "ecum")
    ncum = ph.sb([128, NCH * 4], F32, "ncum")
    ecumL = ph.sb([128, NCH * 4], F32, "ecumL")
    dtd = ph.sb([128, NCH * 4], F32, "dtd")
    win = [ph.sb([128, SC + 3], F32, "win") for _ in range(4)]
    acc = [ph.sb([128, SC], F32, "acc") for _ in range(2)]
    xsT = [ph.sb([128, 2, SC], F32, "xsT") for _ in range(2)]
    BT = [ph.sb([128, SC], BF16, "BT") for _ in range(2)]
    CT = [ph.sb([128, SC], BF16, "CT") for _ in range(2)]
    xtok = [ph.sb([128, 256], F32, "xtok") for _ in range(2)]
    xdt = [ph.sb([128, 256], BF16, "xdt") for _ in range(2)]
    xdd = [ph.sb([128, 256], BF16, "xdd") for _ in range(2)]
    Btok = [ph.sb([128, 128], BF16, "Btok") for _ in range(2)]
    cumT = [ph.sb([4, 128], F32, "cumT") for _ in range(2)]
    LT = [ph.sb([128, 4, 128], F32, "LT") for _ in range(2)]
    MT = [ph.sb([128, 4, 128], BF16, "MT") for _ in range(2)]
    ydsb = [ph.sb([128, 256], F32, "ydsb") for _ in range(2)]
    dsk = [ph.sb([128, 256], F32, "dsk") for _ in range(2)]
    yt = [ph.sb([128, 256], F32, "yt") for _ in range(2)]
    zt = [ph.sb([128, 256], F32, "zt") for _ in range(2)]
    junk = ph.sb([128, 256], F32, "junk")
    nst = ph.sb([128, NCH], F32, "nst")
    hT = ph.sb([128, 256], F32, "hT")
    hTb = ph.sb([128, 256], BF16, "hTb")
    pTr = [ph.ps([128, 512], F32, "pTr") for _ in range(1)]
    pTb = [ph.ps([128, 1024], BF16, "pTb") for _ in range(1)]
    pSm = [ph.ps([128, 512], F32, "pSm") for _ in range(1)]
    pL = [ph.ps([128, 512], F32, "pL") for _ in range(1)]
    pCB = [ph.ps([128, 512], F32, "pCB") for _ in range(1)]
    pYd = [ph.ps([128, 512], F32, "pYd") for _ in range(1)]
    pYo = [ph.ps([128, 512], F32, "pYo") for _ in range(1)]
    pS = [ph.ps([128, 512], F32, "pS") for _ in range(1)]

    ld = lambda dst, src, key: P.dma("sp", lambda e: e.dma_start(out=dst, in_=src), writes=[key])
    ld(identf[:], prm["ident_f"][:, :], "identf")
    ld(identb[:], prm["ident_b"][:, :], "identb")
    ld(tri[:], prm["tri_le"][:, :], "tri")
    ld(onesf[:], prm["ones_f"][:, :], "ones")
    ld(sel4[:], prm["sel4"][:, :, :], "sel4")
    ld(negb[:], prm["negbig_lt"][:, :], "negb")
    ld(cw[:], prm["ssm_cw"][:, :, :], "cw")
    ld(cb[:], prm["ssm_cb"][:, :], "cb")
    ld(dtb[:], prm["ssm_dtb_t"][0:1, :].partition_broadcast(128), "dtb")
    ld(alog[:], prm["ssm_alog_t"][0:1, :].partition_broadcast(128), "alog")
    ld(Dbc[:], prm["ssm_D"][0:1, :].partition_broadcast(128), "Dbc")
    ld(ngb[:], prm["ssm_ng"][0:1, :].partition_broadcast(128), "ngb")
    ld(dt[:], tm[:, TMF_DT:TMF_DT + 4].rearrange("(c l) h -> l c h", l=128), "dt")
    dtf = dt[:].rearrange("p c h -> p (c h)")
    aaf = aa[:].rearrange("p c h -> p (c h)")
    P.op("dve", lambda e: e.tensor_tensor(out=dtf, in0=dtf, in1=dtb[:], op=ALU.add), reads=["dt", "dtb"], writes=["dt"])
    P.op("act", lambda e: e.activation(out=dtf, in_=dtf, func=AF.Exp), reads=["dt"], writes=["dt"])
    P.op("act", lambda e: e.activation(out=dtf, in_=dtf, func=AF.Ln, bias=1.0, scale=1.0), reads=["dt"], writes=["dt"])
    P.op("act", lambda e: e.activation(out=alog[:], in_=alog[:], func=AF.Exp), reads=["alog"], writes=["alog"])
    P.op("dve", lambda e: e.scalar_tensor_tensor(out=aaf, in0=dtf, scalar=-1.0, in1=alog[:], op0=ALU.mult, op1=ALU.mult),
         reads=["dt", "alog"], writes=["aa"])
    P.op("pe", lambda e: e.matmul(pSm[0][:, 0:128], lhsT=tri[:], rhs=aaf, start=True, stop=True),
         reads=["tri", "aa"], writes=["pSm"])
    P.op("dve", lambda e: e.tensor_copy(out=cum[:], in_=pSm[0][:, 0:128]), reads=["pSm"], writes=["cum"])
    P.op("pe", lambda e: e.matmul(pSm[0][:, 128:256], lhsT=onesf[:], rhs=aaf, start=True, stop=True),
         reads=["ones", "aa", "cum"], writes=["pSm"])
    P.op("dve", lambda e: e.tensor_copy(out=cumL[:], in_=pSm[0][:, 128:256]), reads=["pSm"], writes=["cumL"])
    P.op("act", lambda e: e.activation(out=ecum[:], in_=cum[:], func=AF.Exp), reads=["cum"], writes=["ecum"])
    P.op("pool", lambda e: e.tensor_scalar(out=ncum[:], in0=cum[:], scalar1=-1.0, scalar2=None, op0=ALU.mult), reads=["cum"], writes=["ncum"])
    P.op("act", lambda e: e.activation(out=ecumL[:], in_=cumL[:], func=AF.Exp), reads=["cumL"], writes=["ecumL"])
    P.op("dve", lambda e: e.tensor_tensor(out=dtd[:], in0=cumL[:], in1=cum[:], op=ALU.subtract),
         reads=["cumL", "cum"], writes=["dtd"])
    P.op("act", lambda e: e.activation(out=dtd[:], in_=dtd[:], func=AF.Exp), reads=["dtd"], writes=["dtd"])
    P.op("dve", lambda e: e.tensor_tensor(out=dtd[:], in0=dtd[:], in1=dtf, op=ALU.mult), reads=["dtd", "dt"], writes=["dtd"])
    P.op("dve", lambda e: e.memset(hT[:], 0.0), writes=["hT"])
    P.op("dve", lambda e: e.memset(hTb[:], 0.0), writes=["hTb"])
    P.op("dve", lambda e: e.memset(nst[:], 0.0), writes=["nst"])

    recs = []
    rec_cur = [None]

    def new_unit():
        rec = []
        recs.append([rec, []])
        rec_cur[0] = rec
        return record_ops(P, rec)

    for s in range(SEQ // SC):
        t0 = s * SC
        si = s % 2
        restore = new_unit()
        for j in range(4):
            wj = win[j]
            if s == 0:
                P.op("dve", lambda e, wj=wj: e.memset(wj[:, 0:3], 0.0), writes=[("win", j)])
                P.dma("sp", lambda e, wj=wj, j=j: e.dma_start(out=wj[:, 3:SC + 3], in_=fm[j * 128:(j + 1) * 128, 0:SC]),
                      writes=[("win", j)])
            else:
                P.dma("sp", lambda e, wj=wj, j=j, t0=t0: e.dma_start(out=wj[:], in_=fm[j * 128:(j + 1) * 128, t0 - 3:t0 + SC]),
                      writes=[("win", j)])
            ac = acc[j % 2]
            ak = ("acc", j % 2)
            P.op("dve", lambda e, ac=ac, wj=wj, j=j: e.tensor_scalar(out=ac[:], in0=wj[:, 0:SC], scalar1=cw[:, j, 0:1],
                                                                    scalar2=None, op0=ALU.mult),
                 reads=[("win", j), "cw"], writes=[ak])
            for tap in range(1, 4):
                P.op("dve", lambda e, ac=ac, wj=wj, j=j, tap=tap: e.scalar_tensor_tensor(
                    out=ac[:], in0=wj[:, tap:tap + SC], scalar=cw[:, j, tap:tap + 1], in1=ac[:], op0=ALU.mult, op1=ALU.add),
                    reads=[("win", j), "cw", ak], writes=[ak])
            if j < 2:
                dst, dk = xsT[si][:, j, :], ("xsT", si, j)
            elif j == 2:
                dst, dk = BT[si][:], ("BT", si)
            else:
                dst, dk = CT[si][:], ("CT", si)
            P.op("act", lambda e, ac=ac, dst=dst, j=j: e.activation(out=dst, in_=ac[:], func=AF.Silu, bias=cb[:, j:j + 1], scale=1.0),
                 reads=[ak, "cb"], writes=[dk])
        if s < 2:
            ph.dump(f"ssm_xsT{s}", xsT[si][:], [("xsT", si, 0), ("xsT", si, 1)])
            ph.dump(f"ssm_BT{s}", BT[si][:], [("BT", si)])
            ph.dump(f"ssm_CT{s}", CT[si][:], [("CT", si)])
        for cc in range(SC // 128):
            c = s * (SC // 128) + cc
            ci = c % 2
            lo = cc * 128
            c4 = c * 4
            if cc > 0:
                restore = new_unit()
            for j in range(2):
                P.op("pe", lambda e, si=si, j=j, lo=lo: e.transpose(out=pTr[0][:, j * 128:(j + 1) * 128], in_=xsT[si][:, j, lo:lo + 128],
                                                            identity=identf[:]),
                     reads=[("xsT", si, j), "identf"], writes=["pTr"])
            P.op("act", lambda e, ci=ci: e.copy(out=xtok[ci][:], in_=pTr[0][:, 0:256]), reads=["pTr"], writes=[("xtok", ci)])
            P.op("pe", lambda e, si=si, lo=lo: e.transpose(out=pTb[0][:, 0:128], in_=BT[si][:, lo:lo + 128], identity=identb[:]),
                 reads=[("BT", si), "identb"], writes=["pTb"])
            P.op("act", lambda e, ci=ci: e.copy(out=Btok[ci][:], in_=pTb[0][:, 0:128]), reads=["pTb"], writes=[("Btok", ci)])
            P.op("pe", lambda e, c=c: e.matmul(pSm[0][0:4, 256:384], lhsT=aa[:, c, :], rhs=tri[:], start=True, stop=True),
                 reads=["aa", "tri"], writes=["pSm"])
            P.op("dve", lambda e, ci=ci: e.tensor_copy(out=cumT[ci][:], in_=pSm[0][0:4, 256:384]),
                 reads=["pSm"], writes=[("cumT", ci)])
            for h in range(4):
                P.op("pe", lambda e, h=h, ci=ci: e.matmul(pL[0][:, h * 128:(h + 1) * 128], lhsT=sel4[:, h, :], rhs=cumT[ci][:],
                                                          start=True, stop=False),
                     reads=["sel4", ("cumT", ci)], writes=["pL"])
                P.op("pe", lambda e, h=h: e.matmul(pL[0][:, h * 128:(h + 1) * 128], lhsT=identf[:], rhs=negb[:],
                                                   start=False, stop=True),
                     reads=["identf", "negb"], writes=["pL"])
            for h in range(4):
                P.op("act", lambda e, h=h, ci=ci, c4=c4: e.activation(
                    out=LT[ci][:, h, :], in_=pL[0][:, h * 128:(h + 1) * 128], func=AF.Exp, bias=ncum[:, c4 + h:c4 + h + 1], scale=1.0),
                    reads=["pL", "ncum"], writes=[("LT", ci)])
            P.op("pe", lambda e, si=si, lo=lo: e.matmul(pCB[0][:, 0:128], lhsT=BT[si][:, lo:lo + 128], rhs=CT[si][:, lo:lo + 128],
                                                 start=True, stop=True), reads=[("BT", si), ("CT", si)], writes=["pCB"])
            P.op("dve", lambda e, ci=ci: e.tensor_tensor(out=MT[ci][:], in0=pCB[0][:, 0:128].unsqueeze(1).to_broadcast([128, 4, 128]),
                                                         in1=LT[ci][:], op=ALU.mult), reads=["pCB", ("LT", ci)], writes=[("MT", ci)])
            h4 = lambda ap: ap.rearrange("p (h d) -> p h d", h=4)
            P.op("dve", lambda e, ci=ci, c4=c4: e.tensor_tensor(out=h4(xdt[ci][:]), in0=h4(xtok[ci][:]),
                                                                in1=dtf[:, c4:c4 + 4].unsqueeze(2).to_broadcast([128, 4, 64]), op=ALU.mult),
                 reads=[("xtok", ci), "dt"], writes=[("xdt", ci)])
            P.op("pool", lambda e, ci=ci, c4=c4: e.tensor_tensor(out=h4(xdd[ci][:]), in0=h4(xtok[ci][:]),
                                                                 in1=dtd[:, c4:c4 + 4].unsqueeze(2).to_broadcast([128, 4, 64]), op=ALU.mult),
                 reads=[("xtok", ci), "dtd"], writes=[("xdd", ci)])
            for h in range(4):
                P.op("pe", lambda e, h=h, ci=ci: e.matmul(pYd[0][:, h * 64:(h + 1) * 64], lhsT=MT[ci][:, h, :],
                                                          rhs=xdt[ci][:, h * 64:(h + 1) * 64], start=True, stop=True),
                     reads=[("MT", ci), ("xdt", ci)], writes=["pYd"])
            recs[-1][1].append(len(rec_cur[0]))
            for h in range(4):
                P.op("pe", lambda e, si=si, h=h, lo=lo: e.matmul(pYo[0][:, h * 64:(h + 1) * 64], lhsT=CT[si][:, lo:lo + 128],
                                                          rhs=hTb[:, h * 64:(h + 1) * 64], start=True, stop=True),
                     reads=[("CT", si), "hTb"], writes=["pYo"])
            P.op("act", lambda e, ci=ci: e.copy(out=ydsb[ci][:], in_=pYd[0][:, 0:256]), reads=["pYd"], writes=[("ydsb", ci)])
            P.dma("sp", lambda e, ci=ci, c=c: e.dma_start(out=zt[ci][:], in_=tm[c * 128:(c + 1) * 128, TMF_SSMZ:TMF_SSMZ + 256]),
                  writes=[("zt", ci)])
            P.op("dve", lambda e, ci=ci, c4=c4: e.tensor_tensor(out=h4(yt[ci][:]), in0=pYo[0][:, 0:256].rearrange("p (h d) -> p h d", h=4),
                                                                in1=ecum[:, c4:c4 + 4].unsqueeze(2).to_broadcast([128, 4, 64]), op=ALU.mult),
                 reads=["pYo", "ecum"], writes=[("yt", ci)])
            P.op("pool", lambda e, ci=ci: e.tensor_tensor(out=h4(dsk[ci][:]), in0=h4(xtok[ci][:]),
                                                          in1=Dbc[:, 0:4].unsqueeze(2).to_broadcast([128, 4, 64]), op=ALU.mult),
                 reads=[("xtok", ci), "Dbc"], writes=[("dsk", ci)])
            P.op("dve", lambda e, ci=ci: e.tensor_tensor(out=yt[ci][:], in0=yt[ci][:], in1=ydsb[ci][:], op=ALU.add),
                 reads=[("yt", ci), ("ydsb", ci)], writes=[("yt", ci)])
            P.op("dve", lambda e, ci=ci: e.tensor_tensor(out=yt[ci][:], in0=yt[ci][:], in1=dsk[ci][:], op=ALU.add),
                 reads=[("yt", ci), ("dsk", ci)], writes=[("yt", ci)])
            if c in (0, 4):
                ph.dump(f"ssm_xtok{c}", xtok[ci][:], [("xtok", ci)])
                ph.dump(f"ssm_LT{c}", LT[ci][:], [("LT", ci)])
                ph.dump(f"ssm_MT{c}", MT[ci][:], [("MT", ci)])
                ph.dump(f"ssm_yt{c}", yt[ci][:], [("yt", ci)])
            for h in range(4):
                P.op("pe", lambda e, h=h, ci=ci: e.matmul(pS[0][:, h * 64:(h + 1) * 64], lhsT=Btok[ci][:],
                                                          rhs=xdd[ci][:, h * 64:(h + 1) * 64], start=True, stop=True),
                     reads=[("Btok", ci), ("xdd", ci)], writes=["pS"])
            P.op("pool", lambda e, c4=c4: e.tensor_tensor(out=h4(hT[:]), in0=h4(hT[:]),
                                                          in1=ecumL[:, c4:c4 + 4].unsqueeze(2).to_broadcast([128, 4, 64]), op=ALU.mult),
                 reads=["hT", "ecumL"], writes=["hT"])
            P.op("dve", lambda e: e.tensor_tensor(out=hT[:], in0=hT[:], in1=pS[0][:, 0:256], op=ALU.add), reads=["hT", "pS"], writes=["hT"])
            P.op("act", lambda e: e.copy(out=hTb[:], in_=hT[:]), reads=["hT"], writes=["hTb"])
            recs[-1][1].append(len(rec_cur[0]))
            P.op("act", lambda e, ci=ci: e.activation(out=zt[ci][:], in_=zt[ci][:], func=AF.Silu),
                 reads=[("zt", ci)], writes=[("zt", ci)])
            P.op("pool", lambda e, ci=ci: e.tensor_tensor(out=yt[ci][:], in0=yt[ci][:], in1=zt[ci][:], op=ALU.mult),
                 reads=[("yt", ci), ("zt", ci)], writes=[("yt", ci)])
            P.op("act", lambda e, ci=ci, c=c: e.activation(out=junk[:], in_=yt[ci][:], func=AF.Square, accum_out=nst[:, c:c + 1]),
                 reads=[("yt", ci), "nst"], writes=["junk", ("nst", c)])
            P.op("dve", lambda e, c=c: e.tensor_scalar(out=nst[:, c:c + 1], in0=nst[:, c:c + 1], scalar1=1.0 / 256, scalar2=NORM_EPS,
                                                       op0=ALU.mult, op1=ALU.add), reads=[("nst", c)], writes=[("nst", c)])
            P.op("act", lambda e, c=c: e.activation(out=nst[:, c:c + 1], in_=nst[:, c:c + 1], func=AF.Sqrt),
                 reads=[("nst", c)], writes=[("nst", c)])
            P.op("dve", lambda e, c=c: e.reciprocal(out=nst[:, c:c + 1], in_=nst[:, c:c + 1]), reads=[("nst", c)], writes=[("nst", c)])
            P.op("dve", lambda e, ci=ci, c=c: e.scalar_tensor_tensor(out=yt[ci][:], in0=yt[ci][:], scalar=nst[:, c:c + 1], in1=ngb[:],
                                                                     op0=ALU.mult, op1=ALU.mult),
                 reads=[("yt", ci), ("nst", c), "ngb"], writes=[("yt", ci)])
            P.dma("sp", lambda e, ci=ci, c=c: e.dma_start(out=y_dst[c * 128:(c + 1) * 128, :], in_=yt[ci][:]),
                  reads=[("yt", ci)], writes=[("y_ssm", c)])
            restore()
    pipeline_merge([(r, m) for r, m in recs], 3)
    ph.close()


NPAIR = 144
TOPK = 256
NBIS = 17
FILLER = False


def pair_off(i):
    return 2 * i * (i + 1)


def default_blocks():
    return [(i + 1, 0) for i in range(8)]


def pair_offsets(blocks):
    offs, o = [], 0
    for nch, _ in blocks:
        offs.append(o)
        o += 4 * nch
    return offs, o


def phase_indexer(nc, scr, prm, maskT_d, blocks=None):
    blocks = blocks or default_blocks()
    nblk = len(blocks)
    assert nblk % 2 == 0
    NT = nblk * 128
    ncb = max(cb for _, cb in blocks) + 1
    poffs, _ = pair_offsets(blocks)
    ph = Phase(nc)
    P = ph.P
    fmo = scr["fmb_own"]
    fma = scr["fmb_all"]
    tmo = scr["tmf_own"]
    NB4 = 4
    identb = ph.sb([128, 128], BF16, "identb")
    kiT = ph.sb([128, SEQ], BF16, "kiT")
    qiT = [ph.sb([128, 8, 128], BF16, "qiT") for _ in range(NB4)]
    wi = ph.sb([128, nblk, 16], F32, "wi")
    iota = ph.sb([128, 512], F32, "iota")
    qrel = ph.sb([128, ncb], F32, "qrel")
    cbias = ph.sb([128, ncb, 512], F32, "cbias")
    pow2 = ph.sb([128, NBIS], F32, "pow2")
    wdiag = [ph.sb([128, 16, 128], BF16, "wdiag") for _ in range(NB4)]
    R = [ph.sb([128, 512], BF16, "R") for _ in range(4)]
    sc = [ph.sb([128, SEQ], F32, "sc") for _ in range(NB4)]
    junk = [ph.sb([128, SEQ], BF16, "junk") for _ in range(2)]
    mk = [ph.sb([128, SEQ], BF16, "mk") for _ in range(2)]
    mT = [ph.sb([128, 8, 128], BF16, "mT") for _ in range(3)]
    mx = [ph.sb([128, 8], F32, "mx") for _ in range(NB4)]
    bs = [ph.sb([128, 8], F32, "bs") for _ in range(NB4)]
    wk = [ph.sb([128, NBIS], F32, "wk") for _ in range(NB4)]
    cnt = [ph.sb([128, NBIS], F32, "cnt") for _ in range(NB4)]
    pD = [ph.ps([128, 512], F32, "pD") for _ in range(4)]
    pSc = [ph.ps([128, 512], F32, "pSc") for _ in range(2)]
    pT = [ph.ps([128, 1024], BF16, "pT") for _ in range(1)]
    pJ = ph.ps([128, 512], F32, "pJ")

    ld = lambda dst, src, key: P.dma("sp", lambda e: e.dma_start(out=dst, in_=src), writes=[key])
    ld(identb[:], prm["ident_b"][:, :], "identb")
    ld(kiT[:], fma[1024:1152, :], "kiT")
    ld(wi[:], tmo[0:NT, TMF_OWN_WI:TMF_OWN_WI + 16].rearrange("(i p) h -> p i h", p=128), "wi")
    ld(iota[:], prm["iota512"][0:1, :].partition_broadcast(128), "iota")
    ld(qrel[:], prm["qrel"][:, :], "qrel")
    ld(pow2[:], prm["pow2"][0:1, :].partition_broadcast(128), "pow2")
    P.op("dve", lambda e: e.tensor_scalar(out=wi[:], in0=wi[:], scalar1=0.03125, scalar2=None, op0=ALU.mult), reads=["wi"], writes=["wi"])
    for cb in range(ncb):
        P.op("dve", lambda e, o=cbias[:, cb, :], s=qrel[:, cb:cb + 1]: e.tensor_scalar(out=o, in0=iota[:], scalar1=s, scalar2=-1e30,
                                                                                  op0=ALU.is_gt, op1=ALU.mult),
             reads=["iota", "qrel"], writes=["cbias"])
    k = {"d": 0, "r": 0, "s": 0, "t": 0, "m": 0}
    qv = fmo[1024:2048, :].rearrange("(p r) t -> r p t", r=128)

    def scores(i):
        nch, cbi = blocks[i]
        b4 = i % NB4
        scb, mxb, wdb, qb = sc[b4], mx[b4], wdiag[b4], qiT[b4]
        ld(qb[:], qv[:, :, i * 128:(i + 1) * 128], ("qiT", b4))
        P.op("pool", lambda e, o=wdb[:], w_=wi[:, i, :].unsqueeze(2).to_broadcast([128, 16, 128]),
             d_=identb[:].unsqueeze(1).to_broadcast([128, 16, 128]): e.tensor_tensor(out=o, in0=d_, in1=w_, op=ALU.mult),
             reads=["identb", "wi"], writes=[("wdiag", b4)])
        P.op("dve", lambda e, o=mxb[:]: e.memset(o, 0.0), writes=[("mx", b4)])
        P.op("dve", lambda e, o=cnt[b4][:]: e.memset(o, 0.0), writes=[("cnt", b4)])
        for ch in range(nch):
            js = k["s"] % 2
            k["s"] += 1
            jds = {}

            def dots(h):
                jd = k["d"] % 4
                k["d"] += 1
                jds[h] = jd
                r0 = (h % 2) * 64
                P.op("pe", lambda e, o=pD[jd][:], l=qb[r0:r0 + 64, h // 2, :],
                     r=kiT[r0:r0 + 64, ch * 512:(ch + 1) * 512]: e.matmul(o, lhsT=l, rhs=r, start=True, stop=True),
                     reads=[("qiT", b4), "kiT"], writes=[("pD", jd)])

            dots(0)
            dots(1)
            for h in range(16):
                if h + 2 < 16:
                    dots(h + 2)
                jd = jds[h]
                jr = k["r"] % 4
                k["r"] += 1
                P.op("act", lambda e, o=R[jr][:], s=pD[jd][:]: e.activation(out=o, in_=s, func=AF.Relu),
                     reads=[("pD", jd)], writes=[("R", jr)])
                P.op("pe", lambda e, o=pSc[js][:], l=wdb[:, h, :], r=R[jr][:], h=h: e.matmul(o, lhsT=l, rhs=r, start=(h == 0),
                                                                                           stop=(h == 15)),
                     reads=[("wdiag", b4), ("R", jr)], writes=[("pSc", js)])
                if FILLER:
                    P.op("pe", lambda e, l=identb[:], r=kiT[:, ch * 512:(ch + 1) * 512]: e.matmul(pJ[:], lhsT=l, rhs=r, start=True, stop=True),
                         reads=["identb", "kiT"], writes=["pJ"])
            P.op("dve", lambda e, o=mxb[:, ch:ch + 1], s=pSc[js][:]: e.tensor_reduce(out=o, in_=s, axis=AX.X, op=ALU.max,
                                                                                   apply_absolute_value=True),
                 reads=[("pSc", js)], writes=[("mx", b4)])
            if ch == nch - 1:
                P.op("dve", lambda e, o=scb[:, ch * 512:(ch + 1) * 512], s=pSc[js][:], c_=cbias[:, cbi, :]: e.tensor_tensor(
                    out=o, in0=s, in1=c_, op=ALU.add), reads=[("pSc", js), "cbias"], writes=[("sc", b4)])
            else:
                P.op("dve", lambda e, o=scb[:, ch * 512:(ch + 1) * 512], s=pSc[js][:]: e.tensor_copy(out=o, in_=s),
                     reads=[("pSc", js)], writes=[("sc", b4)])
            yield

    def bis_init(i):
        b4 = i % NB4
        bsb, wkb, mxb = bs[b4], wk[b4], mx[b4]
        bk = ("bs", b4)
        P.op("dve", lambda e, o=bsb[:, 0:1], s=mxb[:]: e.tensor_reduce(out=o, in_=s, axis=AX.X, op=ALU.max),
             reads=[("mx", b4)], writes=[bk])
        P.op("dve", lambda e, o=bsb[:, 0:1]: e.tensor_scalar(out=o, in0=o, scalar1=1.001, scalar2=1e-6, op0=ALU.mult, op1=ALU.add),
             reads=[bk], writes=[bk])
        P.op("dve", lambda e, o=wkb[:], s=bsb[:, 0:1]: e.tensor_scalar(out=o, in0=pow2[:], scalar1=s, scalar2=2.0, op0=ALU.mult,
                                                                      op1=ALU.mult), reads=[bk, "pow2"], writes=[("wk", b4)])
        P.op("dve", lambda e, o=bsb[:, 1:2]: e.memset(o, 0.0), reads=[bk], writes=[bk])

    def bis_count(i, it, jj):
        b4 = i % NB4
        n = 512 * blocks[i][0]
        P.op("dve", lambda e, o=junk[jj][:, 0:n], s=sc[b4][:, 0:n], m=bs[b4][:, 1:2], a=cnt[b4][:, it:it + 1]: e.tensor_scalar(
            out=o, in0=s, scalar1=m, scalar2=0.0, op0=ALU.is_ge, op1=ALU.add, accum_out=a),
            reads=[("sc", b4), ("bs", b4), ("cnt", b4)], writes=[("junk", jj), ("cnt", b4)])

    def bis_delta(i, it):
        b4 = i % NB4
        P.op("dve", lambda e, o=bs[b4][:, 2:3], c_=cnt[b4][:, it:it + 1], w_=wk[b4][:, it:it + 1]: e.tensor_scalar(
            out=o, in0=c_, scalar1=TOPK - 0.5, scalar2=w_, op0=ALU.is_ge, op1=ALU.mult),
            reads=[("cnt", b4), ("wk", b4), ("bs", b4)], writes=[("bs", b4)])

    def bis_mid(i, it):
        b4 = i % NB4
        nx = min(it + 1, NBIS - 1)
        P.op("dve", lambda e, o=bs[b4][:, 1:2], d_=bs[b4][:, 2:3], w_=wk[b4][:, nx:nx + 1]: e.scalar_tensor_tensor(
            out=o, in0=d_, scalar=w_, in1=o, op0=ALU.subtract, op1=ALU.add), reads=[("bs", b4), ("wk", b4)], writes=[("bs", b4)])

    def finish_block(i, jj):
        nch, _ = blocks[i]
        b4 = i % NB4
        n = 512 * nch
        mkb = mk[jj]
        P.op("dve", lambda e, o=mkb[:, 0:n], s=sc[b4][:, 0:n], t=bs[b4][:, 1:2]: e.tensor_scalar(
            out=o, in0=s, scalar1=t, scalar2=-1.0, op0=ALU.is_ge, op1=ALU.add), reads=[("sc", b4), ("bs", b4)], writes=[("mk", jj)])
        nkb = 4 * nch
        for g0 in range(0, nkb, 8):
            ng = min(8, nkb - g0)
            jt = 0
            jm = k["m"] % 3
            k["m"] += 1
            for kb in range(g0, g0 + ng):
                P.op("pe", lambda e, o=pT[jt][:, (kb - g0) * 128:(kb - g0 + 1) * 128], s=mkb[:, kb * 128:(kb + 1) * 128]: e.transpose(
                    out=o, in_=s, identity=identb[:]), reads=[("mk", jj), "identb"], writes=[("pT", jt)])
            P.op("act", lambda e, o=mT[jm][:, 0:ng, :], s=pT[jt][:, 0:ng * 128].rearrange("p (k t) -> p k t", k=ng): e.copy(out=o, in_=s),
                 reads=[("pT", jt)], writes=[("mT", jm)])
            po = poffs[i] + g0
            P.dma("sp", lambda e, o=maskT_d[:, po:po + ng, :], s=mT[jm][:, 0:ng, :]: e.dma_start(out=o, in_=s),
                  reads=[("mT", jm)], writes=[("maskT", i, g0)])

    import itertools
    for _ in itertools.chain(scores(0), scores(1)):
        pass
    for pr in range(nblk // 2):
        ia, ib = 2 * pr, 2 * pr + 1
        pending = iter(())
        if 2 * pr + 2 < nblk:
            pending = itertools.chain(scores(2 * pr + 2), scores(2 * pr + 3))
        bis_init(ia)
        bis_init(ib)
        for it in range(NBIS):
            bis_count(ia, it, 0)
            bis_count(ib, it, 1)
            bis_delta(ia, it)
            bis_delta(ib, it)
            bis_mid(ia, it)
            bis_mid(ib, it)
            next(pending, None)
        for _ in pending:
            pass
        finish_block(ia, 0)
        finish_block(ib, 1)
    ph.close()


def phase_attn(nc, scr, prm, maskT_d, y_own, blocks=None):
    blocks = blocks or default_blocks()
    nblk = len(blocks)
    poffs, _ = pair_offsets(blocks)
    ph = Phase(nc)
    P = ph.P
    fmo = scr["fmb_own"]
    fma = scr["fmb_all"]
    tmb = scr["tmb_all"]
    tmo = scr["tmf_own"]
    att_scale = 128 ** -0.5
    i30k = ph.sb([128, 128], BF16, "i30k")
    kT = ph.sb([128, 8, SEQ], BF16, "kT")
    V = ph.sb([128, 32, 8, 129], BF16, "V")
    qT = [ph.sb([128, 8, 128], BF16, "qT") for _ in range(2)]
    mT = [ph.sb([128, 32, 128], BF16, "mT") for _ in range(2)]
    PT = [ph.sb([128, 512], BF16, "PT") for _ in range(4)]
    ot = [ph.sb([128, 1024], F32, "ot") for _ in range(2)]
    zt = [ph.sb([128, 1024], F32, "zt") for _ in range(2)]
    rs = ph.sb([128, 8 * nblk], F32, "rs")
    pS = [ph.ps([128, 512], F32, "pS") for _ in range(4)]
    pO = [ph.ps([128, 512], F32, "pO") for _ in range(2)]

    ld = lambda dst, src, key: P.dma("sp", lambda e: e.dma_start(out=dst, in_=src), writes=[key])
    ld(i30k[:], prm["ident30k_b"][:, :], "i30k")
    for h in range(8):
        ld(kT[:, h, :], fma[h * 128:(h + 1) * 128, :], ("kT", h))
    vv = tmb.rearrange("(kb p) (h d) -> p kb h d", p=128, d=128)
    for kb in range(32):
        ld(V[:, kb, :, 0:128], vv[:, kb, :, :], ("V", kb))
    P.op("pool", lambda e: e.memset(V[:, :, :, 128:129], 1.0), writes=["Vones"])
    qv = fmo[0:1024, :].rearrange("(h d) t -> d h t", d=128)
    k = {"s": 0, "p": 0, "o": 0}
    def loads(i):
        b2_ = i % 2
        nkb_ = 4 * blocks[i][0]
        po_ = poffs[i]
        ld(qT[b2_][:], qv[:, :, i * 128:(i + 1) * 128], ("qT", b2_))
        ld(mT[b2_][:, 0:nkb_, :], maskT_d[:, po_:po_ + nkb_, :], ("mT", b2_))
        ld(zt[b2_][:], tmo[i * 128:(i + 1) * 128, TMF_OWN_ATTZ:TMF_OWN_ATTZ + 1024], ("zt", b2_))

    loads(0)
    for i, (nch, _) in enumerate(blocks):
        b2 = i % 2
        nkb = 4 * nch
        po = poffs[i]
        if i + 1 < nblk:
            loads(i + 1)
        P.op("act", lambda e, o=zt[b2][:]: e.activation(out=o, in_=o, func=AF.Silu), reads=[("zt", b2)], writes=[("zt", b2)])
        for h in range(8):
            jo = k["o"] % 2
            k["o"] += 1
            jss = {}

            def st_mm(ch):
                js = k["s"] % 4
                k["s"] += 1
                jss[ch] = js
                P.op("pe", lambda e, o=pS[js][:], r=mT[b2][:, ch * 4:(ch + 1) * 4, :].rearrange("p k t -> p (k t)"): e.matmul(
                    o, lhsT=i30k[:], rhs=r, start=True, stop=False), reads=["i30k", ("mT", b2)], writes=[("pS", js)])
                for k4 in range(4):
                    kb = ch * 4 + k4
                    P.op("pe", lambda e, o=pS[js][:, k4 * 128:(k4 + 1) * 128], l=kT[:, h, kb * 128:(kb + 1) * 128],
                         r=qT[b2][:, h, :], k4=k4: e.matmul(o, lhsT=l, rhs=r, start=False, stop=(k4 == 3)),
                         reads=[("kT", h), ("qT", b2)], writes=[("pS", js)])

            st_mm(0)
            for ch in range(nch):
                if ch + 1 < nch:
                    st_mm(ch + 1)
                js = jss[ch]
                jp = k["p"] % 4
                k["p"] += 1
                P.op("act", lambda e, o=PT[jp][:], s=pS[js][:]: e.activation(out=o, in_=s, func=AF.Exp, scale=att_scale),
                     reads=[("pS", js)], writes=[("PT", jp)])
                for k4 in range(4):
                    kb = ch * 4 + k4
                    P.op("pe", lambda e, o=pO[jo][:, 0:129], l=PT[jp][:, k4 * 128:(k4 + 1) * 128], r=V[:, kb, h, :], kb=kb, nkb=nkb:
                         e.matmul(o, lhsT=l, rhs=r, start=(kb == 0), stop=(kb == nkb - 1)),
                         reads=[("PT", jp), ("V", kb), "Vones"], writes=[("pO", jo)])
            c = i * 8 + h
            P.op("dve", lambda e, o=rs[:, c:c + 1], s=pO[jo][:, 128:129]: e.reciprocal(out=o, in_=s),
                 reads=[("pO", jo)], writes=[("rs", c)])
            P.op("dve", lambda e, o=ot[b2][:, h * 128:(h + 1) * 128], s=pO[jo][:, 0:128], r=rs[:, c:c + 1],
                 z=zt[b2][:, h * 128:(h + 1) * 128]: e.scalar_tensor_tensor(out=o, in0=s, scalar=r, in1=z, op0=ALU.mult, op1=ALU.mult),
                 reads=[("pO", jo), ("rs", c), ("zt", b2)], writes=[("ot", b2)])
        P.dma("sp", lambda e, o=y_own[i * 128:(i + 1) * 128, 0:1024], s=ot[b2][:]: e.dma_start(out=o, in_=s),
              reads=[("ot", b2)], writes=[("y_att", i)])
    ph.close()


RW_DECAY_C = -0.6065306597126334


class _Stop(Exception):
    pass


def phase_rwkv(nc, scr, prm, y_all, NBLK=SEQ // 128, stop_after=None, y_dst=None, stagger=True):
    ph = Phase(nc)
    P = ph.P
    tm = scr["tmf_all"]
    fm = scr["fmf_all"]
    if y_dst is None:
        y_dst = y_all[:, 256:512]
    cst = {}
    for nm in ("ident_f", "mask_sl", "mask_su", "mask_u", "ones_bd"):
        cst[nm] = ph.sb([128, 128], F32, nm)
    mu_tm = ph.sb([128, 1024], F32, "mu_tm")
    mu_fm = ph.sb([128, 1], F32, "mu_fm")
    w2a2 = ph.sb([128, 256], F32, "w2a2")
    vec = {}
    for nm in ("rw_w0", "rw_a0", "rw_kk", "rw_ka", "rw_rk", "rw_gng", "rw_gnb"):
        vec[nm] = ph.sb([128, 256], F32, nm)
    onecol = ph.sb([128, 1], F32, "onecol")
    NBUF3 = 4
    cur = [ph.sb([128, 1024], F32, "cur") for _ in range(NBUF3)]
    prv = [ph.sb([128, 1024], F32, "prv") for _ in range(NBUF3)]
    lcur = [ph.sb([128, 128], F32, "lcur") for _ in range(NBUF3)]
    lprv = [ph.sb([128, 128], F32, "lprv") for _ in range(NBUF3)]
    T = {}
    BFT = ("rt", "at", "bt", "kt", "bh", "kh", "vb")
    for nm in ("lw", "asig", "kkn", "kp", "aa", "bb", "cum", "cumL", "e1", "rt", "at", "bt", "kt", "bh", "kh", "vb", "tmp", "tmp2", "yb", "bon"):
        T[nm] = [ph.sb([128, 256], BF16 if nm in BFT else F32, nm) for _ in range(NBUF3)]
    st4 = [ph.sb([128, 16], F32, "st4") for _ in range(NBUF3)]
    gL = [ph.sb([128, 4], F32, "gL") for _ in range(NBUF3)]
    TR = {nm: [[ph.sb([128, 128], BF16, nm) for _ in range(2)] for _ in range(NBUF3)] for nm in ("atT", "btT", "ktT", "rtT")}
    H = {}
    for nm in ("N", "NT", "Pa", "PaT", "Pb", "PbT", "TTa", "TTb", "MakT"):
        H[nm] = ph.sb([128, 4, 128], BF16, nm)
    H["W2"] = ph.sb([128, 4, 64], BF16, "W2")
    mask2 = {nm: ph.sb([128, 2, 128], F32, nm + "2") for nm in ("mask_sl", "mask_su", "mask_u")}
    HS = {}
    for nm in ("P1T", "P2", "MrbT", "MrkT"):
        shp = {"P1T": [128, 2, 128], "P2": [128, 4, 64], "MrbT": [128, 4, 128], "MrkT": [128, 4, 128]}[nm]
        HS[nm] = [ph.sb(shp, F32 if nm == "P2" else BF16, nm) for _ in range(NBUF3)]
    Usb = ph.sb([128, 4, 64], BF16, "Usb")
    STb = [ph.sb([128, 64], BF16, "STb") for _ in range(4)]
    identb = ph.sb([128, 128], BF16, "identb")
    ST = [ph.sb([128, 64], F32, "ST") for _ in range(4)]
    pA = [ph.ps([128, 512], F32, "pA") for _ in range(3)]
    pP = [ph.ps([128, 512], F32, "pP") for _ in range(2)]
    pTb = ph.ps([128, 1024], BF16, "pTb")
    pQ = [ph.ps([128, 512], F32, "pQ") for _ in range(2)]

    ld = lambda dst, src, key: P.dma("sp", lambda e: e.dma_start(out=dst, in_=src), writes=[key])
    for nm in cst:
        ld(cst[nm][:], prm[nm][:, :], nm)
    ld(identb[:], prm["ident_b"][:, :], "identb")
    ld(mu_tm[:], prm["rw_mu_tm"][0:1, :].partition_broadcast(128), "mu_tm")
    ld(mu_fm[:], prm["rw_mu_fm"][:, :], "mu_fm")
    ld(w2a2[:], prm["rw_w2a2"][:, :], "w2a2")
    for nm in vec:
        ld(vec[nm][:], prm[nm][0:1, :].partition_broadcast(128), nm)
    P.op("dve", lambda e: e.memset(onecol[:], 1.0), writes=["onecol"])
    for h in range(4):
        P.op("dve", lambda e, o=ST[h][:]: e.memset(o, 0.0), writes=[("ST", h)])
        P.op("dve", lambda e, o=STb[h][:]: e.memset(o, 0.0), writes=[("STb", h)])
    P.op("dve", lambda e: e.memset(Usb[:], 0.0), writes=["Usb"])
    for nm in ("mask_sl", "mask_su", "mask_u"):
        for r_ in range(2):
            P.op("pool", lambda e, o=mask2[nm][:, r_, :], s=cst[nm][:]: e.tensor_copy(out=o, in_=s), reads=[nm], writes=[nm + "2"])
    kq = {"a": 0, "q": 0, "e": 0}
    P.excl.update(["pA", "pQ", "pTb", "pP"])

    def mm(out, lhsT, rhs, reads, writes, start=True, stop=True):
        P.op("pe", lambda e: e.matmul(out, lhsT=lhsT, rhs=rhs, start=start, stop=stop), reads=reads, writes=writes)

    def evac(out, in_, reads, writes, mask=None, mreads=()):
        kq["e"] += 1
        if mask is not None:
            P.op("dve", lambda e: e.tensor_tensor(out=out, in0=in_, in1=mask, op=ALU.mult), reads=list(reads) + list(mreads), writes=writes)
        elif kq["e"] % 4 == 0:
            P.op("dve", lambda e: e.tensor_copy(out=out, in_=in_), reads=reads, writes=writes)
        else:
            P.op("act", lambda e: e.copy(out=out, in_=in_), reads=reads, writes=writes)

    def nextA():
        j = kq["a"] % 3
        kq["a"] += 1
        return j

    def nextQ():
        j = kq["q"] % 2
        kq["q"] += 1
        return j

    def dv(fn, reads, writes, eng="dve"):
        P.op(eng, fn, reads=reads, writes=writes)

    def stage(n):
        if stop_after is not None and n > stop_after:
            raise _Stop()

    real_op, real_dma = P.op, P.dma
    recs = []
    for blk in range(NBLK):
      rec = []
      marks = []
      recs.append((rec, marks))
      P.op = lambda eng, fn, reads=(), writes=(), rec=rec: rec.append((real_op, eng, fn, list(reads), list(writes)))
      P.dma = lambda eng, fn, reads=(), writes=(), rec=rec: rec.append((real_dma, eng, fn, list(reads), list(writes)))
      try:
          b2 = blk % NBUF3
          t0 = blk * 128
          B = {nm: T[nm][b2] for nm in T}
          K2 = lambda nm: (nm, b2)
          cu, pv, lc, lp = cur[b2], prv[b2], lcur[b2], lprv[b2]
          stage(-1)
          ld(cu[:], tm[t0:t0 + 128, TMF_R:TMF_R + 1024], K2("cur"))
          ld(lc[:], fm[FMF_WL:FMF_WL + 128, t0:t0 + 128], K2("lcur"))
          if blk == 0:
              dv(lambda e, o=pv[:]: e.memset(o, 0.0), [], [K2("prv")])
              dv(lambda e, o=lp[:]: e.memset(o, 0.0), [], [K2("lprv")])
              ld(pv[1:128, :], tm[0:127, TMF_R:TMF_R + 1024], K2("prv"))
              ld(lp[:, 1:128], fm[FMF_WL:FMF_WL + 128, 0:127], K2("lprv"))
          else:
              ld(pv[:], tm[t0 - 1:t0 + 127, TMF_R:TMF_R + 1024], K2("prv"))
              ld(lp[:], fm[FMF_WL:FMF_WL + 128, t0 - 1:t0 + 127], K2("lprv"))
          stage(-0.5)
          dv(lambda e, o=pv[:], c=cu[:]: e.tensor_tensor(out=o, in0=o, in1=c, op=ALU.subtract), [K2("prv"), K2("cur")], [K2("prv")])
          dv(lambda e, o=pv[:]: e.tensor_tensor(out=o, in0=o, in1=mu_tm[:], op=ALU.mult), [K2("prv"), "mu_tm"], [K2("prv")], eng="pool")
          dv(lambda e, o=cu[:], d=pv[:]: e.tensor_tensor(out=o, in0=o, in1=d, op=ALU.add), [K2("prv"), K2("cur")], [K2("cur")])
          dv(lambda e, o=lp[:], c=lc[:]: e.tensor_tensor(out=o, in0=o, in1=c, op=ALU.subtract), [K2("lprv"), K2("lcur")], [K2("lprv")])
          dv(lambda e, o=lc[:], d=lp[:]: e.scalar_tensor_tensor(out=o, in0=d, scalar=mu_fm[:, 0:1], in1=o, op0=ALU.mult, op1=ALU.add),
             [K2("lprv"), K2("lcur"), "mu_fm"], [K2("lcur")])
          stage(-0.2)
          P.op("act", lambda e, o=lc[0:64, :]: e.activation(out=o, in_=o, func=AF.Tanh), reads=[K2("lcur")], writes=[K2("lcur")])
          r_, k_, v_, z_ = cu[:, 0:256], cu[:, 256:512], cu[:, 512:768], cu[:, 768:1024]
          P.op("act", lambda e, o=B["vb"][:], v_=v_: e.copy(out=o, in_=v_), reads=[K2("cur")], writes=[K2("vb")])
          stage(1)
          mm(pP[0][:, 0:256], lc[0:64, :], w2a2[0:64, :], [K2("lcur"), "w2a2"], [("pP", 0)])
          mm(pP[1][:, 0:256], lc[64:128, :], w2a2[64:128, :], [K2("lcur"), "w2a2"], [("pP", 1)])
          dv(lambda e, o=B["lw"][:], s=pP[0][:, 0:256]: e.tensor_tensor(out=o, in0=s, in1=vec["rw_w0"][:], op=ALU.add),
             [("pP", 0), "rw_w0"], [K2("lw")])
          dv(lambda e, o=B["asig"][:], s=pP[1][:, 0:256]: e.tensor_tensor(out=o, in0=s, in1=vec["rw_a0"][:], op=ALU.add),
             [("pP", 1), "rw_a0"], [K2("asig")])
          P.op("act", lambda e, o=B["lw"][:]: e.activation(out=o, in_=o, func=AF.Sigmoid), reads=[K2("lw")], writes=[K2("lw")])
          P.op("act", lambda e, o=B["asig"][:]: e.activation(out=o, in_=o, func=AF.Sigmoid), reads=[K2("asig")], writes=[K2("asig")])
          dv(lambda e, o=B["lw"][:]: e.tensor_scalar(out=o, in0=o, scalar1=RW_DECAY_C, scalar2=None, op0=ALU.mult), [K2("lw")], [K2("lw")])
          stage(2)
          s4 = st4[b2]
          v3 = lambda ap: ap.rearrange("p (h j) -> p h j", h=4)
          dv(lambda e, o=B["kkn"][:], k_=k_: e.tensor_tensor(out=o, in0=k_, in1=vec["rw_kk"][:], op=ALU.mult), [K2("cur"), "rw_kk"], [K2("kkn")])
          dv(lambda e, o=B["tmp"][:], s=B["kkn"][:]: e.tensor_tensor(out=o, in0=s, in1=s, op=ALU.mult), [K2("kkn")], [K2("tmp")], eng="pool")
          dv(lambda e, o=s4[:, 0:4], s=v3(B["tmp"][:]): e.tensor_reduce(out=o, in_=s, axis=AX.X, op=ALU.add), [K2("tmp")], [K2("st4")])
          dv(lambda e, o=s4[:, 0:4]: e.tensor_scalar(out=o, in0=o, scalar1=1e-12, scalar2=None, op0=ALU.add), [K2("st4")], [K2("st4")])
          P.op("act", lambda e, o=s4[:, 0:4]: e.activation(out=o, in_=o, func=AF.Sqrt), reads=[K2("st4")], writes=[K2("st4")])
          dv(lambda e, o=s4[:, 0:4]: e.reciprocal(out=o, in_=o), [K2("st4")], [K2("st4")])
          dv(lambda e, o=v3(B["kkn"][:]), s=s4[:, 0:4].unsqueeze(2).to_broadcast([128, 4, 64]): e.tensor_tensor(out=o, in0=o, in1=s, op=ALU.mult),
             [K2("kkn"), K2("st4")], [K2("kkn")])
          dv(lambda e, o=B["tmp"][:], s=B["asig"][:]: e.scalar_tensor_tensor(out=o, in0=s, scalar=-1.0, in1=vec["rw_ka"][:], op0=ALU.add,
                                                                            op1=ALU.mult), [K2("asig"), "rw_ka"], [K2("tmp")])
          dv(lambda e, o=B["kp"][:], s=B["tmp"][:], k_=k_: e.scalar_tensor_tensor(out=o, in0=s, scalar=1.0, in1=k_, op0=ALU.add, op1=ALU.mult),
             [K2("tmp"), K2("cur")], [K2("kp")])
          dv(lambda e, o=B["aa"][:], s=B["kkn"][:]: e.tensor_scalar(out=o, in0=s, scalar1=-1.0, scalar2=None, op0=ALU.mult),
             [K2("kkn")], [K2("aa")], eng="pool")
          dv(lambda e, o=B["bb"][:], s=B["kkn"][:], a=B["asig"][:]: e.tensor_tensor(out=o, in0=s, in1=a, op=ALU.mult),
             [K2("kkn"), K2("asig")], [K2("bb")], eng="pool")
          dv(lambda e, o=B["tmp2"][:], s=B["kp"][:], r_=r_: e.tensor_tensor(out=o, in0=r_, in1=s, op=ALU.mult), [K2("cur"), K2("kp")], [K2("tmp2")])
          dv(lambda e, o=B["tmp2"][:]: e.tensor_tensor(out=o, in0=o, in1=vec["rw_rk"][:], op=ALU.mult), [K2("tmp2"), "rw_rk"], [K2("tmp2")])
          dv(lambda e, o=s4[:, 4:8], s=v3(B["tmp2"][:]): e.tensor_reduce(out=o, in_=s, axis=AX.X, op=ALU.add), [K2("tmp2")], [K2("st4")])
          dv(lambda e, o=v3(B["bon"][:]), s=v3(cu[:, 512:768]), c=s4[:, 4:8].unsqueeze(2).to_broadcast([128, 4, 64]):
             e.tensor_tensor(out=o, in0=s, in1=c, op=ALU.mult), [K2("cur"), K2("st4")], [K2("bon")])
          stage(3)
          mm(pP[0][:, 0:256], cst["mask_u"][:], B["lw"][:], ["mask_u", K2("lw")], [("pP", 0)])
          mm(pP[0][:, 256:512], cst["ones_bd"][:], B["lw"][:], ["ones_bd", K2("lw")], [("pP", 0)])
          evac(B["cum"][:], pP[0][:, 0:256], [("pP", 0)], [K2("cum")])
          evac(B["cumL"][:], pP[0][:, 256:512], [("pP", 0)], [K2("cumL")])
          for p in range(2):
              for c2 in range(2):
                  mm(pP[1][:, c2 * 2 + p:c2 * 2 + p + 1], B["lw"][:, p * 128:(p + 1) * 128], cst["ones_bd"][:, c2 * 64:c2 * 64 + 1],
                     [K2("lw"), "ones_bd"], [("pP", 1)])
          P.op("act", lambda e, o=gL[b2][:], s=pP[1][:, 0:4]: e.activation(out=o, in_=s, func=AF.Exp), reads=[("pP", 1)], writes=[K2("gL")])
          P.op("act", lambda e, o=B["e1"][:], s=B["cum"][:]: e.activation(out=o, in_=s, func=AF.Exp), reads=[K2("cum")], writes=[K2("e1")])
          dv(lambda e, o=B["rt"][:], s=B["e1"][:], r_=r_: e.tensor_tensor(out=o, in0=r_, in1=s, op=ALU.mult), [K2("cur"), K2("e1")], [K2("rt")])
          dv(lambda e, o=B["tmp"][:], s=B["cum"][:], l=B["lw"][:]: e.tensor_tensor(out=o, in0=s, in1=l, op=ALU.subtract),
             [K2("cum"), K2("lw")], [K2("tmp")], eng="pool")
          P.op("act", lambda e, o=B["tmp"][:]: e.activation(out=o, in_=o, func=AF.Exp), reads=[K2("tmp")], writes=[K2("tmp")])
          dv(lambda e, o=B["at"][:], s=B["aa"][:], t=B["tmp"][:]: e.tensor_tensor(out=o, in0=s, in1=t, op=ALU.mult),
             [K2("aa"), K2("tmp")], [K2("at")])
          P.op("act", lambda e, o=B["e1"][:], s=B["cum"][:]: e.activation(out=o, in_=s, func=AF.Exp, scale=-1.0),
               reads=[K2("cum"), K2("rt")], writes=[K2("e1")])
          dv(lambda e, o=B["bt"][:], s=B["bb"][:], t=B["e1"][:]: e.tensor_tensor(out=o, in0=s, in1=t, op=ALU.mult),
             [K2("bb"), K2("e1")], [K2("bt")])
          dv(lambda e, o=B["kt"][:], s=B["kp"][:], t=B["e1"][:]: e.tensor_tensor(out=o, in0=s, in1=t, op=ALU.mult),
             [K2("kp"), K2("e1")], [K2("kt")], eng="pool")
          dv(lambda e, o=B["tmp2"][:], s=B["cumL"][:], c=B["cum"][:]: e.tensor_tensor(out=o, in0=s, in1=c, op=ALU.subtract),
             [K2("cumL"), K2("cum")], [K2("tmp2")], eng="pool")
          P.op("act", lambda e, o=B["tmp2"][:]: e.activation(out=o, in_=o, func=AF.Exp), reads=[K2("tmp2")], writes=[K2("tmp2")])
          dv(lambda e, o=B["bh"][:], s=B["bb"][:], t=B["tmp2"][:]: e.tensor_tensor(out=o, in0=s, in1=t, op=ALU.mult),
             [K2("bb"), K2("tmp2")], [K2("bh")])
          dv(lambda e, o=B["kh"][:], s=B["kp"][:], t=B["tmp2"][:]: e.tensor_tensor(out=o, in0=s, in1=t, op=ALU.mult),
             [K2("kp"), K2("tmp2")], [K2("kh")], eng="pool")
          stage(4)
          for qi_, (nm_src, nm_dst) in enumerate((("at", "atT"), ("bt", "btT"), ("kt", "ktT"), ("rt", "rtT"))):
              for p in range(2):
                  P.op("pe", lambda e, o=pTb[:, (qi_ * 2 + p) * 128:(qi_ * 2 + p + 1) * 128], s=B[nm_src][:, p * 128:(p + 1) * 128]: e.transpose(
                      out=o, in_=s, identity=identb[:]), reads=[K2(nm_src), "identb"], writes=["pTb"])
          for qi_, (nm_src, nm_dst) in enumerate((("at", "atT"), ("bt", "btT"), ("kt", "ktT"), ("rt", "rtT"))):
              for p in range(2):
                  evac(TR[nm_dst][b2][p][:], pTb[:, (qi_ * 2 + p) * 128:(qi_ * 2 + p + 1) * 128], ["pTb"], [(nm_dst, b2, p)])
          marks.append(len(rec))
          slot = lambda h: (h % 2) * 2 + h // 2
          rk = lambda nm, h: (nm, b2, h // 2)
          opd = {}
          for h in range(4):
              p, r0 = h // 2, (h % 2) * 64
              opd[h] = {nm: TR[nm2][b2][p][r0:r0 + 64, :] for nm, nm2 in (("aT", "atT"), ("bT", "btT"), ("kT", "ktT"), ("rT", "rtT"))}
          jx, jy2 = nextA(), nextA()
          for r_, jb in ((0, jx), (1, jy2)):
              for p in range(2):
                  h = 2 * p + r_
                  o_ = opd[h]
                  mm(pA[jb][:, p * 128:(p + 1) * 128], o_["aT"], o_["bT"], [rk("atT", h), rk("btT", h)], [("pA", jb)])
                  mm(pA[jb][:, 256 + p * 128:256 + (p + 1) * 128], o_["bT"], o_["aT"], [rk("atT", h), rk("btT", h)], [("pA", jb)])
          for r_, jb in ((0, jx), (1, jy2)):
              sl_ = slice(2 * r_, 2 * r_ + 2)
              dv(lambda e, o=H["N"][:, sl_, :], s=pA[jb][:, 0:256].rearrange("p (a t) -> p a t", a=2), m=mask2["mask_sl"][:]:
                 e.tensor_tensor(out=o, in0=s, in1=m, op=ALU.mult), [("pA", jb), "mask_sl2"], [("N", r_)])
              dv(lambda e, o=H["NT"][:, sl_, :], s=pA[jb][:, 256:512].rearrange("p (a t) -> p a t", a=2), m=mask2["mask_su"][:]:
                 e.tensor_tensor(out=o, in0=s, in1=m, op=ALU.mult), [("pA", jb), "mask_su2"], [("NT", r_)])
          jz = nextA()
          jx2 = nextA()
          for r_, jb in ((0, jx2), (1, jz)):
              for p in range(2):
                  h = 2 * p + r_
                  o_ = opd[h]
                  mm(pA[jb][:, p * 128:(p + 1) * 128], o_["kT"], o_["aT"], [rk("ktT", h), rk("atT", h)], [("pA", jb)])
                  mm(pA[jb][:, 256 + p * 128:256 + (p + 1) * 128], o_["bT"], o_["rT"], [rk("btT", h), rk("rtT", h)], [("pA", jb)])
          for r_, jb in ((0, jx2), (1, jz)):
              sl_ = slice(2 * r_, 2 * r_ + 2)
              dv(lambda e, o=H["MakT"][:, sl_, :], s=pA[jb][:, 0:256].rearrange("p (a t) -> p a t", a=2), m=mask2["mask_su"][:]:
                 e.tensor_tensor(out=o, in0=s, in1=m, op=ALU.mult), [("pA", jb), "mask_su2"], [("MakT", r_)])
              dv(lambda e, o=HS["MrbT"][b2][:, sl_, :], s=pA[jb][:, 256:512].rearrange("p (a t) -> p a t", a=2), m=mask2["mask_u"][:]:
                 e.tensor_tensor(out=o, in0=s, in1=m, op=ALU.mult), [("pA", jb), "mask_u2"], [("MrbT", b2, r_)])
          jk0, jk1 = nextA(), nextA()
          for r_, jb in ((0, jk0), (1, jk1)):
              for p in range(2):
                  h = 2 * p + r_
                  o_ = opd[h]
                  mm(pA[jb][:, p * 128:(p + 1) * 128], o_["kT"], o_["rT"], [rk("ktT", h), rk("rtT", h)], [("pA", jb)])
          for r_, jb in ((0, jk0), (1, jk1)):
              sl_ = slice(2 * r_, 2 * r_ + 2)
              dv(lambda e, o=HS["MrkT"][b2][:, sl_, :], s=pA[jb][:, 0:256].rearrange("p (a t) -> p a t", a=2), m=mask2["mask_u"][:]:
                 e.tensor_tensor(out=o, in0=s, in1=m, op=ALU.mult), [("pA", jb), "mask_u2"], [("MrkT", b2, r_)])
          dv(lambda e, o=H["TTa"][:], s=H["NT"][:], i_=cst["ident_f"][:].unsqueeze(1).to_broadcast([128, 4, 128]):
             e.tensor_tensor(out=o, in0=s, in1=i_, op=ALU.add), [("NT", 0), ("NT", 1), "ident_f"], ["TTa"], eng="pool")
          cur_, curT_, nxt_, nxtT_ = "N", "NT", "Pa", "PaT"
          tc_, tn_ = "TTa", "TTb"
          kn = lambda nm: [(nm, 0), (nm, 1)] if nm in ("N", "NT") else [nm]
          for lvl in range(1, 6):
              ja = nextA()
              for sl in range(4):
                  mm(pA[ja][:, sl * 128:(sl + 1) * 128], H[curT_][:, sl, :], H[cur_][:, sl, :], kn(cur_) + kn(curT_), [("pA", ja)])
              evac(H[nxt_][:], pA[ja][:].rearrange("p (a t) -> p a t", a=4), [("pA", ja)], [nxt_])
              if lvl < 5:
                  jb = nextA()
                  for sl in range(4):
                      mm(pA[jb][:, sl * 128:(sl + 1) * 128], H[cur_][:, sl, :], H[curT_][:, sl, :], kn(cur_) + kn(curT_), [("pA", jb)])
                  evac(H[nxtT_][:], pA[jb][:].rearrange("p (a t) -> p a t", a=4), [("pA", jb)], [nxtT_])
              jc = nextA()
              for sl in range(4):
                  mm(pA[jc][:, sl * 128:(sl + 1) * 128], H[nxt_][:, sl, :], H[tc_][:, sl, :], [nxt_, tc_], [("pA", jc)])
              dv(lambda e, o=H[tn_][:], s=pA[jc][:].rearrange("p (a t) -> p a t", a=4), t=H[tc_][:]: e.tensor_tensor(out=o, in0=s, in1=t, op=ALU.add),
                 [("pA", jc), tc_], [tn_])
              if lvl == 1:
                  cur_, curT_, nxt_, nxtT_ = "Pa", "PaT", "Pb", "PbT"
              else:
                  cur_, curT_, nxt_, nxtT_ = nxt_, nxtT_, cur_, curT_
              tc_, tn_ = tn_, tc_
          TTn = tc_
          ja = nextA()
          for h in range(4):
              p, r0 = h // 2, (h % 2) * 64
              mm(pA[ja][r0:r0 + 64, p * 128:(p + 1) * 128], B["at"][:, h * 64:(h + 1) * 64], H[TTn][:, slot(h), :], [K2("at"), TTn], [("pA", ja)])
              mm(pA[ja][:, 256 + h * 64:256 + (h + 1) * 64], H["MakT"][:, slot(h), :], B["vb"][:, h * 64:(h + 1) * 64],
                 [("MakT", h % 2), K2("vb")], [("pA", ja)])
          evac(HS["P1T"][b2][:], pA[ja][:, 0:256].rearrange("p (a t) -> p a t", a=2), [("pA", ja)], [("P1T", b2)])
          evac(H["W2"][:], pA[ja][:, 256:512].rearrange("p (h i) -> p h i", h=4), [("pA", ja)], ["W2"])
          jb = nextA()
          for h in range(4):
              mm(pA[jb][:, h * 64:(h + 1) * 64], H[TTn][:, slot(h), :], H["W2"][:, h, :], [TTn, "W2"], [("pA", jb)])
          evac(HS["P2"][b2][:], pA[jb][:, 0:256].rearrange("p (h i) -> p h i", h=4), [("pA", jb)], [("P2", b2)])
          stage(6)
          marks.append(len(rec))
          for c2 in range(2):
              cs = slice(c2 * 64, (c2 + 1) * 64)
              jq = nextQ()
              for h in range(4):
                  mm(pQ[jq][cs, h * 64:(h + 1) * 64], HS["P1T"][b2][:, h // 2, cs], STb[h][:, :], [("P1T", b2), ("STb", h)], [("pQ", jq)])
              dv(lambda e, o=Usb[cs, :, :], s=pQ[jq][cs, 0:256].rearrange("p (h i) -> p h i", h=4), t=HS["P2"][b2][cs, :, :]:
                 e.tensor_tensor(out=o, in0=s, in1=t, op=ALU.add), [("pQ", jq), ("P2", b2)], ["Usb"])
              jy = nextQ()
              for h in range(4):
                  p = h // 2
                  yo = pQ[jy][cs, h * 64:(h + 1) * 64]
                  mm(yo, TR["rtT"][b2][p][:, cs], STb[h][:, :], [("rtT", b2, p), ("STb", h)], [("pQ", jy)], start=True, stop=False)
                  mm(yo, HS["MrkT"][b2][:, slot(h), cs], B["vb"][:, h * 64:(h + 1) * 64], [("MrkT", b2, h % 2), K2("vb")], [("pQ", jy)],
                     start=False, stop=False)
                  mm(yo, HS["MrbT"][b2][:, slot(h), cs], Usb[:, h, :], [("MrbT", b2, h % 2), "Usb"], [("pQ", jy)], start=False, stop=True)
              evac(B["yb"][cs, :], pQ[jy][cs, 0:256], [("pQ", jy)], [K2("yb")])
              js = nextQ()
              for h in range(4):
                  r0 = (h % 2) * 64
                  rows = slice(r0, r0 + 64)
                  so = pQ[js][rows, h * 64:(h + 1) * 64]
                  mm(so, B["kh"][cs, h * 64:(h + 1) * 64], B["vb"][cs, h * 64:(h + 1) * 64], [K2("kh"), K2("vb")], [("pQ", js)],
                     start=True, stop=False)
                  mm(so, B["bh"][cs, h * 64:(h + 1) * 64], Usb[cs, h, :], [K2("bh"), "Usb"], [("pQ", js)], start=False, stop=True)
              for h in range(4):
                  p, r0 = h // 2, (h % 2) * 64
                  rows = slice(r0, r0 + 64)
                  dv(lambda e, o=ST[h][rows, :], s=pQ[js][rows, h * 64:(h + 1) * 64], g=gL[b2][rows, c2 * 2 + p:c2 * 2 + p + 1]:
                     e.scalar_tensor_tensor(out=o, in0=o, scalar=g, in1=s, op0=ALU.mult, op1=ALU.add),
                     [("ST", h), ("pQ", js), K2("gL")], [("ST", h)])
                  P.op("act", lambda e, o=STb[h][rows, :], s=ST[h][rows, :]: e.copy(out=o, in_=s), reads=[("ST", h)], writes=[("STb", h)])
          stage(7)
          marks.append(len(rec))
          yb = B["yb"]
          dv(lambda e, o=s4[:, 8:12], s=v3(yb[:]): e.tensor_reduce(out=o, in_=s, axis=AX.X, op=ALU.add), [K2("yb")], [K2("st4")])
          dv(lambda e, o=B["tmp"][:], s=yb[:]: e.tensor_tensor(out=o, in0=s, in1=s, op=ALU.mult), [K2("yb")], [K2("tmp")], eng="pool")
          dv(lambda e, o=s4[:, 12:16], s=v3(B["tmp"][:]): e.tensor_reduce(out=o, in_=s, axis=AX.X, op=ALU.add), [K2("tmp")], [K2("st4")])
          dv(lambda e, o=s4[:, 8:16]: e.tensor_scalar(out=o, in0=o, scalar1=1.0 / 64, scalar2=None, op0=ALU.mult), [K2("st4")], [K2("st4")])
          dv(lambda e, o=s4[:, 0:4], m=s4[:, 8:12]: e.tensor_tensor(out=o, in0=m, in1=m, op=ALU.mult), [K2("st4")], [K2("st4")])
          dv(lambda e, o=s4[:, 12:16], m2=s4[:, 0:4]: e.tensor_tensor(out=o, in0=o, in1=m2, op=ALU.subtract), [K2("st4")], [K2("st4")])
          dv(lambda e, o=s4[:, 12:16]: e.tensor_scalar(out=o, in0=o, scalar1=GN_EPS, scalar2=None, op0=ALU.add), [K2("st4")], [K2("st4")])
          P.op("act", lambda e, o=s4[:, 12:16]: e.activation(out=o, in_=o, func=AF.Sqrt), reads=[K2("st4")], writes=[K2("st4")])
          dv(lambda e, o=s4[:, 12:16]: e.reciprocal(out=o, in_=o), [K2("st4")], [K2("st4")])
          dv(lambda e, o=v3(yb[:]), m=s4[:, 8:12].unsqueeze(2).to_broadcast([128, 4, 64]): e.tensor_tensor(out=o, in0=o, in1=m, op=ALU.subtract),
             [K2("yb"), K2("st4")], [K2("yb")])
          dv(lambda e, o=v3(yb[:]), r=s4[:, 12:16].unsqueeze(2).to_broadcast([128, 4, 64]): e.tensor_tensor(out=o, in0=o, in1=r, op=ALU.mult),
             [K2("yb"), K2("st4")], [K2("yb")])
          dv(lambda e, o=yb[:]: e.tensor_tensor(out=o, in0=o, in1=vec["rw_gng"][:], op=ALU.mult), [K2("yb"), "rw_gng"], [K2("yb")], eng="pool")
          dv(lambda e, o=yb[:]: e.tensor_tensor(out=o, in0=o, in1=vec["rw_gnb"][:], op=ALU.add), [K2("yb"), "rw_gnb"], [K2("yb")])
          dv(lambda e, o=yb[:], b_=B["bon"][:]: e.tensor_tensor(out=o, in0=o, in1=b_, op=ALU.add), [K2("yb"), K2("bon")], [K2("yb")], eng="pool")
          P.op("act", lambda e, o=B["tmp2"][:], z_=z_: e.activation(out=o, in_=z_, func=AF.Silu), reads=[K2("cur")], writes=[K2("tmp2")])
          dv(lambda e, o=yb[:], z=B["tmp2"][:]: e.tensor_tensor(out=o, in0=o, in1=z, op=ALU.mult), [K2("yb"), K2("tmp2")], [K2("yb")])
          P.dma("sp", lambda e, o=y_dst[t0:t0 + 128, :], s=yb[:]: e.dma_start(out=o, in_=s), reads=[K2("yb")], writes=[("y_rwkv", blk)])
      except _Stop:
          pass
    P.op, P.dma = real_op, real_dma
    if stagger:
        pipeline_merge(recs, 4)
    else:
        for rec, _ in recs:
            for (f, eng, fn, reads, writes) in rec:
                f(eng, fn, reads=reads, writes=writes)
    ph.close()


def phase_outproj(nc, y_tok, x_res, w_out, ident_d, x_new):
    ph = Phase(nc)
    P = ph.P
    D = D_MODEL
    KT = D // 128
    T = OWN
    TT = T // 128
    KQ = 8
    NKQ = KT // KQ
    NB = 512
    ident = ph.sb([128, 128], BF16, "ident")
    hT = ph.sb([128, KT, T], BF16, "hT")
    xt = [ph.sb([128, D], F32, "xt") for _ in range(2)]
    hb = [ph.sb([128, D], BF16, "hb") for _ in range(2)]
    wb = [ph.sb([128, KT, NB], BF16, "wb") for _ in range(2)]
    stg = [ph.sb([128, NB], F32, "stg") for _ in range(4)]
    rs = [ph.sb([128, NB], F32, "rs") for _ in range(4)]
    pT = [ph.ps([128, 1024], BF16, "pT") for _ in range(2)]
    pM = [ph.ps([128, 512], F32, "pM") for _ in range(6)]
    P.dma("sp", lambda e: e.dma_start(out=ident[:], in_=ident_d[:, :]), writes=["ident"])
    tc_ = 0
    for tt in range(TT):
        i = tt % 2
        P.dma("sp", lambda e, o=xt[i][:], s=y_tok[tt * 128:(tt + 1) * 128, :]: e.dma_start(out=o, in_=s), writes=[("xt", i)])
        P.op("act", lambda e, o=hb[i][:], s=xt[i][:]: e.copy(out=o, in_=s), reads=[("xt", i)], writes=[("hb", i)])
        for kq in range(NKQ):
            pb = tc_ % 2
            tc_ += 1
            for k8 in range(KQ):
                kt = kq * KQ + k8
                P.op("pe", lambda e, o=pT[pb][:, k8 * 128:(k8 + 1) * 128], s=hb[i][:, kt * 128:(kt + 1) * 128]: e.transpose(
                    out=o, in_=s, identity=ident[:]), reads=[("hb", i), "ident"], writes=[("pT", pb)])
            dst = hT[:, kq * KQ:(kq + 1) * KQ, tt * 128:(tt + 1) * 128]
            src = pT[pb][:].rearrange("p (k t) -> p k t", k=KQ)
            if kq % 2 == 0:
                P.op("dve", lambda e, dst=dst, src=src: e.tensor_copy(out=dst, in_=src), reads=[("pT", pb)], writes=[("hT", tt, kq)])
            else:
                P.op("act", lambda e, dst=dst, src=src: e.copy(out=dst, in_=src), reads=[("pT", pb)], writes=[("hT", tt, kq)])
    wv = w_out.rearrange("(kt p) n -> p kt n", p=128)
    mc = 0
    for cb in range(D // NB):
        c0 = cb * NB
        wi = cb % 2
        for kq in range(NKQ):
            P.dma("pool", lambda e, o=wb[wi][:, kq * KQ:(kq + 1) * KQ, :], s=wv[:, kq * KQ:(kq + 1) * KQ, c0:c0 + NB]: e.dma_start(
                out=o, in_=s), writes=[("wb", wi, kq)])
        for tt in range(TT):
            j = mc % 6
            s4 = mc % 4
            mc += 1
            for kt in range(KT):
                P.op("pe", lambda e, o=pM[j][:], l=hT[:, kt, tt * 128:(tt + 1) * 128], r=wb[wi][:, kt, :], kt=kt: e.matmul(
                    o, lhsT=l, rhs=r, start=(kt == 0), stop=(kt == KT - 1)),
                    reads=[("hT", tt, kt // KQ), ("wb", wi, kt // KQ)], writes=[("pM", j)])
            P.dma("sp", lambda e, o=rs[s4][:], s=x_res[tt * 128:(tt + 1) * 128, c0:c0 + NB]: e.dma_start(out=o, in_=s), writes=[("rs", s4)])
            P.op("dve", lambda e, o=stg[s4][:], a=pM[j][:], b_=rs[s4][:]: e.tensor_tensor(out=o, in0=a, in1=b_, op=ALU.add),
                 reads=[("pM", j), ("rs", s4)], writes=[("stg", s4)])
            P.dma("sp", lambda e, o=x_new[tt * 128:(tt + 1) * 128, c0:c0 + NB], s=stg[s4][:]: e.dma_start(out=o, in_=s),
                  reads=[("stg", s4)], writes=[("xn", tt, cb)])
    ph.close()


def phase_finalnorm(nc, x_in, g, out, ntiles=OWN // 128):
    ph = Phase(nc)
    P = ph.P
    D = D_MODEL
    gb = ph.sb([128, D], F32, "gb")
    xt = [ph.sb([128, D], F32, "xt") for _ in range(2)]
    junk = ph.sb([128, D], BF16, "junk")
    ss = ph.sb([128, ntiles], F32, "ss")
    P.dma("sp", lambda e: e.dma_start(out=gb[:], in_=g[0:1, :].partition_broadcast(128)), writes=["gb"])
    P.op("dve", lambda e: e.memset(ss[:], 0.0), writes=["ss"])
    for tt in range(ntiles):
        i = tt % 2
        P.dma("sp", lambda e, o=xt[i][:], s=x_in[tt * 128:(tt + 1) * 128, :]: e.dma_start(out=o, in_=s), writes=[("xt", i)])
        sc = ss[:, tt:tt + 1]
        P.op("act", lambda e, s=xt[i][:], sc=sc: e.activation(out=junk[:], in_=s, func=AF.Square, accum_out=sc),
             reads=[("xt", i), "ss"], writes=["junk", ("ssv", tt)])
        P.op("dve", lambda e, sc=sc: e.tensor_scalar(out=sc, in0=sc, scalar1=1.0 / D, scalar2=NORM_EPS, op0=ALU.mult, op1=ALU.add),
             reads=[("ssv", tt)], writes=[("ssv", tt)])
        P.op("act", lambda e, sc=sc: e.activation(out=sc, in_=sc, func=AF.Sqrt), reads=[("ssv", tt)], writes=[("ssv", tt)])
        P.op("dve", lambda e, sc=sc: e.reciprocal(out=sc, in_=sc), reads=[("ssv", tt)], writes=[("ssv", tt)])
        P.op("dve", lambda e, o=xt[i][:], sc=sc: e.scalar_tensor_tensor(out=o, in0=o, scalar=sc, in1=gb[:], op0=ALU.mult, op1=ALU.mult),
             reads=[("xt", i), ("ssv", tt), "gb"], writes=[("xt", i)])
        P.dma("sp", lambda e, o=out[tt * 128:(tt + 1) * 128, :], s=xt[i][:]: e.dma_start(out=o, in_=s), reads=[("xt", i)],
              writes=[("out", tt)])
    ph.close()


def own_tok(q):
    return ((4 * np.arange(8)[:, None] + q) * 128 + np.arange(128)[None, :]).reshape(-1)


def const_inputs():
    i = np.arange(128)
    sel4 = np.zeros((4, 4, 128), np.float32)
    for h in range(4):
        sel4[h, h, :] = 1.0
    same = (i[:, None] // 64) == (i[None, :] // 64)
    sl = ((i[None, :] < i[:, None]) & same).astype(np.float32)
    su = np.ascontiguousarray(sl.T)
    return {
        "ident": np.eye(128, dtype=ml_dtypes.bfloat16),
        "ident_f": np.eye(128, dtype=np.float32), "ident_b": np.eye(128, dtype=ml_dtypes.bfloat16),
        "tri_le": (i[:, None] <= i[None, :]).astype(np.float32), "ones_f": np.ones((128, 128), np.float32),
        "sel4": sel4, "negbig_lt": np.where(i[None, :] < i[:, None], NEG_BIG, 0.0).astype(np.float32),
        "ident30k_b": (30000.0 * np.eye(128)).astype(ml_dtypes.bfloat16),
        "iota512": np.arange(512, dtype=np.float32)[None, :].copy(),
        "pow2": (0.5 ** (np.arange(NBIS) + 1)).astype(np.float32)[None, :].copy(),
        "mask_sl": sl, "mask_su": su, "mask_u": su + np.eye(128, dtype=np.float32), "ones_bd": same.astype(np.float32),
    }


def layer_params(inp, l, q):
    c = np.ascontiguousarray
    p = {}
    p["qrel"] = (q * 128 + np.arange(128, dtype=np.float32))[:, None].copy()
    p["g"] = c(inp["norm_g"][l][None, :])
    p["mlp_ln_g"] = c(inp["mlp_ln_g"][l][None, :])
    p["mlp_ln_b"] = c(inp["mlp_ln_b"][l][None, :])
    p["mlp_wsT"] = c(inp["mlp_w_s"][l].transpose(2, 0, 1))
    p["mlp_bsT"] = c(inp["mlp_b_s"][l].T)
    cols = np.concatenate([np.arange(256 * q, 256 * (q + 1)), 1024 + np.arange(128 * q, 128 * (q + 1)),
                           1536 + np.arange(128 * q, 128 * (q + 1))])
    cw = inp["ssm_conv_w"][l][:, cols]
    cb = inp["ssm_conv_b"][l][cols]
    hs = slice(4 * q, 4 * q + 4)
    p["ssm_cw"] = c(cw.reshape(4, 4, 128).transpose(2, 1, 0))
    p["ssm_cb"] = c(cb.reshape(4, 128).T)
    p["ssm_dtb_t"] = c(np.tile(inp["ssm_dt_bias"][l][hs], 32)[None, :])
    p["ssm_alog_t"] = c(np.tile(inp["ssm_A_log"][l][hs], 32)[None, :])
    p["ssm_D"] = c(inp["ssm_D"][l][hs][None, :])
    p["ssm_ng"] = c(inp["ssm_norm_g"][l][256 * q:256 * (q + 1)][None, :])
    mu = inp["rwkv_mu"][l]
    hc = np.arange(256 * q, 256 * (q + 1))
    p["rw_mu_tm"] = c(np.concatenate([mu[1024 * k + hc] for k in range(4)])[None, :])
    p["rw_mu_fm"] = c(mu[4096:4224][:, None])
    p["rw_w2a2"] = c(np.concatenate([inp["rwkv_w2"][l][:, hc], inp["rwkv_a2"][l][:, hc]], axis=0))
    for nm, src in (("rw_w0", "rwkv_w0"), ("rw_a0", "rwkv_a0"), ("rw_kk", "rwkv_k_k"), ("rw_ka", "rwkv_k_a"), ("rw_rk", "rwkv_r_k"),
                    ("rw_gng", "rwkv_gn_g"), ("rw_gnb", "rwkv_gn_b")):
        p[nm] = c(inp[src][l].reshape(-1)[hc][None, :])
    return p


def _decl(nc, arrs):
    out = {}
    for n, a in arrs.items():
        dt = BF16 if a.dtype == ml_dtypes.bfloat16 else F32
        out[n] = nc.dram_tensor(n, list(a.shape), dt, kind="ExternalInput").ap()
    return out


def build_AB(sample_inputs):
    nc = bass.Bass("TRN2", target_bir_lowering=False)
    d = _decl(nc, sample_inputs)
    wts = {n: d["w_" + n] for n in GROUP_INFO}
    scr = make_scratch(nc)
    maskT = nc.dram_tensor("maskT", [128, NPAIR, 128], BF16).ap()
    y_own = nc.dram_tensor("y_own", [OWN, 2048], F32, kind="ExternalOutput").ap()
    y_all = nc.dram_tensor("y_all", [SEQ, 512], F32, kind="ExternalOutput").ap()
    phase_inproj(nc, d["x_all"], d["x_own"], d["g"], d["ident"], wts, scr)
    phase_indexer(nc, scr, d, maskT)
    phase_attn(nc, scr, d, maskT, y_own)
    phase_mlp(nc, scr, d, y_own)
    phase_ssm(nc, scr, d, y_all)
    phase_rwkv(nc, scr, d, y_all)
    return nc


def build_C(final):
    nc = bass.Bass("TRN2", target_bir_lowering=False)
    y_tok = nc.dram_tensor("y_tok", [OWN, D_MODEL], F32, kind="ExternalInput").ap()
    x_res = nc.dram_tensor("x_res", [OWN, D_MODEL], F32, kind="ExternalInput").ap()
    w_out = nc.dram_tensor("w_out", [D_MODEL, D_MODEL], F32, kind="ExternalInput").ap()
    ident = nc.dram_tensor("ident", [128, 128], BF16, kind="ExternalInput").ap()
    if final:
        g = nc.dram_tensor("gf", [1, D_MODEL], F32, kind="ExternalInput").ap()
        x_mid = nc.dram_tensor("x_mid", [OWN, D_MODEL], F32).ap()
        out = nc.dram_tensor("out", [OWN, D_MODEL], F32, kind="ExternalOutput").ap()
        phase_outproj(nc, y_tok, x_res, w_out, ident, x_mid)
        phase_finalnorm(nc, x_mid, g, out)
    else:
        out = nc.dram_tensor("out", [OWN, D_MODEL], F32, kind="ExternalOutput").ap()
        phase_outproj(nc, y_tok, x_res, w_out, ident, out)
    return nc


def kernel_unfused(**inp):
    inp = {k: np.asarray(v) for k, v in inp.items()}
    x = inp["x"]
    cst = const_inputs()
    otk = [own_tok(q) for q in range(NQ)]
    nc_ab = None
    for l in range(2):
        w_in = inp["w_in"][l]
        in_maps = []
        for c in range(NCORE):
            b, q = c // NQ, c % NQ
            m = dict(cst)
            m.update(layer_params(inp, l, q))
            m["x_all"] = np.ascontiguousarray(x[b])
            m["x_own"] = np.ascontiguousarray(x[b][otk[q]])
            for name, cols in col_groups(q).items():
                m["w_" + name] = np.ascontiguousarray(w_in[:, cols])
            in_maps.append(m)
        if nc_ab is None:
            nc_ab = build_AB(in_maps[0])
        res = run_bass_kernel_spmd(nc_ab, in_maps, core_ids=list(range(NCORE))).results
        in_maps_c = []
        for c in range(NCORE):
            b, q = c // NQ, c % NQ
            y_tok = np.empty((OWN, D_MODEL), np.float32)
            y_tok[:, 0:1024] = res[c]["y_own"][:, 0:1024]
            y_tok[:, 3072:4096] = res[c]["y_own"][:, 1024:2048]
            for q2 in range(NQ):
                ya = res[b * NQ + q2]["y_all"][otk[q]]
                y_tok[:, 1024 + 256 * q2:1024 + 256 * (q2 + 1)] = ya[:, 0:256]
                y_tok[:, 2048 + 256 * q2:2048 + 256 * (q2 + 1)] = ya[:, 256:512]
            m = {"y_tok": y_tok, "x_res": np.ascontiguousarray(x[b][otk[q]]), "w_out": np.ascontiguousarray(inp["w_out"][l]),
                 "ident": cst["ident"]}
            if l == 1:
                m["gf"] = np.ascontiguousarray(inp["final_norm_g"][None, :])
            in_maps_c.append(m)
        nc_c = build_C(final=(l == 1))
        resc = run_bass_kernel_spmd(nc_c, in_maps_c, core_ids=list(range(NCORE))).results
        xn = np.empty_like(x)
        for c in range(NCORE):
            b, q = c // NQ, c % NQ
            xn[b][otk[q]] = resc[c]["out"]
        x = xn
    return x


def fused_ginfo():
    gi = {"fmb_all": (1152, "fm", BF16, "all"), "tmb_all": (1024, "tm", BF16, "all"),
          "fmb_own": (2048, "fm", BF16, "all"), "tmf_own": (4112, "tm", F32, "all")}
    for q in range(NQ):
        gi[f"fmf_all{q}"] = (640, "fm", F32, "all")
        gi[f"tmf_all{q}"] = (1284, "tm", F32, "all")
    return gi


FUSED_BLOCKS = [(qb // 4 + 1, qb % 4) for qb in range(32)]
Q_KEYS = ("ssm_cw", "ssm_cb", "ssm_dtb_t", "ssm_alog_t", "ssm_D", "ssm_ng", "rw_mu_tm", "rw_mu_fm", "rw_w2a2", "rw_w0", "rw_a0",
          "rw_kk", "rw_ka", "rw_rk", "rw_gng", "rw_gnb")
L_KEYS = ("g", "mlp_ln_g", "mlp_ln_b", "mlp_wsT", "mlp_bsT")


def fused_inputs(inp, b):
    c = np.ascontiguousarray
    m = dict(const_inputs())
    m["qrel"] = c((np.arange(4, dtype=np.float32)[None, :] * 128 + np.arange(128, dtype=np.float32)[:, None]))
    m["x"] = c(inp["x"][b])
    m["gf"] = c(inp["final_norm_g"][None, :])
    for l in range(2):
        w_in = inp["w_in"][l]
        cg0 = col_groups(0)
        for name in ("fmb_all", "tmb_all", "fmb_own", "tmf_own"):
            m[f"w{l}_{name}"] = c(w_in[:, cg0[name]])
        for q in range(NQ):
            cg = col_groups(q)
            m[f"w{l}_fmf_all{q}"] = c(w_in[:, cg["fmf_all"]])
            m[f"w{l}_tmf_all{q}"] = c(w_in[:, cg["tmf_all"]])
            lp = layer_params(inp, l, q)
            for key in Q_KEYS:
                m[f"l{l}q{q}_{key}"] = lp[key]
            if q == 0:
                for key in L_KEYS:
                    m[f"l{l}_{key}"] = lp[key]
        m[f"wout{l}"] = c(inp["w_out"][l])
    return m


def build_fused(sample, upto=None, nlayers=2, skip=()):
    nc = bass.Bass("TRN2", target_bir_lowering=False)
    d = _decl(nc, sample)
    gi = fused_ginfo()
    scr = {}
    for name, (ncols, layout, dt, _) in gi.items():
        shape = [SEQ, ncols] if layout == "tm" else [ncols, SEQ]
        scr[name] = nc.dram_tensor("scr_" + name, shape, dt).ap()
    _, npairs = pair_offsets(FUSED_BLOCKS)
    maskT = nc.dram_tensor("maskT", [128, npairs, 128], BF16).ap()
    y_full = nc.dram_tensor("y_full", [SEQ, D_MODEL], F32).ap()
    xs = [d["x"], nc.dram_tensor("x1", [SEQ, D_MODEL], F32).ap(), nc.dram_tensor("x2", [SEQ, D_MODEL], F32).ap()]
    out = nc.dram_tensor("out", [SEQ, D_MODEL], F32, kind="ExternalOutput").ap()
    gnames = list(gi.keys())
    cnt = [0]

    def go():
        cnt[0] += 1
        return (upto is None or cnt[0] <= upto) and cnt[0] not in skip

    for l in range(nlayers):
        x_cur, x_nxt = xs[l], xs[l + 1]
        wts = {name: d[f"w{l}_{name}"] for name in gnames}
        plan = [(x_cur, p * 1024, [(n, p * 1024) for n in gnames]) for p in range(4)]
        if go():
            phase_inproj(nc, None, None, d[f"l{l}_g"], d["ident"], wts, scr, plan=plan, ginfo=gi)
        prm_l = dict(d)
        for key in L_KEYS:
            prm_l[key] = d[f"l{l}_{key}"]
        sc_own = {"fmb_own": scr["fmb_own"], "fmb_all": scr["fmb_all"], "tmf_own": scr["tmf_own"], "tmb_all": scr["tmb_all"]}
        if go():
            phase_indexer(nc, sc_own, prm_l, maskT, blocks=FUSED_BLOCKS)
        if go():
            phase_attn(nc, sc_own, prm_l, maskT, y_full, blocks=FUSED_BLOCKS)
        if go():
            phase_mlp(nc, sc_own, prm_l, y_full, nchunks=SEQ // 128, ycol0=3072)
        for q in range(NQ):
            prm_q = dict(d)
            for key in Q_KEYS:
                prm_q[key] = d[f"l{l}q{q}_{key}"]
            sc_q = {"fmf_all": scr[f"fmf_all{q}"], "tmf_all": scr[f"tmf_all{q}"]}
            if go():
                phase_ssm(nc, sc_q, prm_q, None, y_dst=y_full[:, 1024 + 256 * q:1024 + 256 * (q + 1)])
            if go():
                phase_rwkv(nc, sc_q, prm_q, None, y_dst=y_full[:, 2048 + 256 * q:2048 + 256 * (q + 1)])
        for p in range(4):
            rs_ = slice(p * 1024, (p + 1) * 1024)
            if go():
                phase_outproj(nc, y_full[rs_, :], x_cur[rs_, :], d[f"wout{l}"], d["ident"], x_nxt[rs_, :])
    if upto is None:
        phase_finalnorm(nc, xs[nlayers], d["gf"], out, ntiles=SEQ // 128)
    else:
        phase_finalnorm(nc, y_full, d["gf"], out, ntiles=SEQ // 128)
    return nc


def kernel_fused(**inp):
    inp = {k: np.asarray(v) for k, v in inp.items()}
    nb = inp["x"].shape[0]
    in_maps = [fused_inputs(inp, b) for b in range(nb)]
    nc = build_fused(in_maps[0])
    res = run_bass_kernel_spmd(nc, in_maps, core_ids=list(range(nb))).results
    return np.stack([res[b]["out"] for b in range(nb)], axis=0)


FUSED = True


def kernel(**inp):
    return kernel_fused(**inp) if FUSED else kernel_unfused(**inp)
```

```python
import contextlib
import numpy as np
import ml_dtypes
import concourse.bass as bass
import concourse.mybir as mybir
from concourse.bass_utils import run_bass_kernel_spmd

F32 = mybir.dt.float32
BF16 = mybir.dt.bfloat16
ALU = mybir.AluOpType
AF = mybir.ActivationFunctionType
AX = mybir.AxisListType

D_MODEL = 4096
SEQ = 4096
NCORE = 8
NQ = 4
OWN = SEQ // NQ
NORM_EPS = 1e-5
GN_EPS = 64e-5

COMPUTE = ("pe", "act", "dve", "pool")


SEM_ROLL = 30000


class SemState:
    def __init__(self, nc):
        self.nc = nc
        self.st = contextlib.ExitStack()
        self.sems = {}
        self.count = {e: 0 for e in COMPUTE}
        self.dma_slots = {}
        self.dma_rr = {}
        self.n_dma_slots = 8

    def handle(self, key):
        h = self.sems.get(key)
        if h is None:
            h = self.st.enter_context(self.nc.semaphore("s_" + "_".join(str(x) for x in key)))
            self.sems[key] = h
        return h


def semstate(nc):
    ss = getattr(nc, "_semstate", None)
    if ss is None:
        ss = SemState(nc)
        nc._semstate = ss
    return ss


class Prog:
    def __init__(self, nc):
        self.nc = nc
        self.ss = semstate(nc)
        self.streams = {e: [] for e in ("pe", "act", "dve", "pool", "sp")}
        self.last_writer = {}
        self.readers = {}
        self.waited = {}
        self.used_keys = []
        self.excl = set()

    def _semkey(self, key):
        if key not in self.used_keys:
            self.used_keys.append(key)
        return key

    def _deps_for(self, reads, writes, eng=None):
        deps = set()
        for b in reads:
            w = self.last_writer.get(b)
            if w is not None:
                deps.add(w)
        for b in writes:
            w = self.last_writer.get(b)
            if w is not None:
                deps.add(w)
            for r in self.readers.get(b, ()):
                if eng is not None and r[0][0] == eng:
                    continue
                deps.add(r)
        if eng == "pe":
            deps = {d for d in deps if d[0][0] != "pe"}
        return deps

    def _commit(self, tok, reads, writes):
        for b in reads:
            self.readers.setdefault(b, []).append(tok)
        for b in writes:
            self.last_writer[b] = tok
            self.readers[b] = []

    def _waits(self, eng, deps):
        waits = []
        for (k, v) in sorted(deps, key=lambda t: (str(t[0]), t[1])):
            if self.waited.get((eng, k), -1) >= v:
                continue
            self.waited[(eng, k)] = v
            self._semkey(k)
            waits.append((k, v))
        return waits

    def op(self, eng, fn, reads=(), writes=()):
        ex = [b for b in reads if (b[0] if isinstance(b, tuple) else b) in self.excl]
        if ex:
            writes = list(writes) + [b for b in ex if b not in writes]
        ss = self.ss
        idx = ss.count[eng]
        ss.count[eng] += 1
        ep = idx // SEM_ROLL
        key = self._semkey((eng, ep))
        deps = self._deps_for(reads, writes, eng=eng)
        waits = self._waits(eng, deps)
        self.streams[eng].append((waits, fn, (key, 1)))
        tok = (key, idx - ep * SEM_ROLL + 1)
        self._commit(tok, reads, writes)
        return tok

    def dma(self, eng, fn, reads=(), writes=()):
        ss = self.ss
        slots = ss.dma_slots.get(eng)
        if slots is None:
            slots = [[("dma", eng, i, 0), 0] for i in range(ss.n_dma_slots)]
            ss.dma_slots[eng] = slots
        rr = ss.dma_rr.get(eng, 0)
        ss.dma_rr[eng] = (rr + 1) % len(slots)
        slot = slots[rr]
        deps = self._deps_for(reads, writes)
        if slot[1] > 0:
            deps.add((slot[0], slot[1]))
        if slot[1] + 16 > SEM_ROLL:
            slot[0] = ("dma", eng, slot[0][2], slot[0][3] + 1)
            slot[1] = 0
        key = self._semkey(slot[0])
        waits = self._waits(eng, deps)
        slot[1] += 16
        self.streams[eng].append((waits, fn, (key, 16)))
        tok = (key, slot[1])
        self._commit(tok, reads, writes)
        return tok

    def finish(self, eng="sp"):
        toks = set()
        ss = self.ss
        for q, slots in ss.dma_slots.items():
            for key, cnt in slots:
                if cnt > 0:
                    toks.add((key, cnt))
        for e in COMPUTE:
            n = ss.count[e]
            if n > 0:
                ep = (n - 1) // SEM_ROLL
                toks.add(((e, ep), n - ep * SEM_ROLL))
        waits = self._waits(eng, toks)
        self.streams[eng].append((waits, None, None))

    def emit(self):
        nc = self.nc
        ss = self.ss
        sems = {k: ss.handle(k) for k in self.used_keys}
        with nc.Block() as block:

            def run(engobj, items):
                for waits, fn, inc in items:
                    for (k, v) in waits:
                        engobj.wait_ge(sems[k], v)
                    if fn is not None:
                        ins = fn(engobj)
                        ins.then_inc(sems[inc[0]], inc[1])

            @block.tensor
            def _(e):
                run(e, self.streams["pe"])

            @block.scalar
            def _(e):
                run(e, self.streams["act"])

            @block.vector
            def _(e):
                run(e, self.streams["dve"])

            @block.gpsimd
            def _(e):
                run(e, self.streams["pool"])

            @block.sync
            def _(e):
                run(e, self.streams["sp"])


class Phase:
    _count = [0]

    def __init__(self, nc):
        self.nc = nc
        self.st = contextlib.ExitStack()
        self.P = Prog(nc)
        self.n = 0
        Phase._count[0] += 1
        self.pid = Phase._count[0]

    def sb(self, shape, dt, name=None):
        self.n += 1
        return self.st.enter_context(self.nc.sbuf_tensor(f"{name or 't'}_{self.n}_{self.pid}", list(shape), dt))

    def ps(self, shape, dt, name=None):
        self.n += 1
        return self.st.enter_context(self.nc.psum_tensor(f"{name or 'p'}_{self.n}_{self.pid}", list(shape), dt))

    def dump(self, name, sb_ap, reads):
        dbg = getattr(self.nc, "_dbg", None)
        if not dbg or name not in dbg:
            return
        d = dbg[name]
        self.P.dma("sp", lambda e: e.dma_start(out=d, in_=sb_ap), reads=reads, writes=[("dbg", name)])

    def close(self):
        self.P.finish()
        self.P.emit()
        self.st.close()


ATT0, SSM0, RWKV0, MLP0 = 0, 5200, 8288, 12512


def col_groups(q):
    r = lambda a, n: np.arange(a, a + n)
    att_q, att_k, att_v, att_z = r(0, 1024), r(1024, 1024), r(2048, 1024), r(3072, 1024)
    att_qi, att_ki, att_wi = r(4096, 1024), r(5120, 64), r(5184, 16)
    ssm_z = r(SSM0 + 256 * q, 256)
    ssm_x = r(SSM0 + 1024 + 256 * q, 256)
    ssm_B = r(SSM0 + 2048 + 128 * q, 128)
    ssm_C = r(SSM0 + 2560 + 128 * q, 128)
    ssm_dt = r(SSM0 + 3072 + 4 * q, 4)
    rw = [r(RWKV0 + 1024 * i + 256 * q, 256) for i in range(4)]
    rw_wl, rw_al = r(RWKV0 + 4096, 64), r(RWKV0 + 4160, 64)
    mlp = [r(MLP0 + 1024 * i, 1024) for i in range(3)]
    cat = np.concatenate
    return {
        "fmb_all": cat([att_k, att_ki, att_ki]),
        "fmf_all": cat([ssm_x, ssm_B, ssm_C, rw_wl, rw_al]),
        "tmb_all": att_v,
        "tmf_all": cat([ssm_z, ssm_dt] + rw),
        "fmb_own": cat([att_q, att_qi]),
        "tmf_own": cat([att_z, att_wi] + mlp),
    }


GROUP_INFO = {
    "fmb_all": (1152, "fm", BF16, "all"),
    "fmf_all": (640, "fm", F32, "all"),
    "tmb_all": (1024, "tm", BF16, "all"),
    "tmf_all": (1284, "tm", F32, "all"),
    "fmb_own": (2048, "fm", BF16, "own"),
    "tmf_own": (4112, "tm", F32, "own"),
}


def phase_inproj(nc, x_all, x_own, g, ident_d, wts, scr, plan=None, ginfo=None):
    ph = Phase(nc)
    P = ph.P
    D = D_MODEL
    KT = D // 128
    T = 1024
    TT = T // 128
    KQ = 8
    NKQ = KT // KQ
    NB = 512
    ident = ph.sb([128, 128], BF16, "ident")
    hT = ph.sb([128, KT, T], BF16, "hT")
    xt = [ph.sb([128, D], F32, "xt") for _ in range(2)]
    hb = [ph.sb([128, D], BF16, "hb") for _ in range(2)]
    wb = [ph.sb([128, KT, NB], BF16, "wb") for _ in range(2)]
    stg = [ph.sb([128, NB], F32, "stg") for _ in range(4)]
    stgb = [ph.sb([128, NB], BF16, "stgb") for _ in range(4)]
    gb = ph.sb([128, D], F32, "gb")
    ss = ph.sb([128, 8 * 5], F32, "ss")
    rstd = ph.sb([128, 8 * 5], F32, "rstd")
    pT = [ph.ps([128, 1024], BF16, "pT") for _ in range(2)]
    pM = [ph.ps([128, 512], F32, "pM") for _ in range(6)]

    P.dma("sp", lambda e: e.dma_start(out=ident[:], in_=ident_d[:, :]), writes=["ident"])
    P.dma("sp", lambda e: e.dma_start(out=gb[:], in_=g[0:1, :].partition_broadcast(128)), writes=["gb"])
    cnt = {"t": 0, "m": 0, "w": 0, "x": 0}
    P.op("dve", lambda e: e.memset(ss[:], 0.0), writes=[("ss", i) for i in range(40)])

    def load_hT(x_ap, row0, pidx):
        for tt in range(TT):
            i = cnt["x"] % 2
            cnt["x"] += 1
            xb, hbb = xt[i], hb[i]
            sc = pidx * 8 + tt
            P.dma("sp", lambda e, xb=xb, tt=tt: e.dma_start(out=xb[:], in_=x_ap[row0 + tt * 128:row0 + (tt + 1) * 128, :]),
                  writes=[("xt", i)])
            P.op("act", lambda e, xb=xb, hbb=hbb, sc=sc: e.activation(out=hbb[:], in_=xb[:], func=AF.Square,
                                                                     accum_out=ss[:, sc:sc + 1]),
                 reads=[("xt", i)], writes=[("hb", i), ("ss", sc)])
            P.op("dve", lambda e, sc=sc: e.tensor_scalar(out=rstd[:, sc:sc + 1], in0=ss[:, sc:sc + 1],
                                                         scalar1=1.0 / D, scalar2=NORM_EPS, op0=ALU.mult, op1=ALU.add),
                 reads=[("ss", sc)], writes=[("rstd", sc)])
            P.op("act", lambda e, sc=sc: e.activation(out=rstd[:, sc:sc + 1], in_=rstd[:, sc:sc + 1], func=AF.Sqrt),
                 reads=[("rstd", sc)], writes=[("rstd", sc)])
            P.op("dve", lambda e, sc=sc: e.reciprocal(out=rstd[:, sc:sc + 1], in_=rstd[:, sc:sc + 1]),
                 reads=[("rstd", sc)], writes=[("rstd", sc)])
            P.op("dve", lambda e, xb=xb, hbb=hbb, sc=sc: e.scalar_tensor_tensor(
                out=hbb[:], in0=xb[:], scalar=rstd[:, sc:sc + 1], in1=gb[:], op0=ALU.mult, op1=ALU.mult),
                reads=[("xt", i), ("rstd", sc), "gb", ("hb", i)], writes=[("hb", i)])
            for kq in range(NKQ):
                pb = cnt["t"] % 2
                cnt["t"] += 1
                for k8 in range(KQ):
                    kt = kq * KQ + k8
                    P.op("pe", lambda e, pb=pb, k8=k8, kt=kt, hbb=hbb: e.transpose(
                        out=pT[pb][:, k8 * 128:(k8 + 1) * 128], in_=hbb[:, kt * 128:(kt + 1) * 128], identity=ident[:]),
                        reads=[("hb", i), "ident"], writes=[("pT", pb)])
                dst = hT[:, kq * KQ:(kq + 1) * KQ, tt * 128:(tt + 1) * 128]
                src = pT[pb][:].rearrange("p (k t) -> p k t", k=KQ)
                if kq % 2 == 0:
                    P.op("dve", lambda e, dst=dst, src=src: e.tensor_copy(out=dst, in_=src),
                         reads=[("pT", pb)], writes=[("hT", tt, kq)])
                else:
                    P.op("act", lambda e, dst=dst, src=src: e.copy(out=dst, in_=src),
                         reads=[("pT", pb)], writes=[("hT", tt, kq)])

    def evac_store(j, n_part, nfree, is_bf16, dst_ap, okey):
        s = cnt["m"] % 4
        use_dve = (cnt["m"] % 2 == 0)
        cnt["m"] += 1
        sbuf = (stgb if is_bf16 else stg)[s]
        skey = ("stgb" if is_bf16 else "stg", s)
        if use_dve:
            P.op("dve", lambda e: e.tensor_copy(out=sbuf[0:n_part, 0:nfree], in_=pM[j][0:n_part, 0:nfree]),
                 reads=[("pM", j)], writes=[skey])
        else:
            P.op("act", lambda e: e.copy(out=sbuf[0:n_part, 0:nfree], in_=pM[j][0:n_part, 0:nfree]),
                 reads=[("pM", j)], writes=[skey])
        P.dma("sp", lambda e: e.dma_start(out=dst_ap, in_=sbuf[0:n_part, 0:nfree]), reads=[skey], writes=[okey])

    def do_group(name, tok0):
        ncols, layout, dt, _ = (ginfo or GROUP_INFO)[name]
        w = wts[name]
        dst = scr[name]
        wv = w.rearrange("(kt p) n -> p kt n", p=128)
        is_bf = (dt == BF16)
        for c0 in range(0, ncols, NB):
            nb = min(NB, ncols - c0)
            wi = cnt["w"] % 2
            cnt["w"] += 1
            wbb = wb[wi]
            for kq in range(NKQ):
                P.dma("pool", lambda e, wbb=wbb, kq=kq, c0=c0, nb=nb: e.dma_start(
                    out=wbb[:, kq * KQ:(kq + 1) * KQ, 0:nb], in_=wv[:, kq * KQ:(kq + 1) * KQ, c0:c0 + nb]),
                    writes=[("wb", wi, kq)])
            if layout == "tm":
                for tt in range(TT):
                    j = cnt["m"] % 6
                    for kt in range(KT):
                        P.op("pe", lambda e, j=j, kt=kt, tt=tt, wbb=wbb, nb=nb: e.matmul(
                            pM[j][:, 0:nb], lhsT=hT[:, kt, tt * 128:(tt + 1) * 128], rhs=wbb[:, kt, 0:nb],
                            start=(kt == 0), stop=(kt == KT - 1)),
                            reads=[("hT", tt, kt // KQ), ("wb", wi, kt // KQ)], writes=[("pM", j)])
                    r0 = tok0 + tt * 128
                    evac_store(j, 128, nb, is_bf, dst[r0:r0 + 128, c0:c0 + nb], (name, "o", tok0, tt, c0))
            else:
                for ct in range(nb // 128):
                    for th in range(T // 512):
                        j = cnt["m"] % 6
                        for kt in range(KT):
                            P.op("pe", lambda e, j=j, kt=kt, th=th, ct=ct, wbb=wbb: e.matmul(
                                pM[j][:, 0:512], lhsT=wbb[:, kt, ct * 128:(ct + 1) * 128],
                                rhs=hT[:, kt, th * 512:(th + 1) * 512], start=(kt == 0), stop=(kt == KT - 1)),
                                reads=[("hT", 4 * th, kt // KQ), ("hT", 4 * th + 1, kt // KQ), ("hT", 4 * th + 2, kt // KQ),
                                       ("hT", 4 * th + 3, kt // KQ), ("wb", wi, kt // KQ)], writes=[("pM", j)])
                        cc = c0 + ct * 128
                        t0 = tok0 + th * 512
                        evac_store(j, 128, 512, is_bf, dst[cc:cc + 128, t0:t0 + 512], (name, "o", tok0, th, cc))

    if plan is None:
        plan = [(x_all, p * 1024, [(n, p * 1024) for n in ("fmb_all", "fmf_all", "tmb_all", "tmf_all")]) for p in range(4)]
        plan.append((x_own, 0, [("fmb_own", 0), ("tmf_own", 0)]))
    for pidx, (xsrc, row0, glist) in enumerate(plan):
        load_hT(xsrc, row0, pidx)
        for name, tok0 in glist:
            do_group(name, tok0)
    ph.close()


def make_scratch(nc, kind=None):
    scr = {}
    for name, (ncols, layout, dt, which) in GROUP_INFO.items():
        ntok = SEQ if which == "all" else OWN
        shape = [ntok, ncols] if layout == "tm" else [ncols, ntok]
        if kind:
            scr[name] = nc.dram_tensor("scr_" + name, shape, dt, kind=kind).ap()
        else:
            scr[name] = nc.dram_tensor("scr_" + name, shape, dt).ap()
    return scr


TMF_OWN_ATTZ, TMF_OWN_WI, TMF_OWN_U, TMF_OWN_V, TMF_OWN_Z = 0, 1024, 1040, 2064, 3088


def phase_mlp(nc, scr, prm, y_own, nchunks=OWN // 128, ycol0=1024):
    ph = Phase(nc)
    P = ph.P
    src = scr["tmf_own"]
    W = 1024
    gb = ph.sb([128, W], F32, "lng")
    bb = ph.sb([128, W], F32, "lnb")
    wsT = ph.sb([128, 8, 128], F32, "wsT")
    wcT = ph.sb([128, 8, 128], BF16, "wcT")
    tri = ph.sb([128, 128], F32, "tri")
    bsT = ph.sb([128, 8], F32, "bsT")
    bbc = ph.sb([128, 8, 128], F32, "bbc")
    zero = ph.sb([128, 128], F32, "zero")
    NBUF = 2
    ut = [ph.sb([128, W], F32, "u") for _ in range(NBUF)]
    vt = [ph.sb([128, W], F32, "v") for _ in range(NBUF)]
    zt = [ph.sb([128, W], F32, "z") for _ in range(NBUF)]
    vn = [ph.sb([128, W], F32, "vn") for _ in range(NBUF)]
    vnb = [ph.sb([128, W], BF16, "vnb") for _ in range(NBUF)]
    junk = ph.sb([128, W], BF16, "junk")
    t1 = [ph.sb([128, W], F32, "t1") for _ in range(NBUF)]
    st = ph.sb([128, 8 * nchunks], F32, "stats")
    pV = [ph.ps([128, 512], F32, "pV") for _ in range(4)]

    P.dma("sp", lambda e: e.dma_start(out=gb[:], in_=prm["mlp_ln_g"][0:1, :].partition_broadcast(128)), writes=["gb"])
    P.dma("sp", lambda e: e.dma_start(out=bb[:], in_=prm["mlp_ln_b"][0:1, :].partition_broadcast(128)), writes=["bb"])
    P.dma("sp", lambda e: e.dma_start(out=wsT[:], in_=prm["mlp_wsT"][:, :, :]), writes=["wsT"])
    P.dma("sp", lambda e: e.dma_start(out=tri[:], in_=prm["tri_le"][:, :]), writes=["tri"])
    P.dma("sp", lambda e: e.dma_start(out=bsT[:], in_=prm["mlp_bsT"][:, :]), writes=["bsT"])
    P.op("dve", lambda e: e.memset(zero[:], 0.0), writes=["zero"])
    P.op("dve", lambda e: e.memset(st[:], 0.0), writes=[("st", c) for c in range(nchunks)])
    for g in range(8):
        P.op("dve", lambda e, g=g: e.tensor_tensor(out=wcT[:, g, :], in0=wsT[:, g, :], in1=tri[:], op=ALU.mult),
             reads=["wsT", "tri"], writes=["wcT"])
        P.op("dve", lambda e, g=g: e.tensor_scalar(out=bbc[:, g, :], in0=zero[:], scalar1=bsT[:, g:g + 1], scalar2=None,
                                                   op0=ALU.add), reads=["zero", "bsT"], writes=["bbc"])
    for c in range(nchunks):
        i = c % NBUF
        r0 = c * 128
        P.dma("sp", lambda e, i=i, r0=r0: e.dma_start(out=ut[i][:], in_=src[r0:r0 + 128, TMF_OWN_U:TMF_OWN_U + W]),
              writes=[("u", i)])
        P.dma("sp", lambda e, i=i, r0=r0: e.dma_start(out=vt[i][:], in_=src[r0:r0 + 128, TMF_OWN_V:TMF_OWN_V + W]),
              writes=[("v", i)])
        P.dma("sp", lambda e, i=i, r0=r0: e.dma_start(out=zt[i][:], in_=src[r0:r0 + 128, TMF_OWN_Z:TMF_OWN_Z + W]),
              writes=[("z", i)])
        s0 = c * 8
        P.op("act", lambda e, i=i, s0=s0: e.activation(out=junk[:], in_=vt[i][:], func=AF.Square,
                                                       accum_out=st[:, s0 + 1:s0 + 2]),
             reads=[("v", i)], writes=["junk", ("st", c)])
        P.op("dve", lambda e, i=i, s0=s0: e.reduce_sum(out=st[:, s0:s0 + 1], in_=vt[i][:], axis=AX.X),
             reads=[("v", i), ("st", c)], writes=[("st", c)])
        P.op("dve", lambda e, s0=s0: e.tensor_scalar(out=st[:, s0:s0 + 2], in0=st[:, s0:s0 + 2], scalar1=1.0 / W,
                                                     scalar2=None, op0=ALU.mult), reads=[("st", c)], writes=[("st", c)])
        P.op("dve", lambda e, s0=s0: e.tensor_tensor(out=st[:, s0 + 2:s0 + 3], in0=st[:, s0:s0 + 1], in1=st[:, s0:s0 + 1],
                                                     op=ALU.mult), reads=[("st", c)], writes=[("st", c)])
        P.op("dve", lambda e, s0=s0: e.tensor_tensor(out=st[:, s0 + 3:s0 + 4], in0=st[:, s0 + 1:s0 + 2],
                                                     in1=st[:, s0 + 2:s0 + 3], op=ALU.subtract),
             reads=[("st", c)], writes=[("st", c)])
        P.op("dve", lambda e, s0=s0: e.tensor_scalar(out=st[:, s0 + 3:s0 + 4], in0=st[:, s0 + 3:s0 + 4], scalar1=NORM_EPS,
                                                     scalar2=None, op0=ALU.add), reads=[("st", c)], writes=[("st", c)])
        P.op("act", lambda e, s0=s0: e.activation(out=st[:, s0 + 3:s0 + 4], in_=st[:, s0 + 3:s0 + 4], func=AF.Sqrt),
             reads=[("st", c)], writes=[("st", c)])
        P.op("dve", lambda e, s0=s0: e.reciprocal(out=st[:, s0 + 3:s0 + 4], in_=st[:, s0 + 3:s0 + 4]),
             reads=[("st", c)], writes=[("st", c)])
        P.op("dve", lambda e, i=i, s0=s0: e.tensor_scalar(out=vn[i][:], in0=vt[i][:], scalar1=st[:, s0:s0 + 1],
                                                          scalar2=st[:, s0 + 3:s0 + 4], op0=ALU.subtract, op1=ALU.mult),
             reads=[("v", i), ("st", c)], writes=[("vn", i)])
        P.op("pool", lambda e, i=i: e.tensor_tensor(out=vn[i][:], in0=vn[i][:], in1=gb[:], op=ALU.mult),
             reads=[("vn", i), "gb"], writes=[("vn", i)])
        P.op("dve", lambda e, i=i: e.tensor_tensor(out=vnb[i][:], in0=vn[i][:], in1=bb[:], op=ALU.add),
             reads=[("vn", i), "bb"], writes=[("vnb", i)])
        if c == 0:
            ph.dump("mlp_st", st[:, 0:8], [("st", c)])
            ph.dump("mlp_vn", vn[i][:], [("vn", i)])
            ph.dump("mlp_vnb", vnb[i][:], [("vnb", i)])
        for hf in range(2):
            pj = (2 * c + hf) % 4
            for g4 in range(4):
                g = hf * 4 + g4
                P.op("pe", lambda e, pj=pj, g=g, g4=g4, i=i: e.matmul(
                    pV[pj][:, g4 * 128:(g4 + 1) * 128], lhsT=wcT[:, g, :], rhs=vnb[i][:, g * 128:(g + 1) * 128],
                    start=True, stop=True), reads=["wcT", ("vnb", i)], writes=[("pV", pj)])
            P.op("dve", lambda e, pj=pj, hf=hf, i=i: e.tensor_tensor(
                out=t1[i][:, hf * 512:(hf + 1) * 512], in0=pV[pj][:],
                in1=bbc[:, hf * 4:(hf + 1) * 4, :].rearrange("p g d -> p (g d)"), op=ALU.add),
                reads=[("pV", pj), "bbc"], writes=[("t1", i, hf)])
        if c == 0:
            ph.dump("mlp_t1", t1[i][:], [("t1", i, 0), ("t1", i, 1)])
        P.op("pool", lambda e, i=i: e.tensor_tensor(out=t1[i][:], in0=t1[i][:], in1=ut[i][:], op=ALU.mult),
             reads=[("t1", i, 0), ("t1", i, 1), ("u", i)], writes=[("t1", i, 0), ("t1", i, 1)])
        P.op("act", lambda e, i=i: e.activation(out=zt[i][:], in_=zt[i][:], func=AF.Silu),
             reads=[("z", i)], writes=[("z", i)])
        P.op("dve", lambda e, i=i: e.tensor_tensor(out=t1[i][:], in0=t1[i][:], in1=zt[i][:], op=ALU.mult),
             reads=[("t1", i, 0), ("t1", i, 1), ("z", i)], writes=[("t1", i, 0), ("t1", i, 1)])
        P.dma("sp", lambda e, i=i, r0=r0: e.dma_start(out=y_own[r0:r0 + 128, ycol0:ycol0 + 1024], in_=t1[i][:]),
              reads=[("t1", i, 0), ("t1", i, 1)], writes=[("y_mlp", c)])
    ph.close()


def record_ops(P, rec):
    real_op, real_dma = P.op, P.dma
    P.op = lambda eng, fn, reads=(), writes=(): rec.append((real_op, eng, fn, list(reads), list(writes)))
    P.dma = lambda eng, fn, reads=(), writes=(): rec.append((real_dma, eng, fn, list(reads), list(writes)))

    def restore():
        P.op, P.dma = real_op, real_dma
    return restore


def pipeline_merge(recs, nstage):
    allp = []
    for ops, marks in recs:
        m = [0] + list(marks[:nstage - 1])
        while len(m) < nstage:
            m.append(len(ops))
        m.append(len(ops))
        allp.append([ops[m[k]:m[k + 1]] for k in range(nstage)])
    n = len(recs)
    for t in range(n + nstage - 1):
        lists = []
        for k in range(nstage):
            bi = t - k
            if 0 <= bi < n and allp[bi][k]:
                lists.append(allp[bi][k])
        merged = []
        for li, L in enumerate(lists):
            for pi, item in enumerate(L):
                merged.append(((pi + 0.5) / len(L), li, pi, item))
        merged.sort(key=lambda t_: (t_[0], t_[1]))
        for _, _, _, (f, eng, fn, reads, writes) in merged:
            f(eng, fn, reads=reads, writes=writes)


FMF_X, FMF_B, FMF_C, FMF_WL, FMF_AL = 0, 256, 384, 512, 576
TMF_SSMZ, TMF_DT, TMF_R, TMF_K, TMF_V, TMF_Z = 0, 256, 260, 516, 772, 1028
NEG_BIG = -30000.0


def phase_ssm(nc, scr, prm, y_all, y_dst=None):
    ph = Phase(nc)
    P = ph.P
    fm = scr["fmf_all"]
    tm = scr["tmf_all"]
    if y_dst is None:
        y_dst = y_all[:, 0:256]
    NCH = SEQ // 128
    SC = 512
    identf = ph.sb([128, 128], F32, "identf")
    identb = ph.sb([128, 128], BF16, "identb")
    tri = ph.sb([128, 128], F32, "tri")
    onesf = ph.sb([128, 128], F32, "ones")
    sel4 = ph.sb([4, 4, 128], F32, "sel4")
    negb = ph.sb([128, 128], F32, "negb")
    cw = ph.sb([128, 4, 4], F32, "cw")
    cb = ph.sb([128, 4], F32, "cb")
    dtb = ph.sb([128, 128], F32, "dtb")
    alog = ph.sb([128, 128], F32, "alog")
    Dbc = ph.sb([128, 4], F32, "Dbc")
    ngb = ph.sb([128, 256], F32, "ngb")
    dt = ph.sb([128, NCH, 4], F32, "dt")
    aa = ph.sb([128, NCH, 4], F32, "aa")
    cum = ph.sb([128, NCH * 4], F32, "cum")
    cumL = ph.sb([128, NCH * 4], F32, "cumL")
    ecum = ph.sb([128, NCH * 4], F32, "ecum")
    ncum = ph.sb([128, NCH * 4], F32, "ncum")
    ecumL = ph.sb([128, NCH * 4], F32, "ecumL")
    dtd = ph.sb([128, NCH * 4], F32, "dtd")
    win = [ph.sb([128, SC + 3], F32, "win") for _ in range(4)]
    acc = [ph.sb([128, SC], F32, "acc") for _ in range(2)]
    xsT = [ph.sb([128, 2, SC], F32, "xsT") for _ in range(2)]
    BT = [ph.sb([128, SC], BF16, "BT") for _ in range(2)]
    CT = [ph.sb([128, SC], BF16, "CT") for _ in range(2)]
    xtok = [ph.sb([128, 256], F32, "xtok") for _ in range(2)]
    xdt = [ph.sb([128, 256], BF16, "xdt") for _ in range(2)]
    xdd = [ph.sb([128, 256], BF16, "xdd") for _ in range(2)]
    Btok = [ph.sb([128, 128], BF16, "Btok") for _ in range(2)]
    cumT = [ph.sb([4, 128], F32, "cumT") for _ in range(2)]
    LT = [ph.sb([128, 4, 128], F32, "LT") for _ in range(2)]
    MT = [ph.sb([128, 4, 128], BF16, "MT") for _ in range(2)]
    ydsb = [ph.sb([128, 256], F32, "ydsb") for _ in range(2)]
    dsk = [ph.sb([128, 256], F32, "dsk") for _ in range(2)]
    yt = [ph.sb([128, 256], F32, "yt") for _ in range(2)]
    zt = [ph.sb([128, 256], F32, "zt") for _ in range(2)]
    junk = ph.sb([128, 256], F32, "junk")
    nst = ph.sb([128, NCH], F32, "nst")
    hT = ph.sb([128, 256], F32, "hT")
    hTb = ph.sb([128, 256], BF16, "hTb")
    pTr = [ph.ps([128, 512], F32, "pTr") for _ in range(1)]
    pTb = [ph.ps([128, 1024], BF16, "pTb") for _ in range(1)]
    pSm = [ph.ps([128, 512], F32, "pSm") for _ in range(1)]
    pL = [ph.ps([128, 512], F32, "pL") for _ in range(1)]
    pCB = [ph.ps([128, 512], F32, "pCB") for _ in range(1)]
    pYd = [ph.ps([128, 512], F32, "pYd") for _ in range(1)]
    pYo = [ph.ps([128, 512], F32, "pYo") for _ in range(1)]
    pS = [ph.ps([128, 512], F32, "pS") for _ in range(1)]

    ld = lambda dst, src, key: P.dma("sp", lambda e: e.dma_start(out=dst, in_=src), writes=[key])
    ld(identf[:], prm["ident_f"][:, :], "identf")
    ld(identb[:], prm["ident_b"][:, :], "identb")
    ld(tri[:], prm["tri_le"][:, :], "tri")
    ld(onesf[:], prm["ones_f"][:, :], "ones")
    ld(sel4[:], prm["sel4"][:, :, :], "sel4")
    ld(negb[:], prm["negbig_lt"][:, :], "negb")
    ld(cw[:], prm["ssm_cw"][:, :, :], "cw")
    ld(cb[:], prm["ssm_cb"][:, :], "cb")
    ld(dtb[:], prm["ssm_dtb_t"][0:1, :].partition_broadcast(128), "dtb")
    ld(alog[:], prm["ssm_alog_t"][0:1, :].partition_broadcast(128), "alog")
    ld(Dbc[:], prm["ssm_D"][0:1, :].partition_broadcast(128), "Dbc")
    ld(ngb[:], prm["ssm_ng"][0:1, :].partition_broadcast(128), "ngb")
    ld(dt[:], tm[:, TMF_DT:TMF_DT + 4].rearrange("(c l) h -> l c h", l=128), "dt")
    dtf = dt[:].rearrange("p c h -> p (c h)")
    aaf = aa[:].rearrange("p c h -> p (c h)")
    P.op("dve", lambda e: e.tensor_tensor(out=dtf, in0=dtf, in1=dtb[:], op=ALU.add), reads=["dt", "dtb"], writes=["dt"])
    P.op("act", lambda e: e.activation(out=dtf, in_=dtf, func=AF.Exp), reads=["dt"], writes=["dt"])
    P.op("act", lambda e: e.activation(out=dtf, in_=dtf, func=AF.Ln, bias=1.0, scale=1.0), reads=["dt"], writes=["dt"])
    P.op("act", lambda e: e.activation(out=alog[:], in_=alog[:], func=AF.Exp), reads=["alog"], writes=["alog"])
    P.op("dve", lambda e: e.scalar_tensor_tensor(out=aaf, in0=dtf, scalar=-1.0, in1=alog[:], op0=ALU.mult, op1=ALU.mult),
         reads=["dt", "alog"], writes=["aa"])
    P.op("pe", lambda e: e.matmul(pSm[0][:, 0:128], lhsT=tri[:], rhs=aaf, start=True, stop=True),
         reads=["tri", "aa"], writes=["pSm"])
    P.op("dve", lambda e: e.tensor_copy(out=cum[:], in_=pSm[0][:, 0:128]), reads=["pSm"], writes=["cum"])
    P.op("pe", lambda e: e.matmul(pSm[0][:, 128:256], lhsT=onesf[:], rhs=aaf, start=True, stop=True),
         reads=["ones", "aa", "cum"], writes=["pSm"])
    P.op("dve", lambda e: e.tensor_copy(out=cumL[:], in_=pSm[0][:, 128:256]), reads=["pSm"], writes=["cumL"])
    P.op("act", lambda e: e.activation(out=ecum[:], in_=cum[:], func=AF.Exp), reads=["cum"], writes=["ecum"])
    P.op("pool", lambda e: e.tensor_scalar(out=ncum[:], in0=cum[:], scalar1=-1.0, scalar2=None, op0=ALU.mult), reads=["cum"], writes=["ncum"])
    P.op("act", lambda e: e.activation(out=ecumL[:], in_=cumL[:], func=AF.Exp), reads=["cumL"], writes=["ecumL"])
    P.op("dve", lambda e: e.tensor_tensor(out=dtd[:], in0=cumL[:], in1=cum[:], op=ALU.subtract),
         reads=["cumL", "cum"], writes=["dtd"])
    P.op("act", lambda e: e.activation(out=dtd[:], in_=dtd[:], func=AF.Exp), reads=["dtd"], writes=["dtd"])
    P.op("dve", lambda e: e.tensor_tensor(out=dtd[:], in0=dtd[:], in1=dtf, op=ALU.mult), reads=["dtd", "dt"], writes=["dtd"])
    P.op("dve", lambda e: e.memset(hT[:], 0.0), writes=["hT"])
    P.op("dve", lambda e: e.memset(hTb[:], 0.0), writes=["hTb"])
    P.op("dve", lambda e: e.memset(nst[:], 0.0), writes=["nst"])

    recs = []
    rec_cur = [None]

    def new_unit():
        rec = []
        recs.append([rec, []])
        rec_cur[0] = rec
        return record_ops(P, rec)

    for s in range(SEQ // SC):
        t0 = s * SC
        si = s % 2
        restore = new_unit()
        for j in range(4):
            wj = win[j]
            if s == 0:
                P.op("dve", lambda e, wj=wj: e.memset(wj[:, 0:3], 0.0), writes=[("win", j)])
                P.dma("sp", lambda e, wj=wj, j=j: e.dma_start(out=wj[:, 3:SC + 3], in_=fm[j * 128:(j + 1) * 128, 0:SC]),
                      writes=[("win", j)])
            else:
                P.dma("sp", lambda e, wj=wj, j=j, t0=t0: e.dma_start(out=wj[:], in_=fm[j * 128:(j + 1) * 128, t0 - 3:t0 + SC]),
                      writes=[("win", j)])
            ac = acc[j % 2]
            ak = ("acc", j % 2)
            P.op("dve", lambda e, ac=ac, wj=wj, j=j: e.tensor_scalar(out=ac[:], in0=wj[:, 0:SC], scalar1=cw[:, j, 0:1],
                                                                    scalar2=None, op0=ALU.mult),
                 reads=[("win", j), "cw"], writes=[ak])
            for tap in range(1, 4):
                P.op("dve", lambda e, ac=ac, wj=wj, j=j, tap=tap: e.scalar_tensor_tensor(
                    out=ac[:], in0=wj[:, tap:tap + SC], scalar=cw[:, j, tap:tap + 1], in1=ac[:], op0=ALU.mult, op1=ALU.add),
                    reads=[("win", j), "cw", ak], writes=[ak])
            if j < 2:
                dst, dk = xsT[si][:, j, :], ("xsT", si, j)
            elif j == 2:
                dst, dk = BT[si][:], ("BT", si)
            else:
                dst, dk = CT[si][:], ("CT", si)
            P.op("act", lambda e, ac=ac, dst=dst, j=j: e.activation(out=dst, in_=ac[:], func=AF.Silu, bias=cb[:, j:j + 1], scale=1.0),
                 reads=[ak, "cb"], writes=[dk])
        if s < 2:
            ph.dump(f"ssm_xsT{s}", xsT[si][:], [("xsT", si, 0), ("xsT", si, 1)])
            ph.dump(f"ssm_BT{s}", BT[si][:], [("BT", si)])
            ph.dump(f"ssm_CT{s}", CT[si][:], [("CT", si)])
        for cc in range(SC // 128):
            c = s * (SC // 128) + cc
            ci = c % 2
            lo = cc * 128
            c4 = c * 4
            if cc > 0:
                restore = new_unit()
            for j in range(2):
                P.op("pe", lambda e, si=si, j=j, lo=lo: e.transpose(out=pTr[0][:, j * 128:(j + 1) * 128], in_=xsT[si][:, j, lo:lo + 128],
                                                            identity=identf[:]),
                     reads=[("xsT", si, j), "identf"], writes=["pTr"])
            P.op("act", lambda e, ci=ci: e.copy(out=xtok[ci][:], in_=pTr[0][:, 0:256]), reads=["pTr"], writes=[("xtok", ci)])
            P.op("pe", lambda e, si=si, lo=lo: e.transpose(out=pTb[0][:, 0:128], in_=BT[si][:, lo:lo + 128], identity=identb[:]),
                 reads=[("BT", si), "identb"], writes=["pTb"])
            P.op("act", lambda e, ci=ci: e.copy(out=Btok[ci][:], in_=pTb[0][:, 0:128]), reads=["pTb"], writes=[("Btok", ci)])
            P.op("pe", lambda e, c=c: e.matmul(pSm[0][0:4, 256:384], lhsT=aa[:, c, :], rhs=tri[:], start=True, stop=True),
                 reads=["aa", "tri"], writes=["pSm"])
            P.op("dve", lambda e, ci=ci: e.tensor_copy(out=cumT[ci][:], in_=pSm[0][0:4, 256:384]),
                 reads=["pSm"], writes=[("cumT", ci)])
            for h in range(4):
                P.op("pe", lambda e, h=h, ci=ci: e.matmul(pL[0][:, h * 128:(h + 1) * 128], lhsT=sel4[:, h, :], rhs=cumT[ci][:],
                                                          start=True, stop=False),
                     reads=["sel4", ("cumT", ci)], writes=["pL"])
                P.op("pe", lambda e, h=h: e.matmul(pL[0][:, h * 128:(h + 1) * 128], lhsT=identf[:], rhs=negb[:],
                                                   start=False, stop=True),
                     reads=["identf", "negb"], writes=["pL"])
            for h in range(4):
                P.op("act", lambda e, h=h, ci=ci, c4=c4: e.activation(
                    out=LT[ci][:, h, :], in_=pL[0][:, h * 128:(h + 1) * 128], func=AF.Exp, bias=ncum[:, c4 + h:c4 + h + 1], scale=1.0),
                    reads=["pL", "ncum"], writes=[("LT", ci)])
            P.op("pe", lambda e, si=si, lo=lo: e.matmul(pCB[0][:, 0:128], lhsT=BT[si][:, lo:lo + 128], rhs=CT[si][:, lo:lo + 128],
                                                 start=True, stop=True), reads=[("BT", si), ("CT", si)], writes=["pCB"])
            P.op("dve", lambda e, ci=ci: e.tensor_tensor(out=MT[ci][:], in0=pCB[0][:, 0:128].unsqueeze(1).to_broadcast([128, 4, 128]),
                                                         in1=LT[ci][:], op=ALU.mult), reads=["pCB", ("LT", ci)], writes=[("MT", ci)])
            h4 = lambda ap: ap.rearrange("p (h d) -> p h d", h=4)
            P.op("dve", lambda e, ci=ci, c4=c4: e.tensor_tensor(out=h4(xdt[ci][:]), in0=h4(xtok[ci][:]),
                                                                in1=dtf[:, c4:c4 + 4].unsqueeze(2).to_broadcast([128, 4, 64]), op=ALU.mult),
                 reads=[("xtok", ci), "dt"], writes=[("xdt", ci)])
            P.op("pool", lambda e, ci=ci, c4=c4: e.tensor_tensor(out=h4(xdd[ci][:]), in0=h4(xtok[ci][:]),
                                                                 in1=dtd[:, c4:c4 + 4].unsqueeze(2).to_broadcast([128, 4, 64]), op=ALU.mult),
                 reads=[("xtok", ci), "dtd"], writes=[("xdd", ci)])
            for h in range(4):
                P.op("pe", lambda e, h=h, ci=ci: e.matmul(pYd[0][:, h * 64:(h + 1) * 64], lhsT=MT[ci][:, h, :],
                                                          rhs=xdt[ci][:, h * 64:(h + 1) * 64], start=True, stop=True),
                     reads=[("MT", ci), ("xdt", ci)], writes=["pYd"])
            recs[-1][1].append(len(rec_cur[0]))
            for h in range(4):
                P.op("pe", lambda e, si=si, h=h, lo=lo: e.matmul(pYo[0][:, h * 64:(h + 1) * 64], lhsT=CT[si][:, lo:lo + 128],
                                                          rhs=hTb[:, h * 64:(h + 1) * 64], start=True, stop=True),
                     reads=[("CT", si), "hTb"], writes=["pYo"])
            P.op("act", lambda e, ci=ci: e.copy(out=ydsb[ci][:], in_=pYd[0][:, 0:256]), reads=["pYd"], writes=[("ydsb", ci)])
            P.dma("sp", lambda e, ci=ci, c=c: e.dma_start(out=zt[ci][:], in_=tm[c * 128:(c + 1) * 128, TMF_SSMZ:TMF_SSMZ + 256]),
                  writes=[("zt", ci)])
            P.op("dve", lambda e, ci=ci, c4=c4: e.tensor_tensor(out=h4(yt[ci][:]), in0=pYo[0][:, 0:256].rearrange("p (h d) -> p h d", h=4),
                                                                in1=ecum[:, c4:c4 + 4].unsqueeze(2).to_broadcast([128, 4, 64]), op=ALU.mult),
                 reads=["pYo", "ecum"], writes=[("yt", ci)])
            P.op("pool", lambda e, ci=ci: e.tensor_tensor(out=h4(dsk[ci][:]), in0=h4(xtok[ci][:]),
                                                          in1=Dbc[:, 0:4].unsqueeze(2).to_broadcast([128, 4, 64]), op=ALU.mult),
                 reads=[("xtok", ci), "Dbc"], writes=[("dsk", ci)])
            P.op("dve", lambda e, ci=ci: e.tensor_tensor(out=yt[ci][:], in0=yt[ci][:], in1=ydsb[ci][:], op=ALU.add),
                 reads=[("yt", ci), ("ydsb", ci)], writes=[("yt", ci)])
            P.op("dve", lambda e, ci=ci: e.tensor_tensor(out=yt[ci][:], in0=yt[ci][:], in1=dsk[ci][:], op=ALU.add),
                 reads=[("yt", ci), ("dsk", ci)], writes=[("yt", ci)])
            if c in (0, 4):
                ph.dump(f"ssm_xtok{c}", xtok[ci][:], [("xtok", ci)])
                ph.dump(f"ssm_LT{c}", LT[ci][:], [("LT", ci)])
                ph.dump(f"ssm_MT{c}", MT[ci][:], [("MT", ci)])
                ph.dump(f"ssm_yt{c}", yt[ci][:], [("yt", ci)])
            for h in range(4):
                P.op("pe", lambda e, h=h, ci=ci: e.matmul(pS[0][:, h * 64:(h + 1) * 64], lhsT=Btok[ci][:],
                                                          rhs=xdd[ci][:, h * 64:(h + 1) * 64], start=True, stop=True),
                     reads=[("Btok", ci), ("xdd", ci)], writes=["pS"])
            P.op("pool", lambda e, c4=c4: e.tensor_tensor(out=h4(hT[:]), in0=h4(hT[:]),
                                                          in1=ecumL[:, c4:c4 + 4].unsqueeze(2).to_broadcast([128, 4, 64]), op=ALU.mult),
                 reads=["hT", "ecumL"], writes=["hT"])
            P.op("dve", lambda e: e.tensor_tensor(out=hT[:], in0=hT[:], in1=pS[0][:, 0:256], op=ALU.add), reads=["hT", "pS"], writes=["hT"])
            P.op("act", lambda e: e.copy(out=hTb[:], in_=hT[:]), reads=["hT"], writes=["hTb"])
            recs[-1][1].append(len(rec_cur[0]))
            P.op("act", lambda e, ci=ci: e.activation(out=zt[ci][:], in_=zt[ci][:], func=AF.Silu),
                 reads=[("zt", ci)], writes=[("zt", ci)])
            P.op("pool", lambda e, ci=ci: e.tensor_tensor(out=yt[ci][:], in0=yt[ci][:], in1=zt[ci][:], op=ALU.mult),
                 reads=[("yt", ci), ("zt", ci)], writes=[("yt", ci)])
            P.op("act", lambda e, ci=ci, c=c: e.activation(out=junk[:], in_=yt[ci][:], func=AF.Square, accum_out=nst[:, c:c + 1]),
                 reads=[("yt", ci), "nst"], writes=["junk", ("nst", c)])
            P.op("dve", lambda e, c=c: e.tensor_scalar(out=nst[:, c:c + 1], in0=nst[:, c:c + 1], scalar1=1.0 / 256, scalar2=NORM_EPS,
                                                       op0=ALU.mult, op1=ALU.add), reads=[("nst", c)], writes=[("nst", c)])
            P.op("act", lambda e, c=c: e.activation(out=nst[:, c:c + 1], in_=nst[:, c:c + 1], func=AF.Sqrt),
                 reads=[("nst", c)], writes=[("nst", c)])
            P.op("dve", lambda e, c=c: e.reciprocal(out=nst[:, c:c + 1], in_=nst[:, c:c + 1]), reads=[("nst", c)], writes=[("nst", c)])
            P.op("dve", lambda e, ci=ci, c=c: e.scalar_tensor_tensor(out=yt[ci][:], in0=yt[ci][:], scalar=nst[:, c:c + 1], in1=ngb[:],
                                                                     op0=ALU.mult, op1=ALU.mult),
                 reads=[("yt", ci), ("nst", c), "ngb"], writes=[("yt", ci)])
            P.dma("sp", lambda e, ci=ci, c=c: e.dma_start(out=y_dst[c * 128:(c + 1) * 128, :], in_=yt[ci][:]),
                  reads=[("yt", ci)], writes=[("y_ssm", c)])
            restore()
    pipeline_merge([(r, m) for r, m in recs], 3)
    ph.close()


NPAIR = 144
TOPK = 256
NBIS = 17
FILLER = False


def pair_off(i):
    return 2 * i * (i + 1)


def default_blocks():
    return [(i + 1, 0, 4) for i in range(8)]


def pair_offsets(blocks):
    offs, o = [], 0
    for nch, _, lk in blocks:
        offs.append(o)
        o += 4 * (nch - 1) + lk
    return offs, o


def phase_indexer(nc, scr, prm, maskT_d, blocks=None):
    blocks = blocks or default_blocks()
    nblk = len(blocks)
    assert nblk % 2 == 0
    NT = nblk * 128
    ncb = max(cb for _, cb, _ in blocks) + 1
    poffs, _ = pair_offsets(blocks)
    ph = Phase(nc)
    P = ph.P
    fmo = scr["fmb_own"]
    fma = scr["fmb_all"]
    tmo = scr["tmf_own"]
    NB4 = 4
    identb = ph.sb([128, 128], BF16, "identb")
    kiT = ph.sb([128, SEQ], BF16, "kiT")
    qiT = [ph.sb([128, 8, 128], BF16, "qiT") for _ in range(NB4)]
    wi = ph.sb([128, nblk, 16], F32, "wi")
    iota = ph.sb([128, 512], F32, "iota")
    qrel = ph.sb([128, ncb], F32, "qrel")
    cbias = ph.sb([128, ncb, 512], F32, "cbias")
    pow2 = ph.sb([128, NBIS], F32, "pow2")
    wdiag = [ph.sb([128, 16, 128], BF16, "wdiag") for _ in range(NB4)]
    R = [ph.sb([128, 512], BF16, "R") for _ in range(4)]
    sc = [ph.sb([128, SEQ], F32, "sc") for _ in range(NB4)]
    junk = [ph.sb([128, SEQ], BF16, "junk") for _ in range(2)]
    mk = [ph.sb([128, SEQ], BF16, "mk") for _ in range(2)]
    mT = [ph.sb([128, 8, 128], BF16, "mT") for _ in range(3)]
    mx = [ph.sb([128, 8], F32, "mx") for _ in range(NB4)]
    bs = [ph.sb([128, 8], F32, "bs") for _ in range(NB4)]
    wk = [ph.sb([128, NBIS], F32, "wk") for _ in range(NB4)]
    cnt = [ph.sb([128, NBIS], F32, "cnt") for _ in range(NB4)]
    pD = [ph.ps([128, 512], F32, "pD") for _ in range(4)]
    pSc = [ph.ps([128, 512], F32, "pSc") for _ in range(2)]
    pT = [ph.ps([128, 1024], BF16, "pT") for _ in range(1)]
    pJ = ph.ps([128, 512], F32, "pJ")

    ld = lambda dst, src, key: P.dma("sp", lambda e: e.dma_start(out=dst, in_=src), writes=[key])
    ld(identb[:], prm["ident_b"][:, :], "identb")
    ld(kiT[:], fma[1024:1152, :], "kiT")
    ld(wi[:], tmo[0:NT, TMF_OWN_WI:TMF_OWN_WI + 16].rearrange("(i p) h -> p i h", p=128), "wi")
    ld(iota[:], prm["iota512"][0:1, :].partition_broadcast(128), "iota")
    ld(qrel[:], prm["qrel"][:, :], "qrel")
    ld(pow2[:], prm["pow2"][0:1, :].partition_broadcast(128), "pow2")
    P.op("dve", lambda e: e.tensor_scalar(out=wi[:], in0=wi[:], scalar1=0.03125, scalar2=None, op0=ALU.mult), reads=["wi"], writes=["wi"])
    for cb in range(ncb):
        P.op("dve", lambda e, o=cbias[:, cb, :], s=qrel[:, cb:cb + 1]: e.tensor_scalar(out=o, in0=iota[:], scalar1=s, scalar2=-1e30,
                                                                                  op0=ALU.is_gt, op1=ALU.mult),
             reads=["iota", "qrel"], writes=["cbias"])
    k = {"d": 0, "r": 0, "s": 0, "t": 0, "m": 0}
    qv = fmo[1024:2048, :].rearrange("(p r) t -> r p t", r=128)

    def nkeys(i):
        nch_, _, lk_ = blocks[i]
        return 512 * (nch_ - 1) + 128 * lk_

    def scores(i):
        nch, cbi, lk = blocks[i]
        b4 = i % NB4
        scb, mxb, wdb, qb = sc[b4], mx[b4], wdiag[b4], qiT[b4]
        ld(qb[:], qv[:, :, i * 128:(i + 1) * 128], ("qiT", b4))
        P.op("pool", lambda e, o=wdb[:], w_=wi[:, i, :].unsqueeze(2).to_broadcast([128, 16, 128]),
             d_=identb[:].unsqueeze(1).to_broadcast([128, 16, 128]): e.tensor_tensor(out=o, in0=d_, in1=w_, op=ALU.mult),
             reads=["identb", "wi"], writes=[("wdiag", b4)])
        P.op("dve", lambda e, o=mxb[:]: e.memset(o, 0.0), writes=[("mx", b4)])
        P.op("dve", lambda e, o=cnt[b4][:]: e.memset(o, 0.0), writes=[("cnt", b4)])
        for ch in range(nch):
            js = k["s"] % 2
            k["s"] += 1
            jds = {}
            nl = 512 if ch < nch - 1 else 128 * lk

            def dots(h):
                jd = k["d"] % 4
                k["d"] += 1
                jds[h] = jd
                r0 = (h % 2) * 64
                P.op("pe", lambda e, o=pD[jd][:, 0:nl], l=qb[r0:r0 + 64, h // 2, :],
                     r=kiT[r0:r0 + 64, ch * 512:ch * 512 + nl]: e.matmul(o, lhsT=l, rhs=r, start=True, stop=True),
                     reads=[("qiT", b4), "kiT"], writes=[("pD", jd)])

            dots(0)
            dots(1)
            for h in range(16):
                if h + 2 < 16:
                    dots(h + 2)
                jd = jds[h]
                jr = k["r"] % 4
                k["r"] += 1
                P.op("act", lambda e, o=R[jr][:, 0:nl], s=pD[jd][:, 0:nl]: e.activation(out=o, in_=s, func=AF.Relu),
                     reads=[("pD", jd)], writes=[("R", jr)])
                P.op("pe", lambda e, o=pSc[js][:, 0:nl], l=wdb[:, h, :], r=R[jr][:, 0:nl], h=h: e.matmul(o, lhsT=l, rhs=r, start=(h == 0),
                                                                                           stop=(h == 15)),
                     reads=[("wdiag", b4), ("R", jr)], writes=[("pSc", js)])
                if FILLER:
                    P.op("pe", lambda e, l=identb[:], r=kiT[:, ch * 512:(ch + 1) * 512]: e.matmul(pJ[:], lhsT=l, rhs=r, start=True, stop=True),
                         reads=["identb", "kiT"], writes=["pJ"])
            P.op("dve", lambda e, o=mxb[:, ch:ch + 1], s=pSc[js][:, 0:nl]: e.tensor_reduce(out=o, in_=s, axis=AX.X, op=ALU.max,
                                                                                   apply_absolute_value=True),
                 reads=[("pSc", js)], writes=[("mx", b4)])
            if ch == nch - 1:
                P.op("dve", lambda e, o=scb[:, ch * 512:ch * 512 + nl], s=pSc[js][:, 0:nl], c_=cbias[:, cbi, 0:nl]: e.tensor_tensor(
                    out=o, in0=s, in1=c_, op=ALU.add), reads=[("pSc", js), "cbias"], writes=[("sc", b4)])
            else:
                P.op("dve", lambda e, o=scb[:, ch * 512:(ch + 1) * 512], s=pSc[js][:]: e.tensor_copy(out=o, in_=s),
                     reads=[("pSc", js)], writes=[("sc", b4)])
            yield

    def bis_init(i):
        b4 = i % NB4
        bsb, wkb, mxb = bs[b4], wk[b4], mx[b4]
        bk = ("bs", b4)
        P.op("dve", lambda e, o=bsb[:, 0:1], s=mxb[:]: e.tensor_reduce(out=o, in_=s, axis=AX.X, op=ALU.max),
             reads=[("mx", b4)], writes=[bk])
        P.op("dve", lambda e, o=bsb[:, 0:1]: e.tensor_scalar(out=o, in0=o, scalar1=1.001, scalar2=1e-6, op0=ALU.mult, op1=ALU.add),
             reads=[bk], writes=[bk])
        P.op("dve", lambda e, o=wkb[:], s=bsb[:, 0:1]: e.tensor_scalar(out=o, in0=pow2[:], scalar1=s, scalar2=2.0, op0=ALU.mult,
                                                                      op1=ALU.mult), reads=[bk, "pow2"], writes=[("wk", b4)])
        P.op("dve", lambda e, o=bsb[:, 1:2]: e.memset(o, 0.0), reads=[bk], writes=[bk])

    def bis_count(i, it, jj):
        b4 = i % NB4
        n = nkeys(i)
        P.op("dve", lambda e, o=junk[jj][:, 0:n], s=sc[b4][:, 0:n], m=bs[b4][:, 1:2], a=cnt[b4][:, it:it + 1]: e.tensor_scalar(
            out=o, in0=s, scalar1=m, scalar2=0.0, op0=ALU.is_ge, op1=ALU.add, accum_out=a),
            reads=[("sc", b4), ("bs", b4), ("cnt", b4)], writes=[("junk", jj), ("cnt", b4)])

    def bis_delta(i, it):
        b4 = i % NB4
        P.op("dve", lambda e, o=bs[b4][:, 2:3], c_=cnt[b4][:, it:it + 1], w_=wk[b4][:, it:it + 1]: e.tensor_scalar(
            out=o, in0=c_, scalar1=TOPK - 0.5, scalar2=w_, op0=ALU.is_ge, op1=ALU.mult),
            reads=[("cnt", b4), ("wk", b4), ("bs", b4)], writes=[("bs", b4)])

    def bis_mid(i, it):
        b4 = i % NB4
        nx = min(it + 1, NBIS - 1)
        P.op("dve", lambda e, o=bs[b4][:, 1:2], d_=bs[b4][:, 2:3], w_=wk[b4][:, nx:nx + 1]: e.scalar_tensor_tensor(
            out=o, in0=d_, scalar=w_, in1=o, op0=ALU.subtract, op1=ALU.add), reads=[("bs", b4), ("wk", b4)], writes=[("bs", b4)])

    def finish_block(i, jj):
        nch, _, lk = blocks[i]
        b4 = i % NB4
        n = nkeys(i)
        mkb = mk[jj]
        P.op("dve", lambda e, o=mkb[:, 0:n], s=sc[b4][:, 0:n], t=bs[b4][:, 1:2]: e.tensor_scalar(
            out=o, in0=s, scalar1=t, scalar2=-1.0, op0=ALU.is_ge, op1=ALU.add), reads=[("sc", b4), ("bs", b4)], writes=[("mk", jj)])
        nkb = 4 * (nch - 1) + lk
        for g0 in range(0, nkb, 8):
            ng = min(8, nkb - g0)
            jt = 0
            jm = k["m"] % 3
            k["m"] += 1
            for kb in range(g0, g0 + ng):
                P.op("pe", lambda e, o=pT[jt][:, (kb - g0) * 128:(kb - g0 + 1) * 128], s=mkb[:, kb * 128:(kb + 1) * 128]: e.transpose(
                    out=o, in_=s, identity=identb[:]), reads=[("mk", jj), "identb"], writes=[("pT", jt)])
            P.op("act", lambda e, o=mT[jm][:, 0:ng, :], s=pT[jt][:, 0:ng * 128].rearrange("p (k t) -> p k t", k=ng): e.copy(out=o, in_=s),
                 reads=[("pT", jt)], writes=[("mT", jm)])
            po = poffs[i] + g0
            P.dma("sp", lambda e, o=maskT_d[:, po:po + ng, :], s=mT[jm][:, 0:ng, :]: e.dma_start(out=o, in_=s),
                  reads=[("mT", jm)], writes=[("maskT", i, g0)])

    import itertools
    for _ in itertools.chain(scores(0), scores(1)):
        pass
    for pr in range(nblk // 2):
        ia, ib = 2 * pr, 2 * pr + 1
        pending = iter(())
        if 2 * pr + 2 < nblk:
            pending = itertools.chain(scores(2 * pr + 2), scores(2 * pr + 3))
        bis_init(ia)
        bis_init(ib)
        for it in range(NBIS):
            bis_count(ia, it, 0)
            bis_count(ib, it, 1)
            bis_delta(ia, it)
            bis_delta(ib, it)
            bis_mid(ia, it)
            bis_mid(ib, it)
            next(pending, None)
        for _ in pending:
            pass
        finish_block(ia, 0)
        finish_block(ib, 1)
    ph.close()


def phase_attn(nc, scr, prm, maskT_d, y_own, blocks=None):
    blocks = blocks or default_blocks()
    nblk = len(blocks)
    poffs, _ = pair_offsets(blocks)
    ph = Phase(nc)
    P = ph.P
    fmo = scr["fmb_own"]
    fma = scr["fmb_all"]
    tmb = scr["tmb_all"]
    tmo = scr["tmf_own"]
    att_scale = 128 ** -0.5
    i30k = ph.sb([128, 128], BF16, "i30k")
    kT = ph.sb([128, 8, SEQ], BF16, "kT")
    V = ph.sb([128, 32, 8, 129], BF16, "V")
    qT = [ph.sb([128, 8, 128], BF16, "qT") for _ in range(2)]
    mT = [ph.sb([128, 32, 128], BF16, "mT") for _ in range(2)]
    PT = [ph.sb([128, 512], BF16, "PT") for _ in range(4)]
    ot = [ph.sb([128, 1024], F32, "ot") for _ in range(2)]
    zt = [ph.sb([128, 1024], F32, "zt") for _ in range(2)]
    rs = ph.sb([128, 8 * nblk], F32, "rs")
    pS = [ph.ps([128, 512], F32, "pS") for _ in range(4)]
    pO = [ph.ps([128, 512], F32, "pO") for _ in range(2)]

    ld = lambda dst, src, key: P.dma("sp", lambda e: e.dma_start(out=dst, in_=src), writes=[key])
    ld(i30k[:], prm["ident30k_b"][:, :], "i30k")
    for h in range(8):
        ld(kT[:, h, :], fma[h * 128:(h + 1) * 128, :], ("kT", h))
    vv = tmb.rearrange("(kb p) (h d) -> p kb h d", p=128, d=128)
    for kb in range(32):
        ld(V[:, kb, :, 0:128], vv[:, kb, :, :], ("V", kb))
    P.op("pool", lambda e: e.memset(V[:, :, :, 128:129], 1.0), writes=["Vones"])
    qv = fmo[0:1024, :].rearrange("(h d) t -> d h t", d=128)
    k = {"s": 0, "p": 0, "o": 0}
    def loads(i):
        b2_ = i % 2
        nkb_ = 4 * (blocks[i][0] - 1) + blocks[i][2]
        po_ = poffs[i]
        ld(qT[b2_][:], qv[:, :, i * 128:(i + 1) * 128], ("qT", b2_))
        ld(mT[b2_][:, 0:nkb_, :], maskT_d[:, po_:po_ + nkb_, :], ("mT", b2_))
        ld(zt[b2_][:], tmo[i * 128:(i + 1) * 128, TMF_OWN_ATTZ:TMF_OWN_ATTZ + 1024], ("zt", b2_))

    loads(0)
    for i, (nch, _, lk) in enumerate(blocks):
        b2 = i % 2
        nkb = 4 * (nch - 1) + lk
        po = poffs[i]
        if i + 1 < nblk:
            loads(i + 1)
        P.op("act", lambda e, o=zt[b2][:]: e.activation(out=o, in_=o, func=AF.Silu), reads=[("zt", b2)], writes=[("zt", b2)])
        for h in range(8):
            jo = k["o"] % 2
            k["o"] += 1
            jss = {}

            def st_mm(ch):
                js = k["s"] % 4
                k["s"] += 1
                jss[ch] = js
                nk4 = 4 if ch < nch - 1 else lk
                P.op("pe", lambda e, o=pS[js][:, 0:nk4 * 128], r=mT[b2][:, ch * 4:ch * 4 + nk4, :].rearrange("p k t -> p (k t)"): e.matmul(
                    o, lhsT=i30k[:], rhs=r, start=True, stop=False), reads=["i30k", ("mT", b2)], writes=[("pS", js)])
                for k4 in range(nk4):
                    kb = ch * 4 + k4
                    P.op("pe", lambda e, o=pS[js][:, k4 * 128:(k4 + 1) * 128], l=kT[:, h, kb * 128:(kb + 1) * 128],
                         r=qT[b2][:, h, :], k4=k4, nk4=nk4: e.matmul(o, lhsT=l, rhs=r, start=False, stop=(k4 == nk4 - 1)),
                         reads=[("kT", h), ("qT", b2)], writes=[("pS", js)])

            st_mm(0)
            for ch in range(nch):
                if ch + 1 < nch:
                    st_mm(ch + 1)
                js = jss[ch]
                jp = k["p"] % 4
                k["p"] += 1
                nk4 = 4 if ch < nch - 1 else lk
                P.op("act", lambda e, o=PT[jp][:, 0:nk4 * 128], s=pS[js][:, 0:nk4 * 128]: e.activation(out=o, in_=s, func=AF.Exp, scale=att_scale),
                     reads=[("pS", js)], writes=[("PT", jp)])
                for k4 in range(nk4):
                    kb = ch * 4 + k4
                    P.op("pe", lambda e, o=pO[jo][:, 0:129], l=PT[jp][:, k4 * 128:(k4 + 1) * 128], r=V[:, kb, h, :], kb=kb, nkb=nkb:
                         e.matmul(o, lhsT=l, rhs=r, start=(kb == 0), stop=(kb == nkb - 1)),
                         reads=[("PT", jp), ("V", kb), "Vones"], writes=[("pO", jo)])
            c = i * 8 + h
            P.op("dve", lambda e, o=rs[:, c:c + 1], s=pO[jo][:, 128:129]: e.reciprocal(out=o, in_=s),
                 reads=[("pO", jo)], writes=[("rs", c)])
            P.op("dve", lambda e, o=ot[b2][:, h * 128:(h + 1) * 128], s=pO[jo][:, 0:128], r=rs[:, c:c + 1],
                 z=zt[b2][:, h * 128:(h + 1) * 128]: e.scalar_tensor_tensor(out=o, in0=s, scalar=r, in1=z, op0=ALU.mult, op1=ALU.mult),
                 reads=[("pO", jo), ("rs", c), ("zt", b2)], writes=[("ot", b2)])
        P.dma("sp", lambda e, o=y_own[i * 128:(i + 1) * 128, 0:1024], s=ot[b2][:]: e.dma_start(out=o, in_=s),
              reads=[("ot", b2)], writes=[("y_att", i)])
    ph.close()


RW_DECAY_C = -0.6065306597126334


class _Stop(Exception):
    pass


def phase_rwkv(nc, scr, prm, y_all, NBLK=SEQ // 128, stop_after=None, y_dst=None, stagger=True):
    ph = Phase(nc)
    P = ph.P
    tm = scr["tmf_all"]
    fm = scr["fmf_all"]
    if y_dst is None:
        y_dst = y_all[:, 256:512]
    cst = {}
    for nm in ("ident_f", "mask_sl", "mask_su", "mask_u", "ones_bd"):
        cst[nm] = ph.sb([128, 128], F32, nm)
    mu_tm = ph.sb([128, 1024], F32, "mu_tm")
    mu_fm = ph.sb([128, 1], F32, "mu_fm")
    w2a2 = ph.sb([128, 256], F32, "w2a2")
    vec = {}
    for nm in ("rw_w0", "rw_a0", "rw_kk", "rw_ka", "rw_rk", "rw_gng", "rw_gnb"):
        vec[nm] = ph.sb([128, 256], F32, nm)
    onecol = ph.sb([128, 1], F32, "onecol")
    NBUF3 = 4
    cur = [ph.sb([128, 1024], F32, "cur") for _ in range(NBUF3)]
    prv = [ph.sb([128, 1024], F32, "prv") for _ in range(NBUF3)]
    lcur = [ph.sb([128, 128], F32, "lcur") for _ in range(NBUF3)]
    lprv = [ph.sb([128, 128], F32, "lprv") for _ in range(NBUF3)]
    T = {}
    BFT = ("rt", "at", "bt", "kt", "bh", "kh", "vb")
    for nm in ("lw", "asig", "kkn", "kp", "aa", "bb", "cum", "cumL", "e1", "rt", "at", "bt", "kt", "bh", "kh", "vb", "tmp", "tmp2", "yb", "bon"):
        T[nm] = [ph.sb([128, 256], BF16 if nm in BFT else F32, nm) for _ in range(NBUF3)]
    st4 = [ph.sb([128, 16], F32, "st4") for _ in range(NBUF3)]
    gL = [ph.sb([128, 4], F32, "gL") for _ in range(NBUF3)]
    TR = {nm: [[ph.sb([128, 128], BF16, nm) for _ in range(2)] for _ in range(NBUF3)] for nm in ("atT", "btT", "ktT", "rtT")}
    H = {}
    for nm in ("N", "NT", "Pa", "PaT", "Pb", "PbT", "TTa", "TTb", "MakT"):
        H[nm] = ph.sb([128, 4, 128], BF16, nm)
    H["W2"] = ph.sb([128, 4, 64], BF16, "W2")
    mask2 = {nm: ph.sb([128, 2, 128], F32, nm + "2") for nm in ("mask_sl", "mask_su", "mask_u")}
    HS = {}
    for nm in ("P1T", "P2", "MrbT", "MrkT"):
        shp = {"P1T": [128, 2, 128], "P2": [128, 4, 64], "MrbT": [128, 4, 128], "MrkT": [128, 4, 128]}[nm]
        HS[nm] = [ph.sb(shp, F32 if nm == "P2" else BF16, nm) for _ in range(NBUF3)]
    Usb = ph.sb([128, 4, 64], BF16, "Usb")
    STb = [ph.sb([128, 64], BF16, "STb") for _ in range(4)]
    identb = ph.sb([128, 128], BF16, "identb")
    ST = [ph.sb([128, 64], F32, "ST") for _ in range(4)]
    pA = [ph.ps([128, 512], F32, "pA") for _ in range(3)]
    pP = [ph.ps([128, 512], F32, "pP") for _ in range(2)]
    pTb = ph.ps([128, 1024], BF16, "pTb")
    pQ = [ph.ps([128, 512], F32, "pQ") for _ in range(2)]

    ld = lambda dst, src, key: P.dma("sp", lambda e: e.dma_start(out=dst, in_=src), writes=[key])
    for nm in cst:
        ld(cst[nm][:], prm[nm][:, :], nm)
    ld(identb[:], prm["ident_b"][:, :], "identb")
    ld(mu_tm[:], prm["rw_mu_tm"][0:1, :].partition_broadcast(128), "mu_tm")
    ld(mu_fm[:], prm["rw_mu_fm"][:, :], "mu_fm")
    ld(w2a2[:], prm["rw_w2a2"][:, :], "w2a2")
    for nm in vec:
        ld(vec[nm][:], prm[nm][0:1, :].partition_broadcast(128), nm)
    P.op("dve", lambda e: e.memset(onecol[:], 1.0), writes=["onecol"])
    for h in range(4):
        P.op("dve", lambda e, o=ST[h][:]: e.memset(o, 0.0), writes=[("ST", h)])
        P.op("dve", lambda e, o=STb[h][:]: e.memset(o, 0.0), writes=[("STb", h)])
    P.op("dve", lambda e: e.memset(Usb[:], 0.0), writes=["Usb"])
    for nm in ("mask_sl", "mask_su", "mask_u"):
        for r_ in range(2):
            P.op("pool", lambda e, o=mask2[nm][:, r_, :], s=cst[nm][:]: e.tensor_copy(out=o, in_=s), reads=[nm], writes=[nm + "2"])
    kq = {"a": 0, "q": 0, "e": 0}
    P.excl.update(["pA", "pQ", "pTb", "pP"])

    def mm(out, lhsT, rhs, reads, writes, start=True, stop=True):
        P.op("pe", lambda e: e.matmul(out, lhsT=lhsT, rhs=rhs, start=start, stop=stop), reads=reads, writes=writes)

    def evac(out, in_, reads, writes, mask=None, mreads=()):
        kq["e"] += 1
        if mask is not None:
            P.op("dve", lambda e: e.tensor_tensor(out=out, in0=in_, in1=mask, op=ALU.mult), reads=list(reads) + list(mreads), writes=writes)
        elif kq["e"] % 4 == 0:
            P.op("dve", lambda e: e.tensor_copy(out=out, in_=in_), reads=reads, writes=writes)
        else:
            P.op("act", lambda e: e.copy(out=out, in_=in_), reads=reads, writes=writes)

    def nextA():
        j = kq["a"] % 3
        kq["a"] += 1
        return j

    def nextQ():
        j = kq["q"] % 2
        kq["q"] += 1
        return j

    def dv(fn, reads, writes, eng="dve"):
        P.op(eng, fn, reads=reads, writes=writes)

    def stage(n):
        if stop_after is not None and n > stop_after:
            raise _Stop()

    real_op, real_dma = P.op, P.dma
    recs = []
    for blk in range(NBLK):
      rec = []
      marks = []
      recs.append((rec, marks))
      P.op = lambda eng, fn, reads=(), writes=(), rec=rec: rec.append((real_op, eng, fn, list(reads), list(writes)))
      P.dma = lambda eng, fn, reads=(), writes=(), rec=rec: rec.append((real_dma, eng, fn, list(reads), list(writes)))
      try:
          b2 = blk % NBUF3
          t0 = blk * 128
          B = {nm: T[nm][b2] for nm in T}
          K2 = lambda nm: (nm, b2)
          cu, pv, lc, lp = cur[b2], prv[b2], lcur[b2], lprv[b2]
          stage(-1)
          ld(cu[:], tm[t0:t0 + 128, TMF_R:TMF_R + 1024], K2("cur"))
          ld(lc[:], fm[FMF_WL:FMF_WL + 128, t0:t0 + 128], K2("lcur"))
          if blk == 0:
              dv(lambda e, o=pv[:]: e.memset(o, 0.0), [], [K2("prv")])
              dv(lambda e, o=lp[:]: e.memset(o, 0.0), [], [K2("lprv")])
              ld(pv[1:128, :], tm[0:127, TMF_R:TMF_R + 1024], K2("prv"))
              ld(lp[:, 1:128], fm[FMF_WL:FMF_WL + 128, 0:127], K2("lprv"))
          else:
              ld(pv[:], tm[t0 - 1:t0 + 127, TMF_R:TMF_R + 1024], K2("prv"))
              ld(lp[:], fm[FMF_WL:FMF_WL + 128, t0 - 1:t0 + 127], K2("lprv"))
          stage(-0.5)
          dv(lambda e, o=pv[:], c=cu[:]: e.tensor_tensor(out=o, in0=o, in1=c, op=ALU.subtract), [K2("prv"), K2("cur")], [K2("prv")])
          dv(lambda e, o=pv[:]: e.tensor_tensor(out=o, in0=o, in1=mu_tm[:], op=ALU.mult), [K2("prv"), "mu_tm"], [K2("prv")], eng="pool")
          dv(lambda e, o=cu[:], d=pv[:]: e.tensor_tensor(out=o, in0=o, in1=d, op=ALU.add), [K2("prv"), K2("cur")], [K2("cur")])
          dv(lambda e, o=lp[:], c=lc[:]: e.tensor_tensor(out=o, in0=o, in1=c, op=ALU.subtract), [K2("lprv"), K2("lcur")], [K2("lprv")])
          dv(lambda e, o=lc[:], d=lp[:]: e.scalar_tensor_tensor(out=o, in0=d, scalar=mu_fm[:, 0:1], in1=o, op0=ALU.mult, op1=ALU.add),
             [K2("lprv"), K2("lcur"), "mu_fm"], [K2("lcur")])
          stage(-0.2)
          P.op("act", lambda e, o=lc[0:64, :]: e.activation(out=o, in_=o, func=AF.Tanh), reads=[K2("lcur")], writes=[K2("lcur")])
          r_, k_, v_, z_ = cu[:, 0:256], cu[:, 256:512], cu[:, 512:768], cu[:, 768:1024]
          P.op("act", lambda e, o=B["vb"][:], v_=v_: e.copy(out=o, in_=v_), reads=[K2("cur")], writes=[K2("vb")])
          stage(1)
          mm(pP[0][:, 0:256], lc[0:64, :], w2a2[0:64, :], [K2("lcur"), "w2a2"], [("pP", 0)])
          mm(pP[1][:, 0:256], lc[64:128, :], w2a2[64:128, :], [K2("lcur"), "w2a2"], [("pP", 1)])
          dv(lambda e, o=B["lw"][:], s=pP[0][:, 0:256]: e.tensor_tensor(out=o, in0=s, in1=vec["rw_w0"][:], op=ALU.add),
             [("pP", 0), "rw_w0"], [K2("lw")])
          dv(lambda e, o=B["asig"][:], s=pP[1][:, 0:256]: e.tensor_tensor(out=o, in0=s, in1=vec["rw_a0"][:], op=ALU.add),
             [("pP", 1), "rw_a0"], [K2("asig")])
          P.op("act", lambda e, o=B["lw"][:]: e.activation(out=o, in_=o, func=AF.Sigmoid), reads=[K2("lw")], writes=[K2("lw")])
          P.op("act", lambda e, o=B["asig"][:]: e.activation(out=o, in_=o, func=AF.Sigmoid), reads=[K2("asig")], writes=[K2("asig")])
          dv(lambda e, o=B["lw"][:]: e.tensor_scalar(out=o, in0=o, scalar1=RW_DECAY_C, scalar2=None, op0=ALU.mult), [K2("lw")], [K2("lw")])
          stage(2)
          s4 = st4[b2]
          v3 = lambda ap: ap.rearrange("p (h j) -> p h j", h=4)
          dv(lambda e, o=B["kkn"][:], k_=k_: e.tensor_tensor(out=o, in0=k_, in1=vec["rw_kk"][:], op=ALU.mult), [K2("cur"), "rw_kk"], [K2("kkn")])
          dv(lambda e, o=B["tmp"][:], s=B["kkn"][:]: e.tensor_tensor(out=o, in0=s, in1=s, op=ALU.mult), [K2("kkn")], [K2("tmp")], eng="pool")
          dv(lambda e, o=s4[:, 0:4], s=v3(B["tmp"][:]): e.tensor_reduce(out=o, in_=s, axis=AX.X, op=ALU.add), [K2("tmp")], [K2("st4")])
          dv(lambda e, o=s4[:, 0:4]: e.tensor_scalar(out=o, in0=o, scalar1=1e-12, scalar2=None, op0=ALU.add), [K2("st4")], [K2("st4")])
          P.op("act", lambda e, o=s4[:, 0:4]: e.activation(out=o, in_=o, func=AF.Sqrt), reads=[K2("st4")], writes=[K2("st4")])
          dv(lambda e, o=s4[:, 0:4]: e.reciprocal(out=o, in_=o), [K2("st4")], [K2("st4")])
          dv(lambda e, o=v3(B["kkn"][:]), s=s4[:, 0:4].unsqueeze(2).to_broadcast([128, 4, 64]): e.tensor_tensor(out=o, in0=o, in1=s, op=ALU.mult),
             [K2("kkn"), K2("st4")], [K2("kkn")])
          dv(lambda e, o=B["tmp"][:], s=B["asig"][:]: e.scalar_tensor_tensor(out=o, in0=s, scalar=-1.0, in1=vec["rw_ka"][:], op0=ALU.add,
                                                                            op1=ALU.mult), [K2("asig"), "rw_ka"], [K2("tmp")])
          dv(lambda e, o=B["kp"][:], s=B["tmp"][:], k_=k_: e.scalar_tensor_tensor(out=o, in0=s, scalar=1.0, in1=k_, op0=ALU.add, op1=ALU.mult),
             [K2("tmp"), K2("cur")], [K2("kp")])
          dv(lambda e, o=B["aa"][:], s=B["kkn"][:]: e.tensor_scalar(out=o, in0=s, scalar1=-1.0, scalar2=None, op0=ALU.mult),
             [K2("kkn")], [K2("aa")], eng="pool")
          dv(lambda e, o=B["bb"][:], s=B["kkn"][:], a=B["asig"][:]: e.tensor_tensor(out=o, in0=s, in1=a, op=ALU.mult),
             [K2("kkn"), K2("asig")], [K2("bb")], eng="pool")
          dv(lambda e, o=B["tmp2"][:], s=B["kp"][:], r_=r_: e.tensor_tensor(out=o, in0=r_, in1=s, op=ALU.mult), [K2("cur"), K2("kp")], [K2("tmp2")])
          dv(lambda e, o=B["tmp2"][:]: e.tensor_tensor(out=o, in0=o, in1=vec["rw_rk"][:], op=ALU.mult), [K2("tmp2"), "rw_rk"], [K2("tmp2")])
          dv(lambda e, o=s4[:, 4:8], s=v3(B["tmp2"][:]): e.tensor_reduce(out=o, in_=s, axis=AX.X, op=ALU.add), [K2("tmp2")], [K2("st4")])
          dv(lambda e, o=v3(B["bon"][:]), s=v3(cu[:, 512:768]), c=s4[:, 4:8].unsqueeze(2).to_broadcast([128, 4, 64]):
             e.tensor_tensor(out=o, in0=s, in1=c, op=ALU.mult), [K2("cur"), K2("st4")], [K2("bon")])
          stage(3)
          mm(pP[0][:, 0:256], cst["mask_u"][:], B["lw"][:], ["mask_u", K2("lw")], [("pP", 0)])
          mm(pP[0][:, 256:512], cst["ones_bd"][:], B["lw"][:], ["ones_bd", K2("lw")], [("pP", 0)])
          evac(B["cum"][:], pP[0][:, 0:256], [("pP", 0)], [K2("cum")])
          evac(B["cumL"][:], pP[0][:, 256:512], [("pP", 0)], [K2("cumL")])
          for p in range(2):
              for c2 in range(2):
                  mm(pP[1][:, c2 * 2 + p:c2 * 2 + p + 1], B["lw"][:, p * 128:(p + 1) * 128], cst["ones_bd"][:, c2 * 64:c2 * 64 + 1],
                     [K2("lw"), "ones_bd"], [("pP", 1)])
          P.op("act", lambda e, o=gL[b2][:], s=pP[1][:, 0:4]: e.activation(out=o, in_=s, func=AF.Exp), reads=[("pP", 1)], writes=[K2("gL")])
          P.op("act", lambda e, o=B["e1"][:], s=B["cum"][:]: e.activation(out=o, in_=s, func=AF.Exp), reads=[K2("cum")], writes=[K2("e1")])
          dv(lambda e, o=B["rt"][:], s=B["e1"][:], r_=r_: e.tensor_tensor(out=o, in0=r_, in1=s, op=ALU.mult), [K2("cur"), K2("e1")], [K2("rt")])
          dv(lambda e, o=B["tmp"][:], s=B["cum"][:], l=B["lw"][:]: e.tensor_tensor(out=o, in0=s, in1=l, op=ALU.subtract),
             [K2("cum"), K2("lw")], [K2("tmp")], eng="pool")
          P.op("act", lambda e, o=B["tmp"][:]: e.activation(out=o, in_=o, func=AF.Exp), reads=[K2("tmp")], writes=[K2("tmp")])
          dv(lambda e, o=B["at"][:], s=B["aa"][:], t=B["tmp"][:]: e.tensor_tensor(out=o, in0=s, in1=t, op=ALU.mult),
             [K2("aa"), K2("tmp")], [K2("at")])
          P.op("act", lambda e, o=B["e1"][:], s=B["cum"][:]: e.activation(out=o, in_=s, func=AF.Exp, scale=-1.0),
               reads=[K2("cum"), K2("rt")], writes=[K2("e1")])
          dv(lambda e, o=B["bt"][:], s=B["bb"][:], t=B["e1"][:]: e.tensor_tensor(out=o, in0=s, in1=t, op=ALU.mult),
             [K2("bb"), K2("e1")], [K2("bt")])
          dv(lambda e, o=B["kt"][:], s=B["kp"][:], t=B["e1"][:]: e.tensor_tensor(out=o, in0=s, in1=t, op=ALU.mult),
             [K2("kp"), K2("e1")], [K2("kt")], eng="pool")
          dv(lambda e, o=B["tmp2"][:], s=B["cumL"][:], c=B["cum"][:]: e.tensor_tensor(out=o, in0=s, in1=c, op=ALU.subtract),
             [K2("cumL"), K2("cum")], [K2("tmp2")], eng="pool")
          P.op("act", lambda e, o=B["tmp2"][:]: e.activation(out=o, in_=o, func=AF.Exp), reads=[K2("tmp2")], writes=[K2("tmp2")])
          dv(lambda e, o=B["bh"][:], s=B["bb"][:], t=B["tmp2"][:]: e.tensor_tensor(out=o, in0=s, in1=t, op=ALU.mult),
             [K2("bb"), K2("tmp2")], [K2("bh")])
          dv(lambda e, o=B["kh"][:], s=B["kp"][:], t=B["tmp2"][:]: e.tensor_tensor(out=o, in0=s, in1=t, op=ALU.mult),
             [K2("kp"), K2("tmp2")], [K2("kh")], eng="pool")
          stage(4)
          for qi_, (nm_src, nm_dst) in enumerate((("at", "atT"), ("bt", "btT"), ("kt", "ktT"), ("rt", "rtT"))):
              for p in range(2):
                  P.op("pe", lambda e, o=pTb[:, (qi_ * 2 + p) * 128:(qi_ * 2 + p + 1) * 128], s=B[nm_src][:, p * 128:(p + 1) * 128]: e.transpose(
                      out=o, in_=s, identity=identb[:]), reads=[K2(nm_src), "identb"], writes=["pTb"])
          for qi_, (nm_src, nm_dst) in enumerate((("at", "atT"), ("bt", "btT"), ("kt", "ktT"), ("rt", "rtT"))):
              for p in range(2):
                  evac(TR[nm_dst][b2][p][:], pTb[:, (qi_ * 2 + p) * 128:(qi_ * 2 + p + 1) * 128], ["pTb"], [(nm_dst, b2, p)])
          marks.append(len(rec))
          slot = lambda h: (h % 2) * 2 + h // 2
          rk = lambda nm, h: (nm, b2, h // 2)
          opd = {}
          for h in range(4):
              p, r0 = h // 2, (h % 2) * 64
              opd[h] = {nm: TR[nm2][b2][p][r0:r0 + 64, :] for nm, nm2 in (("aT", "atT"), ("bT", "btT"), ("kT", "ktT"), ("rT", "rtT"))}
          jx, jy2 = nextA(), nextA()
          for r_, jb in ((0, jx), (1, jy2)):
              for p in range(2):
                  h = 2 * p + r_
                  o_ = opd[h]
                  mm(pA[jb][:, p * 128:(p + 1) * 128], o_["aT"], o_["bT"], [rk("atT", h), rk("btT", h)], [("pA", jb)])
                  mm(pA[jb][:, 256 + p * 128:256 + (p + 1) * 128], o_["bT"], o_["aT"], [rk("atT", h), rk("btT", h)], [("pA", jb)])
          for r_, jb in ((0, jx), (1, jy2)):
              sl_ = slice(2 * r_, 2 * r_ + 2)
              dv(lambda e, o=H["N"][:, sl_, :], s=pA[jb][:, 0:256].rearrange("p (a t) -> p a t", a=2), m=mask2["mask_sl"][:]:
                 e.tensor_tensor(out=o, in0=s, in1=m, op=ALU.mult), [("pA", jb), "mask_sl2"], [("N", r_)])
              dv(lambda e, o=H["NT"][:, sl_, :], s=pA[jb][:, 256:512].rearrange("p (a t) -> p a t", a=2), m=mask2["mask_su"][:]:
                 e.tensor_tensor(out=o, in0=s, in1=m, op=ALU.mult), [("pA", jb), "mask_su2"], [("NT", r_)])
          jz = nextA()
          jx2 = nextA()
          for r_, jb in ((0, jx2), (1, jz)):
              for p in range(2):
                  h = 2 * p + r_
                  o_ = opd[h]
                  mm(pA[jb][:, p * 128:(p + 1) * 128], o_["kT"], o_["aT"], [rk("ktT", h), rk("atT", h)], [("pA", jb)])
                  mm(pA[jb][:, 256 + p * 128:256 + (p + 1) * 128], o_["bT"], o_["rT"], [rk("btT", h), rk("rtT", h)], [("pA", jb)])
          for r_, jb in ((0, jx2), (1, jz)):
              sl_ = slice(2 * r_, 2 * r_ + 2)
              dv(lambda e, o=H["MakT"][:, sl_, :], s=pA[jb][:, 0:256].rearrange("p (a t) -> p a t", a=2), m=mask2["mask_su"][:]:
                 e.tensor_tensor(out=o, in0=s, in1=m, op=ALU.mult), [("pA", jb), "mask_su2"], [("MakT", r_)])
              dv(lambda e, o=HS["MrbT"][b2][:, sl_, :], s=pA[jb][:, 256:512].rearrange("p (a t) -> p a t", a=2), m=mask2["mask_u"][:]:
                 e.tensor_tensor(out=o, in0=s, in1=m, op=ALU.mult), [("pA", jb), "mask_u2"], [("MrbT", b2, r_)])
          jk0, jk1 = nextA(), nextA()
          for r_, jb in ((0, jk0), (1, jk1)):
              for p in range(2):
                  h = 2 * p + r_
                  o_ = opd[h]
                  mm(pA[jb][:, p * 128:(p + 1) * 128], o_["kT"], o_["rT"], [rk("ktT", h), rk("rtT", h)], [("pA", jb)])
          for r_, jb in ((0, jk0), (1, jk1)):
              sl_ = slice(2 * r_, 2 * r_ + 2)
              dv(lambda e, o=HS["MrkT"][b2][:, sl_, :], s=pA[jb][:, 0:256].rearrange("p (a t) -> p a t", a=2), m=mask2["mask_u"][:]:
                 e.tensor_tensor(out=o, in0=s, in1=m, op=ALU.mult), [("pA", jb), "mask_u2"], [("MrkT", b2, r_)])
          dv(lambda e, o=H["TTa"][:], s=H["NT"][:], i_=cst["ident_f"][:].unsqueeze(1).to_broadcast([128, 4, 128]):
             e.tensor_tensor(out=o, in0=s, in1=i_, op=ALU.add), [("NT", 0), ("NT", 1), "ident_f"], ["TTa"], eng="pool")
          cur_, curT_, nxt_, nxtT_ = "N", "NT", "Pa", "PaT"
          tc_, tn_ = "TTa", "TTb"
          kn = lambda nm: [(nm, 0), (nm, 1)] if nm in ("N", "NT") else [nm]
          for lvl in range(1, 6):
              ja = nextA()
              for sl in range(4):
                  mm(pA[ja][:, sl * 128:(sl + 1) * 128], H[curT_][:, sl, :], H[cur_][:, sl, :], kn(cur_) + kn(curT_), [("pA", ja)])
              evac(H[nxt_][:], pA[ja][:].rearrange("p (a t) -> p a t", a=4), [("pA", ja)], [nxt_])
              if lvl < 5:
                  jb = nextA()
                  for sl in range(4):
                      mm(pA[jb][:, sl * 128:(sl + 1) * 128], H[cur_][:, sl, :], H[curT_][:, sl, :], kn(cur_) + kn(curT_), [("pA", jb)])
                  evac(H[nxtT_][:], pA[jb][:].rearrange("p (a t) -> p a t", a=4), [("pA", jb)], [nxtT_])
              jc = nextA()
              for sl in range(4):
                  mm(pA[jc][:, sl * 128:(sl + 1) * 128], H[nxt_][:, sl, :], H[tc_][:, sl, :], [nxt_, tc_], [("pA", jc)])
              dv(lambda e, o=H[tn_][:], s=pA[jc][:].rearrange("p (a t) -> p a t", a=4), t=H[tc_][:]: e.tensor_tensor(out=o, in0=s, in1=t, op=ALU.add),
                 [("pA", jc), tc_], [tn_])
              if lvl == 1:
                  cur_, curT_, nxt_, nxtT_ = "Pa", "PaT", "Pb", "PbT"
              else:
                  cur_, curT_, nxt_, nxtT_ = nxt_, nxtT_, cur_, curT_
              tc_, tn_ = tn_, tc_
          TTn = tc_
          ja = nextA()
          for h in range(4):
              p, r0 = h // 2, (h % 2) * 64
              mm(pA[ja][r0:r0 + 64, p * 128:(p + 1) * 128], B["at"][:, h * 64:(h + 1) * 64], H[TTn][:, slot(h), :], [K2("at"), TTn], [("pA", ja)])
              mm(pA[ja][:, 256 + h * 64:256 + (h + 1) * 64], H["MakT"][:, slot(h), :], B["vb"][:, h * 64:(h + 1) * 64],
                 [("MakT", h % 2), K2("vb")], [("pA", ja)])
          evac(HS["P1T"][b2][:], pA[ja][:, 0:256].rearrange("p (a t) -> p a t", a=2), [("pA", ja)], [("P1T", b2)])
          evac(H["W2"][:], pA[ja][:, 256:512].rearrange("p (h i) -> p h i", h=4), [("pA", ja)], ["W2"])
          jb = nextA()
          for h in range(4):
              mm(pA[jb][:, h * 64:(h + 1) * 64], H[TTn][:, slot(h), :], H["W2"][:, h, :], [TTn, "W2"], [("pA", jb)])
          evac(HS["P2"][b2][:], pA[jb][:, 0:256].rearrange("p (h i) -> p h i", h=4), [("pA", jb)], [("P2", b2)])
          stage(6)
          marks.append(len(rec))
          for c2 in range(2):
              cs = slice(c2 * 64, (c2 + 1) * 64)
              jq = nextQ()
              for h in range(4):
                  mm(pQ[jq][cs, h * 64:(h + 1) * 64], HS["P1T"][b2][:, h // 2, cs], STb[h][:, :], [("P1T", b2), ("STb", h)], [("pQ", jq)])
              dv(lambda e, o=Usb[cs, :, :], s=pQ[jq][cs, 0:256].rearrange("p (h i) -> p h i", h=4), t=HS["P2"][b2][cs, :, :]:
                 e.tensor_tensor(out=o, in0=s, in1=t, op=ALU.add), [("pQ", jq), ("P2", b2)], ["Usb"])
              jy = nextQ()
              for h in range(4):
                  p = h // 2
                  yo = pQ[jy][cs, h * 64:(h + 1) * 64]
                  mm(yo, TR["rtT"][b2][p][:, cs], STb[h][:, :], [("rtT", b2, p), ("STb", h)], [("pQ", jy)], start=True, stop=False)
                  mm(yo, HS["MrkT"][b2][:, slot(h), cs], B["vb"][:, h * 64:(h + 1) * 64], [("MrkT", b2, h % 2), K2("vb")], [("pQ", jy)],
                     start=False, stop=False)
                  mm(yo, HS["MrbT"][b2][:, slot(h), cs], Usb[:, h, :], [("MrbT", b2, h % 2), "Usb"], [("pQ", jy)], start=False, stop=True)
              evac(B["yb"][cs, :], pQ[jy][cs, 0:256], [("pQ", jy)], [K2("yb")])
              js = nextQ()
              for h in range(4):
                  r0 = (h % 2) * 64
                  rows = slice(r0, r0 + 64)
                  so = pQ[js][rows, h * 64:(h + 1) * 64]
                  mm(so, B["kh"][cs, h * 64:(h + 1) * 64], B["vb"][cs, h * 64:(h + 1) * 64], [K2("kh"), K2("vb")], [("pQ", js)],
                     start=True, stop=False)
                  mm(so, B["bh"][cs, h * 64:(h + 1) * 64], Usb[cs, h, :], [K2("bh"), "Usb"], [("pQ", js)], start=False, stop=True)
              for h in range(4):
                  p, r0 = h // 2, (h % 2) * 64
                  rows = slice(r0, r0 + 64)
                  dv(lambda e, o=ST[h][rows, :], s=pQ[js][rows, h * 64:(h + 1) * 64], g=gL[b2][rows, c2 * 2 + p:c2 * 2 + p + 1]:
                     e.scalar_tensor_tensor(out=o, in0=o, scalar=g, in1=s, op0=ALU.mult, op1=ALU.add),
                     [("ST", h), ("pQ", js), K2("gL")], [("ST", h)])
                  P.op("act", lambda e, o=STb[h][rows, :], s=ST[h][rows, :]: e.copy(out=o, in_=s), reads=[("ST", h)], writes=[("STb", h)])
          stage(7)
          marks.append(len(rec))
          yb = B["yb"]
          dv(lambda e, o=s4[:, 8:12], s=v3(yb[:]): e.tensor_reduce(out=o, in_=s, axis=AX.X, op=ALU.add), [K2("yb")], [K2("st4")])
          dv(lambda e, o=B["tmp"][:], s=yb[:]: e.tensor_tensor(out=o, in0=s, in1=s, op=ALU.mult), [K2("yb")], [K2("tmp")], eng="pool")
          dv(lambda e, o=s4[:, 12:16], s=v3(B["tmp"][:]): e.tensor_reduce(out=o, in_=s, axis=AX.X, op=ALU.add), [K2("tmp")], [K2("st4")])
          dv(lambda e, o=s4[:, 8:16]: e.tensor_scalar(out=o, in0=o, scalar1=1.0 / 64, scalar2=None, op0=ALU.mult), [K2("st4")], [K2("st4")])
          dv(lambda e, o=s4[:, 0:4], m=s4[:, 8:12]: e.tensor_tensor(out=o, in0=m, in1=m, op=ALU.mult), [K2("st4")], [K2("st4")])
          dv(lambda e, o=s4[:, 12:16], m2=s4[:, 0:4]: e.tensor_tensor(out=o, in0=o, in1=m2, op=ALU.subtract), [K2("st4")], [K2("st4")])
          dv(lambda e, o=s4[:, 12:16]: e.tensor_scalar(out=o, in0=o, scalar1=GN_EPS, scalar2=None, op0=ALU.add), [K2("st4")], [K2("st4")])
          P.op("act", lambda e, o=s4[:, 12:16]: e.activation(out=o, in_=o, func=AF.Sqrt), reads=[K2("st4")], writes=[K2("st4")])
          dv(lambda e, o=s4[:, 12:16]: e.reciprocal(out=o, in_=o), [K2("st4")], [K2("st4")])
          dv(lambda e, o=v3(yb[:]), m=s4[:, 8:12].unsqueeze(2).to_broadcast([128, 4, 64]): e.tensor_tensor(out=o, in0=o, in1=m, op=ALU.subtract),
             [K2("yb"), K2("st4")], [K2("yb")])
          dv(lambda e, o=v3(yb[:]), r=s4[:, 12:16].unsqueeze(2).to_broadcast([128, 4, 64]): e.tensor_tensor(out=o, in0=o, in1=r, op=ALU.mult),
             [K2("yb"), K2("st4")], [K2("yb")])
          dv(lambda e, o=yb[:]: e.tensor_tensor(out=o, in0=o, in1=vec["rw_gng"][:], op=ALU.mult), [K2("yb"), "rw_gng"], [K2("yb")], eng="pool")
          dv(lambda e, o=yb[:]: e.tensor_tensor(out=o, in0=o, in1=vec["rw_gnb"][:], op=ALU.add), [K2("yb"), "rw_gnb"], [K2("yb")])
          dv(lambda e, o=yb[:], b_=B["bon"][:]: e.tensor_tensor(out=o, in0=o, in1=b_, op=ALU.add), [K2("yb"), K2("bon")], [K2("yb")], eng="pool")
          P.op("act", lambda e, o=B["tmp2"][:], z_=z_: e.activation(out=o, in_=z_, func=AF.Silu), reads=[K2("cur")], writes=[K2("tmp2")])
          dv(lambda e, o=yb[:], z=B["tmp2"][:]: e.tensor_tensor(out=o, in0=o, in1=z, op=ALU.mult), [K2("yb"), K2("tmp2")], [K2("yb")])
          P.dma("sp", lambda e, o=y_dst[t0:t0 + 128, :], s=yb[:]: e.dma_start(out=o, in_=s), reads=[K2("yb")], writes=[("y_rwkv", blk)])
      except _Stop:
          pass
    P.op, P.dma = real_op, real_dma
    if stagger:
        pipeline_merge(recs, 4)
    else:
        for rec, _ in recs:
            for (f, eng, fn, reads, writes) in rec:
                f(eng, fn, reads=reads, writes=writes)
    ph.close()


def phase_outproj(nc, y_tok, x_res, w_out, ident_d, x_new):
    ph = Phase(nc)
    P = ph.P
    D = D_MODEL
    KT = D // 128
    T = OWN
    TT = T // 128
    KQ = 8
    NKQ = KT // KQ
    NB = 512
    ident = ph.sb([128, 128], BF16, "ident")
    hT = ph.sb([128, KT, T], BF16, "hT")
    xt = [ph.sb([128, D], F32, "xt") for _ in range(2)]
    hb = [ph.sb([128, D], BF16, "hb") for _ in range(2)]
    wb = [ph.sb([128, KT, NB], BF16, "wb") for _ in range(2)]
    stg = [ph.sb([128, NB], F32, "stg") for _ in range(4)]
    rs = [ph.sb([128, NB], F32, "rs") for _ in range(4)]
    pT = [ph.ps([128, 1024], BF16, "pT") for _ in range(2)]
    pM = [ph.ps([128, 512], F32, "pM") for _ in range(6)]
    P.dma("sp", lambda e: e.dma_start(out=ident[:], in_=ident_d[:, :]), writes=["ident"])
    tc_ = 0
    for tt in range(TT):
        i = tt % 2
        P.dma("sp", lambda e, o=xt[i][:], s=y_tok[tt * 128:(tt + 1) * 128, :]: e.dma_start(out=o, in_=s), writes=[("xt", i)])
        P.op("act", lambda e, o=hb[i][:], s=xt[i][:]: e.copy(out=o, in_=s), reads=[("xt", i)], writes=[("hb", i)])
        for kq in range(NKQ):
            pb = tc_ % 2
            tc_ += 1
            for k8 in range(KQ):
                kt = kq * KQ + k8
                P.op("pe", lambda e, o=pT[pb][:, k8 * 128:(k8 + 1) * 128], s=hb[i][:, kt * 128:(kt + 1) * 128]: e.transpose(
                    out=o, in_=s, identity=ident[:]), reads=[("hb", i), "ident"], writes=[("pT", pb)])
            dst = hT[:, kq * KQ:(kq + 1) * KQ, tt * 128:(tt + 1) * 128]
            src = pT[pb][:].rearrange("p (k t) -> p k t", k=KQ)
            if kq % 2 == 0:
                P.op("dve", lambda e, dst=dst, src=src: e.tensor_copy(out=dst, in_=src), reads=[("pT", pb)], writes=[("hT", tt, kq)])
            else:
                P.op("act", lambda e, dst=dst, src=src: e.copy(out=dst, in_=src), reads=[("pT", pb)], writes=[("hT", tt, kq)])
    wv = w_out.rearrange("(kt p) n -> p kt n", p=128)
    mc = 0
    for cb in range(D // NB):
        c0 = cb * NB
        wi = cb % 2
        for kq in range(NKQ):
            P.dma("pool", lambda e, o=wb[wi][:, kq * KQ:(kq + 1) * KQ, :], s=wv[:, kq * KQ:(kq + 1) * KQ, c0:c0 + NB]: e.dma_start(
                out=o, in_=s), writes=[("wb", wi, kq)])
        for tt in range(TT):
            j = mc % 6
            s4 = mc % 4
            mc += 1
            for kt in range(KT):
                P.op("pe", lambda e, o=pM[j][:], l=hT[:, kt, tt * 128:(tt + 1) * 128], r=wb[wi][:, kt, :], kt=kt: e.matmul(
                    o, lhsT=l, rhs=r, start=(kt == 0), stop=(kt == KT - 1)),
                    reads=[("hT", tt, kt // KQ), ("wb", wi, kt // KQ)], writes=[("pM", j)])
            P.dma("sp", lambda e, o=rs[s4][:], s=x_res[tt * 128:(tt + 1) * 128, c0:c0 + NB]: e.dma_start(out=o, in_=s), writes=[("rs", s4)])
            P.op("dve", lambda e, o=stg[s4][:], a=pM[j][:], b_=rs[s4][:]: e.tensor_tensor(out=o, in0=a, in1=b_, op=ALU.add),
                 reads=[("pM", j), ("rs", s4)], writes=[("stg", s4)])
            P.dma("sp", lambda e, o=x_new[tt * 128:(tt + 1) * 128, c0:c0 + NB], s=stg[s4][:]: e.dma_start(out=o, in_=s),
                  reads=[("stg", s4)], writes=[("xn", tt, cb)])
    ph.close()


def phase_finalnorm(nc, x_in, g, out, ntiles=OWN // 128):
    ph = Phase(nc)
    P = ph.P
    D = D_MODEL
    gb = ph.sb([128, D], F32, "gb")
    xt = [ph.sb([128, D], F32, "xt") for _ in range(2)]
    junk = ph.sb([128, D], BF16, "junk")
    ss = ph.sb([128, ntiles], F32, "ss")
    P.dma("sp", lambda e: e.dma_start(out=gb[:], in_=g[0:1, :].partition_broadcast(128)), writes=["gb"])
    P.op("dve", lambda e: e.memset(ss[:], 0.0), writes=["ss"])
    for tt in range(ntiles):
        i = tt % 2
        P.dma("sp", lambda e, o=xt[i][:], s=x_in[tt * 128:(tt + 1) * 128, :]: e.dma_start(out=o, in_=s), writes=[("xt", i)])
        sc = ss[:, tt:tt + 1]
        P.op("act", lambda e, s=xt[i][:], sc=sc: e.activation(out=junk[:], in_=s, func=AF.Square, accum_out=sc),
             reads=[("xt", i), "ss"], writes=["junk", ("ssv", tt)])
        P.op("dve", lambda e, sc=sc: e.tensor_scalar(out=sc, in0=sc, scalar1=1.0 / D, scalar2=NORM_EPS, op0=ALU.mult, op1=ALU.add),
             reads=[("ssv", tt)], writes=[("ssv", tt)])
        P.op("act", lambda e, sc=sc: e.activation(out=sc, in_=sc, func=AF.Sqrt), reads=[("ssv", tt)], writes=[("ssv", tt)])
        P.op("dve", lambda e, sc=sc: e.reciprocal(out=sc, in_=sc), reads=[("ssv", tt)], writes=[("ssv", tt)])
        P.op("dve", lambda e, o=xt[i][:], sc=sc: e.scalar_tensor_tensor(out=o, in0=o, scalar=sc, in1=gb[:], op0=ALU.mult, op1=ALU.mult),
             reads=[("xt", i), ("ssv", tt), "gb"], writes=[("xt", i)])
        P.dma("sp", lambda e, o=out[tt * 128:(tt + 1) * 128, :], s=xt[i][:]: e.dma_start(out=o, in_=s), reads=[("xt", i)],
              writes=[("out", tt)])
    ph.close()


def own_tok(q):
    return ((4 * np.arange(8)[:, None] + q) * 128 + np.arange(128)[None, :]).reshape(-1)


def const_inputs():
    i = np.arange(128)
    sel4 = np.zeros((4, 4, 128), np.float32)
    for h in range(4):
        sel4[h, h, :] = 1.0
    same = (i[:, None] // 64) == (i[None, :] // 64)
    sl = ((i[None, :] < i[:, None]) & same).astype(np.float32)
    su = np.ascontiguousarray(sl.T)
    return {
        "ident": np.eye(128, dtype=ml_dtypes.bfloat16),
        "ident_f": np.eye(128, dtype=np.float32), "ident_b": np.eye(128, dtype=ml_dtypes.bfloat16),
        "tri_le": (i[:, None] <= i[None, :]).astype(np.float32), "ones_f": np.ones((128, 128), np.float32),
        "sel4": sel4, "negbig_lt": np.where(i[None, :] < i[:, None], NEG_BIG, 0.0).astype(np.float32),
        "ident30k_b": (30000.0 * np.eye(128)).astype(ml_dtypes.bfloat16),
        "iota512": np.arange(512, dtype=np.float32)[None, :].copy(),
        "pow2": (0.5 ** (np.arange(NBIS) + 1)).astype(np.float32)[None, :].copy(),
        "mask_sl": sl, "mask_su": su, "mask_u": su + np.eye(128, dtype=np.float32), "ones_bd": same.astype(np.float32),
    }


def layer_params(inp, l, q):
    c = np.ascontiguousarray
    p = {}
    p["qrel"] = (q * 128 + np.arange(128, dtype=np.float32))[:, None].copy()
    p["g"] = c(inp["norm_g"][l][None, :])
    p["mlp_ln_g"] = c(inp["mlp_ln_g"][l][None, :])
    p["mlp_ln_b"] = c(inp["mlp_ln_b"][l][None, :])
    p["mlp_wsT"] = c(inp["mlp_w_s"][l].transpose(2, 0, 1))
    p["mlp_bsT"] = c(inp["mlp_b_s"][l].T)
    cols = np.concatenate([np.arange(256 * q, 256 * (q + 1)), 1024 + np.arange(128 * q, 128 * (q + 1)),
                           1536 + np.arange(128 * q, 128 * (q + 1))])
    cw = inp["ssm_conv_w"][l][:, cols]
    cb = inp["ssm_conv_b"][l][cols]
    hs = slice(4 * q, 4 * q + 4)
    p["ssm_cw"] = c(cw.reshape(4, 4, 128).transpose(2, 1, 0))
    p["ssm_cb"] = c(cb.reshape(4, 128).T)
    p["ssm_dtb_t"] = c(np.tile(inp["ssm_dt_bias"][l][hs], 32)[None, :])
    p["ssm_alog_t"] = c(np.tile(inp["ssm_A_log"][l][hs], 32)[None, :])
    p["ssm_D"] = c(inp["ssm_D"][l][hs][None, :])
    p["ssm_ng"] = c(inp["ssm_norm_g"][l][256 * q:256 * (q + 1)][None, :])
    mu = inp["rwkv_mu"][l]
    hc = np.arange(256 * q, 256 * (q + 1))
    p["rw_mu_tm"] = c(np.concatenate([mu[1024 * k + hc] for k in range(4)])[None, :])
    p["rw_mu_fm"] = c(mu[4096:4224][:, None])
    p["rw_w2a2"] = c(np.concatenate([inp["rwkv_w2"][l][:, hc], inp["rwkv_a2"][l][:, hc]], axis=0))
    for nm, src in (("rw_w0", "rwkv_w0"), ("rw_a0", "rwkv_a0"), ("rw_kk", "rwkv_k_k"), ("rw_ka", "rwkv_k_a"), ("rw_rk", "rwkv_r_k"),
                    ("rw_gng", "rwkv_gn_g"), ("rw_gnb", "rwkv_gn_b")):
        p[nm] = c(inp[src][l].reshape(-1)[hc][None, :])
    return p


def _decl(nc, arrs):
    out = {}
    for n, a in arrs.items():
        dt = BF16 if a.dtype == ml_dtypes.bfloat16 else F32
        out[n] = nc.dram_tensor(n, list(a.shape), dt, kind="ExternalInput").ap()
    return out


def build_AB(sample_inputs):
    nc = bass.Bass("TRN2", target_bir_lowering=False)
    d = _decl(nc, sample_inputs)
    wts = {n: d["w_" + n] for n in GROUP_INFO}
    scr = make_scratch(nc)
    maskT = nc.dram_tensor("maskT", [128, NPAIR, 128], BF16).ap()
    y_own = nc.dram_tensor("y_own", [OWN, 2048], F32, kind="ExternalOutput").ap()
    y_all = nc.dram_tensor("y_all", [SEQ, 512], F32, kind="ExternalOutput").ap()
    phase_inproj(nc, d["x_all"], d["x_own"], d["g"], d["ident"], wts, scr)
    phase_indexer(nc, scr, d, maskT)
    phase_attn(nc, scr, d, maskT, y_own)
    phase_mlp(nc, scr, d, y_own)
    phase_ssm(nc, scr, d, y_all)
    phase_rwkv(nc, scr, d, y_all)
    return nc


def build_C(final):
    nc = bass.Bass("TRN2", target_bir_lowering=False)
    y_tok = nc.dram_tensor("y_tok", [OWN, D_MODEL], F32, kind="ExternalInput").ap()
    x_res = nc.dram_tensor("x_res", [OWN, D_MODEL], F32, kind="ExternalInput").ap()
    w_out = nc.dram_tensor("w_out", [D_MODEL, D_MODEL], F32, kind="ExternalInput").ap()
    ident = nc.dram_tensor("ident", [128, 128], BF16, kind="ExternalInput").ap()
    if final:
        g = nc.dram_tensor("gf", [1, D_MODEL], F32, kind="ExternalInput").ap()
        x_mid = nc.dram_tensor("x_mid", [OWN, D_MODEL], F32).ap()
        out = nc.dram_tensor("out", [OWN, D_MODEL], F32, kind="ExternalOutput").ap()
        phase_outproj(nc, y_tok, x_res, w_out, ident, x_mid)
        phase_finalnorm(nc, x_mid, g, out)
    else:
        out = nc.dram_tensor("out", [OWN, D_MODEL], F32, kind="ExternalOutput").ap()
        phase_outproj(nc, y_tok, x_res, w_out, ident, out)
    return nc


def kernel_unfused(**inp):
    inp = {k: np.asarray(v) for k, v in inp.items()}
    x = inp["x"]
    cst = const_inputs()
    otk = [own_tok(q) for q in range(NQ)]
    nc_ab = None
    for l in range(2):
        w_in = inp["w_in"][l]
        in_maps = []
        for c in range(NCORE):
            b, q = c // NQ, c % NQ
            m = dict(cst)
            m.update(layer_params(inp, l, q))
            m["x_all"] = np.ascontiguousarray(x[b])
            m["x_own"] = np.ascontiguousarray(x[b][otk[q]])
            for name, cols in col_groups(q).items():
                m["w_" + name] = np.ascontiguousarray(w_in[:, cols])
            in_maps.append(m)
        if nc_ab is None:
            nc_ab = build_AB(in_maps[0])
        res = run_bass_kernel_spmd(nc_ab, in_maps, core_ids=list(range(NCORE))).results
        in_maps_c = []
        for c in range(NCORE):
            b, q = c // NQ, c % NQ
            y_tok = np.empty((OWN, D_MODEL), np.float32)
            y_tok[:, 0:1024] = res[c]["y_own"][:, 0:1024]
            y_tok[:, 3072:4096] = res[c]["y_own"][:, 1024:2048]
            for q2 in range(NQ):
                ya = res[b * NQ + q2]["y_all"][otk[q]]
                y_tok[:, 1024 + 256 * q2:1024 + 256 * (q2 + 1)] = ya[:, 0:256]
                y_tok[:, 2048 + 256 * q2:2048 + 256 * (q2 + 1)] = ya[:, 256:512]
            m = {"y_tok": y_tok, "x_res": np.ascontiguousarray(x[b][otk[q]]), "w_out": np.ascontiguousarray(inp["w_out"][l]),
                 "ident": cst["ident"]}
            if l == 1:
                m["gf"] = np.ascontiguousarray(inp["final_norm_g"][None, :])
            in_maps_c.append(m)
        nc_c = build_C(final=(l == 1))
        resc = run_bass_kernel_spmd(nc_c, in_maps_c, core_ids=list(range(NCORE))).results
        xn = np.empty_like(x)
        for c in range(NCORE):
            b, q = c // NQ, c % NQ
            xn[b][otk[q]] = resc[c]["out"]
        x = xn
    return x


def fused_ginfo():
    gi = {"fmb_all": (1152, "fm", BF16, "all"), "tmb_all": (1024, "tm", BF16, "all"),
          "fmb_own": (2048, "fm", BF16, "all"), "tmf_own": (4112, "tm", F32, "all")}
    for q in range(NQ):
        gi[f"fmf_all{q}"] = (640, "fm", F32, "all")
        gi[f"tmf_all{q}"] = (1284, "tm", F32, "all")
    return gi


FUSED_BLOCKS = [(qb // 4 + 1, qb % 4, qb % 4 + 1) for qb in range(32)]
Q_KEYS = ("ssm_cw", "ssm_cb", "ssm_dtb_t", "ssm_alog_t", "ssm_D", "ssm_ng", "rw_mu_tm", "rw_mu_fm", "rw_w2a2", "rw_w0", "rw_a0",
          "rw_kk", "rw_ka", "rw_rk", "rw_gng", "rw_gnb")
L_KEYS = ("g", "mlp_ln_g", "mlp_ln_b", "mlp_wsT", "mlp_bsT")


def fused_inputs(inp, b):
    c = np.ascontiguousarray
    m = dict(const_inputs())
    m["qrel"] = c((np.arange(4, dtype=np.float32)[None, :] * 128 + np.arange(128, dtype=np.float32)[:, None]))
    m["x"] = c(inp["x"][b])
    m["gf"] = c(inp["final_norm_g"][None, :])
    for l in range(2):
        w_in = inp["w_in"][l]
        cg0 = col_groups(0)
        for name in ("fmb_all", "tmb_all", "fmb_own", "tmf_own"):
            m[f"w{l}_{name}"] = c(w_in[:, cg0[name]])
        for q in range(NQ):
            cg = col_groups(q)
            m[f"w{l}_fmf_all{q}"] = c(w_in[:, cg["fmf_all"]])
            m[f"w{l}_tmf_all{q}"] = c(w_in[:, cg["tmf_all"]])
            lp = layer_params(inp, l, q)
            for key in Q_KEYS:
                m[f"l{l}q{q}_{key}"] = lp[key]
            if q == 0:
                for key in L_KEYS:
                    m[f"l{l}_{key}"] = lp[key]
        m[f"wout{l}"] = c(inp["w_out"][l])
    return m


def build_fused(sample, upto=None, nlayers=2, skip=()):
    nc = bass.Bass("TRN2", target_bir_lowering=False)
    d = _decl(nc, sample)
    gi = fused_ginfo()
    scr = {}
    for name, (ncols, layout, dt, _) in gi.items():
        shape = [SEQ, ncols] if layout == "tm" else [ncols, SEQ]
        scr[name] = nc.dram_tensor("scr_" + name, shape, dt).ap()
    _, npairs = pair_offsets(FUSED_BLOCKS)
    maskT = nc.dram_tensor("maskT", [128, npairs, 128], BF16).ap()
    y_full = nc.dram_tensor("y_full", [SEQ, D_MODEL], F32).ap()
    xs = [d["x"], nc.dram_tensor("x1", [SEQ, D_MODEL], F32).ap(), nc.dram_tensor("x2", [SEQ, D_MODEL], F32).ap()]
    out = nc.dram_tensor("out", [SEQ, D_MODEL], F32, kind="ExternalOutput").ap()
    gnames = list(gi.keys())
    cnt = [0]

    def go():
        cnt[0] += 1
        return (upto is None or cnt[0] <= upto) and cnt[0] not in skip

    for l in range(nlayers):
        x_cur, x_nxt = xs[l], xs[l + 1]
        wts = {name: d[f"w{l}_{name}"] for name in gnames}
        plan = [(x_cur, p * 1024, [(n, p * 1024) for n in gnames]) for p in range(4)]
        if go():
            phase_inproj(nc, None, None, d[f"l{l}_g"], d["ident"], wts, scr, plan=plan, ginfo=gi)
        prm_l = dict(d)
        for key in L_KEYS:
            prm_l[key] = d[f"l{l}_{key}"]
        sc_own = {"fmb_own": scr["fmb_own"], "fmb_all": scr["fmb_all"], "tmf_own": scr["tmf_own"], "tmb_all": scr["tmb_all"]}
        if go():
            phase_indexer(nc, sc_own, prm_l, maskT, blocks=FUSED_BLOCKS)
        if go():
            phase_attn(nc, sc_own, prm_l, maskT, y_full, blocks=FUSED_BLOCKS)
        if go():
            phase_mlp(nc, sc_own, prm_l, y_full, nchunks=SEQ // 128, ycol0=3072)
        for q in range(NQ):
            prm_q = dict(d)
            for key in Q_KEYS:
                prm_q[key] = d[f"l{l}q{q}_{key}"]
            sc_q = {"fmf_all": scr[f"fmf_all{q}"], "tmf_all": scr[f"tmf_all{q}"]}
            if go():
                phase_ssm(nc, sc_q, prm_q, None, y_dst=y_full[:, 1024 + 256 * q:1024 + 256 * (q + 1)])
            if go():
                phase_rwkv(nc, sc_q, prm_q, None, y_dst=y_full[:, 2048 + 256 * q:2048 + 256 * (q + 1)])
        for p in range(4):
            rs_ = slice(p * 1024, (p + 1) * 1024)
            if go():
                phase_outproj(nc, y_full[rs_, :], x_cur[rs_, :], d[f"wout{l}"], d["ident"], x_nxt[rs_, :])
    if upto is None:
        phase_finalnorm(nc, xs[nlayers], d["gf"], out, ntiles=SEQ // 128)
    else:
        phase_finalnorm(nc, y_full, d["gf"], out, ntiles=SEQ // 128)
    return nc


def kernel_fused(**inp):
    inp = {k: np.asarray(v) for k, v in inp.items()}
    nb = inp["x"].shape[0]
    in_maps = [fused_inputs(inp, b) for b in range(nb)]
    nc = build_fused(in_maps[0])
    res = run_bass_kernel_spmd(nc, in_maps, core_ids=list(range(nb))).results
    return np.stack([res[b]["out"] for b in range(nb)], axis=0)


FUSED = True


def kernel(**inp):
    return kernel_fused(**inp) if FUSED else kernel_unfused(**inp)
```

```python
import contextlib
import numpy as np
import ml_dtypes
import concourse.bass as bass
import concourse.mybir as mybir
from concourse.bass_utils import run_bass_kernel_spmd

F32 = mybir.dt.float32
BF16 = mybir.dt.bfloat16
ALU = mybir.AluOpType
AF = mybir.ActivationFunctionType
AX = mybir.AxisListType

D_MODEL = 4096
SEQ = 4096
NCORE = 8
NQ = 4
OWN = SEQ // NQ
NORM_EPS = 1e-5
GN_EPS = 64e-5

COMPUTE = ("pe", "act", "dve", "pool")


SEM_ROLL = 30000


class SemState:
    def __init__(self, nc):
        self.nc = nc
        self.st = contextlib.ExitStack()
        self.sems = {}
        self.count = {e: 0 for e in COMPUTE}
        self.dma_slots = {}
        self.dma_rr = {}
        self.n_dma_slots = 8

    def handle(self, key):
        h = self.sems.get(key)
        if h is None:
            h = self.st.enter_context(self.nc.semaphore("s_" + "_".join(str(x) for x in key)))
            self.sems[key] = h
        return h


def semstate(nc):
    ss = getattr(nc, "_semstate", None)
    if ss is None:
        ss = SemState(nc)
        nc._semstate = ss
    return ss


class Prog:
    def __init__(self, nc):
        self.nc = nc
        self.ss = semstate(nc)
        self.streams = {e: [] for e in ("pe", "act", "dve", "pool", "sp")}
        self.last_writer = {}
        self.readers = {}
        self.waited = {}
        self.used_keys = []
        self.excl = set()

    def _semkey(self, key):
        if key not in self.used_keys:
            self.used_keys.append(key)
        return key

    def _deps_for(self, reads, writes, eng=None):
        deps = set()
        for b in reads:
            w = self.last_writer.get(b)
            if w is not None:
                deps.add(w)
        for b in writes:
            w = self.last_writer.get(b)
            if w is not None:
                deps.add(w)
            for r in self.readers.get(b, ()):
                if eng is not None and r[0][0] == eng:
                    continue
                deps.add(r)
        if eng == "pe":
            deps = {d for d in deps if d[0][0] != "pe"}
        return deps

    def _commit(self, tok, reads, writes):
        for b in reads:
            self.readers.setdefault(b, []).append(tok)
        for b in writes:
            self.last_writer[b] = tok
            self.readers[b] = []

    def _waits(self, eng, deps):
        waits = []
        for (k, v) in sorted(deps, key=lambda t: (str(t[0]), t[1])):
            if self.waited.get((eng, k), -1) >= v:
                continue
            self.waited[(eng, k)] = v
            self._semkey(k)
            waits.append((k, v))
        return waits

    def op(self, eng, fn, reads=(), writes=()):
        ex = [b for b in reads if (b[0] if isinstance(b, tuple) else b) in self.excl]
        if ex:
            writes = list(writes) + [b for b in ex if b not in writes]
        ss = self.ss
        idx = ss.count[eng]
        ss.count[eng] += 1
        ep = idx // SEM_ROLL
        key = self._semkey((eng, ep))
        deps = self._deps_for(reads, writes, eng=eng)
        waits = self._waits(eng, deps)
        self.streams[eng].append((waits, fn, (key, 1)))
        tok = (key, idx - ep * SEM_ROLL + 1)
        self._commit(tok, reads, writes)
        return tok

    def dma(self, eng, fn, reads=(), writes=()):
        ss = self.ss
        slots = ss.dma_slots.get(eng)
        if slots is None:
            slots = [[("dma", eng, i, 0), 0] for i in range(ss.n_dma_slots)]
            ss.dma_slots[eng] = slots
        rr = ss.dma_rr.get(eng, 0)
        ss.dma_rr[eng] = (rr + 1) % len(slots)
        slot = slots[rr]
        deps = self._deps_for(reads, writes)
        if slot[1] > 0:
            deps.add((slot[0], slot[1]))
        if slot[1] + 16 > SEM_ROLL:
            slot[0] = ("dma", eng, slot[0][2], slot[0][3] + 1)
            slot[1] = 0
        key = self._semkey(slot[0])
        waits = self._waits(eng, deps)
        slot[1] += 16
        self.streams[eng].append((waits, fn, (key, 16)))
        tok = (key, slot[1])
        self._commit(tok, reads, writes)
        return tok

    def finish(self, eng="sp"):
        toks = set()
        ss = self.ss
        for q, slots in ss.dma_slots.items():
            for key, cnt in slots:
                if cnt > 0:
                    toks.add((key, cnt))
        for e in COMPUTE:
            n = ss.count[e]
            if n > 0:
                ep = (n - 1) // SEM_ROLL
                toks.add(((e, ep), n - ep * SEM_ROLL))
        waits = self._waits(eng, toks)
        self.streams[eng].append((waits, None, None))

    def emit(self):
        nc = self.nc
        ss = self.ss
        sems = {k: ss.handle(k) for k in self.used_keys}
        with nc.Block() as block:

            def run(engobj, items):
                for waits, fn, inc in items:
                    for (k, v) in waits:
                        engobj.wait_ge(sems[k], v)
                    if fn is not None:
                        ins = fn(engobj)
                        ins.then_inc(sems[inc[0]], inc[1])

            @block.tensor
            def _(e):
                run(e, self.streams["pe"])

            @block.scalar
            def _(e):
                run(e, self.streams["act"])

            @block.vector
            def _(e):
                run(e, self.streams["dve"])

            @block.gpsimd
            def _(e):
                run(e, self.streams["pool"])

            @block.sync
            def _(e):
                run(e, self.streams["sp"])


class Phase:
    _count = [0]

    def __init__(self, nc):
        self.nc = nc
        self.st = contextlib.ExitStack()
        self.P = Prog(nc)
        self.n = 0
        Phase._count[0] += 1
        self.pid = Phase._count[0]

    def sb(self, shape, dt, name=None):
        self.n += 1
        return self.st.enter_context(self.nc.sbuf_tensor(f"{name or 't'}_{self.n}_{self.pid}", list(shape), dt))

    def ps(self, shape, dt, name=None):
        self.n += 1
        return self.st.enter_context(self.nc.psum_tensor(f"{name or 'p'}_{self.n}_{self.pid}", list(shape), dt))

    def dump(self, name, sb_ap, reads):
        dbg = getattr(self.nc, "_dbg", None)
        if not dbg or name not in dbg:
            return
        d = dbg[name]
        self.P.dma("sp", lambda e: e.dma_start(out=d, in_=sb_ap), reads=reads, writes=[("dbg", name)])

    def close(self):
        self.P.finish()
        self.P.emit()
        self.st.close()


ATT0, SSM0, RWKV0, MLP0 = 0, 5200, 8288, 12512


def col_groups(q):
    r = lambda a, n: np.arange(a, a + n)
    att_q, att_k, att_v, att_z = r(0, 1024), r(1024, 1024), r(2048, 1024), r(3072, 1024)
    att_qi, att_ki, att_wi = r(4096, 1024), r(5120, 64), r(5184, 16)
    ssm_z = r(SSM0 + 256 * q, 256)
    ssm_x = r(SSM0 + 1024 + 256 * q, 256)
    ssm_B = r(SSM0 + 2048 + 128 * q, 128)
    ssm_C = r(SSM0 + 2560 + 128 * q, 128)
    ssm_dt = r(SSM0 + 3072 + 4 * q, 4)
    rw = [r(RWKV0 + 1024 * i + 256 * q, 256) for i in range(4)]
    rw_wl, rw_al = r(RWKV0 + 4096, 64), r(RWKV0 + 4160, 64)
    mlp = [r(MLP0 + 1024 * i, 1024) for i in range(3)]
    cat = np.concatenate
    return {
        "fmb_all": cat([att_k, att_ki, att_ki]),
        "fmf_all": cat([ssm_x, ssm_B, ssm_C, rw_wl, rw_al]),
        "tmb_all": att_v,
        "tmf_all": cat([ssm_z, ssm_dt] + rw),
        "fmb_own": cat([att_q, att_qi]),
        "tmf_own": cat([att_z, att_wi] + mlp),
    }


GROUP_INFO = {
    "fmb_all": (1152, "fm", BF16, "all"),
    "fmf_all": (640, "fm", F32, "all"),
    "tmb_all": (1024, "tm", BF16, "all"),
    "tmf_all": (1284, "tm", F32, "all"),
    "fmb_own": (2048, "fm", BF16, "own"),
    "tmf_own": (4112, "tm", F32, "own"),
}


def phase_inproj(nc, x_all, x_own, g, ident_d, wts, scr, plan=None, ginfo=None):
    ph = Phase(nc)
    P = ph.P
    D = D_MODEL
    KT = D // 128
    T = 1024
    TT = T // 128
    KQ = 8
    NKQ = KT // KQ
    NB = 512
    ident = ph.sb([128, 128], BF16, "ident")
    hT = ph.sb([128, KT, T], BF16, "hT")
    xt = [ph.sb([128, D], F32, "xt") for _ in range(2)]
    hb = [ph.sb([128, D], BF16, "hb") for _ in range(2)]
    wb = [ph.sb([128, KT, NB], BF16, "wb") for _ in range(2)]
    stg = [ph.sb([128, NB], F32, "stg") for _ in range(4)]
    stgb = [ph.sb([128, NB], BF16, "stgb") for _ in range(4)]
    gb = ph.sb([128, D], F32, "gb")
    ss = ph.sb([128, 8 * 5], F32, "ss")
    rstd = ph.sb([128, 8 * 5], F32, "rstd")
    pT = [ph.ps([128, 1024], BF16, "pT") for _ in range(2)]
    pM = [ph.ps([128, 512], F32, "pM") for _ in range(6)]

    P.dma("sp", lambda e: e.dma_start(out=ident[:], in_=ident_d[:, :]), writes=["ident"])
    P.dma("sp", lambda e: e.dma_start(out=gb[:], in_=g[0:1, :].partition_broadcast(128)), writes=["gb"])
    cnt = {"t": 0, "m": 0, "w": 0, "x": 0}
    P.op("dve", lambda e: e.memset(ss[:], 0.0), writes=[("ss", i) for i in range(40)])

    def load_hT(x_ap, row0, pidx):
        for tt in range(TT):
            i = cnt["x"] % 2
            cnt["x"] += 1
            xb, hbb = xt[i], hb[i]
            sc = pidx * 8 + tt
            P.dma("sp", lambda e, xb=xb, tt=tt: e.dma_start(out=xb[:], in_=x_ap[row0 + tt * 128:row0 + (tt + 1) * 128, :]),
                  writes=[("xt", i)])
            P.op("act", lambda e, xb=xb, hbb=hbb, sc=sc: e.activation(out=hbb[:], in_=xb[:], func=AF.Square,
                                                                     accum_out=ss[:, sc:sc + 1]),
                 reads=[("xt", i)], writes=[("hb", i), ("ss", sc)])
            P.op("dve", lambda e, sc=sc: e.tensor_scalar(out=rstd[:, sc:sc + 1], in0=ss[:, sc:sc + 1],
                                                         scalar1=1.0 / D, scalar2=NORM_EPS, op0=ALU.mult, op1=ALU.add),
                 reads=[("ss", sc)], writes=[("rstd", sc)])
            P.op("act", lambda e, sc=sc: e.activation(out=rstd[:, sc:sc + 1], in_=rstd[:, sc:sc + 1], func=AF.Sqrt),
                 reads=[("rstd", sc)], writes=[("rstd", sc)])
            P.op("dve", lambda e, sc=sc: e.reciprocal(out=rstd[:, sc:sc + 1], in_=rstd[:, sc:sc + 1]),
                 reads=[("rstd", sc)], writes=[("rstd", sc)])
            P.op("dve", lambda e, xb=xb, hbb=hbb, sc=sc: e.scalar_tensor_tensor(
                out=hbb[:], in0=xb[:], scalar=rstd[:, sc:sc + 1], in1=gb[:], op0=ALU.mult, op1=ALU.mult),
                reads=[("xt", i), ("rstd", sc), "gb", ("hb", i)], writes=[("hb", i)])
            for kq in range(NKQ):
                pb = cnt["t"] % 2
                cnt["t"] += 1
                for k8 in range(KQ):
                    kt = kq * KQ + k8
                    P.op("pe", lambda e, pb=pb, k8=k8, kt=kt, hbb=hbb: e.transpose(
                        out=pT[pb][:, k8 * 128:(k8 + 1) * 128], in_=hbb[:, kt * 128:(kt + 1) * 128], identity=ident[:]),
                        reads=[("hb", i), "ident"], writes=[("pT", pb)])
                dst = hT[:, kq * KQ:(kq + 1) * KQ, tt * 128:(tt + 1) * 128]
                src = pT[pb][:].rearrange("p (k t) -> p k t", k=KQ)
                if kq % 2 == 0:
                    P.op("dve", lambda e, dst=dst, src=src: e.tensor_copy(out=dst, in_=src),
                         reads=[("pT", pb)], writes=[("hT", tt, kq)])
                else:
                    P.op("act", lambda e, dst=dst, src=src: e.copy(out=dst, in_=src),
                         reads=[("pT", pb)], writes=[("hT", tt, kq)])

    def evac_store(j, n_part, nfree, is_bf16, dst_ap, okey):
        s = cnt["m"] % 4
        use_dve = (cnt["m"] % 2 == 0)
        cnt["m"] += 1
        sbuf = (stgb if is_bf16 else stg)[s]
        skey = ("stgb" if is_bf16 else "stg", s)
        if use_dve:
            P.op("dve", lambda e: e.tensor_copy(out=sbuf[0:n_part, 0:nfree], in_=pM[j][0:n_part, 0:nfree]),
                 reads=[("pM", j)], writes=[skey])
        else:
            P.op("act", lambda e: e.copy(out=sbuf[0:n_part, 0:nfree], in_=pM[j][0:n_part, 0:nfree]),
                 reads=[("pM", j)], writes=[skey])
        P.dma("sp", lambda e: e.dma_start(out=dst_ap, in_=sbuf[0:n_part, 0:nfree]), reads=[skey], writes=[okey])

    def do_group(name, tok0):
        ncols, layout, dt, _ = (ginfo or GROUP_INFO)[name]
        w = wts[name]
        dst = scr[name]
        wv = w.rearrange("(kt p) n -> p kt n", p=128)
        is_bf = (dt == BF16)
        for c0 in range(0, ncols, NB):
            nb = min(NB, ncols - c0)
            wi = cnt["w"] % 2
            cnt["w"] += 1
            wbb = wb[wi]
            for kq in range(NKQ):
                P.dma("pool", lambda e, wbb=wbb, kq=kq, c0=c0, nb=nb: e.dma_start(
                    out=wbb[:, kq * KQ:(kq + 1) * KQ, 0:nb], in_=wv[:, kq * KQ:(kq + 1) * KQ, c0:c0 + nb]),
                    writes=[("wb", wi, kq)])
            if layout == "tm":
                for tt in range(TT):
                    j = cnt["m"] % 6
                    for kt in range(KT):
                        P.op("pe", lambda e, j=j, kt=kt, tt=tt, wbb=wbb, nb=nb: e.matmul(
                            pM[j][:, 0:nb], lhsT=hT[:, kt, tt * 128:(tt + 1) * 128], rhs=wbb[:, kt, 0:nb],
                            start=(kt == 0), stop=(kt == KT - 1)),
                            reads=[("hT", tt, kt // KQ), ("wb", wi, kt // KQ)], writes=[("pM", j)])
                    r0 = tok0 + tt * 128
                    evac_store(j, 128, nb, is_bf, dst[r0:r0 + 128, c0:c0 + nb], (name, "o", tok0, tt, c0))
            else:
                for ct in range(nb // 128):
                    for th in range(T // 512):
                        j = cnt["m"] % 6
                        for kt in range(KT):
                            P.op("pe", lambda e, j=j, kt=kt, th=th, ct=ct, wbb=wbb: e.matmul(
                                pM[j][:, 0:512], lhsT=wbb[:, kt, ct * 128:(ct + 1) * 128],
                                rhs=hT[:, kt, th * 512:(th + 1) * 512], start=(kt == 0), stop=(kt == KT - 1)),
                                reads=[("hT", 4 * th, kt // KQ), ("hT", 4 * th + 1, kt // KQ), ("hT", 4 * th + 2, kt // KQ),
                                       ("hT", 4 * th + 3, kt // KQ), ("wb", wi, kt // KQ)], writes=[("pM", j)])
                        cc = c0 + ct * 128
                        t0 = tok0 + th * 512
                        evac_store(j, 128, 512, is_bf, dst[cc:cc + 128, t0:t0 + 512], (name, "o", tok0, th, cc))

    if plan is None:
        plan = [(x_all, p * 1024, [(n, p * 1024) for n in ("fmb_all", "fmf_all", "tmb_all", "tmf_all")]) for p in range(4)]
        plan.append((x_own, 0, [("fmb_own", 0), ("tmf_own", 0)]))
    for pidx, (xsrc, row0, glist) in enumerate(plan):
        load_hT(xsrc, row0, pidx)
        for name, tok0 in glist:
            do_group(name, tok0)
    ph.close()


def make_scratch(nc, kind=None):
    scr = {}
    for name, (ncols, layout, dt, which) in GROUP_INFO.items():
        ntok = SEQ if which == "all" else OWN
        shape = [ntok, ncols] if layout == "tm" else [ncols, ntok]
        if kind:
            scr[name] = nc.dram_tensor("scr_" + name, shape, dt, kind=kind).ap()
        else:
            scr[name] = nc.dram_tensor("scr_" + name, shape, dt).ap()
    return scr


TMF_OWN_ATTZ, TMF_OWN_WI, TMF_OWN_U, TMF_OWN_V, TMF_OWN_Z = 0, 1024, 1040, 2064, 3088


def phase_mlp(nc, scr, prm, y_own, nchunks=OWN // 128, ycol0=1024):
    ph = Phase(nc)
    P = ph.P
    src = scr["tmf_own"]
    W = 1024
    gb = ph.sb([128, W], F32, "lng")
    bb = ph.sb([128, W], F32, "lnb")
    wsT = ph.sb([128, 8, 128], F32, "wsT")
    wcT = ph.sb([128, 8, 128], BF16, "wcT")
    tri = ph.sb([128, 128], F32, "tri")
    bsT = ph.sb([128, 8], F32, "bsT")
    bbc = ph.sb([128, 8, 128], F32, "bbc")
    zero = ph.sb([128, 128], F32, "zero")
    NBUF = 2
    ut = [ph.sb([128, W], F32, "u") for _ in range(NBUF)]
    vt = [ph.sb([128, W], F32, "v") for _ in range(NBUF)]
    zt = [ph.sb([128, W], F32, "z") for _ in range(NBUF)]
    vn = [ph.sb([128, W], F32, "vn") for _ in range(NBUF)]
    vnb = [ph.sb([128, W], BF16, "vnb") for _ in range(NBUF)]
    junk = ph.sb([128, W], BF16, "junk")
    t1 = [ph.sb([128, W], F32, "t1") for _ in range(NBUF)]
    st = ph.sb([128, 8 * nchunks], F32, "stats")
    pV = [ph.ps([128, 512], F32, "pV") for _ in range(4)]

    P.dma("sp", lambda e: e.dma_start(out=gb[:], in_=prm["mlp_ln_g"][0:1, :].partition_broadcast(128)), writes=["gb"])
    P.dma("sp", lambda e: e.dma_start(out=bb[:], in_=prm["mlp_ln_b"][0:1, :].partition_broadcast(128)), writes=["bb"])
    P.dma("sp", lambda e: e.dma_start(out=wsT[:], in_=prm["mlp_wsT"][:, :, :]), writes=["wsT"])
    P.dma("sp", lambda e: e.dma_start(out=tri[:], in_=prm["tri_le"][:, :]), writes=["tri"])
    P.dma("sp", lambda e: e.dma_start(out=bsT[:], in_=prm["mlp_bsT"][:, :]), writes=["bsT"])
    P.op("dve", lambda e: e.memset(zero[:], 0.0), writes=["zero"])
    P.op("dve", lambda e: e.memset(st[:], 0.0), writes=[("st", c) for c in range(nchunks)])
    for g in range(8):
        P.op("dve", lambda e, g=g: e.tensor_tensor(out=wcT[:, g, :], in0=wsT[:, g, :], in1=tri[:], op=ALU.mult),
             reads=["wsT", "tri"], writes=["wcT"])
        P.op("dve", lambda e, g=g: e.tensor_scalar(out=bbc[:, g, :], in0=zero[:], scalar1=bsT[:, g:g + 1], scalar2=None,
                                                   op0=ALU.add), reads=["zero", "bsT"], writes=["bbc"])
    for c in range(nchunks):
        i = c % NBUF
        r0 = c * 128
        P.dma("sp", lambda e, i=i, r0=r0: e.dma_start(out=ut[i][:], in_=src[r0:r0 + 128, TMF_OWN_U:TMF_OWN_U + W]),
              writes=[("u", i)])
        P.dma("sp", lambda e, i=i, r0=r0: e.dma_start(out=vt[i][:], in_=src[r0:r0 + 128, TMF_OWN_V:TMF_OWN_V + W]),
              writes=[("v", i)])
        P.dma("sp", lambda e, i=i, r0=r0: e.dma_start(out=zt[i][:], in_=src[r0:r0 + 128, TMF_OWN_Z:TMF_OWN_Z + W]),
              writes=[("z", i)])
        s0 = c * 8
        P.op("act", lambda e, i=i, s0=s0: e.activation(out=junk[:], in_=vt[i][:], func=AF.Square,
                                                       accum_out=st[:, s0 + 1:s0 + 2]),
             reads=[("v", i)], writes=["junk", ("st", c)])
        P.op("dve", lambda e, i=i, s0=s0: e.reduce_sum(out=st[:, s0:s0 + 1], in_=vt[i][:], axis=AX.X),
             reads=[("v", i), ("st", c)], writes=[("st", c)])
        P.op("dve", lambda e, s0=s0: e.tensor_scalar(out=st[:, s0:s0 + 2], in0=st[:, s0:s0 + 2], scalar1=1.0 / W,
                                                     scalar2=None, op0=ALU.mult), reads=[("st", c)], writes=[("st", c)])
        P.op("dve", lambda e, s0=s0: e.tensor_tensor(out=st[:, s0 + 2:s0 + 3], in0=st[:, s0:s0 + 1], in1=st[:, s0:s0 + 1],
                                                     op=ALU.mult), reads=[("st", c)], writes=[("st", c)])
        P.op("dve", lambda e, s0=s0: e.tensor_tensor(out=st[:, s0 + 3:s0 + 4], in0=st[:, s0 + 1:s0 + 2],
                                                     in1=st[:, s0 + 2:s0 + 3], op=ALU.subtract),
             reads=[("st", c)], writes=[("st", c)])
        P.op("dve", lambda e, s0=s0: e.tensor_scalar(out=st[:, s0 + 3:s0 + 4], in0=st[:, s0 + 3:s0 + 4], scalar1=NORM_EPS,
                                                     scalar2=None, op0=ALU.add), reads=[("st", c)], writes=[("st", c)])
        P.op("act", lambda e, s0=s0: e.activation(out=st[:, s0 + 3:s0 + 4], in_=st[:, s0 + 3:s0 + 4], func=AF.Sqrt),
             reads=[("st", c)], writes=[("st", c)])
        P.op("dve", lambda e, s0=s0: e.reciprocal(out=st[:, s0 + 3:s0 + 4], in_=st[:, s0 + 3:s0 + 4]),
             reads=[("st", c)], writes=[("st", c)])
        P.op("dve", lambda e, i=i, s0=s0: e.tensor_scalar(out=vn[i][:], in0=vt[i][:], scalar1=st[:, s0:s0 + 1],
                                                          scalar2=st[:, s0 + 3:s0 + 4], op0=ALU.subtract, op1=ALU.mult),
             reads=[("v", i), ("st", c)], writes=[("vn", i)])
        P.op("pool", lambda e, i=i: e.tensor_tensor(out=vn[i][:], in0=vn[i][:], in1=gb[:], op=ALU.mult),
             reads=[("vn", i), "gb"], writes=[("vn", i)])
        P.op("dve", lambda e, i=i: e.tensor_tensor(out=vnb[i][:], in0=vn[i][:], in1=bb[:], op=ALU.add),
             reads=[("vn", i), "bb"], writes=[("vnb", i)])
        if c == 0:
            ph.dump("mlp_st", st[:, 0:8], [("st", c)])
            ph.dump("mlp_vn", vn[i][:], [("vn", i)])
            ph.dump("mlp_vnb", vnb[i][:], [("vnb", i)])
        for hf in range(2):
            pj = (2 * c + hf) % 4
            for g4 in range(4):
                g = hf * 4 + g4
                P.op("pe", lambda e, pj=pj, g=g, g4=g4, i=i: e.matmul(
                    pV[pj][:, g4 * 128:(g4 + 1) * 128], lhsT=wcT[:, g, :], rhs=vnb[i][:, g * 128:(g + 1) * 128],
                    start=True, stop=True), reads=["wcT", ("vnb", i)], writes=[("pV", pj)])
            P.op("dve", lambda e, pj=pj, hf=hf, i=i: e.tensor_tensor(
                out=t1[i][:, hf * 512:(hf + 1) * 512], in0=pV[pj][:],
                in1=bbc[:, hf * 4:(hf + 1) * 4, :].rearrange("p g d -> p (g d)"), op=ALU.add),
                reads=[("pV", pj), "bbc"], writes=[("t1", i, hf)])
        if c == 0:
            ph.dump("mlp_t1", t1[i][:], [("t1", i, 0), ("t1", i, 1)])
        P.op("pool", lambda e, i=i: e.tensor_tensor(out=t1[i][:], in0=t1[i][:], in1=ut[i][:], op=ALU.mult),
             reads=[("t1", i, 0), ("t1", i, 1), ("u", i)], writes=[("t1", i, 0), ("t1", i, 1)])
        P.op("act", lambda e, i=i: e.activation(out=zt[i][:], in_=zt[i][:], func=AF.Silu),
             reads=[("z", i)], writes=[("z", i)])
        P.op("dve", lambda e, i=i: e.tensor_tensor(out=t1[i][:], in0=t1[i][:], in1=zt[i][:], op=ALU.mult),
             reads=[("t1", i, 0), ("t1", i, 1), ("z", i)], writes=[("t1", i, 0), ("t1", i, 1)])
        P.dma("sp", lambda e, i=i, r0=r0: e.dma_start(out=y_own[r0:r0 + 128, ycol0:ycol0 + 1024], in_=t1[i][:]),
              reads=[("t1", i, 0), ("t1", i, 1)], writes=[("y_mlp", c)])
    ph.close()


def record_ops(P, rec):
    real_op, real_dma = P.op, P.dma
    P.op = lambda eng, fn, reads=(), writes=(): rec.append((real_op, eng, fn, list(reads), list(writes)))
    P.dma = lambda eng, fn, reads=(), writes=(): rec.append((real_dma, eng, fn, list(reads), list(writes)))

    def restore():
        P.op, P.dma = real_op, real_dma
    return restore


def pipeline_merge(recs, nstage):
    allp = []
    for ops, marks in recs:
        m = [0] + list(marks[:nstage - 1])
        while len(m) < nstage:
            m.append(len(ops))
        m.append(len(ops))
        allp.append([ops[m[k]:m[k + 1]] for k in range(nstage)])
    n = len(recs)
    for t in range(n + nstage - 1):
        lists = []
        for k in range(nstage):
            bi = t - k
            if 0 <= bi < n and allp[bi][k]:
                lists.append(allp[bi][k])
        merged = []
        for li, L in enumerate(lists):
            for pi, item in enumerate(L):
                merged.append(((pi + 0.5) / len(L), li, pi, item))
        merged.sort(key=lambda t_: (t_[0], t_[1]))
        for _, _, _, (f, eng, fn, reads, writes) in merged:
            f(eng, fn, reads=reads, writes=writes)


FMF_X, FMF_B, FMF_C, FMF_WL, FMF_AL = 0, 256, 384, 512, 576
TMF_SSMZ, TMF_DT, TMF_R, TMF_K, TMF_V, TMF_Z = 0, 256, 260, 516, 772, 1028
NEG_BIG = -30000.0


def phase_ssm(nc, scr, prm, y_all, y_dst=None):
    ph = Phase(nc)
    P = ph.P
    fm = scr["fmf_all"]
    tm = scr["tmf_all"]
    if y_dst is None:
        y_dst = y_all[:, 0:256]
    NCH = SEQ // 128
    SC = 512
    identf = ph.sb([128, 128], F32, "identf")
    identb = ph.sb([128, 128], BF16, "identb")
    tri = ph.sb([128, 128], F32, "tri")
    onesf = ph.sb([128, 128], F32, "ones")
    sel4 = ph.sb([4, 4, 128], F32, "sel4")
    negb = ph.sb([128, 128], F32, "negb")
    cw = ph.sb([128, 4, 4], F32, "cw")
    cb = ph.sb([128, 4], F32, "cb")
    dtb = ph.sb([128, 128], F32, "dtb")
    alog = ph.sb([128, 128], F32, "alog")
    Dbc = ph.sb([128, 4], F32, "Dbc")
    ngb = ph.sb([128, 256], F32, "ngb")
    dt = ph.sb([128, NCH, 4], F32, "dt")
    aa = ph.sb([128, NCH, 4], F32, "aa")
    cum = ph.sb([128, NCH * 4], F32, "cum")
    cumL = ph.sb([128, NCH * 4], F32, "cumL")
    ecum = ph.sb([128, NCH * 4], F32, "ecum")
    ncum = ph.sb([128, NCH * 4], F32, "ncum")
    ecumL = ph.sb([128, NCH * 4], F32, "ecumL")
    dtd = ph.sb([128, NCH * 4], F32, "dtd")
    win = [ph.sb([128, SC + 3], F32, "win") for _ in range(4)]
    acc = [ph.sb([128, SC], F32, "acc") for _ in range(2)]
    xsT = [ph.sb([128, 2, SC], F32, "xsT") for _ in range(2)]
    BT = [ph.sb([128, SC], BF16, "BT") for _ in range(2)]
    CT = [ph.sb([128, SC], BF16, "CT") for _ in range(2)]
    xtok = [ph.sb([128, 256], F32, "xtok") for _ in range(2)]
    xdt = [ph.sb([128, 256], BF16, "xdt") for _ in range(2)]
    xdd = [ph.sb([128, 256], BF16, "xdd") for _ in range(2)]
    Btok = [ph.sb([128, 128], BF16, "Btok") for _ in range(2)]
    cumT = [ph.sb([4, 128], F32, "cumT") for _ in range(2)]
    LT = [ph.sb([128, 4, 128], F32, "LT") for _ in range(2)]
    MT = [ph.sb([128, 4, 128], BF16, "MT") for _ in range(2)]
    ydsb = [ph.sb([128, 256], F32, "ydsb") for _ in range(2)]
    dsk = [ph.sb([128, 256], F32, "dsk") for _ in range(2)]
    yt = [ph.sb([128, 256], F32, "yt") for _ in range(2)]
    zt = [ph.sb([128, 256], F32, "zt") for _ in range(2)]
    junk = ph.sb([128, 256], F32, "junk")
    nst = ph.sb([128, NCH], F32, "nst")
    hT = ph.sb([128, 256], F32, "hT")
    hTb = ph.sb([128, 256], BF16, "hTb")
    pTr = [ph.ps([128, 512], F32, "pTr") for _ in range(1)]
    pTb = [ph.ps([128, 1024], BF16, "pTb") for _ in range(1)]
    pSm = [ph.ps([128, 512], F32, "pSm") for _ in range(1)]
    pL = [ph.ps([128, 512], F32, "pL") for _ in range(1)]
    pCB = [ph.ps([128, 512], F32, "pCB") for _ in range(1)]
    pYd = [ph.ps([128, 512], F32, "pYd") for _ in range(1)]
    pYo = [ph.ps([128, 512], F32, "pYo") for _ in range(1)]
    pS = [ph.ps([128, 512], F32, "pS") for _ in range(1)]

    ld = lambda dst, src, key: P.dma("sp", lambda e: e.dma_start(out=dst, in_=src), writes=[key])
    ld(identf[:], prm["ident_f"][:, :], "identf")
    ld(identb[:], prm["ident_b"][:, :], "identb")
    ld(tri[:], prm["tri_le"][:, :], "tri")
    ld(onesf[:], prm["ones_f"][:, :], "ones")
    ld(sel4[:], prm["sel4"][:, :, :], "sel4")
    ld(negb[:], prm["negbig_lt"][:, :], "negb")
    ld(cw[:], prm["ssm_cw"][:, :, :], "cw")
    ld(cb[:], prm["ssm_cb"][:, :], "cb")
    ld(dtb[:], prm["ssm_dtb_t"][0:1, :].partition_broadcast(128), "dtb")
    ld(alog[:], prm["ssm_alog_t"][0:1, :].partition_broadcast(128), "alog")
    ld(Dbc[:], prm["ssm_D"][0:1, :].partition_broadcast(128), "Dbc")
    ld(ngb[:], prm["ssm_ng"][0:1, :].partition_broadcast(128), "ngb")
    ld(dt[:], tm[:, TMF_DT:TMF_DT + 4].rearrange("(c l) h -> l c h", l=128), "dt")
    dtf = dt[:].rearrange("p c h -> p (c h)")
    aaf = aa[:].rearrange("p c h -> p (c h)")
    P.op("dve", lambda e: e.tensor_tensor(out=dtf, in0=dtf, in1=dtb[:], op=ALU.add), reads=["dt", "dtb"], writes=["dt"])
    P.op("act", lambda e: e.activation(out=dtf, in_=dtf, func=AF.Exp), reads=["dt"], writes=["dt"])
    P.op("act", lambda e: e.activation(out=dtf, in_=dtf, func=AF.Ln, bias=1.0, scale=1.0), reads=["dt"], writes=["dt"])
    P.op("act", lambda e: e.activation(out=alog[:], in_=alog[:], func=AF.Exp), reads=["alog"], writes=["alog"])
    P.op("dve", lambda e: e.scalar_tensor_tensor(out=aaf, in0=dtf, scalar=-1.0, in1=alog[:], op0=ALU.mult, op1=ALU.mult),
         reads=["dt", "alog"], writes=["aa"])
    P.op("pe", lambda e: e.matmul(pSm[0][:, 0:128], lhsT=tri[:], rhs=aaf, start=True, stop=True),
         reads=["tri", "aa"], writes=["pSm"])
    P.op("dve", lambda e: e.tensor_copy(out=cum[:], in_=pSm[0][:, 0:128]), reads=["pSm"], writes=["cum"])
    P.op("pe", lambda e: e.matmul(pSm[0][:, 128:256], lhsT=onesf[:], rhs=aaf, start=True, stop=True),
         reads=["ones", "aa", "cum"], writes=["pSm"])
    P.op("dve", lambda e: e.tensor_copy(out=cumL[:], in_=pSm[0][:, 128:256]), reads=["pSm"], writes=["cumL"])
    P.op("act", lambda e: e.activation(out=ecum[:], in_=cum[:], func=AF.Exp), reads=["cum"], writes=["ecum"])
    P.op("pool", lambda e: e.tensor_scalar(out=ncum[:], in0=cum[:], scalar1=-1.0, scalar2=None, op0=ALU.mult), reads=["cum"], writes=["ncum"])
    P.op("act", lambda e: e.activation(out=ecumL[:], in_=cumL[:], func=AF.Exp), reads=["cumL"], writes=["ecumL"])
    P.op("dve", lambda e: e.tensor_tensor(out=dtd[:], in0=cumL[:], in1=cum[:], op=ALU.subtract),
         reads=["cumL", "cum"], writes=["dtd"])
    P.op("act", lambda e: e.activation(out=dtd[:], in_=dtd[:], func=AF.Exp), reads=["dtd"], writes=["dtd"])
    P.op("dve", lambda e: e.tensor_tensor(out=dtd[:], in0=dtd[:], in1=dtf, op=ALU.mult), reads=["dtd", "dt"], writes=["dtd"])
    P.op("dve", lambda e: e.memset(hT[:], 0.0), writes=["hT"])
    P.op("dve", lambda e: e.memset(hTb[:], 0.0), writes=["hTb"])
    P.op("dve", lambda e: e.memset(nst[:], 0.0), writes=["nst"])

    recs = []
    rec_cur = [None]

    def new_unit():
        rec = []
        recs.append([rec, []])
        rec_cur[0] = rec
        return record_ops(P, rec)

    for s in range(SEQ // SC):
        t0 = s * SC
        si = s % 2
        restore = new_unit()
        for j in range(4):
            wj = win[j]
            if s == 0:
                P.op("dve", lambda e, wj=wj: e.memset(wj[:, 0:3], 0.0), writes=[("win", j)])
                P.dma("sp", lambda e, wj=wj, j=j: e.dma_start(out=wj[:, 3:SC + 3], in_=fm[j * 128:(j + 1) * 128, 0:SC]),
                      writes=[("win", j)])
            else:
                P.dma("sp", lambda e, wj=wj, j=j, t0=t0: e.dma_start(out=wj[:], in_=fm[j * 128:(j + 1) * 128, t0 - 3:t0 + SC]),
                      writes=[("win", j)])
            ac = acc[j % 2]
            ak = ("acc", j % 2)
            P.op("dve", lambda e, ac=ac, wj=wj, j=j: e.tensor_scalar(out=ac[:], in0=wj[:, 0:SC], scalar1=cw[:, j, 0:1],
                                                                    scalar2=None, op0=ALU.mult),
                 reads=[("win", j), "cw"], writes=[ak])
            for tap in range(1, 4):
                P.op("dve", lambda e, ac=ac, wj=wj, j=j, tap=tap: e.scalar_tensor_tensor(
                    out=ac[:], in0=wj[:, tap:tap + SC], scalar=cw[:, j, tap:tap + 1], in1=ac[:], op0=ALU.mult, op1=ALU.add),
                    reads=[("win", j), "cw", ak], writes=[ak])
            if j < 2:
                dst, dk = xsT[si][:, j, :], ("xsT", si, j)
            elif j == 2:
                dst, dk = BT[si][:], ("BT", si)
            else:
                dst, dk = CT[si][:], ("CT", si)
            P.op("act", lambda e, ac=ac, dst=dst, j=j: e.activation(out=dst, in_=ac[:], func=AF.Silu, bias=cb[:, j:j + 1], scale=1.0),
                 reads=[ak, "cb"], writes=[dk])
        if s < 2:
            ph.dump(f"ssm_xsT{s}", xsT[si][:], [("xsT", si, 0), ("xsT", si, 1)])
            ph.dump(f"ssm_BT{s}", BT[si][:], [("BT", si)])
            ph.dump(f"ssm_CT{s}", CT[si][:], [("CT", si)])
        for cc in range(SC // 128):
            c = s * (SC // 128) + cc
            ci = c % 2
            lo = cc * 128
            c4 = c * 4
            if cc > 0:
                restore = new_unit()
            for j in range(2):
                P.op("pe", lambda e, si=si, j=j, lo=lo: e.transpose(out=pTr[0][:, j * 128:(j + 1) * 128], in_=xsT[si][:, j, lo:lo + 128],
                                                            identity=identf[:]),
                     reads=[("xsT", si, j), "identf"], writes=["pTr"])
            P.op("act", lambda e, ci=ci: e.copy(out=xtok[ci][:], in_=pTr[0][:, 0:256]), reads=["pTr"], writes=[("xtok", ci)])
            P.op("pe", lambda e, si=si, lo=lo: e.transpose(out=pTb[0][:, 0:128], in_=BT[si][:, lo:lo + 128], identity=identb[:]),
                 reads=[("BT", si), "identb"], writes=["pTb"])
            P.op("act", lambda e, ci=ci: e.copy(out=Btok[ci][:], in_=pTb[0][:, 0:128]), reads=["pTb"], writes=[("Btok", ci)])
            P.op("pe", lambda e, c=c: e.matmul(pSm[0][0:4, 256:384], lhsT=aa[:, c, :], rhs=tri[:], start=True, stop=True),
                 reads=["aa", "tri"], writes=["pSm"])
            P.op("dve", lambda e, ci=ci: e.tensor_copy(out=cumT[ci][:], in_=pSm[0][0:4, 256:384]),
                 reads=["pSm"], writes=[("cumT", ci)])
            for h in range(4):
                P.op("pe", lambda e, h=h, ci=ci: e.matmul(pL[0][:, h * 128:(h + 1) * 128], lhsT=sel4[:, h, :], rhs=cumT[ci][:],
                                                          start=True, stop=False),
                     reads=["sel4", ("cumT", ci)], writes=["pL"])
                P.op("pe", lambda e, h=h: e.matmul(pL[0][:, h * 128:(h + 1) * 128], lhsT=identf[:], rhs=negb[:],
                                                   start=False, stop=True),
                     reads=["identf", "negb"], writes=["pL"])
            for h in range(4):
                P.op("act", lambda e, h=h, ci=ci, c4=c4: e.activation(
                    out=LT[ci][:, h, :], in_=pL[0][:, h * 128:(h + 1) * 128], func=AF.Exp, bias=ncum[:, c4 + h:c4 + h + 1], scale=1.0),
                    reads=["pL", "ncum"], writes=[("LT", ci)])
            P.op("pe", lambda e, si=si, lo=lo: e.matmul(pCB[0][:, 0:128], lhsT=BT[si][:, lo:lo + 128], rhs=CT[si][:, lo:lo + 128],
                                                 start=True, stop=True), reads=[("BT", si), ("CT", si)], writes=["pCB"])
            P.op("dve", lambda e, ci=ci: e.tensor_tensor(out=MT[ci][:], in0=pCB[0][:, 0:128].unsqueeze(1).to_broadcast([128, 4, 128]),
                                                         in1=LT[ci][:], op=ALU.mult), reads=["pCB", ("LT", ci)], writes=[("MT", ci)])
            h4 = lambda ap: ap.rearrange("p (h d) -> p h d", h=4)
            P.op("dve", lambda e, ci=ci, c4=c4: e.tensor_tensor(out=h4(xdt[ci][:]), in0=h4(xtok[ci][:]),
                                                                in1=dtf[:, c4:c4 + 4].unsqueeze(2).to_broadcast([128, 4, 64]), op=ALU.mult),
                 reads=[("xtok", ci), "dt"], writes=[("xdt", ci)])
            P.op("pool", lambda e, ci=ci, c4=c4: e.tensor_tensor(out=h4(xdd[ci][:]), in0=h4(xtok[ci][:]),
                                                                 in1=dtd[:, c4:c4 + 4].unsqueeze(2).to_broadcast([128, 4, 64]), op=ALU.mult),
                 reads=[("xtok", ci), "dtd"], writes=[("xdd", ci)])
            for h in range(4):
                P.op("pe", lambda e, h=h, ci=ci: e.matmul(pYd[0][:, h * 64:(h + 1) * 64], lhsT=MT[ci][:, h, :],
                                                          rhs=xdt[ci][:, h * 64:(h + 1) * 64], start=True, stop=True),
                     reads=[("MT", ci), ("xdt", ci)], writes=["pYd"])
            recs[-1][1].append(len(rec_cur[0]))
            for h in range(4):
                P.op("pe", lambda e, si=si, h=h, lo=lo: e.matmul(pYo[0][:, h * 64:(h + 1) * 64], lhsT=CT[si][:, lo:lo + 128],
                                                          rhs=hTb[:, h * 64:(h + 1) * 64], start=True, stop=True),
                     reads=[("CT", si), "hTb"], writes=["pYo"])
            P.op("act", lambda e, ci=ci: e.copy(out=ydsb[ci][:], in_=pYd[0][:, 0:256]), reads=["pYd"], writes=[("ydsb", ci)])
            P.dma("sp", lambda e, ci=ci, c=c: e.dma_start(out=zt[ci][:], in_=tm[c * 128:(c + 1) * 128, TMF_SSMZ:TMF_SSMZ + 256]),
                  writes=[("zt", ci)])
            P.op("dve", lambda e, ci=ci, c4=c4: e.tensor_tensor(out=h4(yt[ci][:]), in0=pYo[0][:, 0:256].rearrange("p (h d) -> p h d", h=4),
                                                                in1=ecum[:, c4:c4 + 4].unsqueeze(2).to_broadcast([128, 4, 64]), op=ALU.mult),
                 reads=["pYo", "ecum"], writes=[("yt", ci)])
            P.op("pool", lambda e, ci=ci: e.tensor_tensor(out=h4(dsk[ci][:]), in0=h4(xtok[ci][:]),
                                                          in1=Dbc[:, 0:4].unsqueeze(2).to_broadcast([128, 4, 64]), op=ALU.mult),
                 reads=[("xtok", ci), "Dbc"], writes=[("dsk", ci)])
            P.op("dve", lambda e, ci=ci: e.tensor_tensor(out=yt[ci][:], in0=yt[ci][:], in1=ydsb[ci][:], op=ALU.add),
                 reads=[("yt", ci), ("ydsb", ci)], writes=[("yt", ci)])
            P.op("dve", lambda e, ci=ci: e.tensor_tensor(out=yt[ci][:], in0=yt[ci][:], in1=dsk[ci][:], op=ALU.add),
                 reads=[("yt", ci), ("dsk", ci)], writes=[("yt", ci)])
            if c in (0, 4):
                ph.dump(f"ssm_xtok{c}", xtok[ci][:], [("xtok", ci)])
                ph.dump(f"ssm_LT{c}", LT[ci][:], [("LT", ci)])
                ph.dump(f"ssm_MT{c}", MT[ci][:], [("MT", ci)])
                ph.dump(f"ssm_yt{c}", yt[ci][:], [("yt", ci)])
            for h in range(4):
                P.op("pe", lambda e, h=h, ci=ci: e.matmul(pS[0][:, h * 64:(h + 1) * 64], lhsT=Btok[ci][:],
                                                          rhs=xdd[ci][:, h * 64:(h + 1) * 64], start=True, stop=True),
                     reads=[("Btok", ci), ("xdd", ci)], writes=["pS"])
            P.op("pool", lambda e, c4=c4: e.tensor_tensor(out=h4(hT[:]), in0=h4(hT[:]),
                                                          in1=ecumL[:, c4:c4 + 4].unsqueeze(2).to_broadcast([128, 4, 64]), op=ALU.mult),
                 reads=["hT", "ecumL"], writes=["hT"])
            P.op("dve", lambda e: e.tensor_tensor(out=hT[:], in0=hT[:], in1=pS[0][:, 0:256], op=ALU.add), reads=["hT", "pS"], writes=["hT"])
            P.op("act", lambda e: e.copy(out=hTb[:], in_=hT[:]), reads=["hT"], writes=["hTb"])
            recs[-1][1].append(len(rec_cur[0]))
            P.op("act", lambda e, ci=ci: e.activation(out=zt[ci][:], in_=zt[ci][:], func=AF.Silu),
                 reads=[("zt", ci)], writes=[("zt", ci)])
            P.op("pool", lambda e, ci=ci: e.tensor_tensor(out=yt[ci][:], in0=yt[ci][:], in1=zt[ci][:], op=ALU.mult),
                 reads=[("yt", ci), ("zt", ci)], writes=[("yt", ci)])
            P.op("act", lambda e, ci=ci, c=c: e.activation(out=junk[:], in_=yt[ci][:], func=AF.Square, accum_out=nst[:, c:c + 1]),
                 reads=[("yt", ci), "nst"], writes=["junk", ("nst", c)])
            P.op("dve", lambda e, c=c: e.tensor_scalar(out=nst[:, c:c + 1], in0=nst[:, c:c + 1], scalar1=1.0 / 256, scalar2=NORM_EPS,
                                                       op0=ALU.mult, op1=ALU.add), reads=[("nst", c)], writes=[("nst", c)])
            P.op("act", lambda e, c=c: e.activation(out=nst[:, c:c + 1], in_=nst[:, c:c + 1], func=AF.Sqrt),
                 reads=[("nst", c)], writes=[("nst", c)])
            P.op("dve", lambda e, c=c: e.reciprocal(out=nst[:, c:c + 1], in_=nst[:, c:c + 1]), reads=[("nst", c)], writes=[("nst", c)])
            P.op("dve", lambda e, ci=ci, c=c: e.scalar_tensor_tensor(out=yt[ci][:], in0=yt[ci][:], scalar=nst[:, c:c + 1], in1=ngb[:],
                                                                     op0=ALU.mult, op1=ALU.mult),
                 reads=[("yt", ci), ("nst", c), "ngb"], writes=[("yt", ci)])
            P.dma("sp", lambda e, ci=ci, c=c: e.dma_start(out=y_dst[c * 128:(c + 1) * 128, :], in_=yt[ci][:]),
                  reads=[("yt", ci)], writes=[("y_ssm", c)])
            restore()
    pipeline_merge([(r, m) for r, m in recs], 3)
    ph.close()


NPAIR = 144
TOPK = 256
NBIS = 17
FILLER = False


def pair_off(i):
    return 2 * i * (i + 1)


def default_blocks():
    return [(i + 1, 0, 4) for i in range(8)]


def pair_offsets(blocks):
    offs, o = [], 0
    for nch, _, lk in blocks:
        offs.append(o)
        o += 4 * (nch - 1) + lk
    return offs, o


def phase_indexer(nc, scr, prm, maskT_d, blocks=None):
    blocks = blocks or default_blocks()
    nblk = len(blocks)
    assert nblk % 2 == 0
    NT = nblk * 128
    ncb = max(cb for _, cb, _ in blocks) + 1
    poffs, _ = pair_offsets(blocks)
    ph = Phase(nc)
    P = ph.P
    fmo = scr["fmb_own"]
    fma = scr["fmb_all"]
    tmo = scr["tmf_own"]
    NB4 = 4
    identb = ph.sb([128, 128], BF16, "identb")
    kiT = ph.sb([128, SEQ], BF16, "kiT")
    qiT = [ph.sb([128, 8, 128], BF16, "qiT") for _ in range(NB4)]
    wi = ph.sb([128, nblk, 16], F32, "wi")
    iota = ph.sb([128, 512], F32, "iota")
    qrel = ph.sb([128, ncb], F32, "qrel")
    cbias = ph.sb([128, ncb, 512], F32, "cbias")
    pow2 = ph.sb([128, NBIS], F32, "pow2")
    wdiag = [ph.sb([128, 16, 128], BF16, "wdiag") for _ in range(NB4)]
    R = [ph.sb([128, 512], BF16, "R") for _ in range(4)]
    sc = [ph.sb([128, SEQ], F32, "sc") for _ in range(NB4)]
    junk = [ph.sb([128, SEQ], BF16, "junk") for _ in range(2)]
    mk = [ph.sb([128, SEQ], BF16, "mk") for _ in range(2)]
    mT = [ph.sb([128, 8, 128], BF16, "mT") for _ in range(3)]
    mx = [ph.sb([128, 8], F32, "mx") for _ in range(NB4)]
    bs = [ph.sb([128, 8], F32, "bs") for _ in range(NB4)]
    wk = [ph.sb([128, NBIS], F32, "wk") for _ in range(NB4)]
    cnt = [ph.sb([128, NBIS], F32, "cnt") for _ in range(NB4)]
    pD = [ph.ps([128, 512], F32, "pD") for _ in range(4)]
    pSc = [ph.ps([128, 512], F32, "pSc") for _ in range(2)]
    pT = [ph.ps([128, 1024], BF16, "pT") for _ in range(1)]
    pJ = ph.ps([128, 512], F32, "pJ")

    ld = lambda dst, src, key: P.dma("sp", lambda e: e.dma_start(out=dst, in_=src), writes=[key])
    ld(identb[:], prm["ident_b"][:, :], "identb")
    ld(kiT[:], fma[1024:1152, :], "kiT")
    ld(wi[:], tmo[0:NT, TMF_OWN_WI:TMF_OWN_WI + 16].rearrange("(i p) h -> p i h", p=128), "wi")
    ld(iota[:], prm["iota512"][0:1, :].partition_broadcast(128), "iota")
    ld(qrel[:], prm["qrel"][:, :], "qrel")
    ld(pow2[:], prm["pow2"][0:1, :].partition_broadcast(128), "pow2")
    P.op("dve", lambda e: e.tensor_scalar(out=wi[:], in0=wi[:], scalar1=0.03125, scalar2=None, op0=ALU.mult), reads=["wi"], writes=["wi"])
    for cb in range(ncb):
        P.op("dve", lambda e, o=cbias[:, cb, :], s=qrel[:, cb:cb + 1]: e.tensor_scalar(out=o, in0=iota[:], scalar1=s, scalar2=-1e30,
                                                                                  op0=ALU.is_gt, op1=ALU.mult),
             reads=["iota", "qrel"], writes=["cbias"])
    k = {"d": 0, "r": 0, "s": 0, "t": 0, "m": 0}
    qv = fmo[1024:2048, :].rearrange("(p r) t -> r p t", r=128)

    def nkeys(i):
        nch_, _, lk_ = blocks[i]
        return 512 * (nch_ - 1) + 128 * lk_

    def scores(i):
        nch, cbi, lk = blocks[i]
        b4 = i % NB4
        scb, mxb, wdb, qb = sc[b4], mx[b4], wdiag[b4], qiT[b4]
        ld(qb[:], qv[:, :, i * 128:(i + 1) * 128], ("qiT", b4))
        P.op("pool", lambda e, o=wdb[:], w_=wi[:, i, :].unsqueeze(2).to_broadcast([128, 16, 128]),
             d_=identb[:].unsqueeze(1).to_broadcast([128, 16, 128]): e.tensor_tensor(out=o, in0=d_, in1=w_, op=ALU.mult),
             reads=["identb", "wi"], writes=[("wdiag", b4)])
        P.op("dve", lambda e, o=mxb[:]: e.memset(o, 0.0), writes=[("mx", b4)])
        P.op("dve", lambda e, o=cnt[b4][:]: e.memset(o, 0.0), writes=[("cnt", b4)])
        for ch in range(nch):
            js = k["s"] % 2
            k["s"] += 1
            jds = {}
            nl = 512 if ch < nch - 1 else 128 * lk

            def dots(h):
                jd = k["d"] % 4
                k["d"] += 1
                jds[h] = jd
                r0 = (h % 2) * 64
                P.op("pe", lambda e, o=pD[jd][:, 0:nl], l=qb[r0:r0 + 64, h // 2, :],
                     r=kiT[r0:r0 + 64, ch * 512:ch * 512 + nl]: e.matmul(o, lhsT=l, rhs=r, start=True, stop=True),
                     reads=[("qiT", b4), "kiT"], writes=[("pD", jd)])

            for h0 in range(4):
                dots(h0)
            for h in range(16):
                if h % 2 == 0 and h >= 2 and h + 3 < 16:
                    dots(h + 2)
                    dots(h + 3)
                jd = jds[h]
                jr = k["r"] % 4
                k["r"] += 1
                P.op("act", lambda e, o=R[jr][:, 0:nl], s=pD[jd][:, 0:nl]: e.activation(out=o, in_=s, func=AF.Relu),
                     reads=[("pD", jd)], writes=[("R", jr)])
                P.op("pe", lambda e, o=pSc[js][:, 0:nl], l=wdb[:, h, :], r=R[jr][:, 0:nl], h=h: e.matmul(o, lhsT=l, rhs=r, start=(h == 0),
                                                                                           stop=(h == 15)),
                     reads=[("wdiag", b4), ("R", jr)], writes=[("pSc", js)])
                if FILLER:
                    P.op("pe", lambda e, l=identb[:], r=kiT[:, ch * 512:(ch + 1) * 512]: e.matmul(pJ[:], lhsT=l, rhs=r, start=True, stop=True),
                         reads=["identb", "kiT"], writes=["pJ"])
            P.op("dve", lambda e, o=mxb[:, ch:ch + 1], s=pSc[js][:, 0:nl]: e.tensor_reduce(out=o, in_=s, axis=AX.X, op=ALU.max,
                                                                                   apply_absolute_value=True),
                 reads=[("pSc", js)], writes=[("mx", b4)])
            if ch == nch - 1:
                P.op("dve", lambda e, o=scb[:, ch * 512:ch * 512 + nl], s=pSc[js][:, 0:nl], c_=cbias[:, cbi, 0:nl]: e.tensor_tensor(
                    out=o, in0=s, in1=c_, op=ALU.add), reads=[("pSc", js), "cbias"], writes=[("sc", b4)])
            else:
                P.op("dve", lambda e, o=scb[:, ch * 512:(ch + 1) * 512], s=pSc[js][:]: e.tensor_copy(out=o, in_=s),
                     reads=[("pSc", js)], writes=[("sc", b4)])
            yield

    def bis_init(i):
        b4 = i % NB4
        bsb, wkb, mxb = bs[b4], wk[b4], mx[b4]
        bk = ("bs", b4)
        P.op("dve", lambda e, o=bsb[:, 0:1], s=mxb[:]: e.tensor_reduce(out=o, in_=s, axis=AX.X, op=ALU.max),
             reads=[("mx", b4)], writes=[bk])
        P.op("dve", lambda e, o=bsb[:, 0:1]: e.tensor_scalar(out=o, in0=o, scalar1=1.001, scalar2=1e-6, op0=ALU.mult, op1=ALU.add),
             reads=[bk], writes=[bk])
        P.op("dve", lambda e, o=wkb[:], s=bsb[:, 0:1]: e.tensor_scalar(out=o, in0=pow2[:], scalar1=s, scalar2=2.0, op0=ALU.mult,
                                                                      op1=ALU.mult), reads=[bk, "pow2"], writes=[("wk", b4)])
        P.op("dve", lambda e, o=bsb[:, 1:2]: e.memset(o, 0.0), reads=[bk], writes=[bk])

    def bis_count(i, it, jj):
        b4 = i % NB4
        n = nkeys(i)
        P.op("dve", lambda e, o=junk[jj][:, 0:n], s=sc[b4][:, 0:n], m=bs[b4][:, 1:2], a=cnt[b4][:, it:it + 1]: e.tensor_scalar(
            out=o, in0=s, scalar1=m, scalar2=0.0, op0=ALU.is_ge, op1=ALU.add, accum_out=a),
            reads=[("sc", b4), ("bs", b4), ("cnt", b4)], writes=[("junk", jj), ("cnt", b4)])

    def bis_delta(i, it):
        b4 = i % NB4
        P.op("dve", lambda e, o=bs[b4][:, 2:3], c_=cnt[b4][:, it:it + 1], w_=wk[b4][:, it:it + 1]: e.tensor_scalar(
            out=o, in0=c_, scalar1=TOPK - 0.5, scalar2=w_, op0=ALU.is_ge, op1=ALU.mult),
            reads=[("cnt", b4), ("wk", b4), ("bs", b4)], writes=[("bs", b4)])

    def bis_mid(i, it):
        b4 = i % NB4
        nx = min(it + 1, NBIS - 1)
        P.op("dve", lambda e, o=bs[b4][:, 1:2], d_=bs[b4][:, 2:3], w_=wk[b4][:, nx:nx + 1]: e.scalar_tensor_tensor(
            out=o, in0=d_, scalar=w_, in1=o, op0=ALU.subtract, op1=ALU.add), reads=[("bs", b4), ("wk", b4)], writes=[("bs", b4)])

    def finish_block(i, jj):
        nch, _, lk = blocks[i]
        b4 = i % NB4
        n = nkeys(i)
        mkb = mk[jj]
        P.op("dve", lambda e, o=mkb[:, 0:n], s=sc[b4][:, 0:n], t=bs[b4][:, 1:2]: e.tensor_scalar(
            out=o, in0=s, scalar1=t, scalar2=-1.0, op0=ALU.is_ge, op1=ALU.add), reads=[("sc", b4), ("bs", b4)], writes=[("mk", jj)])
        nkb = 4 * (nch - 1) + lk
        for g0 in range(0, nkb, 8):
            ng = min(8, nkb - g0)
            jt = 0
            jm = k["m"] % 3
            k["m"] += 1
            for kb in range(g0, g0 + ng):
                P.op("pe", lambda e, o=pT[jt][:, (kb - g0) * 128:(kb - g0 + 1) * 128], s=mkb[:, kb * 128:(kb + 1) * 128]: e.transpose(
                    out=o, in_=s, identity=identb[:]), reads=[("mk", jj), "identb"], writes=[("pT", jt)])
            P.op("act", lambda e, o=mT[jm][:, 0:ng, :], s=pT[jt][:, 0:ng * 128].rearrange("p (k t) -> p k t", k=ng): e.copy(out=o, in_=s),
                 reads=[("pT", jt)], writes=[("mT", jm)])
            po = poffs[i] + g0
            P.dma("sp", lambda e, o=maskT_d[:, po:po + ng, :], s=mT[jm][:, 0:ng, :]: e.dma_start(out=o, in_=s),
                  reads=[("mT", jm)], writes=[("maskT", i, g0)])

    import itertools
    for _ in itertools.chain(scores(0), scores(1)):
        pass
    for pr in range(nblk // 2):
        ia, ib = 2 * pr, 2 * pr + 1
        pending = iter(())
        if 2 * pr + 2 < nblk:
            pending = itertools.chain(scores(2 * pr + 2), scores(2 * pr + 3))
        bis_init(ia)
        bis_init(ib)
        for it in range(NBIS):
            bis_count(ia, it, 0)
            bis_count(ib, it, 1)
            bis_delta(ia, it)
            bis_delta(ib, it)
            bis_mid(ia, it)
            bis_mid(ib, it)
            next(pending, None)
        for _ in pending:
            pass
        finish_block(ia, 0)
        finish_block(ib, 1)
    ph.close()


def phase_attn(nc, scr, prm, maskT_d, y_own, blocks=None):
    blocks = blocks or default_blocks()
    nblk = len(blocks)
    poffs, _ = pair_offsets(blocks)
    ph = Phase(nc)
    P = ph.P
    fmo = scr["fmb_own"]
    fma = scr["fmb_all"]
    tmb = scr["tmb_all"]
    tmo = scr["tmf_own"]
    att_scale = 128 ** -0.5
    i30k = ph.sb([128, 128], BF16, "i30k")
    kT = ph.sb([128, 8, SEQ], BF16, "kT")
    V = ph.sb([128, 32, 8, 129], BF16, "V")
    qT = [ph.sb([128, 8, 128], BF16, "qT") for _ in range(2)]
    mT = [ph.sb([128, 32, 128], BF16, "mT") for _ in range(2)]
    PT = [ph.sb([128, 512], BF16, "PT") for _ in range(4)]
    ot = [ph.sb([128, 1024], F32, "ot") for _ in range(2)]
    zt = [ph.sb([128, 1024], F32, "zt") for _ in range(2)]
    rs = ph.sb([128, 8 * nblk], F32, "rs")
    pS = [ph.ps([128, 512], F32, "pS") for _ in range(4)]
    pO = [ph.ps([128, 512], F32, "pO") for _ in range(2)]

    ld = lambda dst, src, key: P.dma("sp", lambda e: e.dma_start(out=dst, in_=src), writes=[key])
    ld(i30k[:], prm["ident30k_b"][:, :], "i30k")
    for h in range(8):
        ld(kT[:, h, :], fma[h * 128:(h + 1) * 128, :], ("kT", h))
    vv = tmb.rearrange("(kb p) (h d) -> p kb h d", p=128, d=128)
    for kb in range(32):
        ld(V[:, kb, :, 0:128], vv[:, kb, :, :], ("V", kb))
    P.op("pool", lambda e: e.memset(V[:, :, :, 128:129], 1.0), writes=["Vones"])
    qv = fmo[0:1024, :].rearrange("(h d) t -> d h t", d=128)
    k = {"s": 0, "p": 0, "o": 0}
    def loads(i):
        b2_ = i % 2
        nkb_ = 4 * (blocks[i][0] - 1) + blocks[i][2]
        po_ = poffs[i]
        ld(qT[b2_][:], qv[:, :, i * 128:(i + 1) * 128], ("qT", b2_))
        ld(mT[b2_][:, 0:nkb_, :], maskT_d[:, po_:po_ + nkb_, :], ("mT", b2_))
        ld(zt[b2_][:], tmo[i * 128:(i + 1) * 128, TMF_OWN_ATTZ:TMF_OWN_ATTZ + 1024], ("zt", b2_))

    loads(0)
    for i, (nch, _, lk) in enumerate(blocks):
        b2 = i % 2
        nkb = 4 * (nch - 1) + lk
        po = poffs[i]
        if i + 1 < nblk:
            loads(i + 1)
        P.op("act", lambda e, o=zt[b2][:]: e.activation(out=o, in_=o, func=AF.Silu), reads=[("zt", b2)], writes=[("zt", b2)])
        for h in range(8):
            jo = k["o"] % 2
            k["o"] += 1
            jss = {}

            def st_mm(ch):
                js = k["s"] % 4
                k["s"] += 1
                jss[ch] = js
                nk4 = 4 if ch < nch - 1 else lk
                P.op("pe", lambda e, o=pS[js][:, 0:nk4 * 128], r=mT[b2][:, ch * 4:ch * 4 + nk4, :].rearrange("p k t -> p (k t)"): e.matmul(
                    o, lhsT=i30k[:], rhs=r, start=True, stop=False), reads=["i30k", ("mT", b2)], writes=[("pS", js)])
                for k4 in range(nk4):
                    kb = ch * 4 + k4
                    P.op("pe", lambda e, o=pS[js][:, k4 * 128:(k4 + 1) * 128], l=kT[:, h, kb * 128:(kb + 1) * 128],
                         r=qT[b2][:, h, :], k4=k4, nk4=nk4: e.matmul(o, lhsT=l, rhs=r, start=False, stop=(k4 == nk4 - 1)),
                         reads=[("kT", h), ("qT", b2)], writes=[("pS", js)])

            st_mm(0)
            for ch in range(nch):
                if ch + 1 < nch:
                    st_mm(ch + 1)
                js = jss[ch]
                jp = k["p"] % 4
                k["p"] += 1
                nk4 = 4 if ch < nch - 1 else lk
                P.op("act", lambda e, o=PT[jp][:, 0:nk4 * 128], s=pS[js][:, 0:nk4 * 128]: e.activation(out=o, in_=s, func=AF.Exp, scale=att_scale),
                     reads=[("pS", js)], writes=[("PT", jp)])
                for k4 in range(nk4):
                    kb = ch * 4 + k4
                    P.op("pe", lambda e, o=pO[jo][:, 0:129], l=PT[jp][:, k4 * 128:(k4 + 1) * 128], r=V[:, kb, h, :], kb=kb, nkb=nkb:
                         e.matmul(o, lhsT=l, rhs=r, start=(kb == 0), stop=(kb == nkb - 1)),
                         reads=[("PT", jp), ("V", kb), "Vones"], writes=[("pO", jo)])
            c = i * 8 + h
            P.op("dve", lambda e, o=rs[:, c:c + 1], s=pO[jo][:, 128:129]: e.reciprocal(out=o, in_=s),
                 reads=[("pO", jo)], writes=[("rs", c)])
            P.op("dve", lambda e, o=ot[b2][:, h * 128:(h + 1) * 128], s=pO[jo][:, 0:128], r=rs[:, c:c + 1],
                 z=zt[b2][:, h * 128:(h + 1) * 128]: e.scalar_tensor_tensor(out=o, in0=s, scalar=r, in1=z, op0=ALU.mult, op1=ALU.mult),
                 reads=[("pO", jo), ("rs", c), ("zt", b2)], writes=[("ot", b2)])
        P.dma("sp", lambda e, o=y_own[i * 128:(i + 1) * 128, 0:1024], s=ot[b2][:]: e.dma_start(out=o, in_=s),
              reads=[("ot", b2)], writes=[("y_att", i)])
    ph.close()


RW_DECAY_C = -0.6065306597126334


class _Stop(Exception):
    pass


def phase_rwkv(nc, scr, prm, y_all, NBLK=SEQ // 128, stop_after=None, y_dst=None, stagger=True):
    ph = Phase(nc)
    P = ph.P
    tm = scr["tmf_all"]
    fm = scr["fmf_all"]
    if y_dst is None:
        y_dst = y_all[:, 256:512]
    cst = {}
    for nm in ("ident_f", "mask_sl", "mask_su", "mask_u", "ones_bd"):
        cst[nm] = ph.sb([128, 128], F32, nm)
    mu_tm = ph.sb([128, 1024], F32, "mu_tm")
    mu_fm = ph.sb([128, 1], F32, "mu_fm")
    w2a2 = ph.sb([128, 256], F32, "w2a2")
    vec = {}
    for nm in ("rw_w0", "rw_a0", "rw_kk", "rw_ka", "rw_rk", "rw_gng", "rw_gnb"):
        vec[nm] = ph.sb([128, 256], F32, nm)
    onecol = ph.sb([128, 1], F32, "onecol")
    NBUF3 = 4
    cur = [ph.sb([128, 1024], F32, "cur") for _ in range(NBUF3)]
    prv = [ph.sb([128, 1024], F32, "prv") for _ in range(NBUF3)]
    lcur = [ph.sb([128, 128], F32, "lcur") for _ in range(NBUF3)]
    lprv = [ph.sb([128, 128], F32, "lprv") for _ in range(NBUF3)]
    T = {}
    BFT = ("rt", "at", "bt", "kt", "bh", "kh", "vb")
    for nm in ("lw", "asig", "kkn", "kp", "aa", "bb", "cum", "cumL", "e1", "rt", "at", "bt", "kt", "bh", "kh", "vb", "tmp", "tmp2", "yb", "bon"):
        T[nm] = [ph.sb([128, 256], BF16 if nm in BFT else F32, nm) for _ in range(NBUF3)]
    st4 = [ph.sb([128, 16], F32, "st4") for _ in range(NBUF3)]
    gL = [ph.sb([128, 4], F32, "gL") for _ in range(NBUF3)]
    TR = {nm: [[ph.sb([128, 128], BF16, nm) for _ in range(2)] for _ in range(NBUF3)] for nm in ("atT", "btT", "ktT", "rtT")}
    H = {}
    for nm in ("N", "NT", "Pa", "PaT", "Pb", "PbT", "TTa", "TTb", "MakT"):
        H[nm] = ph.sb([128, 4, 128], BF16, nm)
    H["W2"] = ph.sb([128, 4, 64], BF16, "W2")
    mask2 = {nm: ph.sb([128, 2, 128], F32, nm + "2") for nm in ("mask_sl", "mask_su", "mask_u")}
    HS = {}
    for nm in ("P1T", "P2", "MrbT", "MrkT"):
        shp = {"P1T": [128, 2, 128], "P2": [128, 4, 64], "MrbT": [128, 4, 128], "MrkT": [128, 4, 128]}[nm]
        HS[nm] = [ph.sb(shp, F32 if nm == "P2" else BF16, nm) for _ in range(NBUF3)]
    Usb = ph.sb([128, 4, 64], BF16, "Usb")
    STb = [ph.sb([128, 64], BF16, "STb") for _ in range(4)]
    identb = ph.sb([128, 128], BF16, "identb")
    ST = [ph.sb([128, 64], F32, "ST") for _ in range(4)]
    pA = [ph.ps([128, 512], F32, "pA") for _ in range(3)]
    pP = [ph.ps([128, 512], F32, "pP") for _ in range(2)]
    pTb = ph.ps([128, 1024], BF16, "pTb")
    pQ = [ph.ps([128, 512], F32, "pQ") for _ in range(2)]

    ld = lambda dst, src, key: P.dma("sp", lambda e: e.dma_start(out=dst, in_=src), writes=[key])
    for nm in cst:
        ld(cst[nm][:], prm[nm][:, :], nm)
    ld(identb[:], prm["ident_b"][:, :], "identb")
    ld(mu_tm[:], prm["rw_mu_tm"][0:1, :].partition_broadcast(128), "mu_tm")
    ld(mu_fm[:], prm["rw_mu_fm"][:, :], "mu_fm")
    ld(w2a2[:], prm["rw_w2a2"][:, :], "w2a2")
    for nm in vec:
        ld(vec[nm][:], prm[nm][0:1, :].partition_broadcast(128), nm)
    P.op("dve", lambda e: e.memset(onecol[:], 1.0), writes=["onecol"])
    for h in range(4):
        P.op("dve", lambda e, o=ST[h][:]: e.memset(o, 0.0), writes=[("ST", h)])
        P.op("dve", lambda e, o=STb[h][:]: e.memset(o, 0.0), writes=[("STb", h)])
    P.op("dve", lambda e: e.memset(Usb[:], 0.0), writes=["Usb"])
    for nm in ("mask_sl", "mask_su", "mask_u"):
        for r_ in range(2):
            P.op("pool", lambda e, o=mask2[nm][:, r_, :], s=cst[nm][:]: e.tensor_copy(out=o, in_=s), reads=[nm], writes=[nm + "2"])
    kq = {"a": 0, "q": 0, "e": 0}
    P.excl.update(["pA", "pQ", "pTb", "pP"])

    def mm(out, lhsT, rhs, reads, writes, start=True, stop=True):
        P.op("pe", lambda e: e.matmul(out, lhsT=lhsT, rhs=rhs, start=start, stop=stop), reads=reads, writes=writes)

    def evac(out, in_, reads, writes, mask=None, mreads=()):
        kq["e"] += 1
        if mask is not None:
            P.op("dve", lambda e: e.tensor_tensor(out=out, in0=in_, in1=mask, op=ALU.mult), reads=list(reads) + list(mreads), writes=writes)
        elif kq["e"] % 4 == 0:
            P.op("dve", lambda e: e.tensor_copy(out=out, in_=in_), reads=reads, writes=writes)
        else:
            P.op("act", lambda e: e.copy(out=out, in_=in_), reads=reads, writes=writes)

    def nextA():
        j = kq["a"] % 3
        kq["a"] += 1
        return j

    def nextQ():
        j = kq["q"] % 2
        kq["q"] += 1
        return j

    def dv(fn, reads, writes, eng="dve"):
        P.op(eng, fn, reads=reads, writes=writes)

    def stage(n):
        if stop_after is not None and n > stop_after:
            raise _Stop()

    real_op, real_dma = P.op, P.dma
    recs = []
    for blk in range(NBLK):
      rec = []
      marks = []
      recs.append((rec, marks))
      P.op = lambda eng, fn, reads=(), writes=(), rec=rec: rec.append((real_op, eng, fn, list(reads), list(writes)))
      P.dma = lambda eng, fn, reads=(), writes=(), rec=rec: rec.append((real_dma, eng, fn, list(reads), list(writes)))
      try:
          b2 = blk % NBUF3
          t0 = blk * 128
          B = {nm: T[nm][b2] for nm in T}
          K2 = lambda nm: (nm, b2)
          cu, pv, lc, lp = cur[b2], prv[b2], lcur[b2], lprv[b2]
          stage(-1)
          ld(cu[:], tm[t0:t0 + 128, TMF_R:TMF_R + 1024], K2("cur"))
          ld(lc[:], fm[FMF_WL:FMF_WL + 128, t0:t0 + 128], K2("lcur"))
          if blk == 0:
              dv(lambda e, o=pv[:]: e.memset(o, 0.0), [], [K2("prv")])
              dv(lambda e, o=lp[:]: e.memset(o, 0.0), [], [K2("lprv")])
              ld(pv[1:128, :], tm[0:127, TMF_R:TMF_R + 1024], K2("prv"))
              ld(lp[:, 1:128], fm[FMF_WL:FMF_WL + 128, 0:127], K2("lprv"))
          else:
              ld(pv[:], tm[t0 - 1:t0 + 127, TMF_R:TMF_R + 1024], K2("prv"))
              ld(lp[:], fm[FMF_WL:FMF_WL + 128, t0 - 1:t0 + 127], K2("lprv"))
          stage(-0.5)
          dv(lambda e, o=pv[:], c=cu[:]: e.tensor_tensor(out=o, in0=o, in1=c, op=ALU.subtract), [K2("prv"), K2("cur")], [K2("prv")])
          dv(lambda e, o=pv[:]: e.tensor_tensor(out=o, in0=o, in1=mu_tm[:], op=ALU.mult), [K2("prv"), "mu_tm"], [K2("prv")], eng="pool")
          dv(lambda e, o=cu[:], d=pv[:]: e.tensor_tensor(out=o, in0=o, in1=d, op=ALU.add), [K2("prv"), K2("cur")], [K2("cur")])
          dv(lambda e, o=lp[:], c=lc[:]: e.tensor_tensor(out=o, in0=o, in1=c, op=ALU.subtract), [K2("lprv"), K2("lcur")], [K2("lprv")])
          dv(lambda e, o=lc[:], d=lp[:]: e.scalar_tensor_tensor(out=o, in0=d, scalar=mu_fm[:, 0:1], in1=o, op0=ALU.mult, op1=ALU.add),
             [K2("lprv"), K2("lcur"), "mu_fm"], [K2("lcur")])
          stage(-0.2)
          P.op("act", lambda e, o=lc[0:64, :]: e.activation(out=o, in_=o, func=AF.Tanh), reads=[K2("lcur")], writes=[K2("lcur")])
          r_, k_, v_, z_ = cu[:, 0:256], cu[:, 256:512], cu[:, 512:768], cu[:, 768:1024]
          P.op("act", lambda e, o=B["vb"][:], v_=v_: e.copy(out=o, in_=v_), reads=[K2("cur")], writes=[K2("vb")])
          stage(1)
          mm(pP[0][:, 0:256], lc[0:64, :], w2a2[0:64, :], [K2("lcur"), "w2a2"], [("pP", 0)])
          mm(pP[1][:, 0:256], lc[64:128, :], w2a2[64:128, :], [K2("lcur"), "w2a2"], [("pP", 1)])
          dv(lambda e, o=B["lw"][:], s=pP[0][:, 0:256]: e.tensor_tensor(out=o, in0=s, in1=vec["rw_w0"][:], op=ALU.add),
             [("pP", 0), "rw_w0"], [K2("lw")])
          dv(lambda e, o=B["asig"][:], s=pP[1][:, 0:256]: e.tensor_tensor(out=o, in0=s, in1=vec["rw_a0"][:], op=ALU.add),
             [("pP", 1), "rw_a0"], [K2("asig")])
          P.op("act", lambda e, o=B["lw"][:]: e.activation(out=o, in_=o, func=AF.Sigmoid), reads=[K2("lw")], writes=[K2("lw")])
          P.op("act", lambda e, o=B["asig"][:]: e.activation(out=o, in_=o, func=AF.Sigmoid), reads=[K2("asig")], writes=[K2("asig")])
          dv(lambda e, o=B["lw"][:]: e.tensor_scalar(out=o, in0=o, scalar1=RW_DECAY_C, scalar2=None, op0=ALU.mult), [K2("lw")], [K2("lw")])
          stage(2)
          s4 = st4[b2]
          v3 = lambda ap: ap.rearrange("p (h j) -> p h j", h=4)
          dv(lambda e, o=B["kkn"][:], k_=k_: e.tensor_tensor(out=o, in0=k_, in1=vec["rw_kk"][:], op=ALU.mult), [K2("cur"), "rw_kk"], [K2("kkn")])
          dv(lambda e, o=B["tmp"][:], s=B["kkn"][:]: e.tensor_tensor(out=o, in0=s, in1=s, op=ALU.mult), [K2("kkn")], [K2("tmp")], eng="pool")
          dv(lambda e, o=s4[:, 0:4], s=v3(B["tmp"][:]): e.tensor_reduce(out=o, in_=s, axis=AX.X, op=ALU.add), [K2("tmp")], [K2("st4")])
          dv(lambda e, o=s4[:, 0:4]: e.tensor_scalar(out=o, in0=o, scalar1=1e-12, scalar2=None, op0=ALU.add), [K2("st4")], [K2("st4")])
          P.op("act", lambda e, o=s4[:, 0:4]: e.activation(out=o, in_=o, func=AF.Sqrt), reads=[K2("st4")], writes=[K2("st4")])
          dv(lambda e, o=s4[:, 0:4]: e.reciprocal(out=o, in_=o), [K2("st4")], [K2("st4")])
          dv(lambda e, o=v3(B["kkn"][:]), s=s4[:, 0:4].unsqueeze(2).to_broadcast([128, 4, 64]): e.tensor_tensor(out=o, in0=o, in1=s, op=ALU.mult),
             [K2("kkn"), K2("st4")], [K2("kkn")])
          dv(lambda e, o=B["tmp"][:], s=B["asig"][:]: e.scalar_tensor_tensor(out=o, in0=s, scalar=-1.0, in1=vec["rw_ka"][:], op0=ALU.add,
                                                                            op1=ALU.mult), [K2("asig"), "rw_ka"], [K2("tmp")])
          dv(lambda e, o=B["kp"][:], s=B["tmp"][:], k_=k_: e.scalar_tensor_tensor(out=o, in0=s, scalar=1.0, in1=k_, op0=ALU.add, op1=ALU.mult),
             [K2("tmp"), K2("cur")], [K2("kp")])
          dv(lambda e, o=B["aa"][:], s=B["kkn"][:]: e.tensor_scalar(out=o, in0=s, scalar1=-1.0, scalar2=None, op0=ALU.mult),
             [K2("kkn")], [K2("aa")], eng="pool")
          dv(lambda e, o=B["bb"][:], s=B["kkn"][:], a=B["asig"][:]: e.tensor_tensor(out=o, in0=s, in1=a, op=ALU.mult),
             [K2("kkn"), K2("asig")], [K2("bb")], eng="pool")
          dv(lambda e, o=B["tmp2"][:], s=B["kp"][:], r_=r_: e.tensor_tensor(out=o, in0=r_, in1=s, op=ALU.mult), [K2("cur"), K2("kp")], [K2("tmp2")])
          dv(lambda e, o=B["tmp2"][:]: e.tensor_tensor(out=o, in0=o, in1=vec["rw_rk"][:], op=ALU.mult), [K2("tmp2"), "rw_rk"], [K2("tmp2")])
          dv(lambda e, o=s4[:, 4:8], s=v3(B["tmp2"][:]): e.tensor_reduce(out=o, in_=s, axis=AX.X, op=ALU.add), [K2("tmp2")], [K2("st4")])
          dv(lambda e, o=v3(B["bon"][:]), s=v3(cu[:, 512:768]), c=s4[:, 4:8].unsqueeze(2).to_broadcast([128, 4, 64]):
             e.tensor_tensor(out=o, in0=s, in1=c, op=ALU.mult), [K2("cur"), K2("st4")], [K2("bon")])
          stage(3)
          mm(pP[0][:, 0:256], cst["mask_u"][:], B["lw"][:], ["mask_u", K2("lw")], [("pP", 0)])
          mm(pP[0][:, 256:512], cst["ones_bd"][:], B["lw"][:], ["ones_bd", K2("lw")], [("pP", 0)])
          evac(B["cum"][:], pP[0][:, 0:256], [("pP", 0)], [K2("cum")])
          evac(B["cumL"][:], pP[0][:, 256:512], [("pP", 0)], [K2("cumL")])
          for p in range(2):
              for c2 in range(2):
                  mm(pP[1][:, c2 * 2 + p:c2 * 2 + p + 1], B["lw"][:, p * 128:(p + 1) * 128], cst["ones_bd"][:, c2 * 64:c2 * 64 + 1],
                     [K2("lw"), "ones_bd"], [("pP", 1)])
          P.op("act", lambda e, o=gL[b2][:], s=pP[1][:, 0:4]: e.activation(out=o, in_=s, func=AF.Exp), reads=[("pP", 1)], writes=[K2("gL")])
          P.op("act", lambda e, o=B["e1"][:], s=B["cum"][:]: e.activation(out=o, in_=s, func=AF.Exp), reads=[K2("cum")], writes=[K2("e1")])
          dv(lambda e, o=B["rt"][:], s=B["e1"][:], r_=r_: e.tensor_tensor(out=o, in0=r_, in1=s, op=ALU.mult), [K2("cur"), K2("e1")], [K2("rt")])
          dv(lambda e, o=B["tmp"][:], s=B["cum"][:], l=B["lw"][:]: e.tensor_tensor(out=o, in0=s, in1=l, op=ALU.subtract),
             [K2("cum"), K2("lw")], [K2("tmp")], eng="pool")
          P.op("act", lambda e, o=B["tmp"][:]: e.activation(out=o, in_=o, func=AF.Exp), reads=[K2("tmp")], writes=[K2("tmp")])
          dv(lambda e, o=B["at"][:], s=B["aa"][:], t=B["tmp"][:]: e.tensor_tensor(out=o, in0=s, in1=t, op=ALU.mult),
             [K2("aa"), K2("tmp")], [K2("at")])
          P.op("act", lambda e, o=B["e1"][:], s=B["cum"][:]: e.activation(out=o, in_=s, func=AF.Exp, scale=-1.0),
               reads=[K2("cum"), K2("rt")], writes=[K2("e1")])
          dv(lambda e, o=B["bt"][:], s=B["bb"][:], t=B["e1"][:]: e.tensor_tensor(out=o, in0=s, in1=t, op=ALU.mult),
             [K2("bb"), K2("e1")], [K2("bt")])
          dv(lambda e, o=B["kt"][:], s=B["kp"][:], t=B["e1"][:]: e.tensor_tensor(out=o, in0=s, in1=t, op=ALU.mult),
             [K2("kp"), K2("e1")], [K2("kt")], eng="pool")
          dv(lambda e, o=B["tmp2"][:], s=B["cumL"][:], c=B["cum"][:]: e.tensor_tensor(out=o, in0=s, in1=c, op=ALU.subtract),
             [K2("cumL"), K2("cum")], [K2("tmp2")], eng="pool")
          P.op("act", lambda e, o=B["tmp2"][:]: e.activation(out=o, in_=o, func=AF.Exp), reads=[K2("tmp2")], writes=[K2("tmp2")])
          dv(lambda e, o=B["bh"][:], s=B["bb"][:], t=B["tmp2"][:]: e.tensor_tensor(out=o, in0=s, in1=t, op=ALU.mult),
             [K2("bb"), K2("tmp2")], [K2("bh")])
          dv(lambda e, o=B["kh"][:], s=B["kp"][:], t=B["tmp2"][:]: e.tensor_tensor(out=o, in0=s, in1=t, op=ALU.mult),
             [K2("kp"), K2("tmp2")], [K2("kh")], eng="pool")
          stage(4)
          for qi_, (nm_src, nm_dst) in enumerate((("at", "atT"), ("bt", "btT"), ("kt", "ktT"), ("rt", "rtT"))):
              for p in range(2):
                  P.op("pe", lambda e, o=pTb[:, (qi_ * 2 + p) * 128:(qi_ * 2 + p + 1) * 128], s=B[nm_src][:, p * 128:(p + 1) * 128]: e.transpose(
                      out=o, in_=s, identity=identb[:]), reads=[K2(nm_src), "identb"], writes=["pTb"])
          for qi_, (nm_src, nm_dst) in enumerate((("at", "atT"), ("bt", "btT"), ("kt", "ktT"), ("rt", "rtT"))):
              for p in range(2):
                  evac(TR[nm_dst][b2][p][:], pTb[:, (qi_ * 2 + p) * 128:(qi_ * 2 + p + 1) * 128], ["pTb"], [(nm_dst, b2, p)])
          marks.append(len(rec))
          slot = lambda h: (h % 2) * 2 + h // 2
          rk = lambda nm, h: (nm, b2, h // 2)
          opd = {}
          for h in range(4):
              p, r0 = h // 2, (h % 2) * 64
              opd[h] = {nm: TR[nm2][b2][p][r0:r0 + 64, :] for nm, nm2 in (("aT", "atT"), ("bT", "btT"), ("kT", "ktT"), ("rT", "rtT"))}
          jx, jy2 = nextA(), nextA()
          for r_, jb in ((0, jx), (1, jy2)):
              for p in range(2):
                  h = 2 * p + r_
                  o_ = opd[h]
                  mm(pA[jb][:, p * 128:(p + 1) * 128], o_["aT"], o_["bT"], [rk("atT", h), rk("btT", h)], [("pA", jb)])
                  mm(pA[jb][:, 256 + p * 128:256 + (p + 1) * 128], o_["bT"], o_["aT"], [rk("atT", h), rk("btT", h)], [("pA", jb)])
          for r_, jb in ((0, jx), (1, jy2)):
              sl_ = slice(2 * r_, 2 * r_ + 2)
              dv(lambda e, o=H["N"][:, sl_, :], s=pA[jb][:, 0:256].rearrange("p (a t) -> p a t", a=2), m=mask2["mask_sl"][:]:
                 e.tensor_tensor(out=o, in0=s, in1=m, op=ALU.mult), [("pA", jb), "mask_sl2"], [("N", r_)])
              dv(lambda e, o=H["NT"][:, sl_, :], s=pA[jb][:, 256:512].rearrange("p (a t) -> p a t", a=2), m=mask2["mask_su"][:]:
                 e.tensor_tensor(out=o, in0=s, in1=m, op=ALU.mult), [("pA", jb), "mask_su2"], [("NT", r_)])
          jz = nextA()
          jx2 = nextA()
          for r_, jb in ((0, jx2), (1, jz)):
              for p in range(2):
                  h = 2 * p + r_
                  o_ = opd[h]
                  mm(pA[jb][:, p * 128:(p + 1) * 128], o_["kT"], o_["aT"], [rk("ktT", h), rk("atT", h)], [("pA", jb)])
                  mm(pA[jb][:, 256 + p * 128:256 + (p + 1) * 128], o_["bT"], o_["rT"], [rk("btT", h), rk("rtT", h)], [("pA", jb)])
          for r_, jb in ((0, jx2), (1, jz)):
              sl_ = slice(2 * r_, 2 * r_ + 2)
              dv(lambda e, o=H["MakT"][:, sl_, :], s=pA[jb][:, 0:256].rearrange("p (a t) -> p a t", a=2), m=mask2["mask_su"][:]:
                 e.tensor_tensor(out=o, in0=s, in1=m, op=ALU.mult), [("pA", jb), "mask_su2"], [("MakT", r_)])
              dv(lambda e, o=HS["MrbT"][b2][:, sl_, :], s=pA[jb][:, 256:512].rearrange("p (a t) -> p a t", a=2), m=mask2["mask_u"][:]:
                 e.tensor_tensor(out=o, in0=s, in1=m, op=ALU.mult), [("pA", jb), "mask_u2"], [("MrbT", b2, r_)])
          jk0, jk1 = nextA(), nextA()
          for r_, jb in ((0, jk0), (1, jk1)):
              for p in range(2):
                  h = 2 * p + r_
                  o_ = opd[h]
                  mm(pA[jb][:, p * 128:(p + 1) * 128], o_["kT"], o_["rT"], [rk("ktT", h), rk("rtT", h)], [("pA", jb)])
          for r_, jb in ((0, jk0), (1, jk1)):
              sl_ = slice(2 * r_, 2 * r_ + 2)
              dv(lambda e, o=HS["MrkT"][b2][:, sl_, :], s=pA[jb][:, 0:256].rearrange("p (a t) -> p a t", a=2), m=mask2["mask_u"][:]:
                 e.tensor_tensor(out=o, in0=s, in1=m, op=ALU.mult), [("pA", jb), "mask_u2"], [("MrkT", b2, r_)])
          dv(lambda e, o=H["TTa"][:], s=H["NT"][:], i_=cst["ident_f"][:].unsqueeze(1).to_broadcast([128, 4, 128]):
             e.tensor_tensor(out=o, in0=s, in1=i_, op=ALU.add), [("NT", 0), ("NT", 1), "ident_f"], ["TTa"], eng="pool")
          cur_, curT_, nxt_, nxtT_ = "N", "NT", "Pa", "PaT"
          tc_, tn_ = "TTa", "TTb"
          kn = lambda nm: [(nm, 0), (nm, 1)] if nm in ("N", "NT") else [nm]
          for lvl in range(1, 6):
              ja = nextA()
              for sl in range(4):
                  mm(pA[ja][:, sl * 128:(sl + 1) * 128], H[curT_][:, sl, :], H[cur_][:, sl, :], kn(cur_) + kn(curT_), [("pA", ja)])
              evac(H[nxt_][:], pA[ja][:].rearrange("p (a t) -> p a t", a=4), [("pA", ja)], [nxt_])
              if lvl < 5:
                  jb = nextA()
                  for sl in range(4):
                      mm(pA[jb][:, sl * 128:(sl + 1) * 128], H[cur_][:, sl, :], H[curT_][:, sl, :], kn(cur_) + kn(curT_), [("pA", jb)])
                  evac(H[nxtT_][:], pA[jb][:].rearrange("p (a t) -> p a t", a=4), [("pA", jb)], [nxtT_])
              jc = nextA()
              for sl in range(4):
                  mm(pA[jc][:, sl * 128:(sl + 1) * 128], H[nxt_][:, sl, :], H[tc_][:, sl, :], [nxt_, tc_], [("pA", jc)])
              dv(lambda e, o=H[tn_][:], s=pA[jc][:].rearrange("p (a t) -> p a t", a=4), t=H[tc_][:]: e.tensor_tensor(out=o, in0=s, in1=t, op=ALU.add),
                 [("pA", jc), tc_], [tn_])
              if lvl == 1:
                  cur_, curT_, nxt_, nxtT_ = "Pa", "PaT", "Pb", "PbT"
              else:
                  cur_, curT_, nxt_, nxtT_ = nxt_, nxtT_, cur_, curT_
              tc_, tn_ = tn_, tc_
          TTn = tc_
          ja = nextA()
          for h in range(4):
              p, r0 = h // 2, (h % 2) * 64
              mm(pA[ja][r0:r0 + 64, p * 128:(p + 1) * 128], B["at"][:, h * 64:(h + 1) * 64], H[TTn][:, slot(h), :], [K2("at"), TTn], [("pA", ja)])
              mm(pA[ja][:, 256 + h * 64:256 + (h + 1) * 64], H["MakT"][:, slot(h), :], B["vb"][:, h * 64:(h + 1) * 64],
                 [("MakT", h % 2), K2("vb")], [("pA", ja)])
          evac(HS["P1T"][b2][:], pA[ja][:, 0:256].rearrange("p (a t) -> p a t", a=2), [("pA", ja)], [("P1T", b2)])
          evac(H["W2"][:], pA[ja][:, 256:512].rearrange("p (h i) -> p h i", h=4), [("pA", ja)], ["W2"])
          jb = nextA()
          for h in range(4):
              mm(pA[jb][:, h * 64:(h + 1) * 64], H[TTn][:, slot(h), :], H["W2"][:, h, :], [TTn, "W2"], [("pA", jb)])
          evac(HS["P2"][b2][:], pA[jb][:, 0:256].rearrange("p (h i) -> p h i", h=4), [("pA", jb)], [("P2", b2)])
          stage(6)
          marks.append(len(rec))
          for c2 in range(2):
              cs = slice(c2 * 64, (c2 + 1) * 64)
              jq = nextQ()
              for h in range(4):
                  mm(pQ[jq][cs, h * 64:(h + 1) * 64], HS["P1T"][b2][:, h // 2, cs], STb[h][:, :], [("P1T", b2), ("STb", h)], [("pQ", jq)])
              dv(lambda e, o=Usb[cs, :, :], s=pQ[jq][cs, 0:256].rearrange("p (h i) -> p h i", h=4), t=HS["P2"][b2][cs, :, :]:
                 e.tensor_tensor(out=o, in0=s, in1=t, op=ALU.add), [("pQ", jq), ("P2", b2)], ["Usb"])
              jy = nextQ()
              for h in range(4):
                  p = h // 2
                  yo = pQ[jy][cs, h * 64:(h + 1) * 64]
                  mm(yo, TR["rtT"][b2][p][:, cs], STb[h][:, :], [("rtT", b2, p), ("STb", h)], [("pQ", jy)], start=True, stop=False)
                  mm(yo, HS["MrkT"][b2][:, slot(h), cs], B["vb"][:, h * 64:(h + 1) * 64], [("MrkT", b2, h % 2), K2("vb")], [("pQ", jy)],
                     start=False, stop=False)
                  mm(yo, HS["MrbT"][b2][:, slot(h), cs], Usb[:, h, :], [("MrbT", b2, h % 2), "Usb"], [("pQ", jy)], start=False, stop=True)
              evac(B["yb"][cs, :], pQ[jy][cs, 0:256], [("pQ", jy)], [K2("yb")])
              js = nextQ()
              for h in range(4):
                  r0 = (h % 2) * 64
                  rows = slice(r0, r0 + 64)
                  so = pQ[js][rows, h * 64:(h + 1) * 64]
                  mm(so, B["kh"][cs, h * 64:(h + 1) * 64], B["vb"][cs, h * 64:(h + 1) * 64], [K2("kh"), K2("vb")], [("pQ", js)],
                     start=True, stop=False)
                  mm(so, B["bh"][cs, h * 64:(h + 1) * 64], Usb[cs, h, :], [K2("bh"), "Usb"], [("pQ", js)], start=False, stop=True)
              for h in range(4):
                  p, r0 = h // 2, (h % 2) * 64
                  rows = slice(r0, r0 + 64)
                  dv(lambda e, o=ST[h][rows, :], s=pQ[js][rows, h * 64:(h + 1) * 64], g=gL[b2][rows, c2 * 2 + p:c2 * 2 + p + 1]:
                     e.scalar_tensor_tensor(out=o, in0=o, scalar=g, in1=s, op0=ALU.mult, op1=ALU.add),
                     [("ST", h), ("pQ", js), K2("gL")], [("ST", h)])
                  P.op("act", lambda e, o=STb[h][rows, :], s=ST[h][rows, :]: e.copy(out=o, in_=s), reads=[("ST", h)], writes=[("STb", h)])
          stage(7)
          marks.append(len(rec))
          yb = B["yb"]
          dv(lambda e, o=s4[:, 8:12], s=v3(yb[:]): e.tensor_reduce(out=o, in_=s, axis=AX.X, op=ALU.add), [K2("yb")], [K2("st4")])
          dv(lambda e, o=B["tmp"][:], s=yb[:]: e.tensor_tensor(out=o, in0=s, in1=s, op=ALU.mult), [K2("yb")], [K2("tmp")], eng="pool")
          dv(lambda e, o=s4[:, 12:16], s=v3(B["tmp"][:]): e.tensor_reduce(out=o, in_=s, axis=AX.X, op=ALU.add), [K2("tmp")], [K2("st4")])
          dv(lambda e, o=s4[:, 8:16]: e.tensor_scalar(out=o, in0=o, scalar1=1.0 / 64, scalar2=None, op0=ALU.mult), [K2("st4")], [K2("st4")])
          dv(lambda e, o=s4[:, 0:4], m=s4[:, 8:12]: e.tensor_tensor(out=o, in0=m, in1=m, op=ALU.mult), [K2("st4")], [K2("st4")])
          dv(lambda e, o=s4[:, 12:16], m2=s4[:, 0:4]: e.tensor_tensor(out=o, in0=o, in1=m2, op=ALU.subtract), [K2("st4")], [K2("st4")])
          dv(lambda e, o=s4[:, 12:16]: e.tensor_scalar(out=o, in0=o, scalar1=GN_EPS, scalar2=None, op0=ALU.add), [K2("st4")], [K2("st4")])
          P.op("act", lambda e, o=s4[:, 12:16]: e.activation(out=o, in_=o, func=AF.Sqrt), reads=[K2("st4")], writes=[K2("st4")])
          dv(lambda e, o=s4[:, 12:16]: e.reciprocal(out=o, in_=o), [K2("st4")], [K2("st4")])
          dv(lambda e, o=v3(yb[:]), m=s4[:, 8:12].unsqueeze(2).to_broadcast([128, 4, 64]): e.tensor_tensor(out=o, in0=o, in1=m, op=ALU.subtract),
             [K2("yb"), K2("st4")], [K2("yb")])
          dv(lambda e, o=v3(yb[:]), r=s4[:, 12:16].unsqueeze(2).to_broadcast([128, 4, 64]): e.tensor_tensor(out=o, in0=o, in1=r, op=ALU.mult),
             [K2("yb"), K2("st4")], [K2("yb")])
          dv(lambda e, o=yb[:]: e.tensor_tensor(out=o, in0=o, in1=vec["rw_gng"][:], op=ALU.mult), [K2("yb"), "rw_gng"], [K2("yb")], eng="pool")
          dv(lambda e, o=yb[:]: e.tensor_tensor(out=o, in0=o, in1=vec["rw_gnb"][:], op=ALU.add), [K2("yb"), "rw_gnb"], [K2("yb")])
          dv(lambda e, o=yb[:], b_=B["bon"][:]: e.tensor_tensor(out=o, in0=o, in1=b_, op=ALU.add), [K2("yb"), K2("bon")], [K2("yb")], eng="pool")
          P.op("act", lambda e, o=B["tmp2"][:], z_=z_: e.activation(out=o, in_=z_, func=AF.Silu), reads=[K2("cur")], writes=[K2("tmp2")])
          dv(lambda e, o=yb[:], z=B["tmp2"][:]: e.tensor_tensor(out=o, in0=o, in1=z, op=ALU.mult), [K2("yb"), K2("tmp2")], [K2("yb")])
          P.dma("sp", lambda e, o=y_dst[t0:t0 + 128, :], s=yb[:]: e.dma_start(out=o, in_=s), reads=[K2("yb")], writes=[("y_rwkv", blk)])
      except _Stop:
          pass
    P.op, P.dma = real_op, real_dma
    if stagger:
        pipeline_merge(recs, 4)
    else:
        for rec, _ in recs:
            for (f, eng, fn, reads, writes) in rec:
                f(eng, fn, reads=reads, writes=writes)
    ph.close()


def phase_outproj(nc, y_tok, x_res, w_out, ident_d, x_new):
    ph = Phase(nc)
    P = ph.P
    D = D_MODEL
    KT = D // 128
    T = OWN
    TT = T // 128
    KQ = 8
    NKQ = KT // KQ
    NB = 512
    ident = ph.sb([128, 128], BF16, "ident")
    hT = ph.sb([128, KT, T], BF16, "hT")
    xt = [ph.sb([128, D], F32, "xt") for _ in range(2)]
    hb = [ph.sb([128, D], BF16, "hb") for _ in range(2)]
    wb = [ph.sb([128, KT, NB], BF16, "wb") for _ in range(2)]
    stg = [ph.sb([128, NB], F32, "stg") for _ in range(4)]
    rs = [ph.sb([128, NB], F32, "rs") for _ in range(4)]
    pT = [ph.ps([128, 1024], BF16, "pT") for _ in range(2)]
    pM = [ph.ps([128, 512], F32, "pM") for _ in range(6)]
    P.dma("sp", lambda e: e.dma_start(out=ident[:], in_=ident_d[:, :]), writes=["ident"])
    tc_ = 0
    for tt in range(TT):
        i = tt % 2
        P.dma("sp", lambda e, o=xt[i][:], s=y_tok[tt * 128:(tt + 1) * 128, :]: e.dma_start(out=o, in_=s), writes=[("xt", i)])
        P.op("act", lambda e, o=hb[i][:], s=xt[i][:]: e.copy(out=o, in_=s), reads=[("xt", i)], writes=[("hb", i)])
        for kq in range(NKQ):
            pb = tc_ % 2
            tc_ += 1
            for k8 in range(KQ):
                kt = kq * KQ + k8
                P.op("pe", lambda e, o=pT[pb][:, k8 * 128:(k8 + 1) * 128], s=hb[i][:, kt * 128:(kt + 1) * 128]: e.transpose(
                    out=o, in_=s, identity=ident[:]), reads=[("hb", i), "ident"], writes=[("pT", pb)])
            dst = hT[:, kq * KQ:(kq + 1) * KQ, tt * 128:(tt + 1) * 128]
            src = pT[pb][:].rearrange("p (k t) -> p k t", k=KQ)
            if kq % 2 == 0:
                P.op("dve", lambda e, dst=dst, src=src: e.tensor_copy(out=dst, in_=src), reads=[("pT", pb)], writes=[("hT", tt, kq)])
            else:
                P.op("act", lambda e, dst=dst, src=src: e.copy(out=dst, in_=src), reads=[("pT", pb)], writes=[("hT", tt, kq)])
    wv = w_out.rearrange("(kt p) n -> p kt n", p=128)
    mc = 0
    for cb in range(D // NB):
        c0 = cb * NB
        wi = cb % 2
        for kq in range(NKQ):
            P.dma("pool", lambda e, o=wb[wi][:, kq * KQ:(kq + 1) * KQ, :], s=wv[:, kq * KQ:(kq + 1) * KQ, c0:c0 + NB]: e.dma_start(
                out=o, in_=s), writes=[("wb", wi, kq)])
        for tt in range(TT):
            j = mc % 6
            s4 = mc % 4
            mc += 1
            for kt in range(KT):
                P.op("pe", lambda e, o=pM[j][:], l=hT[:, kt, tt * 128:(tt + 1) * 128], r=wb[wi][:, kt, :], kt=kt: e.matmul(
                    o, lhsT=l, rhs=r, start=(kt == 0), stop=(kt == KT - 1)),
                    reads=[("hT", tt, kt // KQ), ("wb", wi, kt // KQ)], writes=[("pM", j)])
            P.dma("sp", lambda e, o=rs[s4][:], s=x_res[tt * 128:(tt + 1) * 128, c0:c0 + NB]: e.dma_start(out=o, in_=s), writes=[("rs", s4)])
            P.op("dve", lambda e, o=stg[s4][:], a=pM[j][:], b_=rs[s4][:]: e.tensor_tensor(out=o, in0=a, in1=b_, op=ALU.add),
                 reads=[("pM", j), ("rs", s4)], writes=[("stg", s4)])
            P.dma("sp", lambda e, o=x_new[tt * 128:(tt + 1) * 128, c0:c0 + NB], s=stg[s4][:]: e.dma_start(out=o, in_=s),
                  reads=[("stg", s4)], writes=[("xn", tt, cb)])
    ph.close()


def phase_finalnorm(nc, x_in, g, out, ntiles=OWN // 128):
    ph = Phase(nc)
    P = ph.P
    D = D_MODEL
    gb = ph.sb([128, D], F32, "gb")
    xt = [ph.sb([128, D], F32, "xt") for _ in range(2)]
    junk = ph.sb([128, D], BF16, "junk")
    ss = ph.sb([128, ntiles], F32, "ss")
    P.dma("sp", lambda e: e.dma_start(out=gb[:], in_=g[0:1, :].partition_broadcast(128)), writes=["gb"])
    P.op("dve", lambda e: e.memset(ss[:], 0.0), writes=["ss"])
    for tt in range(ntiles):
        i = tt % 2
        P.dma("sp", lambda e, o=xt[i][:], s=x_in[tt * 128:(tt + 1) * 128, :]: e.dma_start(out=o, in_=s), writes=[("xt", i)])
        sc = ss[:, tt:tt + 1]
        P.op("act", lambda e, s=xt[i][:], sc=sc: e.activation(out=junk[:], in_=s, func=AF.Square, accum_out=sc),
             reads=[("xt", i), "ss"], writes=["junk", ("ssv", tt)])
        P.op("dve", lambda e, sc=sc: e.tensor_scalar(out=sc, in0=sc, scalar1=1.0 / D, scalar2=NORM_EPS, op0=ALU.mult, op1=ALU.add),
             reads=[("ssv", tt)], writes=[("ssv", tt)])
        P.op("act", lambda e, sc=sc: e.activation(out=sc, in_=sc, func=AF.Sqrt), reads=[("ssv", tt)], writes=[("ssv", tt)])
        P.op("dve", lambda e, sc=sc: e.reciprocal(out=sc, in_=sc), reads=[("ssv", tt)], writes=[("ssv", tt)])
        P.op("dve", lambda e, o=xt[i][:], sc=sc: e.scalar_tensor_tensor(out=o, in0=o, scalar=sc, in1=gb[:], op0=ALU.mult, op1=ALU.mult),
             reads=[("xt", i), ("ssv", tt), "gb"], writes=[("xt", i)])
        P.dma("sp", lambda e, o=out[tt * 128:(tt + 1) * 128, :], s=xt[i][:]: e.dma_start(out=o, in_=s), reads=[("xt", i)],
              writes=[("out", tt)])
    ph.close()


def own_tok(q):
    return ((4 * np.arange(8)[:, None] + q) * 128 + np.arange(128)[None, :]).reshape(-1)


def const_inputs():
    i = np.arange(128)
    sel4 = np.zeros((4, 4, 128), np.float32)
    for h in range(4):
        sel4[h, h, :] = 1.0
    same = (i[:, None] // 64) == (i[None, :] // 64)
    sl = ((i[None, :] < i[:, None]) & same).astype(np.float32)
    su = np.ascontiguousarray(sl.T)
    return {
        "ident": np.eye(128, dtype=ml_dtypes.bfloat16),
        "ident_f": np.eye(128, dtype=np.float32), "ident_b": np.eye(128, dtype=ml_dtypes.bfloat16),
        "tri_le": (i[:, None] <= i[None, :]).astype(np.float32), "ones_f": np.ones((128, 128), np.float32),
        "sel4": sel4, "negbig_lt": np.where(i[None, :] < i[:, None], NEG_BIG, 0.0).astype(np.float32),
        "ident30k_b": (30000.0 * np.eye(128)).astype(ml_dtypes.bfloat16),
        "iota512": np.arange(512, dtype=np.float32)[None, :].copy(),
        "pow2": (0.5 ** (np.arange(NBIS) + 1)).astype(np.float32)[None, :].copy(),
        "mask_sl": sl, "mask_su": su, "mask_u": su + np.eye(128, dtype=np.float32), "ones_bd": same.astype(np.float32),
    }


def layer_params(inp, l, q):
    c = np.ascontiguousarray
    p = {}
    p["qrel"] = (q * 128 + np.arange(128, dtype=np.float32))[:, None].copy()
    p["g"] = c(inp["norm_g"][l][None, :])
    p["mlp_ln_g"] = c(inp["mlp_ln_g"][l][None, :])
    p["mlp_ln_b"] = c(inp["mlp_ln_b"][l][None, :])
    p["mlp_wsT"] = c(inp["mlp_w_s"][l].transpose(2, 0, 1))
    p["mlp_bsT"] = c(inp["mlp_b_s"][l].T)
    cols = np.concatenate([np.arange(256 * q, 256 * (q + 1)), 1024 + np.arange(128 * q, 128 * (q + 1)),
                           1536 + np.arange(128 * q, 128 * (q + 1))])
    cw = inp["ssm_conv_w"][l][:, cols]
    cb = inp["ssm_conv_b"][l][cols]
    hs = slice(4 * q, 4 * q + 4)
    p["ssm_cw"] = c(cw.reshape(4, 4, 128).transpose(2, 1, 0))
    p["ssm_cb"] = c(cb.reshape(4, 128).T)
    p["ssm_dtb_t"] = c(np.tile(inp["ssm_dt_bias"][l][hs], 32)[None, :])
    p["ssm_alog_t"] = c(np.tile(inp["ssm_A_log"][l][hs], 32)[None, :])
    p["ssm_D"] = c(inp["ssm_D"][l][hs][None, :])
    p["ssm_ng"] = c(inp["ssm_norm_g"][l][256 * q:256 * (q + 1)][None, :])
    mu = inp["rwkv_mu"][l]
    hc = np.arange(256 * q, 256 * (q + 1))
    p["rw_mu_tm"] = c(np.concatenate([mu[1024 * k + hc] for k in range(4)])[None, :])
    p["rw_mu_fm"] = c(mu[4096:4224][:, None])
    p["rw_w2a2"] = c(np.concatenate([inp["rwkv_w2"][l][:, hc], inp["rwkv_a2"][l][:, hc]], axis=0))
    for nm, src in (("rw_w0", "rwkv_w0"), ("rw_a0", "rwkv_a0"), ("rw_kk", "rwkv_k_k"), ("rw_ka", "rwkv_k_a"), ("rw_rk", "rwkv_r_k"),
                    ("rw_gng", "rwkv_gn_g"), ("rw_gnb", "rwkv_gn_b")):
        p[nm] = c(inp[src][l].reshape(-1)[hc][None, :])
    return p


def _decl(nc, arrs):
    out = {}
    for n, a in arrs.items():
        dt = BF16 if a.dtype == ml_dtypes.bfloat16 else F32
        out[n] = nc.dram_tensor(n, list(a.shape), dt, kind="ExternalInput").ap()
    return out


def build_AB(sample_inputs):
    nc = bass.Bass("TRN2", target_bir_lowering=False)
    d = _decl(nc, sample_inputs)
    wts = {n: d["w_" + n] for n in GROUP_INFO}
    scr = make_scratch(nc)
    maskT = nc.dram_tensor("maskT", [128, NPAIR, 128], BF16).ap()
    y_own = nc.dram_tensor("y_own", [OWN, 2048], F32, kind="ExternalOutput").ap()
    y_all = nc.dram_tensor("y_all", [SEQ, 512], F32, kind="ExternalOutput").ap()
    phase_inproj(nc, d["x_all"], d["x_own"], d["g"], d["ident"], wts, scr)
    phase_indexer(nc, scr, d, maskT)
    phase_attn(nc, scr, d, maskT, y_own)
    phase_mlp(nc, scr, d, y_own)
    phase_ssm(nc, scr, d, y_all)
    phase_rwkv(nc, scr, d, y_all)
    return nc


def build_C(final):
    nc = bass.Bass("TRN2", target_bir_lowering=False)
    y_tok = nc.dram_tensor("y_tok", [OWN, D_MODEL], F32, kind="ExternalInput").ap()
    x_res = nc.dram_tensor("x_res", [OWN, D_MODEL], F32, kind="ExternalInput").ap()
    w_out = nc.dram_tensor("w_out", [D_MODEL, D_MODEL], F32, kind="ExternalInput").ap()
    ident = nc.dram_tensor("ident", [128, 128], BF16, kind="ExternalInput").ap()
    if final:
        g = nc.dram_tensor("gf", [1, D_MODEL], F32, kind="ExternalInput").ap()
        x_mid = nc.dram_tensor("x_mid", [OWN, D_MODEL], F32).ap()
        out = nc.dram_tensor("out", [OWN, D_MODEL], F32, kind="ExternalOutput").ap()
        phase_outproj(nc, y_tok, x_res, w_out, ident, x_mid)
        phase_finalnorm(nc, x_mid, g, out)
    else:
        out = nc.dram_tensor("out", [OWN, D_MODEL], F32, kind="ExternalOutput").ap()
        phase_outproj(nc, y_tok, x_res, w_out, ident, out)
    return nc


def kernel_unfused(**inp):
    inp = {k: np.asarray(v) for k, v in inp.items()}
    x = inp["x"]
    cst = const_inputs()
    otk = [own_tok(q) for q in range(NQ)]
    nc_ab = None
    for l in range(2):
        w_in = inp["w_in"][l]
        in_maps = []
        for c in range(NCORE):
            b, q = c // NQ, c % NQ
            m = dict(cst)
            m.update(layer_params(inp, l, q))
            m["x_all"] = np.ascontiguousarray(x[b])
            m["x_own"] = np.ascontiguousarray(x[b][otk[q]])
            for name, cols in col_groups(q).items():
                m["w_" + name] = np.ascontiguousarray(w_in[:, cols])
            in_maps.append(m)
        if nc_ab is None:
            nc_ab = build_AB(in_maps[0])
        res = run_bass_kernel_spmd(nc_ab, in_maps, core_ids=list(range(NCORE))).results
        in_maps_c = []
        for c in range(NCORE):
            b, q = c // NQ, c % NQ
            y_tok = np.empty((OWN, D_MODEL), np.float32)
            y_tok[:, 0:1024] = res[c]["y_own"][:, 0:1024]
            y_tok[:, 3072:4096] = res[c]["y_own"][:, 1024:2048]
            for q2 in range(NQ):
                ya = res[b * NQ + q2]["y_all"][otk[q]]
                y_tok[:, 1024 + 256 * q2:1024 + 256 * (q2 + 1)] = ya[:, 0:256]
                y_tok[:, 2048 + 256 * q2:2048 + 256 * (q2 + 1)] = ya[:, 256:512]
            m = {"y_tok": y_tok, "x_res": np.ascontiguousarray(x[b][otk[q]]), "w_out": np.ascontiguousarray(inp["w_out"][l]),
                 "ident": cst["ident"]}
            if l == 1:
                m["gf"] = np.ascontiguousarray(inp["final_norm_g"][None, :])
            in_maps_c.append(m)
        nc_c = build_C(final=(l == 1))
        resc = run_bass_kernel_spmd(nc_c, in_maps_c, core_ids=list(range(NCORE))).results
        xn = np.empty_like(x)
        for c in range(NCORE):
            b, q = c // NQ, c % NQ
            xn[b][otk[q]] = resc[c]["out"]
        x = xn
    return x


def fused_ginfo():
    gi = {"fmb_all": (1152, "fm", BF16, "all"), "tmb_all": (1024, "tm", BF16, "all"),
          "fmb_own": (2048, "fm", BF16, "all"), "tmf_own": (4112, "tm", F32, "all")}
    for q in range(NQ):
        gi[f"fmf_all{q}"] = (640, "fm", F32, "all")
        gi[f"tmf_all{q}"] = (1284, "tm", F32, "all")
    return gi


FUSED_BLOCKS = [(qb // 4 + 1, qb % 4, qb % 4 + 1) for qb in range(32)]
Q_KEYS = ("ssm_cw", "ssm_cb", "ssm_dtb_t", "ssm_alog_t", "ssm_D", "ssm_ng", "rw_mu_tm", "rw_mu_fm", "rw_w2a2", "rw_w0", "rw_a0",
          "rw_kk", "rw_ka", "rw_rk", "rw_gng", "rw_gnb")
L_KEYS = ("g", "mlp_ln_g", "mlp_ln_b", "mlp_wsT", "mlp_bsT")


def fused_inputs(inp, b):
    c = np.ascontiguousarray
    m = dict(const_inputs())
    m["qrel"] = c((np.arange(4, dtype=np.float32)[None, :] * 128 + np.arange(128, dtype=np.float32)[:, None]))
    m["x"] = c(inp["x"][b])
    m["gf"] = c(inp["final_norm_g"][None, :])
    for l in range(2):
        w_in = inp["w_in"][l]
        cg0 = col_groups(0)
        for name in ("fmb_all", "tmb_all", "fmb_own", "tmf_own"):
            m[f"w{l}_{name}"] = c(w_in[:, cg0[name]])
        for q in range(NQ):
            cg = col_groups(q)
            m[f"w{l}_fmf_all{q}"] = c(w_in[:, cg["fmf_all"]])
            m[f"w{l}_tmf_all{q}"] = c(w_in[:, cg["tmf_all"]])
            lp = layer_params(inp, l, q)
            for key in Q_KEYS:
                m[f"l{l}q{q}_{key}"] = lp[key]
            if q == 0:
                for key in L_KEYS:
                    m[f"l{l}_{key}"] = lp[key]
        m[f"wout{l}"] = c(inp["w_out"][l])
    return m


def build_fused(sample, upto=None, nlayers=2, skip=()):
    nc = bass.Bass("TRN2", target_bir_lowering=False)
    d = _decl(nc, sample)
    gi = fused_ginfo()
    scr = {}
    for name, (ncols, layout, dt, _) in gi.items():
        shape = [SEQ, ncols] if layout == "tm" else [ncols, SEQ]
        scr[name] = nc.dram_tensor("scr_" + name, shape, dt).ap()
    _, npairs = pair_offsets(FUSED_BLOCKS)
    maskT = nc.dram_tensor("maskT", [128, npairs, 128], BF16).ap()
    y_full = nc.dram_tensor("y_full", [SEQ, D_MODEL], F32).ap()
    xs = [d["x"], nc.dram_tensor("x1", [SEQ, D_MODEL], F32).ap(), nc.dram_tensor("x2", [SEQ, D_MODEL], F32).ap()]
    out = nc.dram_tensor("out", [SEQ, D_MODEL], F32, kind="ExternalOutput").ap()
    gnames = list(gi.keys())
    cnt = [0]

    def go():
        cnt[0] += 1
        return (upto is None or cnt[0] <= upto) and cnt[0] not in skip

    for l in range(nlayers):
        x_cur, x_nxt = xs[l], xs[l + 1]
        wts = {name: d[f"w{l}_{name}"] for name in gnames}
        plan = [(x_cur, p * 1024, [(n, p * 1024) for n in gnames]) for p in range(4)]
        if go():
            phase_inproj(nc, None, None, d[f"l{l}_g"], d["ident"], wts, scr, plan=plan, ginfo=gi)
        prm_l = dict(d)
        for key in L_KEYS:
            prm_l[key] = d[f"l{l}_{key}"]
        sc_own = {"fmb_own": scr["fmb_own"], "fmb_all": scr["fmb_all"], "tmf_own": scr["tmf_own"], "tmb_all": scr["tmb_all"]}
        if go():
            phase_indexer(nc, sc_own, prm_l, maskT, blocks=FUSED_BLOCKS)
        if go():
            phase_attn(nc, sc_own, prm_l, maskT, y_full, blocks=FUSED_BLOCKS)
        if go():
            phase_mlp(nc, sc_own, prm_l, y_full, nchunks=SEQ // 128, ycol0=3072)
        for q in range(NQ):
            prm_q = dict(d)
            for key in Q_KEYS:
                prm_q[key] = d[f"l{l}q{q}_{key}"]
            sc_q = {"fmf_all": scr[f"fmf_all{q}"], "tmf_all": scr[f"tmf_all{q}"]}
            if go():
                phase_ssm(nc, sc_q, prm_q, None, y_dst=y_full[:, 1024 + 256 * q:1024 + 256 * (q + 1)])
            if go():
                phase_rwkv(nc, sc_q, prm_q, None, y_dst=y_full[:, 2048 + 256 * q:2048 + 256 * (q + 1)])
        for p in range(4):
            rs_ = slice(p * 1024, (p + 1) * 1024)
            if go():
                phase_outproj(nc, y_full[rs_, :], x_cur[rs_, :], d[f"wout{l}"], d["ident"], x_nxt[rs_, :])
    if upto is None:
        phase_finalnorm(nc, xs[nlayers], d["gf"], out, ntiles=SEQ // 128)
    else:
        phase_finalnorm(nc, y_full, d["gf"], out, ntiles=SEQ // 128)
    return nc


def kernel_fused(**inp):
    inp = {k: np.asarray(v) for k, v in inp.items()}
    nb = inp["x"].shape[0]
    in_maps = [fused_inputs(inp, b) for b in range(nb)]
    nc = build_fused(in_maps[0])
    res = run_bass_kernel_spmd(nc, in_maps, core_ids=list(range(nb))).results
    return np.stack([res[b]["out"] for b in range(nb)], axis=0)


FUSED = True


def kernel(**inp):
    return kernel_fused(**inp) if FUSED else kernel_unfused(**inp)
```

```python
import contextlib
import numpy as np
import ml_dtypes
import concourse.bass as bass
import concourse.mybir as mybir
from concourse.bass_utils import run_bass_kernel_spmd

F32 = mybir.dt.float32
BF16 = mybir.dt.bfloat16
ALU = mybir.AluOpType
AF = mybir.ActivationFunctionType
AX = mybir.AxisListType

D_MODEL = 4096
SEQ = 4096
NCORE = 8
NQ = 4
OWN = SEQ // NQ
NORM_EPS = 1e-5
GN_EPS = 64e-5

COMPUTE = ("pe", "act", "dve", "pool")


SEM_ROLL = 30000


class SemState:
    def __init__(self, nc):
        self.nc = nc
        self.st = contextlib.ExitStack()
        self.sems = {}
        self.count = {e: 0 for e in COMPUTE}
        self.dma_slots = {}
        self.dma_rr = {}
        self.n_dma_slots = 8

    def handle(self, key):
        h = self.sems.get(key)
        if h is None:
            h = self.st.enter_context(self.nc.semaphore("s_" + "_".join(str(x) for x in key)))
            self.sems[key] = h
        return h


def semstate(nc):
    ss = getattr(nc, "_semstate", None)
    if ss is None:
        ss = SemState(nc)
        nc._semstate = ss
    return ss


class Prog:
    def __init__(self, nc):
        self.nc = nc
        self.ss = semstate(nc)
        self.streams = {e: [] for e in ("pe", "act", "dve", "pool", "sp")}
        self.last_writer = {}
        self.readers = {}
        self.waited = {}
        self.used_keys = []
        self.excl = set()

    def _semkey(self, key):
        if key not in self.used_keys:
            self.used_keys.append(key)
        return key

    def _deps_for(self, reads, writes, eng=None):
        deps = set()
        for b in reads:
            w = self.last_writer.get(b)
            if w is not None:
                deps.add(w)
        for b in writes:
            w = self.last_writer.get(b)
            if w is not None:
                deps.add(w)
            for r in self.readers.get(b, ()):
                if eng is not None and r[0][0] == eng:
                    continue
                deps.add(r)
        if eng == "pe":
            deps = {d for d in deps if d[0][0] != "pe"}
        return deps

    def _commit(self, tok, reads, writes):
        for b in reads:
            self.readers.setdefault(b, []).append(tok)
        for b in writes:
            self.last_writer[b] = tok
            self.readers[b] = []

    def _waits(self, eng, deps):
        waits = []
        for (k, v) in sorted(deps, key=lambda t: (str(t[0]), t[1])):
            if self.waited.get((eng, k), -1) >= v:
                continue
            self.waited[(eng, k)] = v
            self._semkey(k)
            waits.append((k, v))
        return waits

    def op(self, eng, fn, reads=(), writes=()):
        ex = [b for b in reads if (b[0] if isinstance(b, tuple) else b) in self.excl]
        if ex:
            writes = list(writes) + [b for b in ex if b not in writes]
        ss = self.ss
        idx = ss.count[eng]
        ss.count[eng] += 1
        ep = idx // SEM_ROLL
        key = self._semkey((eng, ep))
        deps = self._deps_for(reads, writes, eng=eng)
        waits = self._waits(eng, deps)
        self.streams[eng].append((waits, fn, (key, 1)))
        tok = (key, idx - ep * SEM_ROLL + 1)
        self._commit(tok, reads, writes)
        return tok

    def dma(self, eng, fn, reads=(), writes=()):
        ss = self.ss
        slots = ss.dma_slots.get(eng)
        if slots is None:
            slots = [[("dma", eng, i, 0), 0] for i in range(ss.n_dma_slots)]
            ss.dma_slots[eng] = slots
        rr = ss.dma_rr.get(eng, 0)
        ss.dma_rr[eng] = (rr + 1) % len(slots)
        slot = slots[rr]
        deps = self._deps_for(reads, writes)
        if slot[1] > 0:
            deps.add((slot[0], slot[1]))
        if slot[1] + 16 > SEM_ROLL:
            slot[0] = ("dma", eng, slot[0][2], slot[0][3] + 1)
            slot[1] = 0
        key = self._semkey(slot[0])
        waits = self._waits(eng, deps)
        slot[1] += 16
        self.streams[eng].append((waits, fn, (key, 16)))
        tok = (key, slot[1])
        self._commit(tok, reads, writes)
        return tok

    def finish(self, eng="sp"):
        toks = set()
        ss = self.ss
        for q, slots in ss.dma_slots.items():
            for key, cnt in slots:
                if cnt > 0:
                    toks.add((key, cnt))
        for e in COMPUTE:
            n = ss.count[e]
            if n > 0:
                ep = (n - 1) // SEM_ROLL
                toks.add(((e, ep), n - ep * SEM_ROLL))
        waits = self._waits(eng, toks)
        self.streams[eng].append((waits, None, None))

    def emit(self):
        nc = self.nc
        ss = self.ss
        sems = {k: ss.handle(k) for k in self.used_keys}
        with nc.Block() as block:

            def run(engobj, items):
                for waits, fn, inc in items:
                    for (k, v) in waits:
                        engobj.wait_ge(sems[k], v)
                    if fn is not None:
                        ins = fn(engobj)
                        ins.then_inc(sems[inc[0]], inc[1])

            @block.tensor
            def _(e):
                run(e, self.streams["pe"])

            @block.scalar
            def _(e):
                run(e, self.streams["act"])

            @block.vector
            def _(e):
                run(e, self.streams["dve"])

            @block.gpsimd
            def _(e):
                run(e, self.streams["pool"])

            @block.sync
            def _(e):
                run(e, self.streams["sp"])


class Phase:
    _count = [0]

    def __init__(self, nc):
        self.nc = nc
        self.st = contextlib.ExitStack()
        self.P = Prog(nc)
        self.n = 0
        Phase._count[0] += 1
        self.pid = Phase._count[0]

    def sb(self, shape, dt, name=None):
        self.n += 1
        return self.st.enter_context(self.nc.sbuf_tensor(f"{name or 't'}_{self.n}_{self.pid}", list(shape), dt))

    def ps(self, shape, dt, name=None):
        self.n += 1
        return self.st.enter_context(self.nc.psum_tensor(f"{name or 'p'}_{self.n}_{self.pid}", list(shape), dt))

    def dump(self, name, sb_ap, reads):
        dbg = getattr(self.nc, "_dbg", None)
        if not dbg or name not in dbg:
            return
        d = dbg[name]
        self.P.dma("sp", lambda e: e.dma_start(out=d, in_=sb_ap), reads=reads, writes=[("dbg", name)])

    def close(self):
        self.P.finish()
        self.P.emit()
        self.st.close()


ATT0, SSM0, RWKV0, MLP0 = 0, 5200, 8288, 12512


def col_groups(q):
    r = lambda a, n: np.arange(a, a + n)
    att_q, att_k, att_v, att_z = r(0, 1024), r(1024, 1024), r(2048, 1024), r(3072, 1024)
    att_qi, att_ki, att_wi = r(4096, 1024), r(5120, 64), r(5184, 16)
    ssm_z = r(SSM0 + 256 * q, 256)
    ssm_x = r(SSM0 + 1024 + 256 * q, 256)
    ssm_B = r(SSM0 + 2048 + 128 * q, 128)
    ssm_C = r(SSM0 + 2560 + 128 * q, 128)
    ssm_dt = r(SSM0 + 3072 + 4 * q, 4)
    rw = [r(RWKV0 + 1024 * i + 256 * q, 256) for i in range(4)]
    rw_wl, rw_al = r(RWKV0 + 4096, 64), r(RWKV0 + 4160, 64)
    mlp = [r(MLP0 + 1024 * i, 1024) for i in range(3)]
    cat = np.concatenate
    return {
        "fmb_all": cat([att_k, att_ki, att_ki]),
        "fmf_all": cat([ssm_x, ssm_B, ssm_C, rw_wl, rw_al]),
        "tmb_all": att_v,
        "tmf_all": cat([ssm_z, ssm_dt] + rw),
        "fmb_own": cat([att_q, att_qi]),
        "tmf_own": cat([att_z, att_wi] + mlp),
    }


GROUP_INFO = {
    "fmb_all": (1152, "fm", BF16, "all"),
    "fmf_all": (640, "fm", F32, "all"),
    "tmb_all": (1024, "tm", BF16, "all"),
    "tmf_all": (1284, "tm", F32, "all"),
    "fmb_own": (2048, "fm", BF16, "own"),
    "tmf_own": (4112, "tm", F32, "own"),
}


def phase_inproj(nc, x_all, x_own, g, ident_d, wts, scr, plan=None, ginfo=None):
    ph = Phase(nc)
    P = ph.P
    D = D_MODEL
    KT = D // 128
    T = 1024
    TT = T // 128
    KQ = 8
    NKQ = KT // KQ
    NB = 512
    ident = ph.sb([128, 128], BF16, "ident")
    hT = ph.sb([128, KT, T], BF16, "hT")
    xt = [ph.sb([128, D], F32, "xt") for _ in range(2)]
    hb = [ph.sb([128, D], BF16, "hb") for _ in range(2)]
    wb = [ph.sb([128, KT, NB], BF16, "wb") for _ in range(2)]
    stg = [ph.sb([128, NB], F32, "stg") for _ in range(4)]
    stgb = [ph.sb([128, NB], BF16, "stgb") for _ in range(4)]
    gb = ph.sb([128, D], F32, "gb")
    ss = ph.sb([128, 8 * 5], F32, "ss")
    rstd = ph.sb([128, 8 * 5], F32, "rstd")
    pT = [ph.ps([128, 1024], BF16, "pT") for _ in range(2)]
    pM = [ph.ps([128, 512], F32, "pM") for _ in range(6)]

    P.dma("sp", lambda e: e.dma_start(out=ident[:], in_=ident_d[:, :]), writes=["ident"])
    P.dma("sp", lambda e: e.dma_start(out=gb[:], in_=g[0:1, :].partition_broadcast(128)), writes=["gb"])
    cnt = {"t": 0, "m": 0, "w": 0, "x": 0}
    P.op("dve", lambda e: e.memset(ss[:], 0.0), writes=[("ss", i) for i in range(40)])

    def load_hT(x_ap, row0, pidx):
        for tt in range(TT):
            i = cnt["x"] % 2
            cnt["x"] += 1
            xb, hbb = xt[i], hb[i]
            sc = pidx * 8 + tt
            P.dma("sp", lambda e, xb=xb, tt=tt: e.dma_start(out=xb[:], in_=x_ap[row0 + tt * 128:row0 + (tt + 1) * 128, :]),
                  writes=[("xt", i)])
            P.op("act", lambda e, xb=xb, hbb=hbb, sc=sc: e.activation(out=hbb[:], in_=xb[:], func=AF.Square,
                                                                     accum_out=ss[:, sc:sc + 1]),
                 reads=[("xt", i)], writes=[("hb", i), ("ss", sc)])
            P.op("dve", lambda e, sc=sc: e.tensor_scalar(out=rstd[:, sc:sc + 1], in0=ss[:, sc:sc + 1],
                                                         scalar1=1.0 / D, scalar2=NORM_EPS, op0=ALU.mult, op1=ALU.add),
                 reads=[("ss", sc)], writes=[("rstd", sc)])
            P.op("act", lambda e, sc=sc: e.activation(out=rstd[:, sc:sc + 1], in_=rstd[:, sc:sc + 1], func=AF.Sqrt),
                 reads=[("rstd", sc)], writes=[("rstd", sc)])
            P.op("dve", lambda e, sc=sc: e.reciprocal(out=rstd[:, sc:sc + 1], in_=rstd[:, sc:sc + 1]),
                 reads=[("rstd", sc)], writes=[("rstd", sc)])
            P.op("dve", lambda e, xb=xb, hbb=hbb, sc=sc: e.scalar_tensor_tensor(
                out=hbb[:], in0=xb[:], scalar=rstd[:, sc:sc + 1], in1=gb[:], op0=ALU.mult, op1=ALU.mult),
                reads=[("xt", i), ("rstd", sc), "gb", ("hb", i)], writes=[("hb", i)])
            for kq in range(NKQ):
                pb = cnt["t"] % 2
                cnt["t"] += 1
                for k8 in range(KQ):
                    kt = kq * KQ + k8
                    P.op("pe", lambda e, pb=pb, k8=k8, kt=kt, hbb=hbb: e.transpose(
                        out=pT[pb][:, k8 * 128:(k8 + 1) * 128], in_=hbb[:, kt * 128:(kt + 1) * 128], identity=ident[:]),
                        reads=[("hb", i), "ident"], writes=[("pT", pb)])
                dst = hT[:, kq * KQ:(kq + 1) * KQ, tt * 128:(tt + 1) * 128]
                src = pT[pb][:].rearrange("p (k t) -> p k t", k=KQ)
                if kq % 2 == 0:
                    P.op("dve", lambda e, dst=dst, src=src: e.tensor_copy(out=dst, in_=src),
                         reads=[("pT", pb)], writes=[("hT", tt, kq)])
                else:
                    P.op("act", lambda e, dst=dst, src=src: e.copy(out=dst, in_=src),
                         reads=[("pT", pb)], writes=[("hT", tt, kq)])

    def evac_store(j, n_part, nfree, is_bf16, dst_ap, okey):
        s = cnt["m"] % 4
        use_dve = (cnt["m"] % 2 == 0)
        cnt["m"] += 1
        sbuf = (stgb if is_bf16 else stg)[s]
        skey = ("stgb" if is_bf16 else "stg", s)
        if use_dve:
            P.op("dve", lambda e: e.tensor_copy(out=sbuf[0:n_part, 0:nfree], in_=pM[j][0:n_part, 0:nfree]),
                 reads=[("pM", j)], writes=[skey])
        else:
            P.op("act", lambda e: e.copy(out=sbuf[0:n_part, 0:nfree], in_=pM[j][0:n_part, 0:nfree]),
                 reads=[("pM", j)], writes=[skey])
        P.dma("sp", lambda e: e.dma_start(out=dst_ap, in_=sbuf[0:n_part, 0:nfree]), reads=[skey], writes=[okey])

    def do_group(name, tok0):
        ncols, layout, dt, _ = (ginfo or GROUP_INFO)[name]
        w = wts[name]
        dst = scr[name]
        wv = w.rearrange("(kt p) n -> p kt n", p=128)
        is_bf = (dt == BF16)
        for c0 in range(0, ncols, NB):
            nb = min(NB, ncols - c0)
            wi = cnt["w"] % 2
            cnt["w"] += 1
            wbb = wb[wi]
            for kq in range(NKQ):
                P.dma("pool", lambda e, wbb=wbb, kq=kq, c0=c0, nb=nb: e.dma_start(
                    out=wbb[:, kq * KQ:(kq + 1) * KQ, 0:nb], in_=wv[:, kq * KQ:(kq + 1) * KQ, c0:c0 + nb]),
                    writes=[("wb", wi, kq)])
            if layout == "tm":
                for tt in range(TT):
                    j = cnt["m"] % 6
                    for kt in range(KT):
                        P.op("pe", lambda e, j=j, kt=kt, tt=tt, wbb=wbb, nb=nb: e.matmul(
                            pM[j][:, 0:nb], lhsT=hT[:, kt, tt * 128:(tt + 1) * 128], rhs=wbb[:, kt, 0:nb],
                            start=(kt == 0), stop=(kt == KT - 1)),
                            reads=[("hT", tt, kt // KQ), ("wb", wi, kt // KQ)], writes=[("pM", j)])
                    r0 = tok0 + tt * 128
                    evac_store(j, 128, nb, is_bf, dst[r0:r0 + 128, c0:c0 + nb], (name, "o", tok0, tt, c0))
            else:
                for ct in range(nb // 128):
                    for th in range(T // 512):
                        j = cnt["m"] % 6
                        for kt in range(KT):
                            P.op("pe", lambda e, j=j, kt=kt, th=th, ct=ct, wbb=wbb: e.matmul(
                                pM[j][:, 0:512], lhsT=wbb[:, kt, ct * 128:(ct + 1) * 128],
                                rhs=hT[:, kt, th * 512:(th + 1) * 512], start=(kt == 0), stop=(kt == KT - 1)),
                                reads=[("hT", 4 * th, kt // KQ), ("hT", 4 * th + 1, kt // KQ), ("hT", 4 * th + 2, kt // KQ),
                                       ("hT", 4 * th + 3, kt // KQ), ("wb", wi, kt // KQ)], writes=[("pM", j)])
                        cc = c0 + ct * 128
                        t0 = tok0 + th * 512
                        evac_store(j, 128, 512, is_bf, dst[cc:cc + 128, t0:t0 + 512], (name, "o", tok0, th, cc))

    if plan is None:
        plan = [(x_all, p * 1024, [(n, p * 1024) for n in ("fmb_all", "fmf_all", "tmb_all", "tmf_all")]) for p in range(4)]
        plan.append((x_own, 0, [("fmb_own", 0), ("tmf_own", 0)]))
    for pidx, (xsrc, row0, glist) in enumerate(plan):
        load_hT(xsrc, row0, pidx)
        for name, tok0 in glist:
            do_group(name, tok0)
    ph.close()


def make_scratch(nc, kind=None):
    scr = {}
    for name, (ncols, layout, dt, which) in GROUP_INFO.items():
        ntok = SEQ if which == "all" else OWN
        shape = [ntok, ncols] if layout == "tm" else [ncols, ntok]
        if kind:
            scr[name] = nc.dram_tensor("scr_" + name, shape, dt, kind=kind).ap()
        else:
            scr[name] = nc.dram_tensor("scr_" + name, shape, dt).ap()
    return scr


TMF_OWN_ATTZ, TMF_OWN_WI, TMF_OWN_U, TMF_OWN_V, TMF_OWN_Z = 0, 1024, 1040, 2064, 3088


def phase_mlp(nc, scr, prm, y_own, nchunks=OWN // 128, ycol0=1024):
    ph = Phase(nc)
    P = ph.P
    src = scr["tmf_own"]
    W = 1024
    gb = ph.sb([128, W], F32, "lng")
    bb = ph.sb([128, W], F32, "lnb")
    wsT = ph.sb([128, 8, 128], F32, "wsT")
    wcT = ph.sb([128, 8, 128], BF16, "wcT")
    tri = ph.sb([128, 128], F32, "tri")
    bsT = ph.sb([128, 8], F32, "bsT")
    bbc = ph.sb([128, 8, 128], F32, "bbc")
    zero = ph.sb([128, 128], F32, "zero")
    NBUF = 2
    ut = [ph.sb([128, W], F32, "u") for _ in range(NBUF)]
    vt = [ph.sb([128, W], F32, "v") for _ in range(NBUF)]
    zt = [ph.sb([128, W], F32, "z") for _ in range(NBUF)]
    vn = [ph.sb([128, W], F32, "vn") for _ in range(NBUF)]
    vnb = [ph.sb([128, W], BF16, "vnb") for _ in range(NBUF)]
    junk = ph.sb([128, W], BF16, "junk")
    t1 = [ph.sb([128, W], F32, "t1") for _ in range(NBUF)]
    st = ph.sb([128, 8 * nchunks], F32, "stats")
    pV = [ph.ps([128, 512], F32, "pV") for _ in range(4)]

    P.dma("sp", lambda e: e.dma_start(out=gb[:], in_=prm["mlp_ln_g"][0:1, :].partition_broadcast(128)), writes=["gb"])
    P.dma("sp", lambda e: e.dma_start(out=bb[:], in_=prm["mlp_ln_b"][0:1, :].partition_broadcast(128)), writes=["bb"])
    P.dma("sp", lambda e: e.dma_start(out=wsT[:], in_=prm["mlp_wsT"][:, :, :]), writes=["wsT"])
    P.dma("sp", lambda e: e.dma_start(out=tri[:], in_=prm["tri_le"][:, :]), writes=["tri"])
    P.dma("sp", lambda e: e.dma_start(out=bsT[:], in_=prm["mlp_bsT"][:, :]), writes=["bsT"])
    P.op("dve", lambda e: e.memset(zero[:], 0.0), writes=["zero"])
    P.op("dve", lambda e: e.memset(st[:], 0.0), writes=[("st", c) for c in range(nchunks)])
    for g in range(8):
        P.op("dve", lambda e, g=g: e.tensor_tensor(out=wcT[:, g, :], in0=wsT[:, g, :], in1=tri[:], op=ALU.mult),
             reads=["wsT", "tri"], writes=["wcT"])
        P.op("dve", lambda e, g=g: e.tensor_scalar(out=bbc[:, g, :], in0=zero[:], scalar1=bsT[:, g:g + 1], scalar2=None,
                                                   op0=ALU.add), reads=["zero", "bsT"], writes=["bbc"])
    for c in range(nchunks):
        i = c % NBUF
        r0 = c * 128
        P.dma("sp", lambda e, i=i, r0=r0: e.dma_start(out=ut[i][:], in_=src[r0:r0 + 128, TMF_OWN_U:TMF_OWN_U + W]),
              writes=[("u", i)])
        P.dma("sp", lambda e, i=i, r0=r0: e.dma_start(out=vt[i][:], in_=src[r0:r0 + 128, TMF_OWN_V:TMF_OWN_V + W]),
              writes=[("v", i)])
        P.dma("sp", lambda e, i=i, r0=r0: e.dma_start(out=zt[i][:], in_=src[r0:r0 + 128, TMF_OWN_Z:TMF_OWN_Z + W]),
              writes=[("z", i)])
        s0 = c * 8
        P.op("act", lambda e, i=i, s0=s0: e.activation(out=junk[:], in_=vt[i][:], func=AF.Square,
                                                       accum_out=st[:, s0 + 1:s0 + 2]),
             reads=[("v", i)], writes=["junk", ("st", c)])
        P.op("dve", lambda e, i=i, s0=s0: e.reduce_sum(out=st[:, s0:s0 + 1], in_=vt[i][:], axis=AX.X),
             reads=[("v", i), ("st", c)], writes=[("st", c)])
        P.op("dve", lambda e, s0=s0: e.tensor_scalar(out=st[:, s0:s0 + 2], in0=st[:, s0:s0 + 2], scalar1=1.0 / W,
                                                     scalar2=None, op0=ALU.mult), reads=[("st", c)], writes=[("st", c)])
        P.op("dve", lambda e, s0=s0: e.tensor_tensor(out=st[:, s0 + 2:s0 + 3], in0=st[:, s0:s0 + 1], in1=st[:, s0:s0 + 1],
                                                     op=ALU.mult), reads=[("st", c)], writes=[("st", c)])
        P.op("dve", lambda e, s0=s0: e.tensor_tensor(out=st[:, s0 + 3:s0 + 4], in0=st[:, s0 + 1:s0 + 2],
                                                     in1=st[:, s0 + 2:s0 + 3], op=ALU.subtract),
             reads=[("st", c)], writes=[("st", c)])
        P.op("dve", lambda e, s0=s0: e.tensor_scalar(out=st[:, s0 + 3:s0 + 4], in0=st[:, s0 + 3:s0 + 4], scalar1=NORM_EPS,
                                                     scalar2=None, op0=ALU.add), reads=[("st", c)], writes=[("st", c)])
        P.op("act", lambda e, s0=s0: e.activation(out=st[:, s0 + 3:s0 + 4], in_=st[:, s0 + 3:s0 + 4], func=AF.Sqrt),
             reads=[("st", c)], writes=[("st", c)])
        P.op("dve", lambda e, s0=s0: e.reciprocal(out=st[:, s0 + 3:s0 + 4], in_=st[:, s0 + 3:s0 + 4]),
             reads=[("st", c)], writes=[("st", c)])
        P.op("dve", lambda e, i=i, s0=s0: e.tensor_scalar(out=vn[i][:], in0=vt[i][:], scalar1=st[:, s0:s0 + 1],
                                                          scalar2=st[:, s0 + 3:s0 + 4], op0=ALU.subtract, op1=ALU.mult),
             reads=[("v", i), ("st", c)], writes=[("vn", i)])
        P.op("pool", lambda e, i=i: e.tensor_tensor(out=vn[i][:], in0=vn[i][:], in1=gb[:], op=ALU.mult),
             reads=[("vn", i), "gb"], writes=[("vn", i)])
        P.op("dve", lambda e, i=i: e.tensor_tensor(out=vnb[i][:], in0=vn[i][:], in1=bb[:], op=ALU.add),
             reads=[("vn", i), "bb"], writes=[("vnb", i)])
        if c == 0:
            ph.dump("mlp_st", st[:, 0:8], [("st", c)])
            ph.dump("mlp_vn", vn[i][:], [("vn", i)])
            ph.dump("mlp_vnb", vnb[i][:], [("vnb", i)])
        for hf in range(2):
            pj = (2 * c + hf) % 4
            for g4 in range(4):
                g = hf * 4 + g4
                P.op("pe", lambda e, pj=pj, g=g, g4=g4, i=i: e.matmul(
                    pV[pj][:, g4 * 128:(g4 + 1) * 128], lhsT=wcT[:, g, :], rhs=vnb[i][:, g * 128:(g + 1) * 128],
                    start=True, stop=True), reads=["wcT", ("vnb", i)], writes=[("pV", pj)])
            P.op("dve", lambda e, pj=pj, hf=hf, i=i: e.tensor_tensor(
                out=t1[i][:, hf * 512:(hf + 1) * 512], in0=pV[pj][:],
                in1=bbc[:, hf * 4:(hf + 1) * 4, :].rearrange("p g d -> p (g d)"), op=ALU.add),
                reads=[("pV", pj), "bbc"], writes=[("t1", i, hf)])
        if c == 0:
            ph.dump("mlp_t1", t1[i][:], [("t1", i, 0), ("t1", i, 1)])
        P.op("pool", lambda e, i=i: e.tensor_tensor(out=t1[i][:], in0=t1[i][:], in1=ut[i][:], op=ALU.mult),
             reads=[("t1", i, 0), ("t1", i, 1), ("u", i)], writes=[("t1", i, 0), ("t1", i, 1)])
        P.op("act", lambda e, i=i: e.activation(out=zt[i][:], in_=zt[i][:], func=AF.Silu),
             reads=[("z", i)], writes=[("z", i)])
        P.op("dve", lambda e, i=i: e.tensor_tensor(out=t1[i][:], in0=t1[i][:], in1=zt[i][:], op=ALU.mult),
             reads=[("t1", i, 0), ("t1", i, 1), ("z", i)], writes=[("t1", i, 0), ("t1", i, 1)])
        P.dma("sp", lambda e, i=i, r0=r0: e.dma_start(out=y_own[r0:r0 + 128, ycol0:ycol0 + 1024], in_=t1[i][:]),
              reads=[("t1", i, 0), ("t1", i, 1)], writes=[("y_mlp", c)])
    ph.close()


def record_ops(P, rec):
    real_op, real_dma = P.op, P.dma
    P.op = lambda eng, fn, reads=(), writes=(): rec.append((real_op, eng, fn, list(reads), list(writes)))
    P.dma = lambda eng, fn, reads=(), writes=(): rec.append((real_dma, eng, fn, list(reads), list(writes)))

    def restore():
        P.op, P.dma = real_op, real_dma
    return restore


def pipeline_merge(recs, nstage):
    allp = []
    for ops, marks in recs:
        m = [0] + list(marks[:nstage - 1])
        while len(m) < nstage:
            m.append(len(ops))
        m.append(len(ops))
        allp.append([ops[m[k]:m[k + 1]] for k in range(nstage)])
    n = len(recs)
    for t in range(n + nstage - 1):
        lists = []
        for k in range(nstage):
            bi = t - k
            if 0 <= bi < n and allp[bi][k]:
                lists.append(allp[bi][k])
        merged = []
        for li, L in enumerate(lists):
            for pi, item in enumerate(L):
                merged.append(((pi + 0.5) / len(L), li, pi, item))
        merged.sort(key=lambda t_: (t_[0], t_[1]))
        for _, _, _, (f, eng, fn, reads, writes) in merged:
            f(eng, fn, reads=reads, writes=writes)


FMF_X, FMF_B, FMF_C, FMF_WL, FMF_AL = 0, 256, 384, 512, 576
TMF_SSMZ, TMF_DT, TMF_R, TMF_K, TMF_V, TMF_Z = 0, 256, 260, 516, 772, 1028
NEG_BIG = -30000.0


def phase_ssm(nc, scr, prm, y_all, y_dst=None):
    ph = Phase(nc)
    P = ph.P
    fm = scr["fmf_all"]
    tm = scr["tmf_all"]
    if y_dst is None:
        y_dst = y_all[:, 0:256]
    NCH = SEQ // 128
    SC = 512
    identf = ph.sb([128, 128], F32, "identf")
    identb = ph.sb([128, 128], BF16, "identb")
    tri = ph.sb([128, 128], F32, "tri")
    onesf = ph.sb([128, 128], F32, "ones")
    sel4 = ph.sb([4, 4, 128], F32, "sel4")
    negb = ph.sb([128, 128], F32, "negb")
    cw = ph.sb([128, 4, 4], F32, "cw")
    cb = ph.sb([128, 4], F32, "cb")
    dtb = ph.sb([128, 128], F32, "dtb")
    alog = ph.sb([128, 128], F32, "alog")
    Dbc = ph.sb([128, 4], F32, "Dbc")
    ngb = ph.sb([128, 256], F32, "ngb")
    dt = ph.sb([128, NCH, 4], F32, "dt")
    aa = ph.sb([128, NCH, 4], F32, "aa")
    cum = ph.sb([128, NCH * 4], F32, "cum")
    cumL = ph.sb([128, NCH * 4], F32, "cumL")
    ecum = ph.sb([128, NCH * 4], F32, "ecum")
    ncum = ph.sb([128, NCH * 4], F32, "ncum")
    ecumL = ph.sb([128, NCH * 4], F32, "ecumL")
    dtd = ph.sb([128, NCH * 4], F32, "dtd")
    win = [ph.sb([128, SC + 3], F32, "win") for _ in range(4)]
    acc = [ph.sb([128, SC], F32, "acc") for _ in range(2)]
    xsT = [ph.sb([128, 2, SC], F32, "xsT") for _ in range(2)]
    BT = [ph.sb([128, SC], BF16, "BT") for _ in range(2)]
    CT = [ph.sb([128, SC], BF16, "CT") for _ in range(2)]
    xtok = [ph.sb([128, 256], F32, "xtok") for _ in range(2)]
    xdt = [ph.sb([128, 256], BF16, "xdt") for _ in range(2)]
    xdd = [ph.sb([128, 256], BF16, "xdd") for _ in range(2)]
    Btok = [ph.sb([128, 128], BF16, "Btok") for _ in range(2)]
    cumT = [ph.sb([4, 128], F32, "cumT") for _ in range(2)]
    LT = [ph.sb([128, 4, 128], F32, "LT") for _ in range(2)]
    MT = [ph.sb([128, 4, 128], BF16, "MT") for _ in range(2)]
    ydsb = [ph.sb([128, 256], F32, "ydsb") for _ in range(2)]
    dsk = [ph.sb([128, 256], F32, "dsk") for _ in range(2)]
    yt = [ph.sb([128, 256], F32, "yt") for _ in range(2)]
    zt = [ph.sb([128, 256], F32, "zt") for _ in range(2)]
    junk = ph.sb([128, 256], F32, "junk")
    nst = ph.sb([128, NCH], F32, "nst")
    hT = ph.sb([128, 256], F32, "hT")
    hTb = ph.sb([128, 256], BF16, "hTb")
    pTr = [ph.ps([128, 512], F32, "pTr") for _ in range(1)]
    pTb = [ph.ps([128, 1024], BF16, "pTb") for _ in range(1)]
    pSm = [ph.ps([128, 512], F32, "pSm") for _ in range(1)]
    pL = [ph.ps([128, 512], F32, "pL") for _ in range(1)]
    pCB = [ph.ps([128, 512], F32, "pCB") for _ in range(1)]
    pYd = [ph.ps([128, 512], F32, "pYd") for _ in range(1)]
    pYo = [ph.ps([128, 512], F32, "pYo") for _ in range(1)]
    pS = [ph.ps([128, 512], F32, "pS") for _ in range(1)]

    ld = lambda dst, src, key: P.dma("sp", lambda e: e.dma_start(out=dst, in_=src), writes=[key])
    ld(identf[:], prm["ident_f"][:, :], "identf")
    ld(identb[:], prm["ident_b"][:, :], "identb")
    ld(tri[:], prm["tri_le"][:, :], "tri")
    ld(onesf[:], prm["ones_f"][:, :], "ones")
    ld(sel4[:], prm["sel4"][:, :, :], "sel4")
    ld(negb[:], prm["negbig_lt"][:, :], "negb")
    ld(cw[:], prm["ssm_cw"][:, :, :], "cw")
    ld(cb[:], prm["ssm_cb"][:, :], "cb")
    ld(dtb[:], prm["ssm_dtb_t"][0:1, :].partition_broadcast(128), "dtb")
    ld(alog[:], prm["ssm_alog_t"][0:1, :].partition_broadcast(128), "alog")
    ld(Dbc[:], prm["ssm_D"][0:1, :].partition_broadcast(128), "Dbc")
    ld(ngb[:], prm["ssm_ng"][0:1, :].partition_broadcast(128), "ngb")
    ld(dt[:], tm[:, TMF_DT:TMF_DT + 4].rearrange("(c l) h -> l c h", l=128), "dt")
    dtf = dt[:].rearrange("p c h -> p (c h)")
    aaf = aa[:].rearrange("p c h -> p (c h)")
    P.op("dve", lambda e: e.tensor_tensor(out=dtf, in0=dtf, in1=dtb[:], op=ALU.add), reads=["dt", "dtb"], writes=["dt"])
    P.op("act", lambda e: e.activation(out=dtf, in_=dtf, func=AF.Exp), reads=["dt"], writes=["dt"])
    P.op("act", lambda e: e.activation(out=dtf, in_=dtf, func=AF.Ln, bias=1.0, scale=1.0), reads=["dt"], writes=["dt"])
    P.op("act", lambda e: e.activation(out=alog[:], in_=alog[:], func=AF.Exp), reads=["alog"], writes=["alog"])
    P.op("dve", lambda e: e.scalar_tensor_tensor(out=aaf, in0=dtf, scalar=-1.0, in1=alog[:], op0=ALU.mult, op1=ALU.mult),
         reads=["dt", "alog"], writes=["aa"])
    P.op("pe", lambda e: e.matmul(pSm[0][:, 0:128], lhsT=tri[:], rhs=aaf, start=True, stop=True),
         reads=["tri", "aa"], writes=["pSm"])
    P.op("dve", lambda e: e.tensor_copy(out=cum[:], in_=pSm[0][:, 0:128]), reads=["pSm"], writes=["cum"])
    P.op("pe", lambda e: e.matmul(pSm[0][:, 128:256], lhsT=onesf[:], rhs=aaf, start=True, stop=True),
         reads=["ones", "aa", "cum"], writes=["pSm"])
    P.op("dve", lambda e: e.tensor_copy(out=cumL[:], in_=pSm[0][:, 128:256]), reads=["pSm"], writes=["cumL"])
    P.op("act", lambda e: e.activation(out=ecum[:], in_=cum[:], func=AF.Exp), reads=["cum"], writes=["ecum"])
    P.op("pool", lambda e: e.tensor_scalar(out=ncum[:], in0=cum[:], scalar1=-1.0, scalar2=None, op0=ALU.mult), reads=["cum"], writes=["ncum"])
    P.op("act", lambda e: e.activation(out=ecumL[:], in_=cumL[:], func=AF.Exp), reads=["cumL"], writes=["ecumL"])
    P.op("dve", lambda e: e.tensor_tensor(out=dtd[:], in0=cumL[:], in1=cum[:], op=ALU.subtract),
         reads=["cumL", "cum"], writes=["dtd"])
    P.op("act", lambda e: e.activation(out=dtd[:], in_=dtd[:], func=AF.Exp), reads=["dtd"], writes=["dtd"])
    P.op("dve", lambda e: e.tensor_tensor(out=dtd[:], in0=dtd[:], in1=dtf, op=ALU.mult), reads=["dtd", "dt"], writes=["dtd"])
    P.op("dve", lambda e: e.memset(hT[:], 0.0), writes=["hT"])
    P.op("dve", lambda e: e.memset(hTb[:], 0.0), writes=["hTb"])
    P.op("dve", lambda e: e.memset(nst[:], 0.0), writes=["nst"])

    recs = []
    rec_cur = [None]

    def new_unit():
        rec = []
        recs.append([rec, []])
        rec_cur[0] = rec
        return record_ops(P, rec)

    for s in range(SEQ // SC):
        t0 = s * SC
        si = s % 2
        restore = new_unit()
        for j in range(4):
            wj = win[j]
            if s == 0:
                P.op("dve", lambda e, wj=wj: e.memset(wj[:, 0:3], 0.0), writes=[("win", j)])
                P.dma("sp", lambda e, wj=wj, j=j: e.dma_start(out=wj[:, 3:SC + 3], in_=fm[j * 128:(j + 1) * 128, 0:SC]),
                      writes=[("win", j)])
            else:
                P.dma("sp", lambda e, wj=wj, j=j, t0=t0: e.dma_start(out=wj[:], in_=fm[j * 128:(j + 1) * 128, t0 - 3:t0 + SC]),
                      writes=[("win", j)])
            ac = acc[j % 2]
            ak = ("acc", j % 2)
            P.op("dve", lambda e, ac=ac, wj=wj, j=j: e.tensor_scalar(out=ac[:], in0=wj[:, 0:SC], scalar1=cw[:, j, 0:1],
                                                                    scalar2=None, op0=ALU.mult),
                 reads=[("win", j), "cw"], writes=[ak])
            for tap in range(1, 4):
                P.op("dve", lambda e, ac=ac, wj=wj, j=j, tap=tap: e.scalar_tensor_tensor(
                    out=ac[:], in0=wj[:, tap:tap + SC], scalar=cw[:, j, tap:tap + 1], in1=ac[:], op0=ALU.mult, op1=ALU.add),
                    reads=[("win", j), "cw", ak], writes=[ak])
            if j < 2:
                dst, dk = xsT[si][:, j, :], ("xsT", si, j)
            elif j == 2:
                dst, dk = BT[si][:], ("BT", si)
            else:
                dst, dk = CT[si][:], ("CT", si)
            P.op("act", lambda e, ac=ac, dst=dst, j=j: e.activation(out=dst, in_=ac[:], func=AF.Silu, bias=cb[:, j:j + 1], scale=1.0),
                 reads=[ak, "cb"], writes=[dk])
        if s < 2:
            ph.dump(f"ssm_xsT{s}", xsT[si][:], [("xsT", si, 0), ("xsT", si, 1)])
            ph.dump(f"ssm_BT{s}", BT[si][:], [("BT", si)])
            ph.dump(f"ssm_CT{s}", CT[si][:], [("CT", si)])
        for cc in range(SC // 128):
            c = s * (SC // 128) + cc
            ci = c % 2
            lo = cc * 128
            c4 = c * 4
            if cc > 0:
                restore = new_unit()
            for j in range(2):
                P.op("pe", lambda e, si=si, j=j, lo=lo: e.transpose(out=pTr[0][:, j * 128:(j + 1) * 128], in_=xsT[si][:, j, lo:lo + 128],
                                                            identity=identf[:]),
                     reads=[("xsT", si, j), "identf"], writes=["pTr"])
            P.op("act", lambda e, ci=ci: e.copy(out=xtok[ci][:], in_=pTr[0][:, 0:256]), reads=["pTr"], writes=[("xtok", ci)])
            P.op("pe", lambda e, si=si, lo=lo: e.transpose(out=pTb[0][:, 0:128], in_=BT[si][:, lo:lo + 128], identity=identb[:]),
                 reads=[("BT", si), "identb"], writes=["pTb"])
            P.op("act", lambda e, ci=ci: e.copy(out=Btok[ci][:], in_=pTb[0][:, 0:128]), reads=["pTb"], writes=[("Btok", ci)])
            P.op("pe", lambda e, c=c: e.matmul(pSm[0][0:4, 256:384], lhsT=aa[:, c, :], rhs=tri[:], start=True, stop=True),
                 reads=["aa", "tri"], writes=["pSm"])
            P.op("dve", lambda e, ci=ci: e.tensor_copy(out=cumT[ci][:], in_=pSm[0][0:4, 256:384]),
                 reads=["pSm"], writes=[("cumT", ci)])
            for h in range(4):
                P.op("pe", lambda e, h=h, ci=ci: e.matmul(pL[0][:, h * 128:(h + 1) * 128], lhsT=sel4[:, h, :], rhs=cumT[ci][:],
                                                          start=True, stop=False),
                     reads=["sel4", ("cumT", ci)], writes=["pL"])
                P.op("pe", lambda e, h=h: e.matmul(pL[0][:, h * 128:(h + 1) * 128], lhsT=identf[:], rhs=negb[:],
                                                   start=False, stop=True),
                     reads=["identf", "negb"], writes=["pL"])
            for h in range(4):
                P.op("act", lambda e, h=h, ci=ci, c4=c4: e.activation(
                    out=LT[ci][:, h, :], in_=pL[0][:, h * 128:(h + 1) * 128], func=AF.Exp, bias=ncum[:, c4 + h:c4 + h + 1], scale=1.0),
                    reads=["pL", "ncum"], writes=[("LT", ci)])
            P.op("pe", lambda e, si=si, lo=lo: e.matmul(pCB[0][:, 0:128], lhsT=BT[si][:, lo:lo + 128], rhs=CT[si][:, lo:lo + 128],
                                                 start=True, stop=True), reads=[("BT", si), ("CT", si)], writes=["pCB"])
            P.op("dve", lambda e, ci=ci: e.tensor_tensor(out=MT[ci][:], in0=pCB[0][:, 0:128].unsqueeze(1).to_broadcast([128, 4, 128]),
                                                         in1=LT[ci][:], op=ALU.mult), reads=["pCB", ("LT", ci)], writes=[("MT", ci)])
            h4 = lambda ap: ap.rearrange("p (h d) -> p h d", h=4)
            P.op("dve", lambda e, ci=ci, c4=c4: e.tensor_tensor(out=h4(xdt[ci][:]), in0=h4(xtok[ci][:]),
                                                                in1=dtf[:, c4:c4 + 4].unsqueeze(2).to_broadcast([128, 4, 64]), op=ALU.mult),
                 reads=[("xtok", ci), "dt"], writes=[("xdt", ci)])
            P.op("pool", lambda e, ci=ci, c4=c4: e.tensor_tensor(out=h4(xdd[ci][:]), in0=h4(xtok[ci][:]),
                                                                 in1=dtd[:, c4:c4 + 4].unsqueeze(2).to_broadcast([128, 4, 64]), op=ALU.mult),
                 reads=[("xtok", ci), "dtd"], writes=[("xdd", ci)])
            for h in range(4):
                P.op("pe", lambda e, h=h, ci=ci: e.matmul(pYd[0][:, h * 64:(h + 1) * 64], lhsT=MT[ci][:, h, :],
                                                          rhs=xdt[ci][:, h * 64:(h + 1) * 64], start=True, stop=True),
                     reads=[("MT", ci), ("xdt", ci)], writes=["pYd"])
            recs[-1][1].append(len(rec_cur[0]))
            for h in range(4):
                P.op("pe", lambda e, si=si, h=h, lo=lo: e.matmul(pYo[0][:, h * 64:(h + 1) * 64], lhsT=CT[si][:, lo:lo + 128],
                                                          rhs=hTb[:, h * 64:(h + 1) * 64], start=True, stop=True),
                     reads=[("CT", si), "hTb"], writes=["pYo"])
            P.op("act", lambda e, ci=ci: e.copy(out=ydsb[ci][:], in_=pYd[0][:, 0:256]), reads=["pYd"], writes=[("ydsb", ci)])
            P.dma("sp", lambda e, ci=ci, c=c: e.dma_start(out=zt[ci][:], in_=tm[c * 128:(c + 1) * 128, TMF_SSMZ:TMF_SSMZ + 256]),
                  writes=[("zt", ci)])
            P.op("dve", lambda e, ci=ci, c4=c4: e.tensor_tensor(out=h4(yt[ci][:]), in0=pYo[0][:, 0:256].rearrange("p (h d) -> p h d", h=4),
                                                                in1=ecum[:, c4:c4 + 4].unsqueeze(2).to_broadcast([128, 4, 64]), op=ALU.mult),
                 reads=["pYo", "ecum"], writes=[("yt", ci)])
            P.op("pool", lambda e, ci=ci: e.tensor_tensor(out=h4(dsk[ci][:]), in0=h4(xtok[ci][:]),
                                                          in1=Dbc[:, 0:4].unsqueeze(2).to_broadcast([128, 4, 64]), op=ALU.mult),
                 reads=[("xtok", ci), "Dbc"], writes=[("dsk", ci)])
            P.op("dve", lambda e, ci=ci: e.tensor_tensor(out=yt[ci][:], in0=yt[ci][:], in1=ydsb[ci][:], op=ALU.add),
                 reads=[("yt", ci), ("ydsb", ci)], writes=[("yt", ci)])
            P.op("dve", lambda e, ci=ci: e.tensor_tensor(out=yt[ci][:], in0=yt[ci][:], in1=dsk[ci][:], op=ALU.add),
                 reads=[("yt", ci), ("dsk", ci)], writes=[("yt", ci)])
            if c in (0, 4):
                ph.dump(f"ssm_xtok{c}", xtok[ci][:], [("xtok", ci)])
                ph.dump(f"ssm_LT{c}", LT[ci][:], [("LT", ci)])
                ph.dump(f"ssm_MT{c}", MT[ci][:], [("MT", ci)])
                ph.dump(f"ssm_yt{c}", yt[ci][:], [("yt", ci)])
            for h in range(4):
                P.op("pe", lambda e, h=h, ci=ci: e.matmul(pS[0][:, h * 64:(h + 1) * 64], lhsT=Btok[ci][:],
                                                          rhs=xdd[ci][:, h * 64:(h + 1) * 64], start=True, stop=True),
                     reads=[("Btok", ci), ("xdd", ci)], writes=["pS"])
            P.op("pool", lambda e, c4=c4: e.tensor_tensor(out=h4(hT[:]), in0=h4(hT[:]),
                                                          in1=ecumL[:, c4:c4 + 4].unsqueeze(2).to_broadcast([128, 4, 64]), op=ALU.mult),
                 reads=["hT", "ecumL"], writes=["hT"])
            P.op("dve", lambda e: e.tensor_tensor(out=hT[:], in0=hT[:], in1=pS[0][:, 0:256], op=ALU.add), reads=["hT", "pS"], writes=["hT"])
            P.op("act", lambda e: e.copy(out=hTb[:], in_=hT[:]), reads=["hT"], writes=["hTb"])
            recs[-1][1].append(len(rec_cur[0]))
            P.op("act", lambda e, ci=ci: e.activation(out=zt[ci][:], in_=zt[ci][:], func=AF.Silu),
                 reads=[("zt", ci)], writes=[("zt", ci)])
            P.op("pool", lambda e, ci=ci: e.tensor_tensor(out=yt[ci][:], in0=yt[ci][:], in1=zt[ci][:], op=ALU.mult),
                 reads=[("yt", ci), ("zt", ci)], writes=[("yt", ci)])
            P.op("act", lambda e, ci=ci, c=c: e.activation(out=junk[:], in_=yt[ci][:], func=AF.Square, accum_out=nst[:, c:c + 1]),
                 reads=[("yt", ci), "nst"], writes=["junk", ("nst", c)])
            P.op("dve", lambda e, c=c: e.tensor_scalar(out=nst[:, c:c + 1], in0=nst[:, c:c + 1], scalar1=1.0 / 256, scalar2=NORM_EPS,
                                                       op0=ALU.mult, op1=ALU.add), reads=[("nst", c)], writes=[("nst", c)])
            P.op("act", lambda e, c=c: e.activation(out=nst[:, c:c + 1], in_=nst[:, c:c + 1], func=AF.Sqrt),
                 reads=[("nst", c)], writes=[("nst", c)])
            P.op("dve", lambda e, c=c: e.reciprocal(out=nst[:, c:c + 1], in_=nst[:, c:c + 1]), reads=[("nst", c)], writes=[("nst", c)])
            P.op("dve", lambda e, ci=ci, c=c: e.scalar_tensor_tensor(out=yt[ci][:], in0=yt[ci][:], scalar=nst[:, c:c + 1], in1=ngb[:],
                                                                     op0=ALU.mult, op1=ALU.mult),
                 reads=[("yt", ci), ("nst", c), "ngb"], writes=[("yt", ci)])
            P.dma("sp", lambda e, ci=ci, c=c: e.dma_start(out=y_dst[c * 128:(c + 1) * 128, :], in_=yt[ci][:]),
                  reads=[("yt", ci)], writes=[("y_ssm", c)])
            restore()
    pipeline_merge([(r, m) for r, m in recs], 3)
    ph.close()


NPAIR = 144
TOPK = 256
NBIS = 17
FILLER = False


def pair_off(i):
    return 2 * i * (i + 1)


def default_blocks():
    return [(i + 1, 0, 4) for i in range(8)]


def pair_offsets(blocks):
    offs, o = [], 0
    for nch, _, lk in blocks:
        offs.append(o)
        o += 4 * (nch - 1) + lk
    return offs, o


def phase_indexer(nc, scr, prm, maskT_d, blocks=None):
    blocks = blocks or default_blocks()
    nblk = len(blocks)
    assert nblk % 2 == 0
    NT = nblk * 128
    ncb = max(cb for _, cb, _ in blocks) + 1
    poffs, _ = pair_offsets(blocks)
    ph = Phase(nc)
    P = ph.P
    fmo = scr["fmb_own"]
    fma = scr["fmb_all"]
    tmo = scr["tmf_own"]
    NB4 = 4
    identb = ph.sb([128, 128], BF16, "identb")
    kiT = ph.sb([128, SEQ], BF16, "kiT")
    qiT = [ph.sb([128, 8, 128], BF16, "qiT") for _ in range(NB4)]
    wi = ph.sb([128, nblk, 16], F32, "wi")
    iota = ph.sb([128, 512], F32, "iota")
    qrel = ph.sb([128, ncb], F32, "qrel")
    cbias = ph.sb([128, ncb, 512], F32, "cbias")
    pow2 = ph.sb([128, NBIS], F32, "pow2")
    wdiag = [ph.sb([128, 16, 128], BF16, "wdiag") for _ in range(NB4)]
    R = [ph.sb([128, 512], BF16, "R") for _ in range(4)]
    sc = [ph.sb([128, SEQ], F32, "sc") for _ in range(NB4)]
    junk = [ph.sb([128, SEQ], BF16, "junk") for _ in range(2)]
    mk = [ph.sb([128, SEQ], BF16, "mk") for _ in range(2)]
    mT = [ph.sb([128, 8, 128], BF16, "mT") for _ in range(3)]
    mx = [ph.sb([128, 8], F32, "mx") for _ in range(NB4)]
    bs = [ph.sb([128, 8], F32, "bs") for _ in range(NB4)]
    wk = [ph.sb([128, NBIS], F32, "wk") for _ in range(NB4)]
    cnt = [ph.sb([128, NBIS], F32, "cnt") for _ in range(NB4)]
    pD = [ph.ps([128, 512], F32, "pD") for _ in range(4)]
    pSc = [ph.ps([128, 512], F32, "pSc") for _ in range(2)]
    pT = [ph.ps([128, 1024], BF16, "pT") for _ in range(1)]
    pJ = ph.ps([128, 512], F32, "pJ")

    ld = lambda dst, src, key: P.dma("sp", lambda e: e.dma_start(out=dst, in_=src), writes=[key])
    ld(identb[:], prm["ident_b"][:, :], "identb")
    ld(kiT[:], fma[1024:1152, :], "kiT")
    ld(wi[:], tmo[0:NT, TMF_OWN_WI:TMF_OWN_WI + 16].rearrange("(i p) h -> p i h", p=128), "wi")
    ld(iota[:], prm["iota512"][0:1, :].partition_broadcast(128), "iota")
    ld(qrel[:], prm["qrel"][:, :], "qrel")
    ld(pow2[:], prm["pow2"][0:1, :].partition_broadcast(128), "pow2")
    P.op("dve", lambda e: e.tensor_scalar(out=wi[:], in0=wi[:], scalar1=0.03125, scalar2=None, op0=ALU.mult), reads=["wi"], writes=["wi"])
    for cb in range(ncb):
        P.op("dve", lambda e, o=cbias[:, cb, :], s=qrel[:, cb:cb + 1]: e.tensor_scalar(out=o, in0=iota[:], scalar1=s, scalar2=-1e30,
                                                                                  op0=ALU.is_gt, op1=ALU.mult),
             reads=["iota", "qrel"], writes=["cbias"])
    k = {"d": 0, "r": 0, "s": 0, "t": 0, "m": 0}
    qv = fmo[1024:2048, :].rearrange("(p r) t -> r p t", r=128)

    def nkeys(i):
        nch_, _, lk_ = blocks[i]
        return 512 * (nch_ - 1) + 128 * lk_

    def scores(i):
        nch, cbi, lk = blocks[i]
        b4 = i % NB4
        scb, mxb, wdb, qb = sc[b4], mx[b4], wdiag[b4], qiT[b4]
        ld(qb[:], qv[:, :, i * 128:(i + 1) * 128], ("qiT", b4))
        P.op("pool", lambda e, o=wdb[:], w_=wi[:, i, :].unsqueeze(2).to_broadcast([128, 16, 128]),
             d_=identb[:].unsqueeze(1).to_broadcast([128, 16, 128]): e.tensor_tensor(out=o, in0=d_, in1=w_, op=ALU.mult),
             reads=["identb", "wi"], writes=[("wdiag", b4)])
        P.op("dve", lambda e, o=mxb[:]: e.memset(o, 0.0), writes=[("mx", b4)])
        P.op("dve", lambda e, o=cnt[b4][:]: e.memset(o, 0.0), writes=[("cnt", b4)])
        for ch in range(nch):
            js = k["s"] % 2
            k["s"] += 1
            jds = {}
            nl = 512 if ch < nch - 1 else 128 * lk

            def dots(h):
                jd = k["d"] % 4
                k["d"] += 1
                jds[h] = jd
                r0 = (h % 2) * 64
                P.op("pe", lambda e, o=pD[jd][:, 0:nl], l=qb[r0:r0 + 64, h // 2, :],
                     r=kiT[r0:r0 + 64, ch * 512:ch * 512 + nl]: e.matmul(o, lhsT=l, rhs=r, start=True, stop=True),
                     reads=[("qiT", b4), "kiT"], writes=[("pD", jd)])

            for h0 in range(4):
                dots(h0)
            for h in range(16):
                if h % 2 == 0 and h >= 2 and h + 3 < 16:
                    dots(h + 2)
                    dots(h + 3)
                jd = jds[h]
                jr = k["r"] % 4
                k["r"] += 1
                P.op("act", lambda e, o=R[jr][:, 0:nl], s=pD[jd][:, 0:nl]: e.activation(out=o, in_=s, func=AF.Relu),
                     reads=[("pD", jd)], writes=[("R", jr)])
                P.op("pe", lambda e, o=pSc[js][:, 0:nl], l=wdb[:, h, :], r=R[jr][:, 0:nl], h=h: e.matmul(o, lhsT=l, rhs=r, start=(h == 0),
                                                                                           stop=(h == 15)),
                     reads=[("wdiag", b4), ("R", jr)], writes=[("pSc", js)])
                if FILLER:
                    P.op("pe", lambda e, l=identb[:], r=kiT[:, ch * 512:(ch + 1) * 512]: e.matmul(pJ[:], lhsT=l, rhs=r, start=True, stop=True),
                         reads=["identb", "kiT"], writes=["pJ"])
            P.op("dve", lambda e, o=mxb[:, ch:ch + 1], s=pSc[js][:, 0:nl]: e.tensor_reduce(out=o, in_=s, axis=AX.X, op=ALU.max,
                                                                                   apply_absolute_value=True),
                 reads=[("pSc", js)], writes=[("mx", b4)])
            if ch == nch - 1:
                P.op("dve", lambda e, o=scb[:, ch * 512:ch * 512 + nl], s=pSc[js][:, 0:nl], c_=cbias[:, cbi, 0:nl]: e.tensor_tensor(
                    out=o, in0=s, in1=c_, op=ALU.add), reads=[("pSc", js), "cbias"], writes=[("sc", b4)])
            else:
                P.op("dve", lambda e, o=scb[:, ch * 512:(ch + 1) * 512], s=pSc[js][:]: e.tensor_copy(out=o, in_=s),
                     reads=[("pSc", js)], writes=[("sc", b4)])
            yield

    def bis_init(i):
        b4 = i % NB4
        bsb, wkb, mxb = bs[b4], wk[b4], mx[b4]
        bk = ("bs", b4)
        P.op("dve", lambda e, o=bsb[:, 0:1], s=mxb[:]: e.tensor_reduce(out=o, in_=s, axis=AX.X, op=ALU.max),
             reads=[("mx", b4)], writes=[bk])
        P.op("dve", lambda e, o=bsb[:, 0:1]: e.tensor_scalar(out=o, in0=o, scalar1=1.001, scalar2=1e-6, op0=ALU.mult, op1=ALU.add),
             reads=[bk], writes=[bk])
        P.op("dve", lambda e, o=wkb[:], s=bsb[:, 0:1]: e.tensor_scalar(out=o, in0=pow2[:], scalar1=s, scalar2=2.0, op0=ALU.mult,
                                                                      op1=ALU.mult), reads=[bk, "pow2"], writes=[("wk", b4)])
        P.op("dve", lambda e, o=bsb[:, 1:2]: e.memset(o, 0.0), reads=[bk], writes=[bk])

    def bis_count(i, it, jj):
        b4 = i % NB4
        n = nkeys(i)
        P.op("dve", lambda e, o=junk[jj][:, 0:n], s=sc[b4][:, 0:n], m=bs[b4][:, 1:2], a=cnt[b4][:, it:it + 1]: e.tensor_scalar(
            out=o, in0=s, scalar1=m, scalar2=0.0, op0=ALU.is_ge, op1=ALU.add, accum_out=a),
            reads=[("sc", b4), ("bs", b4), ("cnt", b4)], writes=[("junk", jj), ("cnt", b4)])

    def bis_delta(i, it):
        b4 = i % NB4
        P.op("dve", lambda e, o=bs[b4][:, 2:3], c_=cnt[b4][:, it:it + 1], w_=wk[b4][:, it:it + 1]: e.tensor_scalar(
            out=o, in0=c_, scalar1=TOPK - 0.5, scalar2=w_, op0=ALU.is_ge, op1=ALU.mult),
            reads=[("cnt", b4), ("wk", b4), ("bs", b4)], writes=[("bs", b4)])

    def bis_mid(i, it):
        b4 = i % NB4
        nx = min(it + 1, NBIS - 1)
        P.op("dve", lambda e, o=bs[b4][:, 1:2], d_=bs[b4][:, 2:3], w_=wk[b4][:, nx:nx + 1]: e.scalar_tensor_tensor(
            out=o, in0=d_, scalar=w_, in1=o, op0=ALU.subtract, op1=ALU.add), reads=[("bs", b4), ("wk", b4)], writes=[("bs", b4)])

    def finish_block(i, jj):
        nch, _, lk = blocks[i]
        b4 = i % NB4
        n = nkeys(i)
        mkb = mk[jj]
        P.op("dve", lambda e, o=mkb[:, 0:n], s=sc[b4][:, 0:n], t=bs[b4][:, 1:2]: e.tensor_scalar(
            out=o, in0=s, scalar1=t, scalar2=-1.0, op0=ALU.is_ge, op1=ALU.add), reads=[("sc", b4), ("bs", b4)], writes=[("mk", jj)])
        nkb = 4 * (nch - 1) + lk
        for g0 in range(0, nkb, 8):
            ng = min(8, nkb - g0)
            jt = 0
            jm = k["m"] % 3
            k["m"] += 1
            for kb in range(g0, g0 + ng):
                P.op("pe", lambda e, o=pT[jt][:, (kb - g0) * 128:(kb - g0 + 1) * 128], s=mkb[:, kb * 128:(kb + 1) * 128]: e.transpose(
                    out=o, in_=s, identity=identb[:]), reads=[("mk", jj), "identb"], writes=[("pT", jt)])
            P.op("act", lambda e, o=mT[jm][:, 0:ng, :], s=pT[jt][:, 0:ng * 128].rearrange("p (k t) -> p k t", k=ng): e.copy(out=o, in_=s),
                 reads=[("pT", jt)], writes=[("mT", jm)])
            po = poffs[i] + g0
            P.dma("sp", lambda e, o=maskT_d[:, po:po + ng, :], s=mT[jm][:, 0:ng, :]: e.dma_start(out=o, in_=s),
                  reads=[("mT", jm)], writes=[("maskT", i, g0)])

    import itertools
    for _ in itertools.chain(scores(0), scores(1)):
        pass
    for pr in range(nblk // 2):
        ia, ib = 2 * pr, 2 * pr + 1
        pending = iter(())
        if 2 * pr + 2 < nblk:
            pending = itertools.chain(scores(2 * pr + 2), scores(2 * pr + 3))
        bis_init(ia)
        bis_init(ib)
        for it in range(NBIS):
            bis_count(ia, it, 0)
            bis_count(ib, it, 1)
            bis_delta(ia, it)
            bis_delta(ib, it)
            bis_mid(ia, it)
            bis_mid(ib, it)
            next(pending, None)
        for _ in pending:
            pass
        finish_block(ia, 0)
        finish_block(ib, 1)
    ph.close()


def phase_attn(nc, scr, prm, maskT_d, y_own, blocks=None):
    blocks = blocks or default_blocks()
    nblk = len(blocks)
    poffs, _ = pair_offsets(blocks)
    ph = Phase(nc)
    P = ph.P
    fmo = scr["fmb_own"]
    fma = scr["fmb_all"]
    tmb = scr["tmb_all"]
    tmo = scr["tmf_own"]
    att_scale = 128 ** -0.5
    i30k = ph.sb([128, 128], BF16, "i30k")
    kT = ph.sb([128, 8, SEQ], BF16, "kT")
    V = ph.sb([128, 32, 8, 129], BF16, "V")
    qT = [ph.sb([128, 8, 128], BF16, "qT") for _ in range(2)]
    mT = [ph.sb([128, 32, 128], BF16, "mT") for _ in range(2)]
    PT = [ph.sb([128, 512], BF16, "PT") for _ in range(4)]
    ot = [ph.sb([128, 1024], F32, "ot") for _ in range(2)]
    zt = [ph.sb([128, 1024], F32, "zt") for _ in range(2)]
    rs = ph.sb([128, 8 * nblk], F32, "rs")
    pS = [ph.ps([128, 512], F32, "pS") for _ in range(4)]
    pO = [ph.ps([128, 512], F32, "pO") for _ in range(2)]

    ld = lambda dst, src, key: P.dma("sp", lambda e: e.dma_start(out=dst, in_=src), writes=[key])
    ld(i30k[:], prm["ident30k_b"][:, :], "i30k")
    for h in range(8):
        ld(kT[:, h, :], fma[h * 128:(h + 1) * 128, :], ("kT", h))
    vv = tmb.rearrange("(kb p) (h d) -> p kb h d", p=128, d=128)
    for kb in range(32):
        ld(V[:, kb, :, 0:128], vv[:, kb, :, :], ("V", kb))
    P.op("pool", lambda e: e.memset(V[:, :, :, 128:129], 1.0), writes=["Vones"])
    qv = fmo[0:1024, :].rearrange("(h d) t -> d h t", d=128)
    k = {"s": 0, "p": 0, "o": 0}
    def loads(i):
        b2_ = i % 2
        nkb_ = 4 * (blocks[i][0] - 1) + blocks[i][2]
        po_ = poffs[i]
        ld(qT[b2_][:], qv[:, :, i * 128:(i + 1) * 128], ("qT", b2_))
        ld(mT[b2_][:, 0:nkb_, :], maskT_d[:, po_:po_ + nkb_, :], ("mT", b2_))
        ld(zt[b2_][:], tmo[i * 128:(i + 1) * 128, TMF_OWN_ATTZ:TMF_OWN_ATTZ + 1024], ("zt", b2_))

    loads(0)
    for i, (nch, _, lk) in enumerate(blocks):
        b2 = i % 2
        nkb = 4 * (nch - 1) + lk
        po = poffs[i]
        if i + 1 < nblk:
            loads(i + 1)
        P.op("act", lambda e, o=zt[b2][:]: e.activation(out=o, in_=o, func=AF.Silu), reads=[("zt", b2)], writes=[("zt", b2)])
        for h in range(8):
            jo = k["o"] % 2
            k["o"] += 1
            jss = {}

            def st_mm(ch):
                js = k["s"] % 4
                k["s"] += 1
                jss[ch] = js
                nk4 = 4 if ch < nch - 1 else lk
                P.op("pe", lambda e, o=pS[js][:, 0:nk4 * 128], r=mT[b2][:, ch * 4:ch * 4 + nk4, :].rearrange("p k t -> p (k t)"): e.matmul(
                    o, lhsT=i30k[:], rhs=r, start=True, stop=False), reads=["i30k", ("mT", b2)], writes=[("pS", js)])
                for k4 in range(nk4):
                    kb = ch * 4 + k4
                    P.op("pe", lambda e, o=pS[js][:, k4 * 128:(k4 + 1) * 128], l=kT[:, h, kb * 128:(kb + 1) * 128],
                         r=qT[b2][:, h, :], k4=k4, nk4=nk4: e.matmul(o, lhsT=l, rhs=r, start=False, stop=(k4 == nk4 - 1)),
                         reads=[("kT", h), ("qT", b2)], writes=[("pS", js)])

            for c0_ in range(min(3, nch)):
                st_mm(c0_)
            for ch in range(nch):
                if ch + 3 < nch:
                    st_mm(ch + 3)
                js = jss[ch]
                jp = k["p"] % 4
                k["p"] += 1
                nk4 = 4 if ch < nch - 1 else lk
                P.op("act", lambda e, o=PT[jp][:, 0:nk4 * 128], s=pS[js][:, 0:nk4 * 128]: e.activation(out=o, in_=s, func=AF.Exp, scale=att_scale),
                     reads=[("pS", js)], writes=[("PT", jp)])
                for k4 in range(nk4):
                    kb = ch * 4 + k4
                    P.op("pe", lambda e, o=pO[jo][:, 0:129], l=PT[jp][:, k4 * 128:(k4 + 1) * 128], r=V[:, kb, h, :], kb=kb, nkb=nkb:
                         e.matmul(o, lhsT=l, rhs=r, start=(kb == 0), stop=(kb == nkb - 1)),
                         reads=[("PT", jp), ("V", kb), "Vones"], writes=[("pO", jo)])
            c = i * 8 + h
            P.op("dve", lambda e, o=rs[:, c:c + 1], s=pO[jo][:, 128:129]: e.reciprocal(out=o, in_=s),
                 reads=[("pO", jo)], writes=[("rs", c)])
            P.op("dve", lambda e, o=ot[b2][:, h * 128:(h + 1) * 128], s=pO[jo][:, 0:128], r=rs[:, c:c + 1],
                 z=zt[b2][:, h * 128:(h + 1) * 128]: e.scalar_tensor_tensor(out=o, in0=s, scalar=r, in1=z, op0=ALU.mult, op1=ALU.mult),
                 reads=[("pO", jo), ("rs", c), ("zt", b2)], writes=[("ot", b2)])
        P.dma("sp", lambda e, o=y_own[i * 128:(i + 1) * 128, 0:1024], s=ot[b2][:]: e.dma_start(out=o, in_=s),
              reads=[("ot", b2)], writes=[("y_att", i)])
    ph.close()


RW_DECAY_C = -0.6065306597126334


class _Stop(Exception):
    pass


def phase_rwkv(nc, scr, prm, y_all, NBLK=SEQ // 128, stop_after=None, y_dst=None, stagger=True):
    ph = Phase(nc)
    P = ph.P
    tm = scr["tmf_all"]
    fm = scr["fmf_all"]
    if y_dst is None:
        y_dst = y_all[:, 256:512]
    cst = {}
    for nm in ("ident_f", "mask_sl", "mask_su", "mask_u", "ones_bd"):
        cst[nm] = ph.sb([128, 128], F32, nm)
    mu_tm = ph.sb([128, 1024], F32, "mu_tm")
    mu_fm = ph.sb([128, 1], F32, "mu_fm")
    w2a2 = ph.sb([128, 256], F32, "w2a2")
    vec = {}
    for nm in ("rw_w0", "rw_a0", "rw_kk", "rw_ka", "rw_rk", "rw_gng", "rw_gnb"):
        vec[nm] = ph.sb([128, 256], F32, nm)
    onecol = ph.sb([128, 1], F32, "onecol")
    NBUF3 = 4
    cur = [ph.sb([128, 1024], F32, "cur") for _ in range(NBUF3)]
    prv = [ph.sb([128, 1024], F32, "prv") for _ in range(NBUF3)]
    lcur = [ph.sb([128, 128], F32, "lcur") for _ in range(NBUF3)]
    lprv = [ph.sb([128, 128], F32, "lprv") for _ in range(NBUF3)]
    T = {}
    BFT = ("rt", "at", "bt", "kt", "bh", "kh", "vb")
    for nm in ("lw", "asig", "kkn", "kp", "aa", "bb", "cum", "cumL", "e1", "rt", "at", "bt", "kt", "bh", "kh", "vb", "tmp", "tmp2", "yb", "bon"):
        T[nm] = [ph.sb([128, 256], BF16 if nm in BFT else F32, nm) for _ in range(NBUF3)]
    st4 = [ph.sb([128, 16], F32, "st4") for _ in range(NBUF3)]
    gL = [ph.sb([128, 4], F32, "gL") for _ in range(NBUF3)]
    TR = {nm: [[ph.sb([128, 128], BF16, nm) for _ in range(2)] for _ in range(NBUF3)] for nm in ("atT", "btT", "ktT", "rtT")}
    H = {}
    for nm in ("N", "NT", "Pa", "PaT", "Pb", "PbT", "TTa", "TTb", "MakT"):
        H[nm] = ph.sb([128, 4, 128], BF16, nm)
    H["W2"] = ph.sb([128, 4, 64], BF16, "W2")
    mask2 = {nm: ph.sb([128, 2, 128], F32, nm + "2") for nm in ("mask_sl", "mask_su", "mask_u")}
    HS = {}
    for nm in ("P1T", "P2", "MrbT", "MrkT"):
        shp = {"P1T": [128, 2, 128], "P2": [128, 4, 64], "MrbT": [128, 4, 128], "MrkT": [128, 4, 128]}[nm]
        HS[nm] = [ph.sb(shp, F32 if nm == "P2" else BF16, nm) for _ in range(NBUF3)]
    Usb = ph.sb([128, 4, 64], BF16, "Usb")
    STb = [ph.sb([128, 64], BF16, "STb") for _ in range(4)]
    identb = ph.sb([128, 128], BF16, "identb")
    ST = [ph.sb([128, 64], F32, "ST") for _ in range(4)]
    pA = [ph.ps([128, 512], F32, "pA") for _ in range(3)]
    pP = [ph.ps([128, 512], F32, "pP") for _ in range(2)]
    pTb = ph.ps([128, 1024], BF16, "pTb")
    pQ = [ph.ps([128, 512], F32, "pQ") for _ in range(2)]

    ld = lambda dst, src, key: P.dma("sp", lambda e: e.dma_start(out=dst, in_=src), writes=[key])
    for nm in cst:
        ld(cst[nm][:], prm[nm][:, :], nm)
    ld(identb[:], prm["ident_b"][:, :], "identb")
    ld(mu_tm[:], prm["rw_mu_tm"][0:1, :].partition_broadcast(128), "mu_tm")
    ld(mu_fm[:], prm["rw_mu_fm"][:, :], "mu_fm")
    ld(w2a2[:], prm["rw_w2a2"][:, :], "w2a2")
    for nm in vec:
        ld(vec[nm][:], prm[nm][0:1, :].partition_broadcast(128), nm)
    P.op("dve", lambda e: e.memset(onecol[:], 1.0), writes=["onecol"])
    for h in range(4):
        P.op("dve", lambda e, o=ST[h][:]: e.memset(o, 0.0), writes=[("ST", h)])
        P.op("dve", lambda e, o=STb[h][:]: e.memset(o, 0.0), writes=[("STb", h)])
    P.op("dve", lambda e: e.memset(Usb[:], 0.0), writes=["Usb"])
    for nm in ("mask_sl", "mask_su", "mask_u"):
        for r_ in range(2):
            P.op("pool", lambda e, o=mask2[nm][:, r_, :], s=cst[nm][:]: e.tensor_copy(out=o, in_=s), reads=[nm], writes=[nm + "2"])
    kq = {"a": 0, "q": 0, "e": 0}
    P.excl.update(["pA", "pQ", "pTb", "pP"])

    def mm(out, lhsT, rhs, reads, writes, start=True, stop=True):
        P.op("pe", lambda e: e.matmul(out, lhsT=lhsT, rhs=rhs, start=start, stop=stop), reads=reads, writes=writes)

    def evac(out, in_, reads, writes, mask=None, mreads=()):
        kq["e"] += 1
        if mask is not None:
            P.op("dve", lambda e: e.tensor_tensor(out=out, in0=in_, in1=mask, op=ALU.mult), reads=list(reads) + list(mreads), writes=writes)
        elif kq["e"] % 4 == 0:
            P.op("dve", lambda e: e.tensor_copy(out=out, in_=in_), reads=reads, writes=writes)
        else:
            P.op("act", lambda e: e.copy(out=out, in_=in_), reads=reads, writes=writes)

    def nextA():
        j = kq["a"] % 3
        kq["a"] += 1
        return j

    def nextQ():
        j = kq["q"] % 2
        kq["q"] += 1
        return j

    def dv(fn, reads, writes, eng="dve"):
        P.op(eng, fn, reads=reads, writes=writes)

    def stage(n):
        if stop_after is not None and n > stop_after:
            raise _Stop()

    real_op, real_dma = P.op, P.dma
    recs = []
    for blk in range(NBLK):
      rec = []
      marks = []
      recs.append((rec, marks))
      P.op = lambda eng, fn, reads=(), writes=(), rec=rec: rec.append((real_op, eng, fn, list(reads), list(writes)))
      P.dma = lambda eng, fn, reads=(), writes=(), rec=rec: rec.append((real_dma, eng, fn, list(reads), list(writes)))
      try:
          b2 = blk % NBUF3
          t0 = blk * 128
          B = {nm: T[nm][b2] for nm in T}
          K2 = lambda nm: (nm, b2)
          cu, pv, lc, lp = cur[b2], prv[b2], lcur[b2], lprv[b2]
          stage(-1)
          ld(cu[:], tm[t0:t0 + 128, TMF_R:TMF_R + 1024], K2("cur"))
          ld(lc[:], fm[FMF_WL:FMF_WL + 128, t0:t0 + 128], K2("lcur"))
          if blk == 0:
              dv(lambda e, o=pv[:]: e.memset(o, 0.0), [], [K2("prv")])
              dv(lambda e, o=lp[:]: e.memset(o, 0.0), [], [K2("lprv")])
              ld(pv[1:128, :], tm[0:127, TMF_R:TMF_R + 1024], K2("prv"))
              ld(lp[:, 1:128], fm[FMF_WL:FMF_WL + 128, 0:127], K2("lprv"))
          else:
              ld(pv[:], tm[t0 - 1:t0 + 127, TMF_R:TMF_R + 1024], K2("prv"))
              ld(lp[:], fm[FMF_WL:FMF_WL + 128, t0 - 1:t0 + 127], K2("lprv"))
          stage(-0.5)
          dv(lambda e, o=pv[:], c=cu[:]: e.tensor_tensor(out=o, in0=o, in1=c, op=ALU.subtract), [K2("prv"), K2("cur")], [K2("prv")])
          dv(lambda e, o=pv[:]: e.tensor_tensor(out=o, in0=o, in1=mu_tm[:], op=ALU.mult), [K2("prv"), "mu_tm"], [K2("prv")], eng="pool")
          dv(lambda e, o=cu[:], d=pv[:]: e.tensor_tensor(out=o, in0=o, in1=d, op=ALU.add), [K2("prv"), K2("cur")], [K2("cur")])
          dv(lambda e, o=lp[:], c=lc[:]: e.tensor_tensor(out=o, in0=o, in1=c, op=ALU.subtract), [K2("lprv"), K2("lcur")], [K2("lprv")])
          dv(lambda e, o=lc[:], d=lp[:]: e.scalar_tensor_tensor(out=o, in0=d, scalar=mu_fm[:, 0:1], in1=o, op0=ALU.mult, op1=ALU.add),
             [K2("lprv"), K2("lcur"), "mu_fm"], [K2("lcur")])
          stage(-0.2)
          P.op("act", lambda e, o=lc[0:64, :]: e.activation(out=o, in_=o, func=AF.Tanh), reads=[K2("lcur")], writes=[K2("lcur")])
          r_, k_, v_, z_ = cu[:, 0:256], cu[:, 256:512], cu[:, 512:768], cu[:, 768:1024]
          P.op("act", lambda e, o=B["vb"][:], v_=v_: e.copy(out=o, in_=v_), reads=[K2("cur")], writes=[K2("vb")])
          stage(1)
          mm(pP[0][:, 0:256], lc[0:64, :], w2a2[0:64, :], [K2("lcur"), "w2a2"], [("pP", 0)])
          mm(pP[1][:, 0:256], lc[64:128, :], w2a2[64:128, :], [K2("lcur"), "w2a2"], [("pP", 1)])
          dv(lambda e, o=B["lw"][:], s=pP[0][:, 0:256]: e.tensor_tensor(out=o, in0=s, in1=vec["rw_w0"][:], op=ALU.add),
             [("pP", 0), "rw_w0"], [K2("lw")])
          dv(lambda e, o=B["asig"][:], s=pP[1][:, 0:256]: e.tensor_tensor(out=o, in0=s, in1=vec["rw_a0"][:], op=ALU.add),
             [("pP", 1), "rw_a0"], [K2("asig")])
          P.op("act", lambda e, o=B["lw"][:]: e.activation(out=o, in_=o, func=AF.Sigmoid), reads=[K2("lw")], writes=[K2("lw")])
          P.op("act", lambda e, o=B["asig"][:]: e.activation(out=o, in_=o, func=AF.Sigmoid), reads=[K2("asig")], writes=[K2("asig")])
          dv(lambda e, o=B["lw"][:]: e.tensor_scalar(out=o, in0=o, scalar1=RW_DECAY_C, scalar2=None, op0=ALU.mult), [K2("lw")], [K2("lw")])
          stage(2)
          s4 = st4[b2]
          v3 = lambda ap: ap.rearrange("p (h j) -> p h j", h=4)
          dv(lambda e, o=B["kkn"][:], k_=k_: e.tensor_tensor(out=o, in0=k_, in1=vec["rw_kk"][:], op=ALU.mult), [K2("cur"), "rw_kk"], [K2("kkn")])
          dv(lambda e, o=B["tmp"][:], s=B["kkn"][:]: e.tensor_tensor(out=o, in0=s, in1=s, op=ALU.mult), [K2("kkn")], [K2("tmp")], eng="pool")
          dv(lambda e, o=s4[:, 0:4], s=v3(B["tmp"][:]): e.tensor_reduce(out=o, in_=s, axis=AX.X, op=ALU.add), [K2("tmp")], [K2("st4")])
          dv(lambda e, o=s4[:, 0:4]: e.tensor_scalar(out=o, in0=o, scalar1=1e-12, scalar2=None, op0=ALU.add), [K2("st4")], [K2("st4")])
          P.op("act", lambda e, o=s4[:, 0:4]: e.activation(out=o, in_=o, func=AF.Sqrt), reads=[K2("st4")], writes=[K2("st4")])
          dv(lambda e, o=s4[:, 0:4]: e.reciprocal(out=o, in_=o), [K2("st4")], [K2("st4")])
          dv(lambda e, o=v3(B["kkn"][:]), s=s4[:, 0:4].unsqueeze(2).to_broadcast([128, 4, 64]): e.tensor_tensor(out=o, in0=o, in1=s, op=ALU.mult),
             [K2("kkn"), K2("st4")], [K2("kkn")])
          dv(lambda e, o=B["tmp"][:], s=B["asig"][:]: e.scalar_tensor_tensor(out=o, in0=s, scalar=-1.0, in1=vec["rw_ka"][:], op0=ALU.add,
                                                                            op1=ALU.mult), [K2("asig"), "rw_ka"], [K2("tmp")])
          dv(lambda e, o=B["kp"][:], s=B["tmp"][:], k_=k_: e.scalar_tensor_tensor(out=o, in0=s, scalar=1.0, in1=k_, op0=ALU.add, op1=ALU.mult),
             [K2("tmp"), K2("cur")], [K2("kp")])
          dv(lambda e, o=B["aa"][:], s=B["kkn"][:]: e.tensor_scalar(out=o, in0=s, scalar1=-1.0, scalar2=None, op0=ALU.mult),
             [K2("kkn")], [K2("aa")], eng="pool")
          dv(lambda e, o=B["bb"][:], s=B["kkn"][:], a=B["asig"][:]: e.tensor_tensor(out=o, in0=s, in1=a, op=ALU.mult),
             [K2("kkn"), K2("asig")], [K2("bb")], eng="pool")
          dv(lambda e, o=B["tmp2"][:], s=B["kp"][:], r_=r_: e.tensor_tensor(out=o, in0=r_, in1=s, op=ALU.mult), [K2("cur"), K2("kp")], [K2("tmp2")])
          dv(lambda e, o=B["tmp2"][:]: e.tensor_tensor(out=o, in0=o, in1=vec["rw_rk"][:], op=ALU.mult), [K2("tmp2"), "rw_rk"], [K2("tmp2")])
          dv(lambda e, o=s4[:, 4:8], s=v3(B["tmp2"][:]): e.tensor_reduce(out=o, in_=s, axis=AX.X, op=ALU.add), [K2("tmp2")], [K2("st4")])
          dv(lambda e, o=v3(B["bon"][:]), s=v3(cu[:, 512:768]), c=s4[:, 4:8].unsqueeze(2).to_broadcast([128, 4, 64]):
             e.tensor_tensor(out=o, in0=s, in1=c, op=ALU.mult), [K2("cur"), K2("st4")], [K2("bon")])
          stage(3)
          mm(pP[0][:, 0:256], cst["mask_u"][:], B["lw"][:], ["mask_u", K2("lw")], [("pP", 0)])
          mm(pP[0][:, 256:512], cst["ones_bd"][:], B["lw"][:], ["ones_bd", K2("lw")], [("pP", 0)])
          evac(B["cum"][:], pP[0][:, 0:256], [("pP", 0)], [K2("cum")])
          evac(B["cumL"][:], pP[0][:, 256:512], [("pP", 0)], [K2("cumL")])
          for p in range(2):
              for c2 in range(2):
                  mm(pP[1][:, c2 * 2 + p:c2 * 2 + p + 1], B["lw"][:, p * 128:(p + 1) * 128], cst["ones_bd"][:, c2 * 64:c2 * 64 + 1],
                     [K2("lw"), "ones_bd"], [("pP", 1)])
          P.op("act", lambda e, o=gL[b2][:], s=pP[1][:, 0:4]: e.activation(out=o, in_=s, func=AF.Exp), reads=[("pP", 1)], writes=[K2("gL")])
          P.op("act", lambda e, o=B["e1"][:], s=B["cum"][:]: e.activation(out=o, in_=s, func=AF.Exp), reads=[K2("cum")], writes=[K2("e1")])
          dv(lambda e, o=B["rt"][:], s=B["e1"][:], r_=r_: e.tensor_tensor(out=o, in0=r_, in1=s, op=ALU.mult), [K2("cur"), K2("e1")], [K2("rt")])
          dv(lambda e, o=B["tmp"][:], s=B["cum"][:], l=B["lw"][:]: e.tensor_tensor(out=o, in0=s, in1=l, op=ALU.subtract),
             [K2("cum"), K2("lw")], [K2("tmp")], eng="pool")
          P.op("act", lambda e, o=B["tmp"][:]: e.activation(out=o, in_=o, func=AF.Exp), reads=[K2("tmp")], writes=[K2("tmp")])
          dv(lambda e, o=B["at"][:], s=B["aa"][:], t=B["tmp"][:]: e.tensor_tensor(out=o, in0=s, in1=t, op=ALU.mult),
             [K2("aa"), K2("tmp")], [K2("at")])
          P.op("act", lambda e, o=B["e1"][:], s=B["cum"][:]: e.activation(out=o, in_=s, func=AF.Exp, scale=-1.0),
               reads=[K2("cum"), K2("rt")], writes=[K2("e1")])
          dv(lambda e, o=B["bt"][:], s=B["bb"][:], t=B["e1"][:]: e.tensor_tensor(out=o, in0=s, in1=t, op=ALU.mult),
             [K2("bb"), K2("e1")], [K2("bt")])
          dv(lambda e, o=B["kt"][:], s=B["kp"][:], t=B["e1"][:]: e.tensor_tensor(out=o, in0=s, in1=t, op=ALU.mult),
             [K2("kp"), K2("e1")], [K2("kt")], eng="pool")
          dv(lambda e, o=B["tmp2"][:], s=B["cumL"][:], c=B["cum"][:]: e.tensor_tensor(out=o, in0=s, in1=c, op=ALU.subtract),
             [K2("cumL"), K2("cum")], [K2("tmp2")], eng="pool")
          P.op("act", lambda e, o=B["tmp2"][:]: e.activation(out=o, in_=o, func=AF.Exp), reads=[K2("tmp2")], writes=[K2("tmp2")])
          dv(lambda e, o=B["bh"][:], s=B["bb"][:], t=B["tmp2"][:]: e.tensor_tensor(out=o, in0=s, in1=t, op=ALU.mult),
             [K2("bb"), K2("tmp2")], [K2("bh")])
          dv(lambda e, o=B["kh"][:], s=B["kp"][:], t=B["tmp2"][:]: e.tensor_tensor(out=o, in0=s, in1=t, op=ALU.mult),
             [K2("kp"), K2("tmp2")], [K2("kh")], eng="pool")
          stage(4)
          for qi_, (nm_src, nm_dst) in enumerate((("at", "atT"), ("bt", "btT"), ("kt", "ktT"), ("rt", "rtT"))):
              for p in range(2):
                  P.op("pe", lambda e, o=pTb[:, (qi_ * 2 + p) * 128:(qi_ * 2 + p + 1) * 128], s=B[nm_src][:, p * 128:(p + 1) * 128]: e.transpose(
                      out=o, in_=s, identity=identb[:]), reads=[K2(nm_src), "identb"], writes=["pTb"])
          for qi_, (nm_src, nm_dst) in enumerate((("at", "atT"), ("bt", "btT"), ("kt", "ktT"), ("rt", "rtT"))):
              for p in range(2):
                  evac(TR[nm_dst][b2][p][:], pTb[:, (qi_ * 2 + p) * 128:(qi_ * 2 + p + 1) * 128], ["pTb"], [(nm_dst, b2, p)])
          marks.append(len(rec))
          slot = lambda h: (h % 2) * 2 + h // 2
          rk = lambda nm, h: (nm, b2, h // 2)
          opd = {}
          for h in range(4):
              p, r0 = h // 2, (h % 2) * 64
              opd[h] = {nm: TR[nm2][b2][p][r0:r0 + 64, :] for nm, nm2 in (("aT", "atT"), ("bT", "btT"), ("kT", "ktT"), ("rT", "rtT"))}
          jx, jy2 = nextA(), nextA()
          for r_, jb in ((0, jx), (1, jy2)):
              for p in range(2):
                  h = 2 * p + r_
                  o_ = opd[h]
                  mm(pA[jb][:, p * 128:(p + 1) * 128], o_["aT"], o_["bT"], [rk("atT", h), rk("btT", h)], [("pA", jb)])
                  mm(pA[jb][:, 256 + p * 128:256 + (p + 1) * 128], o_["bT"], o_["aT"], [rk("atT", h), rk("btT", h)], [("pA", jb)])
          for r_, jb in ((0, jx), (1, jy2)):
              sl_ = slice(2 * r_, 2 * r_ + 2)
              dv(lambda e, o=H["N"][:, sl_, :], s=pA[jb][:, 0:256].rearrange("p (a t) -> p a t", a=2), m=mask2["mask_sl"][:]:
                 e.tensor_tensor(out=o, in0=s, in1=m, op=ALU.mult), [("pA", jb), "mask_sl2"], [("N", r_)])
              dv(lambda e, o=H["NT"][:, sl_, :], s=pA[jb][:, 256:512].rearrange("p (a t) -> p a t", a=2), m=mask2["mask_su"][:]:
                 e.tensor_tensor(out=o, in0=s, in1=m, op=ALU.mult), [("pA", jb), "mask_su2"], [("NT", r_)])
          jz = nextA()
          jx2 = nextA()
          for r_, jb in ((0, jx2), (1, jz)):
              for p in range(2):
                  h = 2 * p + r_
                  o_ = opd[h]
                  mm(pA[jb][:, p * 128:(p + 1) * 128], o_["kT"], o_["aT"], [rk("ktT", h), rk("atT", h)], [("pA", jb)])
                  mm(pA[jb][:, 256 + p * 128:256 + (p + 1) * 128], o_["bT"], o_["rT"], [rk("btT", h), rk("rtT", h)], [("pA", jb)])
          for r_, jb in ((0, jx2), (1, jz)):
              sl_ = slice(2 * r_, 2 * r_ + 2)
              dv(lambda e, o=H["MakT"][:, sl_, :], s=pA[jb][:, 0:256].rearrange("p (a t) -> p a t", a=2), m=mask2["mask_su"][:]:
                 e.tensor_tensor(out=o, in0=s, in1=m, op=ALU.mult), [("pA", jb), "mask_su2"], [("MakT", r_)])
              dv(lambda e, o=HS["MrbT"][b2][:, sl_, :], s=pA[jb][:, 256:512].rearrange("p (a t) -> p a t", a=2), m=mask2["mask_u"][:]:
                 e.tensor_tensor(out=o, in0=s, in1=m, op=ALU.mult), [("pA", jb), "mask_u2"], [("MrbT", b2, r_)])
          jk0, jk1 = nextA(), nextA()
          for r_, jb in ((0, jk0), (1, jk1)):
              for p in range(2):
                  h = 2 * p + r_
                  o_ = opd[h]
                  mm(pA[jb][:, p * 128:(p + 1) * 128], o_["kT"], o_["rT"], [rk("ktT", h), rk("rtT", h)], [("pA", jb)])
          for r_, jb in ((0, jk0), (1, jk1)):
              sl_ = slice(2 * r_, 2 * r_ + 2)
              dv(lambda e, o=HS["MrkT"][b2][:, sl_, :], s=pA[jb][:, 0:256].rearrange("p (a t) -> p a t", a=2), m=mask2["mask_u"][:]:
                 e.tensor_tensor(out=o, in0=s, in1=m, op=ALU.mult), [("pA", jb), "mask_u2"], [("MrkT", b2, r_)])
          dv(lambda e, o=H["TTa"][:], s=H["NT"][:], i_=cst["ident_f"][:].unsqueeze(1).to_broadcast([128, 4, 128]):
             e.tensor_tensor(out=o, in0=s, in1=i_, op=ALU.add), [("NT", 0), ("NT", 1), "ident_f"], ["TTa"], eng="pool")
          cur_, curT_, nxt_, nxtT_ = "N", "NT", "Pa", "PaT"
          tc_, tn_ = "TTa", "TTb"
          kn = lambda nm: [(nm, 0), (nm, 1)] if nm in ("N", "NT") else [nm]
          for lvl in range(1, 6):
              ja = nextA()
              for sl in range(4):
                  mm(pA[ja][:, sl * 128:(sl + 1) * 128], H[curT_][:, sl, :], H[cur_][:, sl, :], kn(cur_) + kn(curT_), [("pA", ja)])
              evac(H[nxt_][:], pA[ja][:].rearrange("p (a t) -> p a t", a=4), [("pA", ja)], [nxt_])
              if lvl < 5:
                  jb = nextA()
                  for sl in range(4):
                      mm(pA[jb][:, sl * 128:(sl + 1) * 128], H[cur_][:, sl, :], H[curT_][:, sl, :], kn(cur_) + kn(curT_), [("pA", jb)])
                  evac(H[nxtT_][:], pA[jb][:].rearrange("p (a t) -> p a t", a=4), [("pA", jb)], [nxtT_])
              jc = nextA()
              for sl in range(4):
                  mm(pA[jc][:, sl * 128:(sl + 1) * 128], H[nxt_][:, sl, :], H[tc_][:, sl, :], [nxt_, tc_], [("pA", jc)])
              dv(lambda e, o=H[tn_][:], s=pA[jc][:].rearrange("p (a t) -> p a t", a=4), t=H[tc_][:]: e.tensor_tensor(out=o, in0=s, in1=t, op=ALU.add),
                 [("pA", jc), tc_], [tn_])
              if lvl == 1:
                  cur_, curT_, nxt_, nxtT_ = "Pa", "PaT", "Pb", "PbT"
              else:
                  cur_, curT_, nxt_, nxtT_ = nxt_, nxtT_, cur_, curT_
              tc_, tn_ = tn_, tc_
          TTn = tc_
          ja = nextA()
          for h in range(4):
              p, r0 = h // 2, (h % 2) * 64
              mm(pA[ja][r0:r0 + 64, p * 128:(p + 1) * 128], B["at"][:, h * 64:(h + 1) * 64], H[TTn][:, slot(h), :], [K2("at"), TTn], [("pA", ja)])
              mm(pA[ja][:, 256 + h * 64:256 + (h + 1) * 64], H["MakT"][:, slot(h), :], B["vb"][:, h * 64:(h + 1) * 64],
                 [("MakT", h % 2), K2("vb")], [("pA", ja)])
          evac(HS["P1T"][b2][:], pA[ja][:, 0:256].rearrange("p (a t) -> p a t", a=2), [("pA", ja)], [("P1T", b2)])
          evac(H["W2"][:], pA[ja][:, 256:512].rearrange("p (h i) -> p h i", h=4), [("pA", ja)], ["W2"])
          jb = nextA()
          for h in range(4):
              mm(pA[jb][:, h * 64:(h + 1) * 64], H[TTn][:, slot(h), :], H["W2"][:, h, :], [TTn, "W2"], [("pA", jb)])
          evac(HS["P2"][b2][:], pA[jb][:, 0:256].rearrange("p (h i) -> p h i", h=4), [("pA", jb)], [("P2", b2)])
          stage(6)
          marks.append(len(rec))
          for c2 in range(2):
              cs = slice(c2 * 64, (c2 + 1) * 64)
              jq = nextQ()
              for h in range(4):
                  mm(pQ[jq][cs, h * 64:(h + 1) * 64], HS["P1T"][b2][:, h // 2, cs], STb[h][:, :], [("P1T", b2), ("STb", h)], [("pQ", jq)])
              dv(lambda e, o=Usb[cs, :, :], s=pQ[jq][cs, 0:256].rearrange("p (h i) -> p h i", h=4), t=HS["P2"][b2][cs, :, :]:
                 e.tensor_tensor(out=o, in0=s, in1=t, op=ALU.add), [("pQ", jq), ("P2", b2)], ["Usb"])
              jy = nextQ()
              for h in range(4):
                  p = h // 2
                  yo = pQ[jy][cs, h * 64:(h + 1) * 64]
                  mm(yo, TR["rtT"][b2][p][:, cs], STb[h][:, :], [("rtT", b2, p), ("STb", h)], [("pQ", jy)], start=True, stop=False)
                  mm(yo, HS["MrkT"][b2][:, slot(h), cs], B["vb"][:, h * 64:(h + 1) * 64], [("MrkT", b2, h % 2), K2("vb")], [("pQ", jy)],
                     start=False, stop=False)
                  mm(yo, HS["MrbT"][b2][:, slot(h), cs], Usb[:, h, :], [("MrbT", b2, h % 2), "Usb"], [("pQ", jy)], start=False, stop=True)
              evac(B["yb"][cs, :], pQ[jy][cs, 0:256], [("pQ", jy)], [K2("yb")])
              js = nextQ()
              for h in range(4):
                  r0 = (h % 2) * 64
                  rows = slice(r0, r0 + 64)
                  so = pQ[js][rows, h * 64:(h + 1) * 64]
                  mm(so, B["kh"][cs, h * 64:(h + 1) * 64], B["vb"][cs, h * 64:(h + 1) * 64], [K2("kh"), K2("vb")], [("pQ", js)],
                     start=True, stop=False)
                  mm(so, B["bh"][cs, h * 64:(h + 1) * 64], Usb[cs, h, :], [K2("bh"), "Usb"], [("pQ", js)], start=False, stop=True)
              for h in range(4):
                  p, r0 = h // 2, (h % 2) * 64
                  rows = slice(r0, r0 + 64)
                  dv(lambda e, o=ST[h][rows, :], s=pQ[js][rows, h * 64:(h + 1) * 64], g=gL[b2][rows, c2 * 2 + p:c2 * 2 + p + 1]:
                     e.scalar_tensor_tensor(out=o, in0=o, scalar=g, in1=s, op0=ALU.mult, op1=ALU.add),
                     [("ST", h), ("pQ", js), K2("gL")], [("ST", h)])
                  P.op("act", lambda e, o=STb[h][rows, :], s=ST[h][rows, :]: e.copy(out=o, in_=s), reads=[("ST", h)], writes=[("STb", h)])
          stage(7)
          marks.append(len(rec))
          yb = B["yb"]
          dv(lambda e, o=s4[:, 8:12], s=v3(yb[:]): e.tensor_reduce(out=o, in_=s, axis=AX.X, op=ALU.add), [K2("yb")], [K2("st4")])
          dv(lambda e, o=B["tmp"][:], s=yb[:]: e.tensor_tensor(out=o, in0=s, in1=s, op=ALU.mult), [K2("yb")], [K2("tmp")], eng="pool")
          dv(lambda e, o=s4[:, 12:16], s=v3(B["tmp"][:]): e.tensor_reduce(out=o, in_=s, axis=AX.X, op=ALU.add), [K2("tmp")], [K2("st4")])
          dv(lambda e, o=s4[:, 8:16]: e.tensor_scalar(out=o, in0=o, scalar1=1.0 / 64, scalar2=None, op0=ALU.mult), [K2("st4")], [K2("st4")])
          dv(lambda e, o=s4[:, 0:4], m=s4[:, 8:12]: e.tensor_tensor(out=o, in0=m, in1=m, op=ALU.mult), [K2("st4")], [K2("st4")])
          dv(lambda e, o=s4[:, 12:16], m2=s4[:, 0:4]: e.tensor_tensor(out=o, in0=o, in1=m2, op=ALU.subtract), [K2("st4")], [K2("st4")])
          dv(lambda e, o=s4[:, 12:16]: e.tensor_scalar(out=o, in0=o, scalar1=GN_EPS, scalar2=None, op0=ALU.add), [K2("st4")], [K2("st4")])
          P.op("act", lambda e, o=s4[:, 12:16]: e.activation(out=o, in_=o, func=AF.Sqrt), reads=[K2("st4")], writes=[K2("st4")])
          dv(lambda e, o=s4[:, 12:16]: e.reciprocal(out=o, in_=o), [K2("st4")], [K2("st4")])
          dv(lambda e, o=v3(yb[:]), m=s4[:, 8:12].unsqueeze(2).to_broadcast([128, 4, 64]): e.tensor_tensor(out=o, in0=o, in1=m, op=ALU.subtract),
             [K2("yb"), K2("st4")], [K2("yb")])
          dv(lambda e, o=v3(yb[:]), r=s4[:, 12:16].unsqueeze(2).to_broadcast([128, 4, 64]): e.tensor_tensor(out=o, in0=o, in1=r, op=ALU.mult),
             [K2("yb"), K2("st4")], [K2("yb")])
          dv(lambda e, o=yb[:]: e.tensor_tensor(out=o, in0=o, in1=vec["rw_gng"][:], op=ALU.mult), [K2("yb"), "rw_gng"], [K2("yb")], eng="pool")
          dv(lambda e, o=yb[:]: e.tensor_tensor(out=o, in0=o, in1=vec["rw_gnb"][:], op=ALU.add), [K2("yb"), "rw_gnb"], [K2("yb")])
          dv(lambda e, o=yb[:], b_=B["bon"][:]: e.tensor_tensor(out=o, in0=o, in1=b_, op=ALU.add), [K2("yb"), K2("bon")], [K2("yb")], eng="pool")
          P.op("act", lambda e, o=B["tmp2"][:], z_=z_: e.activation(out=o, in_=z_, func=AF.Silu), reads=[K2("cur")], writes=[K2("tmp2")])
          dv(lambda e, o=yb[:], z=B["tmp2"][:]: e.tensor_tensor(out=o, in0=o, in1=z, op=ALU.mult), [K2("yb"), K2("tmp2")], [K2("yb")])
          P.dma("sp", lambda e, o=y_dst[t0:t0 + 128, :], s=yb[:]: e.dma_start(out=o, in_=s), reads=[K2("yb")], writes=[("y_rwkv", blk)])
      except _Stop:
          pass
    P.op, P.dma = real_op, real_dma
    if stagger:
        pipeline_merge(recs, 4)
    else:
        for rec, _ in recs:
            for (f, eng, fn, reads, writes) in rec:
                f(eng, fn, reads=reads, writes=writes)
    ph.close()


def phase_outproj(nc, y_tok, x_res, w_out, ident_d, x_new):
    ph = Phase(nc)
    P = ph.P
    D = D_MODEL
    KT = D // 128
    T = OWN
    TT = T // 128
    KQ = 8
    NKQ = KT // KQ
    NB = 512
    ident = ph.sb([128, 128], BF16, "ident")
    hT = ph.sb([128, KT, T], BF16, "hT")
    xt = [ph.sb([128, D], F32, "xt") for _ in range(2)]
    hb = [ph.sb([128, D], BF16, "hb") for _ in range(2)]
    wb = [ph.sb([128, KT, NB], BF16, "wb") for _ in range(2)]
    stg = [ph.sb([128, NB], F32, "stg") for _ in range(4)]
    rs = [ph.sb([128, NB], F32, "rs") for _ in range(4)]
    pT = [ph.ps([128, 1024], BF16, "pT") for _ in range(2)]
    pM = [ph.ps([128, 512], F32, "pM") for _ in range(6)]
    P.dma("sp", lambda e: e.dma_start(out=ident[:], in_=ident_d[:, :]), writes=["ident"])
    tc_ = 0
    for tt in range(TT):
        i = tt % 2
        P.dma("sp", lambda e, o=xt[i][:], s=y_tok[tt * 128:(tt + 1) * 128, :]: e.dma_start(out=o, in_=s), writes=[("xt", i)])
        P.op("act", lambda e, o=hb[i][:], s=xt[i][:]: e.copy(out=o, in_=s), reads=[("xt", i)], writes=[("hb", i)])
        for kq in range(NKQ):
            pb = tc_ % 2
            tc_ += 1
            for k8 in range(KQ):
                kt = kq * KQ + k8
                P.op("pe", lambda e, o=pT[pb][:, k8 * 128:(k8 + 1) * 128], s=hb[i][:, kt * 128:(kt + 1) * 128]: e.transpose(
                    out=o, in_=s, identity=ident[:]), reads=[("hb", i), "ident"], writes=[("pT", pb)])
            dst = hT[:, kq * KQ:(kq + 1) * KQ, tt * 128:(tt + 1) * 128]
            src = pT[pb][:].rearrange("p (k t) -> p k t", k=KQ)
            if kq % 2 == 0:
                P.op("dve", lambda e, dst=dst, src=src: e.tensor_copy(out=dst, in_=src), reads=[("pT", pb)], writes=[("hT", tt, kq)])
            else:
                P.op("act", lambda e, dst=dst, src=src: e.copy(out=dst, in_=src), reads=[("pT", pb)], writes=[("hT", tt, kq)])
    wv = w_out.rearrange("(kt p) n -> p kt n", p=128)
    mc = 0
    for cb in range(D // NB):
        c0 = cb * NB
        wi = cb % 2
        for kq in range(NKQ):
            P.dma("pool", lambda e, o=wb[wi][:, kq * KQ:(kq + 1) * KQ, :], s=wv[:, kq * KQ:(kq + 1) * KQ, c0:c0 + NB]: e.dma_start(
                out=o, in_=s), writes=[("wb", wi, kq)])
        for tt in range(TT):
            j = mc % 6
            s4 = mc % 4
            mc += 1
            for kt in range(KT):
                P.op("pe", lambda e, o=pM[j][:], l=hT[:, kt, tt * 128:(tt + 1) * 128], r=wb[wi][:, kt, :], kt=kt: e.matmul(
                    o, lhsT=l, rhs=r, start=(kt == 0), stop=(kt == KT - 1)),
                    reads=[("hT", tt, kt // KQ), ("wb", wi, kt // KQ)], writes=[("pM", j)])
            P.dma("sp", lambda e, o=rs[s4][:], s=x_res[tt * 128:(tt + 1) * 128, c0:c0 + NB]: e.dma_start(out=o, in_=s), writes=[("rs", s4)])
            P.op("dve", lambda e, o=stg[s4][:], a=pM[j][:], b_=rs[s4][:]: e.tensor_tensor(out=o, in0=a, in1=b_, op=ALU.add),
                 reads=[("pM", j), ("rs", s4)], writes=[("stg", s4)])
            P.dma("sp", lambda e, o=x_new[tt * 128:(tt + 1) * 128, c0:c0 + NB], s=stg[s4][:]: e.dma_start(out=o, in_=s),
                  reads=[("stg", s4)], writes=[("xn", tt, cb)])
    ph.close()


def phase_finalnorm(nc, x_in, g, out, ntiles=OWN // 128):
    ph = Phase(nc)
    P = ph.P
    D = D_MODEL
    gb = ph.sb([128, D], F32, "gb")
    xt = [ph.sb([128, D], F32, "xt") for _ in range(2)]
    junk = ph.sb([128, D], BF16, "junk")
    ss = ph.sb([128, ntiles], F32, "ss")
    P.dma("sp", lambda e: e.dma_start(out=gb[:], in_=g[0:1, :].partition_broadcast(128)), writes=["gb"])
    P.op("dve", lambda e: e.memset(ss[:], 0.0), writes=["ss"])
    for tt in range(ntiles):
        i = tt % 2
        P.dma("sp", lambda e, o=xt[i][:], s=x_in[tt * 128:(tt + 1) * 128, :]: e.dma_start(out=o, in_=s), writes=[("xt", i)])
        sc = ss[:, tt:tt + 1]
        P.op("act", lambda e, s=xt[i][:], sc=sc: e.activation(out=junk[:], in_=s, func=AF.Square, accum_out=sc),
             reads=[("xt", i), "ss"], writes=["junk", ("ssv", tt)])
        P.op("dve", lambda e, sc=sc: e.tensor_scalar(out=sc, in0=sc, scalar1=1.0 / D, scalar2=NORM_EPS, op0=ALU.mult, op1=ALU.add),
             reads=[("ssv", tt)], writes=[("ssv", tt)])
        P.op("act", lambda e, sc=sc: e.activation(out=sc, in_=sc, func=AF.Sqrt), reads=[("ssv", tt)], writes=[("ssv", tt)])
        P.op("dve", lambda e, sc=sc: e.reciprocal(out=sc, in_=sc), reads=[("ssv", tt)], writes=[("ssv", tt)])
        P.op("dve", lambda e, o=xt[i][:], sc=sc: e.scalar_tensor_tensor(out=o, in0=o, scalar=sc, in1=gb[:], op0=ALU.mult, op1=ALU.mult),
             reads=[("xt", i), ("ssv", tt), "gb"], writes=[("xt", i)])
        P.dma("sp", lambda e, o=out[tt * 128:(tt + 1) * 128, :], s=xt[i][:]: e.dma_start(out=o, in_=s), reads=[("xt", i)],
              writes=[("out", tt)])
    ph.close()


def own_tok(q):
    return ((4 * np.arange(8)[:, None] + q) * 128 + np.arange(128)[None, :]).reshape(-1)


def const_inputs():
    i = np.arange(128)
    sel4 = np.zeros((4, 4, 128), np.float32)
    for h in range(4):
        sel4[h, h, :] = 1.0
    same = (i[:, None] // 64) == (i[None, :] // 64)
    sl = ((i[None, :] < i[:, None]) & same).astype(np.float32)
    su = np.ascontiguousarray(sl.T)
    return {
        "ident": np.eye(128, dtype=ml_dtypes.bfloat16),
        "ident_f": np.eye(128, dtype=np.float32), "ident_b": np.eye(128, dtype=ml_dtypes.bfloat16),
        "tri_le": (i[:, None] <= i[None, :]).astype(np.float32), "ones_f": np.ones((128, 128), np.float32),
        "sel4": sel4, "negbig_lt": np.where(i[None, :] < i[:, None], NEG_BIG, 0.0).astype(np.float32),
        "ident30k_b": (30000.0 * np.eye(128)).astype(ml_dtypes.bfloat16),
        "iota512": np.arange(512, dtype=np.float32)[None, :].copy(),
        "pow2": (0.5 ** (np.arange(NBIS) + 1)).astype(np.float32)[None, :].copy(),
        "mask_sl": sl, "mask_su": su, "mask_u": su + np.eye(128, dtype=np.float32), "ones_bd": same.astype(np.float32),
    }


def layer_params(inp, l, q):
    c = np.ascontiguousarray
    p = {}
    p["qrel"] = (q * 128 + np.arange(128, dtype=np.float32))[:, None].copy()
    p["g"] = c(inp["norm_g"][l][None, :])
    p["mlp_ln_g"] = c(inp["mlp_ln_g"][l][None, :])
    p["mlp_ln_b"] = c(inp["mlp_ln_b"][l][None, :])
    p["mlp_wsT"] = c(inp["mlp_w_s"][l].transpose(2, 0, 1))
    p["mlp_bsT"] = c(inp["mlp_b_s"][l].T)
    cols = np.concatenate([np.arange(256 * q, 256 * (q + 1)), 1024 + np.arange(128 * q, 128 * (q + 1)),
                           1536 + np.arange(128 * q, 128 * (q + 1))])
    cw = inp["ssm_conv_w"][l][:, cols]
    cb = inp["ssm_conv_b"][l][cols]
    hs = slice(4 * q, 4 * q + 4)
    p["ssm_cw"] = c(cw.reshape(4, 4, 128).transpose(2, 1, 0))
    p["ssm_cb"] = c(cb.reshape(4, 128).T)
    p["ssm_dtb_t"] = c(np.tile(inp["ssm_dt_bias"][l][hs], 32)[None, :])
    p["ssm_alog_t"] = c(np.tile(inp["ssm_A_log"][l][hs], 32)[None, :])
    p["ssm_D"] = c(inp["ssm_D"][l][hs][None, :])
    p["ssm_ng"] = c(inp["ssm_norm_g"][l][256 * q:256 * (q + 1)][None, :])
    mu = inp["rwkv_mu"][l]
    hc = np.arange(256 * q, 256 * (q + 1))
    p["rw_mu_tm"] = c(np.concatenate([mu[1024 * k + hc] for k in range(4)])[None, :])
    p["rw_mu_fm"] = c(mu[4096:4224][:, None])
    p["rw_w2a2"] = c(np.concatenate([inp["rwkv_w2"][l][:, hc], inp["rwkv_a2"][l][:, hc]], axis=0))
    for nm, src in (("rw_w0", "rwkv_w0"), ("rw_a0", "rwkv_a0"), ("rw_kk", "rwkv_k_k"), ("rw_ka", "rwkv_k_a"), ("rw_rk", "rwkv_r_k"),
                    ("rw_gng", "rwkv_gn_g"), ("rw_gnb", "rwkv_gn_b")):
        p[nm] = c(inp[src][l].reshape(-1)[hc][None, :])
    return p


def _decl(nc, arrs):
    out = {}
    for n, a in arrs.items():
        dt = BF16 if a.dtype == ml_dtypes.bfloat16 else F32
        out[n] = nc.dram_tensor(n, list(a.shape), dt, kind="ExternalInput").ap()
    return out


def build_AB(sample_inputs):
    nc = bass.Bass("TRN2", target_bir_lowering=False)
    d = _decl(nc, sample_inputs)
    wts = {n: d["w_" + n] for n in GROUP_INFO}
    scr = make_scratch(nc)
    maskT = nc.dram_tensor("maskT", [128, NPAIR, 128], BF16).ap()
    y_own = nc.dram_tensor("y_own", [OWN, 2048], F32, kind="ExternalOutput").ap()
    y_all = nc.dram_tensor("y_all", [SEQ, 512], F32, kind="ExternalOutput").ap()
    phase_inproj(nc, d["x_all"], d["x_own"], d["g"], d["ident"], wts, scr)
    phase_indexer(nc, scr, d, maskT)
    phase_attn(nc, scr, d, maskT, y_own)
    phase_mlp(nc, scr, d, y_own)
    phase_ssm(nc, scr, d, y_all)
    phase_rwkv(nc, scr, d, y_all)
    return nc


def build_C(final):
    nc = bass.Bass("TRN2", target_bir_lowering=False)
    y_tok = nc.dram_tensor("y_tok", [OWN, D_MODEL], F32, kind="ExternalInput").ap()
    x_res = nc.dram_tensor("x_res", [OWN, D_MODEL], F32, kind="ExternalInput").ap()
    w_out = nc.dram_tensor("w_out", [D_MODEL, D_MODEL], F32, kind="ExternalInput").ap()
    ident = nc.dram_tensor("ident", [128, 128], BF16, kind="ExternalInput").ap()
    if final:
        g = nc.dram_tensor("gf", [1, D_MODEL], F32, kind="ExternalInput").ap()
        x_mid = nc.dram_tensor("x_mid", [OWN, D_MODEL], F32).ap()
        out = nc.dram_tensor("out", [OWN, D_MODEL], F32, kind="ExternalOutput").ap()
        phase_outproj(nc, y_tok, x_res, w_out, ident, x_mid)
        phase_finalnorm(nc, x_mid, g, out)
    else:
        out = nc.dram_tensor("out", [OWN, D_MODEL], F32, kind="ExternalOutput").ap()
        phase_outproj(nc, y_tok, x_res, w_out, ident, out)
    return nc


def kernel_unfused(**inp):
    inp = {k: np.asarray(v) for k, v in inp.items()}
    x = inp["x"]
    cst = const_inputs()
    otk = [own_tok(q) for q in range(NQ)]
    nc_ab = None
    for l in range(2):
        w_in = inp["w_in"][l]
        in_maps = []
        for c in range(NCORE):
            b, q = c // NQ, c % NQ
            m = dict(cst)
            m.update(layer_params(inp, l, q))
            m["x_all"] = np.ascontiguousarray(x[b])
            m["x_own"] = np.ascontiguousarray(x[b][otk[q]])
            for name, cols in col_groups(q).items():
                m["w_" + name] = np.ascontiguousarray(w_in[:, cols])
            in_maps.append(m)
        if nc_ab is None:
            nc_ab = build_AB(in_maps[0])
        res = run_bass_kernel_spmd(nc_ab, in_maps, core_ids=list(range(NCORE))).results
        in_maps_c = []
        for c in range(NCORE):
            b, q = c // NQ, c % NQ
            y_tok = np.empty((OWN, D_MODEL), np.float32)
            y_tok[:, 0:1024] = res[c]["y_own"][:, 0:1024]
            y_tok[:, 3072:4096] = res[c]["y_own"][:, 1024:2048]
            for q2 in range(NQ):
                ya = res[b * NQ + q2]["y_all"][otk[q]]
                y_tok[:, 1024 + 256 * q2:1024 + 256 * (q2 + 1)] = ya[:, 0:256]
                y_tok[:, 2048 + 256 * q2:2048 + 256 * (q2 + 1)] = ya[:, 256:512]
            m = {"y_tok": y_tok, "x_res": np.ascontiguousarray(x[b][otk[q]]), "w_out": np.ascontiguousarray(inp["w_out"][l]),
                 "ident": cst["ident"]}
            if l == 1:
                m["gf"] = np.ascontiguousarray(inp["final_norm_g"][None, :])
            in_maps_c.append(m)
        nc_c = build_C(final=(l == 1))
        resc = run_bass_kernel_spmd(nc_c, in_maps_c, core_ids=list(range(NCORE))).results
        xn = np.empty_like(x)
        for c in range(NCORE):
            b, q = c // NQ, c % NQ
            xn[b][otk[q]] = resc[c]["out"]
        x = xn
    return x


def fused_ginfo():
    gi = {"fmb_all": (1152, "fm", BF16, "all"), "tmb_all": (1024, "tm", BF16, "all"),
          "fmb_own": (2048, "fm", BF16, "all"), "tmf_own": (4112, "tm", F32, "all")}
    for q in range(NQ):
        gi[f"fmf_all{q}"] = (640, "fm", F32, "all")
        gi[f"tmf_all{q}"] = (1284, "tm", F32, "all")
    return gi


FUSED_BLOCKS = [(qb // 4 + 1, qb % 4, qb % 4 + 1) for qb in range(32)]
Q_KEYS = ("ssm_cw", "ssm_cb", "ssm_dtb_t", "ssm_alog_t", "ssm_D", "ssm_ng", "rw_mu_tm", "rw_mu_fm", "rw_w2a2", "rw_w0", "rw_a0",
          "rw_kk", "rw_ka", "rw_rk", "rw_gng", "rw_gnb")
L_KEYS = ("g", "mlp_ln_g", "mlp_ln_b", "mlp_wsT", "mlp_bsT")


def fused_inputs(inp, b):
    c = np.ascontiguousarray
    m = dict(const_inputs())
    m["qrel"] = c((np.arange(4, dtype=np.float32)[None, :] * 128 + np.arange(128, dtype=np.float32)[:, None]))
    m["x"] = c(inp["x"][b])
    m["gf"] = c(inp["final_norm_g"][None, :])
    for l in range(2):
        w_in = inp["w_in"][l]
        cg0 = col_groups(0)
        for name in ("fmb_all", "tmb_all", "fmb_own", "tmf_own"):
            m[f"w{l}_{name}"] = c(w_in[:, cg0[name]])
        for q in range(NQ):
            cg = col_groups(q)
            m[f"w{l}_fmf_all{q}"] = c(w_in[:, cg["fmf_all"]])
            m[f"w{l}_tmf_all{q}"] = c(w_in[:, cg["tmf_all"]])
            lp = layer_params(inp, l, q)
            for key in Q_KEYS:
                m[f"l{l}q{q}_{key}"] = lp[key]
            if q == 0:
                for key in L_KEYS:
                    m[f"l{l}_{key}"] = lp[key]
        m[f"wout{l}"] = c(inp["w_out"][l])
    return m


def build_fused(sample, upto=None, nlayers=2, skip=()):
    nc = bass.Bass("TRN2", target_bir_lowering=False)
    d = _decl(nc, sample)
    gi = fused_ginfo()
    scr = {}
    for name, (ncols, layout, dt, _) in gi.items():
        shape = [SEQ, ncols] if layout == "tm" else [ncols, SEQ]
        scr[name] = nc.dram_tensor("scr_" + name, shape, dt).ap()
    _, npairs = pair_offsets(FUSED_BLOCKS)
    maskT = nc.dram_tensor("maskT", [128, npairs, 128], BF16).ap()
    y_full = nc.dram_tensor("y_full", [SEQ, D_MODEL], F32).ap()
    xs = [d["x"], nc.dram_tensor("x1", [SEQ, D_MODEL], F32).ap(), nc.dram_tensor("x2", [SEQ, D_MODEL], F32).ap()]
    out = nc.dram_tensor("out", [SEQ, D_MODEL], F32, kind="ExternalOutput").ap()
    gnames = list(gi.keys())
    cnt = [0]

    def go():
        cnt[0] += 1
        return (upto is None or cnt[0] <= upto) and cnt[0] not in skip

    for l in range(nlayers):
        x_cur, x_nxt = xs[l], xs[l + 1]
        wts = {name: d[f"w{l}_{name}"] for name in gnames}
        plan = [(x_cur, p * 1024, [(n, p * 1024) for n in gnames]) for p in range(4)]
        if go():
            phase_inproj(nc, None, None, d[f"l{l}_g"], d["ident"], wts, scr, plan=plan, ginfo=gi)
        prm_l = dict(d)
        for key in L_KEYS:
            prm_l[key] = d[f"l{l}_{key}"]
        sc_own = {"fmb_own": scr["fmb_own"], "fmb_all": scr["fmb_all"], "tmf_own": scr["tmf_own"], "tmb_all": scr["tmb_all"]}
        if go():
            phase_indexer(nc, sc_own, prm_l, maskT, blocks=FUSED_BLOCKS)
        if go():
            phase_attn(nc, sc_own, prm_l, maskT, y_full, blocks=FUSED_BLOCKS)
        if go():
            phase_mlp(nc, sc_own, prm_l, y_full, nchunks=SEQ // 128, ycol0=3072)
        for q in range(NQ):
            prm_q = dict(d)
            for key in Q_KEYS:
                prm_q[key] = d[f"l{l}q{q}_{key}"]
            sc_q = {"fmf_all": scr[f"fmf_all{q}"], "tmf_all": scr[f"tmf_all{q}"]}
            if go():
                phase_ssm(nc, sc_q, prm_q, None, y_dst=y_full[:, 1024 + 256 * q:1024 + 256 * (q + 1)])
            if go():
                phase_rwkv(nc, sc_q, prm_q, None, y_dst=y_full[:, 2048 + 256 * q:2048 + 256 * (q + 1)])
        for p in range(4):
            rs_ = slice(p * 1024, (p + 1) * 1024)
            if go():
                phase_outproj(nc, y_full[rs_, :], x_cur[rs_, :], d[f"wout{l}"], d["ident"], x_nxt[rs_, :])
    if upto is None:
        phase_finalnorm(nc, xs[nlayers], d["gf"], out, ntiles=SEQ // 128)
    else:
        phase_finalnorm(nc, y_full, d["gf"], out, ntiles=SEQ // 128)
    return nc


def kernel_fused(**inp):
    inp = {k: np.asarray(v) for k, v in inp.items()}
    nb = inp["x"].shape[0]
    in_maps = [fused_inputs(inp, b) for b in range(nb)]
    nc = build_fused(in_maps[0])
    res = run_bass_kernel_spmd(nc, in_maps, core_ids=list(range(nb))).results
    return np.stack([res[b]["out"] for b in range(nb)], axis=0)


FUSED = True


def kernel(**inp):
    return kernel_fused(**inp) if FUSED else kernel_unfused(**inp)
```

```python
import contextlib
import numpy as np
import ml_dtypes
import concourse.bass as bass
import concourse.mybir as mybir
from concourse.bass_utils import run_bass_kernel_spmd

F32 = mybir.dt.float32
BF16 = mybir.dt.bfloat16
ALU = mybir.AluOpType
AF = mybir.ActivationFunctionType
AX = mybir.AxisListType

D_MODEL = 4096
SEQ = 4096
NCORE = 8
NQ = 4
OWN = SEQ // NQ
NORM_EPS = 1e-5
GN_EPS = 64e-5

COMPUTE = ("pe", "act", "dve", "pool")


SEM_ROLL = 30000


class SemState:
    def __init__(self, nc):
        self.nc = nc
        self.st = contextlib.ExitStack()
        self.sems = {}
        self.count = {e: 0 for e in COMPUTE}
        self.dma_slots = {}
        self.dma_rr = {}
        self.n_dma_slots = 8

    def handle(self, key):
        h = self.sems.get(key)
        if h is None:
            h = self.st.enter_context(self.nc.semaphore("s_" + "_".join(str(x) for x in key)))
            self.sems[key] = h
        return h


def semstate(nc):
    ss = getattr(nc, "_semstate", None)
    if ss is None:
        ss = SemState(nc)
        nc._semstate = ss
    return ss


class Prog:
    def __init__(self, nc):
        self.nc = nc
        self.ss = semstate(nc)
        self.streams = {e: [] for e in ("pe", "act", "dve", "pool", "sp")}
        self.last_writer = {}
        self.readers = {}
        self.waited = {}
        self.used_keys = []
        self.excl = set()

    def _semkey(self, key):
        if key not in self.used_keys:
            self.used_keys.append(key)
        return key

    def _deps_for(self, reads, writes, eng=None):
        deps = set()
        for b in reads:
            w = self.last_writer.get(b)
            if w is not None:
                deps.add(w)
        for b in writes:
            w = self.last_writer.get(b)
            if w is not None:
                deps.add(w)
            for r in self.readers.get(b, ()):
                if eng is not None and r[0][0] == eng:
                    continue
                deps.add(r)
        if eng == "pe":
            deps = {d for d in deps if d[0][0] != "pe"}
        return deps

    def _commit(self, tok, reads, writes):
        for b in reads:
            self.readers.setdefault(b, []).append(tok)
        for b in writes:
            self.last_writer[b] = tok
            self.readers[b] = []

    def _waits(self, eng, deps):
        waits = []
        for (k, v) in sorted(deps, key=lambda t: (str(t[0]), t[1])):
            if self.waited.get((eng, k), -1) >= v:
                continue
            self.waited[(eng, k)] = v
            self._semkey(k)
            waits.append((k, v))
        return waits

    def op(self, eng, fn, reads=(), writes=()):
        ex = [b for b in reads if (b[0] if isinstance(b, tuple) else b) in self.excl]
        if ex:
            writes = list(writes) + [b for b in ex if b not in writes]
        ss = self.ss
        idx = ss.count[eng]
        ss.count[eng] += 1
        ep = idx // SEM_ROLL
        key = self._semkey((eng, ep))
        deps = self._deps_for(reads, writes, eng=eng)
        waits = self._waits(eng, deps)
        self.streams[eng].append((waits, fn, (key, 1)))
        tok = (key, idx - ep * SEM_ROLL + 1)
        self._commit(tok, reads, writes)
        return tok

    def dma(self, eng, fn, reads=(), writes=()):
        ss = self.ss
        slots = ss.dma_slots.get(eng)
        if slots is None:
            slots = [[("dma", eng, i, 0), 0] for i in range(ss.n_dma_slots)]
            ss.dma_slots[eng] = slots
        rr = ss.dma_rr.get(eng, 0)
        ss.dma_rr[eng] = (rr + 1) % len(slots)
        slot = slots[rr]
        deps = self._deps_for(reads, writes)
        if slot[1] > 0:
            deps.add((slot[0], slot[1]))
        if slot[1] + 16 > SEM_ROLL:
            slot[0] = ("dma", eng, slot[0][2], slot[0][3] + 1)
            slot[1] = 0
        key = self._semkey(slot[0])
        waits = self._waits(eng, deps)
        slot[1] += 16
        self.streams[eng].append((waits, fn, (key, 16)))
        tok = (key, slot[1])
        self._commit(tok, reads, writes)
        return tok

    def finish(self, eng="sp"):
        toks = set()
        ss = self.ss
        for q, slots in ss.dma_slots.items():
            for key, cnt in slots:
                if cnt > 0:
                    toks.add((key, cnt))
        for e in COMPUTE:
            n = ss.count[e]
            if n > 0:
                ep = (n - 1) // SEM_ROLL
                toks.add(((e, ep), n - ep * SEM_ROLL))
        waits = self._waits(eng, toks)
        self.streams[eng].append((waits, None, None))

    def emit(self):
        nc = self.nc
        ss = self.ss
        sems = {k: ss.handle(k) for k in self.used_keys}
        with nc.Block() as block:

            def run(engobj, items):
                for waits, fn, inc in items:
                    for (k, v) in waits:
                        engobj.wait_ge(sems[k], v)
                    if fn is not None:
                        ins = fn(engobj)
                        ins.then_inc(sems[inc[0]], inc[1])

            @block.tensor
            def _(e):
                run(e, self.streams["pe"])

            @block.scalar
            def _(e):
                run(e, self.streams["act"])

            @block.vector
            def _(e):
                run(e, self.streams["dve"])

            @block.gpsimd
            def _(e):
                run(e, self.streams["pool"])

            @block.sync
            def _(e):
                run(e, self.streams["sp"])


class Phase:
    _count = [0]

    def __init__(self, nc):
        self.nc = nc
        self.st = contextlib.ExitStack()
        self.P = Prog(nc)
        self.n = 0
        Phase._count[0] += 1
        self.pid = Phase._count[0]

    def sb(self, shape, dt, name=None):
        self.n += 1
        return self.st.enter_context(self.nc.sbuf_tensor(f"{name or 't'}_{self.n}_{self.pid}", list(shape), dt))

    def ps(self, shape, dt, name=None):
        self.n += 1
        return self.st.enter_context(self.nc.psum_tensor(f"{name or 'p'}_{self.n}_{self.pid}", list(shape), dt))

    def dump(self, name, sb_ap, reads):
        dbg = getattr(self.nc, "_dbg", None)
        if not dbg or name not in dbg:
            return
        d = dbg[name]
        self.P.dma("sp", lambda e: e.dma_start(out=d, in_=sb_ap), reads=reads, writes=[("dbg", name)])

    def close(self):
        self.P.finish()
        self.P.emit()
        self.st.close()


ATT0, SSM0, RWKV0, MLP0 = 0, 5200, 8288, 12512


def col_groups(q):
    r = lambda a, n: np.arange(a, a + n)
    att_q, att_k, att_v, att_z = r(0, 1024), r(1024, 1024), r(2048, 1024), r(3072, 1024)
    att_qi, att_ki, att_wi = r(4096, 1024), r(5120, 64), r(5184, 16)
    ssm_z = r(SSM0 + 256 * q, 256)
    ssm_x = r(SSM0 + 1024 + 256 * q, 256)
    ssm_B = r(SSM0 + 2048 + 128 * q, 128)
    ssm_C = r(SSM0 + 2560 + 128 * q, 128)
    ssm_dt = r(SSM0 + 3072 + 4 * q, 4)
    rw = [r(RWKV0 + 1024 * i + 256 * q, 256) for i in range(4)]
    rw_wl, rw_al = r(RWKV0 + 4096, 64), r(RWKV0 + 4160, 64)
    mlp = [r(MLP0 + 1024 * i, 1024) for i in range(3)]
    cat = np.concatenate
    return {
        "fmb_all": cat([att_k, att_ki, att_ki]),
        "fmf_all": cat([ssm_x, ssm_B, ssm_C, rw_wl, rw_al]),
        "tmb_all": att_v,
        "tmf_all": cat([ssm_z, ssm_dt] + rw),
        "fmb_own": cat([att_q, att_qi]),
        "tmf_own": cat([att_z, att_wi] + mlp),
    }


GROUP_INFO = {
    "fmb_all": (1152, "fm", BF16, "all"),
    "fmf_all": (640, "fm", F32, "all"),
    "tmb_all": (1024, "tm", BF16, "all"),
    "tmf_all": (1284, "tm", F32, "all"),
    "fmb_own": (2048, "fm", BF16, "own"),
    "tmf_own": (4112, "tm", F32, "own"),
}


def phase_inproj(nc, x_all, x_own, g, ident_d, wts, scr, plan=None, ginfo=None):
    ph = Phase(nc)
    P = ph.P
    D = D_MODEL
    KT = D // 128
    T = 1024
    TT = T // 128
    KQ = 8
    NKQ = KT // KQ
    NB = 512
    ident = ph.sb([128, 128], BF16, "ident")
    hT = ph.sb([128, KT, T], BF16, "hT")
    xt = [ph.sb([128, D], F32, "xt") for _ in range(2)]
    hb = [ph.sb([128, D], BF16, "hb") for _ in range(2)]
    wb = [ph.sb([128, KT, NB], BF16, "wb") for _ in range(2)]
    stg = [ph.sb([128, NB], F32, "stg") for _ in range(4)]
    stgb = [ph.sb([128, NB], BF16, "stgb") for _ in range(4)]
    gb = ph.sb([128, D], F32, "gb")
    ss = ph.sb([128, 8 * 5], F32, "ss")
    rstd = ph.sb([128, 8 * 5], F32, "rstd")
    pT = [ph.ps([128, 1024], BF16, "pT") for _ in range(2)]
    pM = [ph.ps([128, 512], F32, "pM") for _ in range(6)]

    P.dma("sp", lambda e: e.dma_start(out=ident[:], in_=ident_d[:, :]), writes=["ident"])
    P.dma("sp", lambda e: e.dma_start(out=gb[:], in_=g[0:1, :].partition_broadcast(128)), writes=["gb"])
    cnt = {"t": 0, "m": 0, "w": 0, "x": 0}
    P.op("dve", lambda e: e.memset(ss[:], 0.0), writes=[("ss", i) for i in range(40)])

    def load_hT(x_ap, row0, pidx):
        for tt in range(TT):
            i = cnt["x"] % 2
            cnt["x"] += 1
            xb, hbb = xt[i], hb[i]
            sc = pidx * 8 + tt
            P.dma("sp", lambda e, xb=xb, tt=tt: e.dma_start(out=xb[:], in_=x_ap[row0 + tt * 128:row0 + (tt + 1) * 128, :]),
                  writes=[("xt", i)])
            P.op("act", lambda e, xb=xb, hbb=hbb, sc=sc: e.activation(out=hbb[:], in_=xb[:], func=AF.Square,
                                                                     accum_out=ss[:, sc:sc + 1]),
                 reads=[("xt", i)], writes=[("hb", i), ("ss", sc)])
            P.op("dve", lambda e, sc=sc: e.tensor_scalar(out=rstd[:, sc:sc + 1], in0=ss[:, sc:sc + 1],
                                                         scalar1=1.0 / D, scalar2=NORM_EPS, op0=ALU.mult, op1=ALU.add),
                 reads=[("ss", sc)], writes=[("rstd", sc)])
            P.op("act", lambda e, sc=sc: e.activation(out=rstd[:, sc:sc + 1], in_=rstd[:, sc:sc + 1], func=AF.Sqrt),
                 reads=[("rstd", sc)], writes=[("rstd", sc)])
            P.op("dve", lambda e, sc=sc: e.reciprocal(out=rstd[:, sc:sc + 1], in_=rstd[:, sc:sc + 1]),
                 reads=[("rstd", sc)], writes=[("rstd", sc)])
            P.op("dve", lambda e, xb=xb, hbb=hbb, sc=sc: e.scalar_tensor_tensor(
                out=hbb[:], in0=xb[:], scalar=rstd[:, sc:sc + 1], in1=gb[:], op0=ALU.mult, op1=ALU.mult),
                reads=[("xt", i), ("rstd", sc), "gb", ("hb", i)], writes=[("hb", i)])
            for kq in range(NKQ):
                pb = cnt["t"] % 2
                cnt["t"] += 1
                for k8 in range(KQ):
                    kt = kq * KQ + k8
                    P.op("pe", lambda e, pb=pb, k8=k8, kt=kt, hbb=hbb: e.transpose(
                        out=pT[pb][:, k8 * 128:(k8 + 1) * 128], in_=hbb[:, kt * 128:(kt + 1) * 128], identity=ident[:]),
                        reads=[("hb", i), "ident"], writes=[("pT", pb)])
                dst = hT[:, kq * KQ:(kq + 1) * KQ, tt * 128:(tt + 1) * 128]
                src = pT[pb][:].rearrange("p (k t) -> p k t", k=KQ)
                if kq % 2 == 0:
                    P.op("dve", lambda e, dst=dst, src=src: e.tensor_copy(out=dst, in_=src),
                         reads=[("pT", pb)], writes=[("hT", tt, kq)])
                else:
                    P.op("act", lambda e, dst=dst, src=src: e.copy(out=dst, in_=src),
                         reads=[("pT", pb)], writes=[("hT", tt, kq)])

    def evac_store(j, n_part, nfree, is_bf16, dst_ap, okey):
        s = cnt["m"] % 4
        use_dve = (cnt["m"] % 2 == 0)
        cnt["m"] += 1
        sbuf = (stgb if is_bf16 else stg)[s]
        skey = ("stgb" if is_bf16 else "stg", s)
        if use_dve:
            P.op("dve", lambda e: e.tensor_copy(out=sbuf[0:n_part, 0:nfree], in_=pM[j][0:n_part, 0:nfree]),
                 reads=[("pM", j)], writes=[skey])
        else:
            P.op("act", lambda e: e.copy(out=sbuf[0:n_part, 0:nfree], in_=pM[j][0:n_part, 0:nfree]),
                 reads=[("pM", j)], writes=[skey])
        P.dma("sp", lambda e: e.dma_start(out=dst_ap, in_=sbuf[0:n_part, 0:nfree]), reads=[skey], writes=[okey])

    def do_group(name, tok0):
        ncols, layout, dt, _ = (ginfo or GROUP_INFO)[name]
        w = wts[name]
        dst = scr[name]
        wv = w.rearrange("(kt p) n -> p kt n", p=128)
        is_bf = (dt == BF16)
        for c0 in range(0, ncols, NB):
            nb = min(NB, ncols - c0)
            wi = cnt["w"] % 2
            cnt["w"] += 1
            wbb = wb[wi]
            for kq in range(NKQ):
                P.dma("pool", lambda e, wbb=wbb, kq=kq, c0=c0, nb=nb: e.dma_start(
                    out=wbb[:, kq * KQ:(kq + 1) * KQ, 0:nb], in_=wv[:, kq * KQ:(kq + 1) * KQ, c0:c0 + nb]),
                    writes=[("wb", wi, kq)])
            if layout == "tm":
                for tt in range(TT):
                    j = cnt["m"] % 6
                    for kt in range(KT):
                        P.op("pe", lambda e, j=j, kt=kt, tt=tt, wbb=wbb, nb=nb: e.matmul(
                            pM[j][:, 0:nb], lhsT=hT[:, kt, tt * 128:(tt + 1) * 128], rhs=wbb[:, kt, 0:nb],
                            start=(kt == 0), stop=(kt == KT - 1)),
                            reads=[("hT", tt, kt // KQ), ("wb", wi, kt // KQ)], writes=[("pM", j)])
                    r0 = tok0 + tt * 128
                    evac_store(j, 128, nb, is_bf, dst[r0:r0 + 128, c0:c0 + nb], (name, "o", tok0, tt, c0))
            else:
                for ct in range(nb // 128):
                    for th in range(T // 512):
                        j = cnt["m"] % 6
                        for kt in range(KT):
                            P.op("pe", lambda e, j=j, kt=kt, th=th, ct=ct, wbb=wbb: e.matmul(
                                pM[j][:, 0:512], lhsT=wbb[:, kt, ct * 128:(ct + 1) * 128],
                                rhs=hT[:, kt, th * 512:(th + 1) * 512], start=(kt == 0), stop=(kt == KT - 1)),
                                reads=[("hT", 4 * th, kt // KQ), ("hT", 4 * th + 1, kt // KQ), ("hT", 4 * th + 2, kt // KQ),
                                       ("hT", 4 * th + 3, kt // KQ), ("wb", wi, kt // KQ)], writes=[("pM", j)])
                        cc = c0 + ct * 128
                        t0 = tok0 + th * 512
                        evac_store(j, 128, 512, is_bf, dst[cc:cc + 128, t0:t0 + 512], (name, "o", tok0, th, cc))

    if plan is None:
        plan = [(x_all, p * 1024, [(n, p * 1024) for n in ("fmb_all", "fmf_all", "tmb_all", "tmf_all")]) for p in range(4)]
        plan.append((x_own, 0, [("fmb_own", 0), ("tmf_own", 0)]))
    for pidx, (xsrc, row0, glist) in enumerate(plan):
        load_hT(xsrc, row0, pidx)
        for name, tok0 in glist:
            do_group(name, tok0)
    ph.close()


def make_scratch(nc, kind=None):
    scr = {}
    for name, (ncols, layout, dt, which) in GROUP_INFO.items():
        ntok = SEQ if which == "all" else OWN
        shape = [ntok, ncols] if layout == "tm" else [ncols, ntok]
        if kind:
            scr[name] = nc.dram_tensor("scr_" + name, shape, dt, kind=kind).ap()
        else:
            scr[name] = nc.dram_tensor("scr_" + name, shape, dt).ap()
    return scr


TMF_OWN_ATTZ, TMF_OWN_WI, TMF_OWN_U, TMF_OWN_V, TMF_OWN_Z = 0, 1024, 1040, 2064, 3088


def phase_mlp(nc, scr, prm, y_own, nchunks=OWN // 128, ycol0=1024):
    ph = Phase(nc)
    P = ph.P
    src = scr["tmf_own"]
    W = 1024
    gb = ph.sb([128, W], F32, "lng")
    bb = ph.sb([128, W], F32, "lnb")
    wsT = ph.sb([128, 8, 128], F32, "wsT")
    wcT = ph.sb([128, 8, 128], BF16, "wcT")
    tri = ph.sb([128, 128], F32, "tri")
    bsT = ph.sb([128, 8], F32, "bsT")
    bbc = ph.sb([128, 8, 128], F32, "bbc")
    zero = ph.sb([128, 128], F32, "zero")
    NBUF = 2
    ut = [ph.sb([128, W], F32, "u") for _ in range(NBUF)]
    vt = [ph.sb([128, W], F32, "v") for _ in range(NBUF)]
    zt = [ph.sb([128, W], F32, "z") for _ in range(NBUF)]
    vn = [ph.sb([128, W], F32, "vn") for _ in range(NBUF)]
    vnb = [ph.sb([128, W], BF16, "vnb") for _ in range(NBUF)]
    junk = ph.sb([128, W], BF16, "junk")
    t1 = [ph.sb([128, W], F32, "t1") for _ in range(NBUF)]
    st = ph.sb([128, 8 * nchunks], F32, "stats")
    pV = [ph.ps([128, 512], F32, "pV") for _ in range(4)]

    P.dma("sp", lambda e: e.dma_start(out=gb[:], in_=prm["mlp_ln_g"][0:1, :].partition_broadcast(128)), writes=["gb"])
    P.dma("sp", lambda e: e.dma_start(out=bb[:], in_=prm["mlp_ln_b"][0:1, :].partition_broadcast(128)), writes=["bb"])
    P.dma("sp", lambda e: e.dma_start(out=wsT[:], in_=prm["mlp_wsT"][:, :, :]), writes=["wsT"])
    P.dma("sp", lambda e: e.dma_start(out=tri[:], in_=prm["tri_le"][:, :]), writes=["tri"])
    P.dma("sp", lambda e: e.dma_start(out=bsT[:], in_=prm["mlp_bsT"][:, :]), writes=["bsT"])
    P.op("dve", lambda e: e.memset(zero[:], 0.0), writes=["zero"])
    P.op("dve", lambda e: e.memset(st[:], 0.0), writes=[("st", c) for c in range(nchunks)])
    for g in range(8):
        P.op("dve", lambda e, g=g: e.tensor_tensor(out=wcT[:, g, :], in0=wsT[:, g, :], in1=tri[:], op=ALU.mult),
             reads=["wsT", "tri"], writes=["wcT"])
        P.op("dve", lambda e, g=g: e.tensor_scalar(out=bbc[:, g, :], in0=zero[:], scalar1=bsT[:, g:g + 1], scalar2=None,
                                                   op0=ALU.add), reads=["zero", "bsT"], writes=["bbc"])
    recs = []
    for c in range(nchunks):
        i = c % NBUF
        r0 = c * 128
        rec = []
        marks = []
        recs.append((rec, marks))
        restore = record_ops(P, rec)
        P.dma("sp", lambda e, i=i, r0=r0: e.dma_start(out=ut[i][:], in_=src[r0:r0 + 128, TMF_OWN_U:TMF_OWN_U + W]),
              writes=[("u", i)])
        P.dma("sp", lambda e, i=i, r0=r0: e.dma_start(out=vt[i][:], in_=src[r0:r0 + 128, TMF_OWN_V:TMF_OWN_V + W]),
              writes=[("v", i)])
        P.dma("sp", lambda e, i=i, r0=r0: e.dma_start(out=zt[i][:], in_=src[r0:r0 + 128, TMF_OWN_Z:TMF_OWN_Z + W]),
              writes=[("z", i)])
        s0 = c * 8
        P.op("act", lambda e, i=i, s0=s0: e.activation(out=junk[:], in_=vt[i][:], func=AF.Square,
                                                       accum_out=st[:, s0 + 1:s0 + 2]),
             reads=[("v", i)], writes=["junk", ("st", c)])
        P.op("dve", lambda e, i=i, s0=s0: e.reduce_sum(out=st[:, s0:s0 + 1], in_=vt[i][:], axis=AX.X),
             reads=[("v", i), ("st", c)], writes=[("st", c)])
        P.op("dve", lambda e, s0=s0: e.tensor_scalar(out=st[:, s0:s0 + 2], in0=st[:, s0:s0 + 2], scalar1=1.0 / W,
                                                     scalar2=None, op0=ALU.mult), reads=[("st", c)], writes=[("st", c)])
        P.op("dve", lambda e, s0=s0: e.tensor_tensor(out=st[:, s0 + 2:s0 + 3], in0=st[:, s0:s0 + 1], in1=st[:, s0:s0 + 1],
                                                     op=ALU.mult), reads=[("st", c)], writes=[("st", c)])
        P.op("dve", lambda e, s0=s0: e.tensor_tensor(out=st[:, s0 + 3:s0 + 4], in0=st[:, s0 + 1:s0 + 2],
                                                     in1=st[:, s0 + 2:s0 + 3], op=ALU.subtract),
             reads=[("st", c)], writes=[("st", c)])
        P.op("dve", lambda e, s0=s0: e.tensor_scalar(out=st[:, s0 + 3:s0 + 4], in0=st[:, s0 + 3:s0 + 4], scalar1=NORM_EPS,
                                                     scalar2=None, op0=ALU.add), reads=[("st", c)], writes=[("st", c)])
        P.op("act", lambda e, s0=s0: e.activation(out=st[:, s0 + 3:s0 + 4], in_=st[:, s0 + 3:s0 + 4], func=AF.Sqrt),
             reads=[("st", c)], writes=[("st", c)])
        P.op("dve", lambda e, s0=s0: e.reciprocal(out=st[:, s0 + 3:s0 + 4], in_=st[:, s0 + 3:s0 + 4]),
             reads=[("st", c)], writes=[("st", c)])
        P.op("dve", lambda e, i=i, s0=s0: e.tensor_scalar(out=vn[i][:], in0=vt[i][:], scalar1=st[:, s0:s0 + 1],
                                                          scalar2=st[:, s0 + 3:s0 + 4], op0=ALU.subtract, op1=ALU.mult),
             reads=[("v", i), ("st", c)], writes=[("vn", i)])
        P.op("pool", lambda e, i=i: e.tensor_tensor(out=vn[i][:], in0=vn[i][:], in1=gb[:], op=ALU.mult),
             reads=[("vn", i), "gb"], writes=[("vn", i)])
        P.op("dve", lambda e, i=i: e.tensor_tensor(out=vnb[i][:], in0=vn[i][:], in1=bb[:], op=ALU.add),
             reads=[("vn", i), "bb"], writes=[("vnb", i)])
        if c == 0:
            ph.dump("mlp_st", st[:, 0:8], [("st", c)])
            ph.dump("mlp_vn", vn[i][:], [("vn", i)])
            ph.dump("mlp_vnb", vnb[i][:], [("vnb", i)])
        marks.append(len(rec))
        for hf in range(2):
            pj = (2 * c + hf) % 4
            for g4 in range(4):
                g = hf * 4 + g4
                P.op("pe", lambda e, pj=pj, g=g, g4=g4, i=i: e.matmul(
                    pV[pj][:, g4 * 128:(g4 + 1) * 128], lhsT=wcT[:, g, :], rhs=vnb[i][:, g * 128:(g + 1) * 128],
                    start=True, stop=True), reads=["wcT", ("vnb", i)], writes=[("pV", pj)])
            P.op("dve", lambda e, pj=pj, hf=hf, i=i: e.tensor_tensor(
                out=t1[i][:, hf * 512:(hf + 1) * 512], in0=pV[pj][:],
                in1=bbc[:, hf * 4:(hf + 1) * 4, :].rearrange("p g d -> p (g d)"), op=ALU.add),
                reads=[("pV", pj), "bbc"], writes=[("t1", i, hf)])
        if c == 0:
            ph.dump("mlp_t1", t1[i][:], [("t1", i, 0), ("t1", i, 1)])
        P.op("pool", lambda e, i=i: e.tensor_tensor(out=t1[i][:], in0=t1[i][:], in1=ut[i][:], op=ALU.mult),
             reads=[("t1", i, 0), ("t1", i, 1), ("u", i)], writes=[("t1", i, 0), ("t1", i, 1)])
        P.op("act", lambda e, i=i: e.activation(out=zt[i][:], in_=zt[i][:], func=AF.Silu),
             reads=[("z", i)], writes=[("z", i)])
        P.op("dve", lambda e, i=i: e.tensor_tensor(out=t1[i][:], in0=t1[i][:], in1=zt[i][:], op=ALU.mult),
             reads=[("t1", i, 0), ("t1", i, 1), ("z", i)], writes=[("t1", i, 0), ("t1", i, 1)])
        P.dma("sp", lambda e, i=i, r0=r0: e.dma_start(out=y_own[r0:r0 + 128, ycol0:ycol0 + 1024], in_=t1[i][:]),
              reads=[("t1", i, 0), ("t1", i, 1)], writes=[("y_mlp", c)])
        restore()
    pipeline_merge(recs, 2)
    ph.close()


def record_ops(P, rec):
    real_op, real_dma = P.op, P.dma
    P.op = lambda eng, fn, reads=(), writes=(): rec.append((real_op, eng, fn, list(reads), list(writes)))
    P.dma = lambda eng, fn, reads=(), writes=(): rec.append((real_dma, eng, fn, list(reads), list(writes)))

    def restore():
        P.op, P.dma = real_op, real_dma
    return restore


def pipeline_merge(recs, nstage):
    allp = []
    for ops, marks in recs:
        m = [0] + list(marks[:nstage - 1])
        while len(m) < nstage:
            m.append(len(ops))
        m.append(len(ops))
        allp.append([ops[m[k]:m[k + 1]] for k in range(nstage)])
    n = len(recs)
    for t in range(n + nstage - 1):
        lists = []
        for k in range(nstage):
            bi = t - k
            if 0 <= bi < n and allp[bi][k]:
                lists.append(allp[bi][k])
        merged = []
        for li, L in enumerate(lists):
            for pi, item in enumerate(L):
                merged.append(((pi + 0.5) / len(L), li, pi, item))
        merged.sort(key=lambda t_: (t_[0], t_[1]))
        for _, _, _, (f, eng, fn, reads, writes) in merged:
            f(eng, fn, reads=reads, writes=writes)


FMF_X, FMF_B, FMF_C, FMF_WL, FMF_AL = 0, 256, 384, 512, 576
TMF_SSMZ, TMF_DT, TMF_R, TMF_K, TMF_V, TMF_Z = 0, 256, 260, 516, 772, 1028
NEG_BIG = -30000.0


def phase_ssm(nc, scr, prm, y_all, y_dst=None):
    ph = Phase(nc)
    P = ph.P
    fm = scr["fmf_all"]
    tm = scr["tmf_all"]
    if y_dst is None:
        y_dst = y_all[:, 0:256]
    NCH = SEQ // 128
    SC = 512
    identf = ph.sb([128, 128], F32, "identf")
    identb = ph.sb([128, 128], BF16, "identb")
    tri = ph.sb([128, 128], F32, "tri")
    onesf = ph.sb([128, 128], F32, "ones")
    sel4 = ph.sb([4, 4, 128], F32, "sel4")
    negb = ph.sb([128, 128], F32, "negb")
    cw = ph.sb([128, 4, 4], F32, "cw")
    cb = ph.sb([128, 4], F32, "cb")
    dtb = ph.sb([128, 128], F32, "dtb")
    alog = ph.sb([128, 128], F32, "alog")
    Dbc = ph.sb([128, 4], F32, "Dbc")
    ngb = ph.sb([128, 256], F32, "ngb")
    dt = ph.sb([128, NCH, 4], F32, "dt")
    aa = ph.sb([128, NCH, 4], F32, "aa")
    cum = ph.sb([128, NCH * 4], F32, "cum")
    cumL = ph.sb([128, NCH * 4], F32, "cumL")
    ecum = ph.sb([128, NCH * 4], F32, "ecum")
    ncum = ph.sb([128, NCH * 4], F32, "ncum")
    ecumL = ph.sb([128, NCH * 4], F32, "ecumL")
    dtd = ph.sb([128, NCH * 4], F32, "dtd")
    win = [ph.sb([128, SC + 3], F32, "win") for _ in range(4)]
    acc = [ph.sb([128, SC], F32, "acc") for _ in range(2)]
    xsT = [ph.sb([128, 2, SC], F32, "xsT") for _ in range(2)]
    BT = [ph.sb([128, SC], BF16, "BT") for _ in range(2)]
    CT = [ph.sb([128, SC], BF16, "CT") for _ in range(2)]
    xtok = [ph.sb([128, 256], F32, "xtok") for _ in range(2)]
    xdt = [ph.sb([128, 256], BF16, "xdt") for _ in range(2)]
    xdd = [ph.sb([128, 256], BF16, "xdd") for _ in range(2)]
    Btok = [ph.sb([128, 128], BF16, "Btok") for _ in range(2)]
    cumT = [ph.sb([4, 128], F32, "cumT") for _ in range(2)]
    LT = [ph.sb([128, 4, 128], F32, "LT") for _ in range(2)]
    MT = [ph.sb([128, 4, 128], BF16, "MT") for _ in range(2)]
    ydsb = [ph.sb([128, 256], F32, "ydsb") for _ in range(2)]
    dsk = [ph.sb([128, 256], F32, "dsk") for _ in range(2)]
    yt = [ph.sb([128, 256], F32, "yt") for _ in range(2)]
    zt = [ph.sb([128, 256], F32, "zt") for _ in range(2)]
    junk = ph.sb([128, 256], F32, "junk")
    nst = ph.sb([128, NCH], F32, "nst")
    hT = ph.sb([128, 256], F32, "hT")
    hTb = ph.sb([128, 256], BF16, "hTb")
    pTr = [ph.ps([128, 512], F32, "pTr") for _ in range(1)]
    pTb = [ph.ps([128, 1024], BF16, "pTb") for _ in range(1)]
    pSm = [ph.ps([128, 512], F32, "pSm") for _ in range(1)]
    pL = [ph.ps([128, 512], F32, "pL") for _ in range(1)]
    pCB = [ph.ps([128, 512], F32, "pCB") for _ in range(1)]
    pYd = [ph.ps([128, 512], F32, "pYd") for _ in range(1)]
    pYo = [ph.ps([128, 512], F32, "pYo") for _ in range(1)]
    pS = [ph.ps([128, 512], F32, "pS") for _ in range(1)]

    ld = lambda dst, src, key: P.dma("sp", lambda e: e.dma_start(out=dst, in_=src), writes=[key])
    ld(identf[:], prm["ident_f"][:, :], "identf")
    ld(identb[:], prm["ident_b"][:, :], "identb")
    ld(tri[:], prm["tri_le"][:, :], "tri")
    ld(onesf[:], prm["ones_f"][:, :], "ones")
    ld(sel4[:], prm["sel4"][:, :, :], "sel4")
    ld(negb[:], prm["negbig_lt"][:, :], "negb")
    ld(cw[:], prm["ssm_cw"][:, :, :], "cw")
    ld(cb[:], prm["ssm_cb"][:, :], "cb")
    ld(dtb[:], prm["ssm_dtb_t"][0:1, :].partition_broadcast(128), "dtb")
    ld(alog[:], prm["ssm_alog_t"][0:1, :].partition_broadcast(128), "alog")
    ld(Dbc[:], prm["ssm_D"][0:1, :].partition_broadcast(128), "Dbc")
    ld(ngb[:], prm["ssm_ng"][0:1, :].partition_broadcast(128), "ngb")
    ld(dt[:], tm[:, TMF_DT:TMF_DT + 4].rearrange("(c l) h -> l c h", l=128), "dt")
    dtf = dt[:].rearrange("p c h -> p (c h)")
    aaf = aa[:].rearrange("p c h -> p (c h)")
    P.op("dve", lambda e: e.tensor_tensor(out=dtf, in0=dtf, in1=dtb[:], op=ALU.add), reads=["dt", "dtb"], writes=["dt"])
    P.op("act", lambda e: e.activation(out=dtf, in_=dtf, func=AF.Exp), reads=["dt"], writes=["dt"])
    P.op("act", lambda e: e.activation(out=dtf, in_=dtf, func=AF.Ln, bias=1.0, scale=1.0), reads=["dt"], writes=["dt"])
    P.op("act", lambda e: e.activation(out=alog[:], in_=alog[:], func=AF.Exp), reads=["alog"], writes=["alog"])
    P.op("dve", lambda e: e.scalar_tensor_tensor(out=aaf, in0=dtf, scalar=-1.0, in1=alog[:], op0=ALU.mult, op1=ALU.mult),
         reads=["dt", "alog"], writes=["aa"])
    P.op("pe", lambda e: e.matmul(pSm[0][:, 0:128], lhsT=tri[:], rhs=aaf, start=True, stop=True),
         reads=["tri", "aa"], writes=["pSm"])
    P.op("dve", lambda e: e.tensor_copy(out=cum[:], in_=pSm[0][:, 0:128]), reads=["pSm"], writes=["cum"])
    P.op("pe", lambda e: e.matmul(pSm[0][:, 128:256], lhsT=onesf[:], rhs=aaf, start=True, stop=True),
         reads=["ones", "aa", "cum"], writes=["pSm"])
    P.op("dve", lambda e: e.tensor_copy(out=cumL[:], in_=pSm[0][:, 128:256]), reads=["pSm"], writes=["cumL"])
    P.op("act", lambda e: e.activation(out=ecum[:], in_=cum[:], func=AF.Exp), reads=["cum"], writes=["ecum"])
    P.op("pool", lambda e: e.tensor_scalar(out=ncum[:], in0=cum[:], scalar1=-1.0, scalar2=None, op0=ALU.mult), reads=["cum"], writes=["ncum"])
    P.op("act", lambda e: e.activation(out=ecumL[:], in_=cumL[:], func=AF.Exp), reads=["cumL"], writes=["ecumL"])
    P.op("dve", lambda e: e.tensor_tensor(out=dtd[:], in0=cumL[:], in1=cum[:], op=ALU.subtract),
         reads=["cumL", "cum"], writes=["dtd"])
    P.op("act", lambda e: e.activation(out=dtd[:], in_=dtd[:], func=AF.Exp), reads=["dtd"], writes=["dtd"])
    P.op("dve", lambda e: e.tensor_tensor(out=dtd[:], in0=dtd[:], in1=dtf, op=ALU.mult), reads=["dtd", "dt"], writes=["dtd"])
    P.op("dve", lambda e: e.memset(hT[:], 0.0), writes=["hT"])
    P.op("dve", lambda e: e.memset(hTb[:], 0.0), writes=["hTb"])
    P.op("dve", lambda e: e.memset(nst[:], 0.0), writes=["nst"])

    recs = []
    rec_cur = [None]

    def new_unit():
        rec = []
        recs.append([rec, []])
        rec_cur[0] = rec
        return record_ops(P, rec)

    for s in range(SEQ // SC):
        t0 = s * SC
        si = s % 2
        restore = new_unit()
        for j in range(4):
            wj = win[j]
            if s == 0:
                P.op("dve", lambda e, wj=wj: e.memset(wj[:, 0:3], 0.0), writes=[("win", j)])
                P.dma("sp", lambda e, wj=wj, j=j: e.dma_start(out=wj[:, 3:SC + 3], in_=fm[j * 128:(j + 1) * 128, 0:SC]),
                      writes=[("win", j)])
            else:
                P.dma("sp", lambda e, wj=wj, j=j, t0=t0: e.dma_start(out=wj[:], in_=fm[j * 128:(j + 1) * 128, t0 - 3:t0 + SC]),
                      writes=[("win", j)])
            ac = acc[j % 2]
            ak = ("acc", j % 2)
            P.op("dve", lambda e, ac=ac, wj=wj, j=j: e.tensor_scalar(out=ac[:], in0=wj[:, 0:SC], scalar1=cw[:, j, 0:1],
                                                                    scalar2=None, op0=ALU.mult),
                 reads=[("win", j), "cw"], writes=[ak])
            for tap in range(1, 4):
                P.op("dve", lambda e, ac=ac, wj=wj, j=j, tap=tap: e.scalar_tensor_tensor(
                    out=ac[:], in0=wj[:, tap:tap + SC], scalar=cw[:, j, tap:tap + 1], in1=ac[:], op0=ALU.mult, op1=ALU.add),
                    reads=[("win", j), "cw", ak], writes=[ak])
            if j < 2:
                dst, dk = xsT[si][:, j, :], ("xsT", si, j)
            elif j == 2:
                dst, dk = BT[si][:], ("BT", si)
            else:
                dst, dk = CT[si][:], ("CT", si)
            P.op("act", lambda e, ac=ac, dst=dst, j=j: e.activation(out=dst, in_=ac[:], func=AF.Silu, bias=cb[:, j:j + 1], scale=1.0),
                 reads=[ak, "cb"], writes=[dk])
        if s < 2:
            ph.dump(f"ssm_xsT{s}", xsT[si][:], [("xsT", si, 0), ("xsT", si, 1)])
            ph.dump(f"ssm_BT{s}", BT[si][:], [("BT", si)])
            ph.dump(f"ssm_CT{s}", CT[si][:], [("CT", si)])
        for cc in range(SC // 128):
            c = s * (SC // 128) + cc
            ci = c % 2
            lo = cc * 128
            c4 = c * 4
            if cc > 0:
                restore = new_unit()
            for j in range(2):
                P.op("pe", lambda e, si=si, j=j, lo=lo: e.transpose(out=pTr[0][:, j * 128:(j + 1) * 128], in_=xsT[si][:, j, lo:lo + 128],
                                                            identity=identf[:]),
                     reads=[("xsT", si, j), "identf"], writes=["pTr"])
            P.op("act", lambda e, ci=ci: e.copy(out=xtok[ci][:], in_=pTr[0][:, 0:256]), reads=["pTr"], writes=[("xtok", ci)])
            P.op("pe", lambda e, si=si, lo=lo: e.transpose(out=pTb[0][:, 0:128], in_=BT[si][:, lo:lo + 128], identity=identb[:]),
                 reads=[("BT", si), "identb"], writes=["pTb"])
            P.op("act", lambda e, ci=ci: e.copy(out=Btok[ci][:], in_=pTb[0][:, 0:128]), reads=["pTb"], writes=[("Btok", ci)])
            P.op("pe", lambda e, c=c: e.matmul(pSm[0][0:4, 256:384], lhsT=aa[:, c, :], rhs=tri[:], start=True, stop=True),
                 reads=["aa", "tri"], writes=["pSm"])
            P.op("dve", lambda e, ci=ci: e.tensor_copy(out=cumT[ci][:], in_=pSm[0][0:4, 256:384]),
                 reads=["pSm"], writes=[("cumT", ci)])
            for h in range(4):
                P.op("pe", lambda e, h=h, ci=ci: e.matmul(pL[0][:, h * 128:(h + 1) * 128], lhsT=sel4[:, h, :], rhs=cumT[ci][:],
                                                          start=True, stop=False),
                     reads=["sel4", ("cumT", ci)], writes=["pL"])
                P.op("pe", lambda e, h=h: e.matmul(pL[0][:, h * 128:(h + 1) * 128], lhsT=identf[:], rhs=negb[:],
                                                   start=False, stop=True),
                     reads=["identf", "negb"], writes=["pL"])
            for h in range(4):
                P.op("act", lambda e, h=h, ci=ci, c4=c4: e.activation(
                    out=LT[ci][:, h, :], in_=pL[0][:, h * 128:(h + 1) * 128], func=AF.Exp, bias=ncum[:, c4 + h:c4 + h + 1], scale=1.0),
                    reads=["pL", "ncum"], writes=[("LT", ci)])
            P.op("pe", lambda e, si=si, lo=lo: e.matmul(pCB[0][:, 0:128], lhsT=BT[si][:, lo:lo + 128], rhs=CT[si][:, lo:lo + 128],
                                                 start=True, stop=True), reads=[("BT", si), ("CT", si)], writes=["pCB"])
            P.op("dve", lambda e, ci=ci: e.tensor_tensor(out=MT[ci][:], in0=pCB[0][:, 0:128].unsqueeze(1).to_broadcast([128, 4, 128]),
                                                         in1=LT[ci][:], op=ALU.mult), reads=["pCB", ("LT", ci)], writes=[("MT", ci)])
            h4 = lambda ap: ap.rearrange("p (h d) -> p h d", h=4)
            P.op("dve", lambda e, ci=ci, c4=c4: e.tensor_tensor(out=h4(xdt[ci][:]), in0=h4(xtok[ci][:]),
                                                                in1=dtf[:, c4:c4 + 4].unsqueeze(2).to_broadcast([128, 4, 64]), op=ALU.mult),
                 reads=[("xtok", ci), "dt"], writes=[("xdt", ci)])
            P.op("pool", lambda e, ci=ci, c4=c4: e.tensor_tensor(out=h4(xdd[ci][:]), in0=h4(xtok[ci][:]),
                                                                 in1=dtd[:, c4:c4 + 4].unsqueeze(2).to_broadcast([128, 4, 64]), op=ALU.mult),
                 reads=[("xtok", ci), "dtd"], writes=[("xdd", ci)])
            for h in range(4):
                P.op("pe", lambda e, h=h, ci=ci: e.matmul(pYd[0][:, h * 64:(h + 1) * 64], lhsT=MT[ci][:, h, :],
                                                          rhs=xdt[ci][:, h * 64:(h + 1) * 64], start=True, stop=True),
                     reads=[("MT", ci), ("xdt", ci)], writes=["pYd"])
            recs[-1][1].append(len(rec_cur[0]))
            for h in range(4):
                P.op("pe", lambda e, si=si, h=h, lo=lo: e.matmul(pYo[0][:, h * 64:(h + 1) * 64], lhsT=CT[si][:, lo:lo + 128],
                                                          rhs=hTb[:, h * 64:(h + 1) * 64], start=True, stop=True),
                     reads=[("CT", si), "hTb"], writes=["pYo"])
            P.op("act", lambda e, ci=ci: e.copy(out=ydsb[ci][:], in_=pYd[0][:, 0:256]), reads=["pYd"], writes=[("ydsb", ci)])
            P.dma("sp", lambda e, ci=ci, c=c: e.dma_start(out=zt[ci][:], in_=tm[c * 128:(c + 1) * 128, TMF_SSMZ:TMF_SSMZ + 256]),
                  writes=[("zt", ci)])
            P.op("dve", lambda e, ci=ci, c4=c4: e.tensor_tensor(out=h4(yt[ci][:]), in0=pYo[0][:, 0:256].rearrange("p (h d) -> p h d", h=4),
                                                                in1=ecum[:, c4:c4 + 4].unsqueeze(2).to_broadcast([128, 4, 64]), op=ALU.mult),
                 reads=["pYo", "ecum"], writes=[("yt", ci)])
            P.op("pool", lambda e, ci=ci: e.tensor_tensor(out=h4(dsk[ci][:]), in0=h4(xtok[ci][:]),
                                                          in1=Dbc[:, 0:4].unsqueeze(2).to_broadcast([128, 4, 64]), op=ALU.mult),
                 reads=[("xtok", ci), "Dbc"], writes=[("dsk", ci)])
            P.op("dve", lambda e, ci=ci: e.tensor_tensor(out=yt[ci][:], in0=yt[ci][:], in1=ydsb[ci][:], op=ALU.add),
                 reads=[("yt", ci), ("ydsb", ci)], writes=[("yt", ci)])
            P.op("dve", lambda e, ci=ci: e.tensor_tensor(out=yt[ci][:], in0=yt[ci][:], in1=dsk[ci][:], op=ALU.add),
                 reads=[("yt", ci), ("dsk", ci)], writes=[("yt", ci)])
            if c in (0, 4):
                ph.dump(f"ssm_xtok{c}", xtok[ci][:], [("xtok", ci)])
                ph.dump(f"ssm_LT{c}", LT[ci][:], [("LT", ci)])
                ph.dump(f"ssm_MT{c}", MT[ci][:], [("MT", ci)])
                ph.dump(f"ssm_yt{c}", yt[ci][:], [("yt", ci)])
            for h in range(4):
                P.op("pe", lambda e, h=h, ci=ci: e.matmul(pS[0][:, h * 64:(h + 1) * 64], lhsT=Btok[ci][:],
                                                          rhs=xdd[ci][:, h * 64:(h + 1) * 64], start=True, stop=True),
                     reads=[("Btok", ci), ("xdd", ci)], writes=["pS"])
            P.op("pool", lambda e, c4=c4: e.tensor_tensor(out=h4(hT[:]), in0=h4(hT[:]),
                                                          in1=ecumL[:, c4:c4 + 4].unsqueeze(2).to_broadcast([128, 4, 64]), op=ALU.mult),
                 reads=["hT", "ecumL"], writes=["hT"])
            P.op("dve", lambda e: e.tensor_tensor(out=hT[:], in0=hT[:], in1=pS[0][:, 0:256], op=ALU.add), reads=["hT", "pS"], writes=["hT"])
            P.op("act", lambda e: e.copy(out=hTb[:], in_=hT[:]), reads=["hT"], writes=["hTb"])
            recs[-1][1].append(len(rec_cur[0]))
            P.op("act", lambda e, ci=ci: e.activation(out=zt[ci][:], in_=zt[ci][:], func=AF.Silu),
                 reads=[("zt", ci)], writes=[("zt", ci)])
            P.op("pool", lambda e, ci=ci: e.tensor_tensor(out=yt[ci][:], in0=yt[ci][:], in1=zt[ci][:], op=ALU.mult),
                 reads=[("yt", ci), ("zt", ci)], writes=[("yt", ci)])
            P.op("act", lambda e, ci=ci, c=c: e.activation(out=junk[:], in_=yt[ci][:], func=AF.Square, accum_out=nst[:, c:c + 1]),
                 reads=[("yt", ci), "nst"], writes=["junk", ("nst", c)])
            P.op("dve", lambda e, c=c: e.tensor_scalar(out=nst[:, c:c + 1], in0=nst[:, c:c + 1], scalar1=1.0 / 256, scalar2=NORM_EPS,
                                                       op0=ALU.mult, op1=ALU.add), reads=[("nst", c)], writes=[("nst", c)])
            P.op("act", lambda e, c=c: e.activation(out=nst[:, c:c + 1], in_=nst[:, c:c + 1], func=AF.Sqrt),
                 reads=[("nst", c)], writes=[("nst", c)])
            P.op("dve", lambda e, c=c: e.reciprocal(out=nst[:, c:c + 1], in_=nst[:, c:c + 1]), reads=[("nst", c)], writes=[("nst", c)])
            P.op("dve", lambda e, ci=ci, c=c: e.scalar_tensor_tensor(out=yt[ci][:], in0=yt[ci][:], scalar=nst[:, c:c + 1], in1=ngb[:],
                                                                     op0=ALU.mult, op1=ALU.mult),
                 reads=[("yt", ci), ("nst", c), "ngb"], writes=[("yt", ci)])
            P.dma("sp", lambda e, ci=ci, c=c: e.dma_start(out=y_dst[c * 128:(c + 1) * 128, :], in_=yt[ci][:]),
                  reads=[("yt", ci)], writes=[("y_ssm", c)])
            restore()
    pipeline_merge([(r, m) for r, m in recs], 3)
    ph.close()


NPAIR = 144
TOPK = 256
NBIS = 17
FILLER = False


def pair_off(i):
    return 2 * i * (i + 1)


def default_blocks():
    return [(i + 1, 0, 4) for i in range(8)]


def pair_offsets(blocks):
    offs, o = [], 0
    for nch, _, lk in blocks:
        offs.append(o)
        o += 4 * (nch - 1) + lk
    return offs, o


def phase_indexer(nc, scr, prm, maskT_d, blocks=None):
    blocks = blocks or default_blocks()
    nblk = len(blocks)
    assert nblk % 2 == 0
    NT = nblk * 128
    ncb = max(cb for _, cb, _ in blocks) + 1
    poffs, _ = pair_offsets(blocks)
    ph = Phase(nc)
    P = ph.P
    fmo = scr["fmb_own"]
    fma = scr["fmb_all"]
    tmo = scr["tmf_own"]
    NB4 = 4
    identb = ph.sb([128, 128], BF16, "identb")
    kiT = ph.sb([128, SEQ], BF16, "kiT")
    qiT = [ph.sb([128, 8, 128], BF16, "qiT") for _ in range(NB4)]
    wi = ph.sb([128, nblk, 16], F32, "wi")
    iota = ph.sb([128, 512], F32, "iota")
    qrel = ph.sb([128, ncb], F32, "qrel")
    cbias = ph.sb([128, ncb, 512], F32, "cbias")
    pow2 = ph.sb([128, NBIS], F32, "pow2")
    wdiag = [ph.sb([128, 16, 128], BF16, "wdiag") for _ in range(NB4)]
    R = [ph.sb([128, 512], BF16, "R") for _ in range(4)]
    sc = [ph.sb([128, SEQ], F32, "sc") for _ in range(NB4)]
    junk = [ph.sb([128, SEQ], BF16, "junk") for _ in range(2)]
    mk = [ph.sb([128, SEQ], BF16, "mk") for _ in range(2)]
    mT = [ph.sb([128, 8, 128], BF16, "mT") for _ in range(3)]
    mx = [ph.sb([128, 8], F32, "mx") for _ in range(NB4)]
    bs = [ph.sb([128, 8], F32, "bs") for _ in range(NB4)]
    wk = [ph.sb([128, NBIS], F32, "wk") for _ in range(NB4)]
    cnt = [ph.sb([128, NBIS], F32, "cnt") for _ in range(NB4)]
    pD = [ph.ps([128, 512], F32, "pD") for _ in range(4)]
    pSc = [ph.ps([128, 512], F32, "pSc") for _ in range(2)]
    pT = [ph.ps([128, 1024], BF16, "pT") for _ in range(1)]
    pJ = ph.ps([128, 512], F32, "pJ")

    ld = lambda dst, src, key: P.dma("sp", lambda e: e.dma_start(out=dst, in_=src), writes=[key])
    ld(identb[:], prm["ident_b"][:, :], "identb")
    ld(kiT[:], fma[1024:1152, :], "kiT")
    ld(wi[:], tmo[0:NT, TMF_OWN_WI:TMF_OWN_WI + 16].rearrange("(i p) h -> p i h", p=128), "wi")
    ld(iota[:], prm["iota512"][0:1, :].partition_broadcast(128), "iota")
    ld(qrel[:], prm["qrel"][:, :], "qrel")
    ld(pow2[:], prm["pow2"][0:1, :].partition_broadcast(128), "pow2")
    P.op("dve", lambda e: e.tensor_scalar(out=wi[:], in0=wi[:], scalar1=0.03125, scalar2=None, op0=ALU.mult), reads=["wi"], writes=["wi"])
    for cb in range(ncb):
        P.op("dve", lambda e, o=cbias[:, cb, :], s=qrel[:, cb:cb + 1]: e.tensor_scalar(out=o, in0=iota[:], scalar1=s, scalar2=-1e30,
                                                                                  op0=ALU.is_gt, op1=ALU.mult),
             reads=["iota", "qrel"], writes=["cbias"])
    k = {"d": 0, "r": 0, "s": 0, "t": 0, "m": 0}
    qv = fmo[1024:2048, :].rearrange("(p r) t -> r p t", r=128)

    def nkeys(i):
        nch_, _, lk_ = blocks[i]
        return 512 * (nch_ - 1) + 128 * lk_

    def scores(i):
        nch, cbi, lk = blocks[i]
        b4 = i % NB4
        scb, mxb, wdb, qb = sc[b4], mx[b4], wdiag[b4], qiT[b4]
        ld(qb[:], qv[:, :, i * 128:(i + 1) * 128], ("qiT", b4))
        P.op("pool", lambda e, o=wdb[:], w_=wi[:, i, :].unsqueeze(2).to_broadcast([128, 16, 128]),
             d_=identb[:].unsqueeze(1).to_broadcast([128, 16, 128]): e.tensor_tensor(out=o, in0=d_, in1=w_, op=ALU.mult),
             reads=["identb", "wi"], writes=[("wdiag", b4)])
        P.op("dve", lambda e, o=mxb[:]: e.memset(o, 0.0), writes=[("mx", b4)])
        P.op("dve", lambda e, o=cnt[b4][:]: e.memset(o, 0.0), writes=[("cnt", b4)])
        for ch in range(nch):
            js = k["s"] % 2
            k["s"] += 1
            jds = {}
            nl = 512 if ch < nch - 1 else 128 * lk

            def dots(h):
                jd = k["d"] % 4
                k["d"] += 1
                jds[h] = jd
                r0 = (h % 2) * 64
                P.op("pe", lambda e, o=pD[jd][:, 0:nl], l=qb[r0:r0 + 64, h // 2, :],
                     r=kiT[r0:r0 + 64, ch * 512:ch * 512 + nl]: e.matmul(o, lhsT=l, rhs=r, start=True, stop=True),
                     reads=[("qiT", b4), "kiT"], writes=[("pD", jd)])

            for h0 in range(4):
                dots(h0)
            for h in range(16):
                if h % 2 == 0 and h >= 2 and h + 3 < 16:
                    dots(h + 2)
                    dots(h + 3)
                jd = jds[h]
                jr = k["r"] % 4
                k["r"] += 1
                P.op("act", lambda e, o=R[jr][:, 0:nl], s=pD[jd][:, 0:nl]: e.activation(out=o, in_=s, func=AF.Relu),
                     reads=[("pD", jd)], writes=[("R", jr)])
                P.op("pe", lambda e, o=pSc[js][:, 0:nl], l=wdb[:, h, :], r=R[jr][:, 0:nl], h=h: e.matmul(o, lhsT=l, rhs=r, start=(h == 0),
                                                                                           stop=(h == 15)),
                     reads=[("wdiag", b4), ("R", jr)], writes=[("pSc", js)])
                if FILLER:
                    P.op("pe", lambda e, l=identb[:], r=kiT[:, ch * 512:(ch + 1) * 512]: e.matmul(pJ[:], lhsT=l, rhs=r, start=True, stop=True),
                         reads=["identb", "kiT"], writes=["pJ"])
            P.op("dve", lambda e, o=mxb[:, ch:ch + 1], s=pSc[js][:, 0:nl]: e.tensor_reduce(out=o, in_=s, axis=AX.X, op=ALU.max,
                                                                                   apply_absolute_value=True),
                 reads=[("pSc", js)], writes=[("mx", b4)])
            if ch == nch - 1:
                P.op("dve", lambda e, o=scb[:, ch * 512:ch * 512 + nl], s=pSc[js][:, 0:nl], c_=cbias[:, cbi, 0:nl]: e.tensor_tensor(
                    out=o, in0=s, in1=c_, op=ALU.add), reads=[("pSc", js), "cbias"], writes=[("sc", b4)])
            else:
                P.op("dve", lambda e, o=scb[:, ch * 512:(ch + 1) * 512], s=pSc[js][:]: e.tensor_copy(out=o, in_=s),
                     reads=[("pSc", js)], writes=[("sc", b4)])
            yield

    def bis_init(i):
        b4 = i % NB4
        bsb, wkb, mxb = bs[b4], wk[b4], mx[b4]
        bk = ("bs", b4)
        P.op("dve", lambda e, o=bsb[:, 0:1], s=mxb[:]: e.tensor_reduce(out=o, in_=s, axis=AX.X, op=ALU.max),
             reads=[("mx", b4)], writes=[bk])
        P.op("dve", lambda e, o=bsb[:, 0:1]: e.tensor_scalar(out=o, in0=o, scalar1=1.001, scalar2=1e-6, op0=ALU.mult, op1=ALU.add),
             reads=[bk], writes=[bk])
        P.op("dve", lambda e, o=wkb[:], s=bsb[:, 0:1]: e.tensor_scalar(out=o, in0=pow2[:], scalar1=s, scalar2=2.0, op0=ALU.mult,
                                                                      op1=ALU.mult), reads=[bk, "pow2"], writes=[("wk", b4)])
        P.op("dve", lambda e, o=bsb[:, 1:2]: e.memset(o, 0.0), reads=[bk], writes=[bk])

    def bis_count(i, it, jj):
        b4 = i % NB4
        n = nkeys(i)
        P.op("dve", lambda e, o=junk[jj][:, 0:n], s=sc[b4][:, 0:n], m=bs[b4][:, 1:2], a=cnt[b4][:, it:it + 1]: e.tensor_scalar(
            out=o, in0=s, scalar1=m, scalar2=0.0, op0=ALU.is_ge, op1=ALU.add, accum_out=a),
            reads=[("sc", b4), ("bs", b4), ("cnt", b4)], writes=[("junk", jj), ("cnt", b4)])

    def bis_delta(i, it):
        b4 = i % NB4
        P.op("dve", lambda e, o=bs[b4][:, 2:3], c_=cnt[b4][:, it:it + 1], w_=wk[b4][:, it:it + 1]: e.tensor_scalar(
            out=o, in0=c_, scalar1=TOPK - 0.5, scalar2=w_, op0=ALU.is_ge, op1=ALU.mult),
            reads=[("cnt", b4), ("wk", b4), ("bs", b4)], writes=[("bs", b4)])

    def bis_mid(i, it):
        b4 = i % NB4
        nx = min(it + 1, NBIS - 1)
        P.op("dve", lambda e, o=bs[b4][:, 1:2], d_=bs[b4][:, 2:3], w_=wk[b4][:, nx:nx + 1]: e.scalar_tensor_tensor(
            out=o, in0=d_, scalar=w_, in1=o, op0=ALU.subtract, op1=ALU.add), reads=[("bs", b4), ("wk", b4)], writes=[("bs", b4)])

    def finish_block(i, jj):
        nch, _, lk = blocks[i]
        b4 = i % NB4
        n = nkeys(i)
        mkb = mk[jj]
        P.op("dve", lambda e, o=mkb[:, 0:n], s=sc[b4][:, 0:n], t=bs[b4][:, 1:2]: e.tensor_scalar(
            out=o, in0=s, scalar1=t, scalar2=-1.0, op0=ALU.is_ge, op1=ALU.add), reads=[("sc", b4), ("bs", b4)], writes=[("mk", jj)])
        nkb = 4 * (nch - 1) + lk
        for g0 in range(0, nkb, 8):
            ng = min(8, nkb - g0)
            jt = 0
            jm = k["m"] % 3
            k["m"] += 1
            for kb in range(g0, g0 + ng):
                P.op("pe", lambda e, o=pT[jt][:, (kb - g0) * 128:(kb - g0 + 1) * 128], s=mkb[:, kb * 128:(kb + 1) * 128]: e.transpose(
                    out=o, in_=s, identity=identb[:]), reads=[("mk", jj), "identb"], writes=[("pT", jt)])
            P.op("act", lambda e, o=mT[jm][:, 0:ng, :], s=pT[jt][:, 0:ng * 128].rearrange("p (k t) -> p k t", k=ng): e.copy(out=o, in_=s),
                 reads=[("pT", jt)], writes=[("mT", jm)])
            po = poffs[i] + g0
            P.dma("sp", lambda e, o=maskT_d[:, po:po + ng, :], s=mT[jm][:, 0:ng, :]: e.dma_start(out=o, in_=s),
                  reads=[("mT", jm)], writes=[("maskT", i, g0)])

    import itertools
    for _ in itertools.chain(scores(0), scores(1)):
        pass
    for pr in range(nblk // 2):
        ia, ib = 2 * pr, 2 * pr + 1
        pending = iter(())
        if 2 * pr + 2 < nblk:
            pending = itertools.chain(scores(2 * pr + 2), scores(2 * pr + 3))
        bis_init(ia)
        bis_init(ib)
        for it in range(NBIS):
            bis_count(ia, it, 0)
            bis_count(ib, it, 1)
            bis_delta(ia, it)
            bis_delta(ib, it)
            bis_mid(ia, it)
            bis_mid(ib, it)
            next(pending, None)
        for _ in pending:
            pass
        finish_block(ia, 0)
        finish_block(ib, 1)
    ph.close()


def phase_attn(nc, scr, prm, maskT_d, y_own, blocks=None):
    blocks = blocks or default_blocks()
    nblk = len(blocks)
    poffs, _ = pair_offsets(blocks)
    ph = Phase(nc)
    P = ph.P
    fmo = scr["fmb_own"]
    fma = scr["fmb_all"]
    tmb = scr["tmb_all"]
    tmo = scr["tmf_own"]
    att_scale = 128 ** -0.5
    i30k = ph.sb([128, 128], BF16, "i30k")
    kT = ph.sb([128, 8, SEQ], BF16, "kT")
    V = ph.sb([128, 32, 8, 129], BF16, "V")
    qT = [ph.sb([128, 8, 128], BF16, "qT") for _ in range(2)]
    mT = [ph.sb([128, 32, 128], BF16, "mT") for _ in range(2)]
    PT = [ph.sb([128, 512], BF16, "PT") for _ in range(4)]
    ot = [ph.sb([128, 1024], F32, "ot") for _ in range(2)]
    zt = [ph.sb([128, 1024], F32, "zt") for _ in range(2)]
    rs = ph.sb([128, 8 * nblk], F32, "rs")
    pS = [ph.ps([128, 512], F32, "pS") for _ in range(4)]
    pO = [ph.ps([128, 512], F32, "pO") for _ in range(2)]

    ld = lambda dst, src, key: P.dma("sp", lambda e: e.dma_start(out=dst, in_=src), writes=[key])
    ld(i30k[:], prm["ident30k_b"][:, :], "i30k")
    for h in range(8):
        ld(kT[:, h, :], fma[h * 128:(h + 1) * 128, :], ("kT", h))
    vv = tmb.rearrange("(kb p) (h d) -> p kb h d", p=128, d=128)
    for kb in range(32):
        ld(V[:, kb, :, 0:128], vv[:, kb, :, :], ("V", kb))
    P.op("pool", lambda e: e.memset(V[:, :, :, 128:129], 1.0), writes=["Vones"])
    qv = fmo[0:1024, :].rearrange("(h d) t -> d h t", d=128)
    k = {"s": 0, "p": 0, "o": 0}
    def loads(i):
        b2_ = i % 2
        nkb_ = 4 * (blocks[i][0] - 1) + blocks[i][2]
        po_ = poffs[i]
        ld(qT[b2_][:], qv[:, :, i * 128:(i + 1) * 128], ("qT", b2_))
        ld(mT[b2_][:, 0:nkb_, :], maskT_d[:, po_:po_ + nkb_, :], ("mT", b2_))
        ld(zt[b2_][:], tmo[i * 128:(i + 1) * 128, TMF_OWN_ATTZ:TMF_OWN_ATTZ + 1024], ("zt", b2_))

    loads(0)
    for i, (nch, _, lk) in enumerate(blocks):
        b2 = i % 2
        nkb = 4 * (nch - 1) + lk
        po = poffs[i]
        if i + 1 < nblk:
            loads(i + 1)
        P.op("act", lambda e, o=zt[b2][:]: e.activation(out=o, in_=o, func=AF.Silu), reads=[("zt", b2)], writes=[("zt", b2)])
        for h in range(8):
            jo = k["o"] % 2
            k["o"] += 1
            jss = {}

            def st_mm(ch):
                js = k["s"] % 4
                k["s"] += 1
                jss[ch] = js
                nk4 = 4 if ch < nch - 1 else lk
                P.op("pe", lambda e, o=pS[js][:, 0:nk4 * 128], r=mT[b2][:, ch * 4:ch * 4 + nk4, :].rearrange("p k t -> p (k t)"): e.matmul(
                    o, lhsT=i30k[:], rhs=r, start=True, stop=False), reads=["i30k", ("mT", b2)], writes=[("pS", js)])
                for k4 in range(nk4):
                    kb = ch * 4 + k4
                    P.op("pe", lambda e, o=pS[js][:, k4 * 128:(k4 + 1) * 128], l=kT[:, h, kb * 128:(kb + 1) * 128],
                         r=qT[b2][:, h, :], k4=k4, nk4=nk4: e.matmul(o, lhsT=l, rhs=r, start=False, stop=(k4 == nk4 - 1)),
                         reads=[("kT", h), ("qT", b2)], writes=[("pS", js)])

            for c0_ in range(min(3, nch)):
                st_mm(c0_)
            for ch in range(nch):
                if ch + 3 < nch:
                    st_mm(ch + 3)
                js = jss[ch]
                jp = k["p"] % 4
                k["p"] += 1
                nk4 = 4 if ch < nch - 1 else lk
                P.op("act", lambda e, o=PT[jp][:, 0:nk4 * 128], s=pS[js][:, 0:nk4 * 128]: e.activation(out=o, in_=s, func=AF.Exp, scale=att_scale),
                     reads=[("pS", js)], writes=[("PT", jp)])
                for k4 in range(nk4):
                    kb = ch * 4 + k4
                    P.op("pe", lambda e, o=pO[jo][:, 0:129], l=PT[jp][:, k4 * 128:(k4 + 1) * 128], r=V[:, kb, h, :], kb=kb, nkb=nkb:
                         e.matmul(o, lhsT=l, rhs=r, start=(kb == 0), stop=(kb == nkb - 1)),
                         reads=[("PT", jp), ("V", kb), "Vones"], writes=[("pO", jo)])
            c = i * 8 + h
            P.op("dve", lambda e, o=rs[:, c:c + 1], s=pO[jo][:, 128:129]: e.reciprocal(out=o, in_=s),
                 reads=[("pO", jo)], writes=[("rs", c)])
            P.op("dve", lambda e, o=ot[b2][:, h * 128:(h + 1) * 128], s=pO[jo][:, 0:128], r=rs[:, c:c + 1],
                 z=zt[b2][:, h * 128:(h + 1) * 128]: e.scalar_tensor_tensor(out=o, in0=s, scalar=r, in1=z, op0=ALU.mult, op1=ALU.mult),
                 reads=[("pO", jo), ("rs", c), ("zt", b2)], writes=[("ot", b2)])
        P.dma("sp", lambda e, o=y_own[i * 128:(i + 1) * 128, 0:1024], s=ot[b2][:]: e.dma_start(out=o, in_=s),
              reads=[("ot", b2)], writes=[("y_att", i)])
    ph.close()


RW_DECAY_C = -0.6065306597126334


class _Stop(Exception):
    pass


def phase_rwkv(nc, scr, prm, y_all, NBLK=SEQ // 128, stop_after=None, y_dst=None, stagger=True):
    ph = Phase(nc)
    P = ph.P
    tm = scr["tmf_all"]
    fm = scr["fmf_all"]
    if y_dst is None:
        y_dst = y_all[:, 256:512]
    cst = {}
    for nm in ("ident_f", "mask_sl", "mask_su", "mask_u", "ones_bd"):
        cst[nm] = ph.sb([128, 128], F32, nm)
    mu_tm = ph.sb([128, 1024], F32, "mu_tm")
    mu_fm = ph.sb([128, 1], F32, "mu_fm")
    w2a2 = ph.sb([128, 256], F32, "w2a2")
    vec = {}
    for nm in ("rw_w0", "rw_a0", "rw_kk", "rw_ka", "rw_rk", "rw_gng", "rw_gnb"):
        vec[nm] = ph.sb([128, 256], F32, nm)
    onecol = ph.sb([128, 1], F32, "onecol")
    NBUF3 = 4
    cur = [ph.sb([128, 1024], F32, "cur") for _ in range(NBUF3)]
    prv = [ph.sb([128, 1024], F32, "prv") for _ in range(NBUF3)]
    lcur = [ph.sb([128, 128], F32, "lcur") for _ in range(NBUF3)]
    lprv = [ph.sb([128, 128], F32, "lprv") for _ in range(NBUF3)]
    T = {}
    BFT = ("rt", "at", "bt", "kt", "bh", "kh", "vb")
    for nm in ("lw", "asig", "kkn", "kp", "aa", "bb", "cum", "cumL", "e1", "rt", "at", "bt", "kt", "bh", "kh", "vb", "tmp", "tmp2", "yb", "bon"):
        T[nm] = [ph.sb([128, 256], BF16 if nm in BFT else F32, nm) for _ in range(NBUF3)]
    st4 = [ph.sb([128, 16], F32, "st4") for _ in range(NBUF3)]
    gL = [ph.sb([128, 4], F32, "gL") for _ in range(NBUF3)]
    TR = {nm: [[ph.sb([128, 128], BF16, nm) for _ in range(2)] for _ in range(NBUF3)] for nm in ("atT", "btT", "ktT", "rtT")}
    H = {}
    for nm in ("N", "NT", "Pa", "PaT", "Pb", "PbT", "TTa", "TTb", "MakT"):
        H[nm] = ph.sb([128, 4, 128], BF16, nm)
    H["W2"] = ph.sb([128, 4, 64], BF16, "W2")
    mask2 = {nm: ph.sb([128, 2, 128], F32, nm + "2") for nm in ("mask_sl", "mask_su", "mask_u")}
    HS = {}
    for nm in ("P1T", "P2", "MrbT", "MrkT"):
        shp = {"P1T": [128, 2, 128], "P2": [128, 4, 64], "MrbT": [128, 4, 128], "MrkT": [128, 4, 128]}[nm]
        HS[nm] = [ph.sb(shp, F32 if nm == "P2" else BF16, nm) for _ in range(NBUF3)]
    Usb = ph.sb([128, 4, 64], BF16, "Usb")
    STb = [ph.sb([128, 64], BF16, "STb") for _ in range(4)]
    identb = ph.sb([128, 128], BF16, "identb")
    ST = [ph.sb([128, 64], F32, "ST") for _ in range(4)]
    pA = [ph.ps([128, 512], F32, "pA") for _ in range(3)]
    pP = [ph.ps([128, 512], F32, "pP") for _ in range(2)]
    pTb = ph.ps([128, 1024], BF16, "pTb")
    pQ = [ph.ps([128, 512], F32, "pQ") for _ in range(2)]

    ld = lambda dst, src, key: P.dma("sp", lambda e: e.dma_start(out=dst, in_=src), writes=[key])
    for nm in cst:
        ld(cst[nm][:], prm[nm][:, :], nm)
    ld(identb[:], prm["ident_b"][:, :], "identb")
    ld(mu_tm[:], prm["rw_mu_tm"][0:1, :].partition_broadcast(128), "mu_tm")
    ld(mu_fm[:], prm["rw_mu_fm"][:, :], "mu_fm")
    ld(w2a2[:], prm["rw_w2a2"][:, :], "w2a2")
    for nm in vec:
        ld(vec[nm][:], prm[nm][0:1, :].partition_broadcast(128), nm)
    P.op("dve", lambda e: e.memset(onecol[:], 1.0), writes=["onecol"])
    for h in range(4):
        P.op("dve", lambda e, o=ST[h][:]: e.memset(o, 0.0), writes=[("ST", h)])
        P.op("dve", lambda e, o=STb[h][:]: e.memset(o, 0.0), writes=[("STb", h)])
    P.op("dve", lambda e: e.memset(Usb[:], 0.0), writes=["Usb"])
    for nm in ("mask_sl", "mask_su", "mask_u"):
        for r_ in range(2):
            P.op("pool", lambda e, o=mask2[nm][:, r_, :], s=cst[nm][:]: e.tensor_copy(out=o, in_=s), reads=[nm], writes=[nm + "2"])
    kq = {"a": 0, "q": 0, "e": 0}
    P.excl.update(["pA", "pQ", "pTb", "pP"])

    def mm(out, lhsT, rhs, reads, writes, start=True, stop=True):
        P.op("pe", lambda e: e.matmul(out, lhsT=lhsT, rhs=rhs, start=start, stop=stop), reads=reads, writes=writes)

    def evac(out, in_, reads, writes, mask=None, mreads=()):
        kq["e"] += 1
        if mask is not None:
            P.op("dve", lambda e: e.tensor_tensor(out=out, in0=in_, in1=mask, op=ALU.mult), reads=list(reads) + list(mreads), writes=writes)
        elif kq["e"] % 4 == 0:
            P.op("dve", lambda e: e.tensor_copy(out=out, in_=in_), reads=reads, writes=writes)
        else:
            P.op("act", lambda e: e.copy(out=out, in_=in_), reads=reads, writes=writes)

    def nextA():
        j = kq["a"] % 3
        kq["a"] += 1
        return j

    def nextQ():
        j = kq["q"] % 2
        kq["q"] += 1
        return j

    def dv(fn, reads, writes, eng="dve"):
        P.op(eng, fn, reads=reads, writes=writes)

    def stage(n):
        if stop_after is not None and n > stop_after:
            raise _Stop()

    real_op, real_dma = P.op, P.dma
    recs = []
    for blk in range(NBLK):
      rec = []
      marks = []
      recs.append((rec, marks))
      P.op = lambda eng, fn, reads=(), writes=(), rec=rec: rec.append((real_op, eng, fn, list(reads), list(writes)))
      P.dma = lambda eng, fn, reads=(), writes=(), rec=rec: rec.append((real_dma, eng, fn, list(reads), list(writes)))
      try:
          b2 = blk % NBUF3
          t0 = blk * 128
          B = {nm: T[nm][b2] for nm in T}
          K2 = lambda nm: (nm, b2)
          cu, pv, lc, lp = cur[b2], prv[b2], lcur[b2], lprv[b2]
          stage(-1)
          ld(cu[:], tm[t0:t0 + 128, TMF_R:TMF_R + 1024], K2("cur"))
          ld(lc[:], fm[FMF_WL:FMF_WL + 128, t0:t0 + 128], K2("lcur"))
          if blk == 0:
              dv(lambda e, o=pv[:]: e.memset(o, 0.0), [], [K2("prv")])
              dv(lambda e, o=lp[:]: e.memset(o, 0.0), [], [K2("lprv")])
              ld(pv[1:128, :], tm[0:127, TMF_R:TMF_R + 1024], K2("prv"))
              ld(lp[:, 1:128], fm[FMF_WL:FMF_WL + 128, 0:127], K2("lprv"))
          else:
              ld(pv[:], tm[t0 - 1:t0 + 127, TMF_R:TMF_R + 1024], K2("prv"))
              ld(lp[:], fm[FMF_WL:FMF_WL + 128, t0 - 1:t0 + 127], K2("lprv"))
          stage(-0.5)
          dv(lambda e, o=pv[:], c=cu[:]: e.tensor_tensor(out=o, in0=o, in1=c, op=ALU.subtract), [K2("prv"), K2("cur")], [K2("prv")])
          dv(lambda e, o=pv[:]: e.tensor_tensor(out=o, in0=o, in1=mu_tm[:], op=ALU.mult), [K2("prv"), "mu_tm"], [K2("prv")], eng="pool")
          dv(lambda e, o=cu[:], d=pv[:]: e.tensor_tensor(out=o, in0=o, in1=d, op=ALU.add), [K2("prv"), K2("cur")], [K2("cur")])
          dv(lambda e, o=lp[:], c=lc[:]: e.tensor_tensor(out=o, in0=o, in1=c, op=ALU.subtract), [K2("lprv"), K2("lcur")], [K2("lprv")])
          dv(lambda e, o=lc[:], d=lp[:]: e.scalar_tensor_tensor(out=o, in0=d, scalar=mu_fm[:, 0:1], in1=o, op0=ALU.mult, op1=ALU.add),
             [K2("lprv"), K2("lcur"), "mu_fm"], [K2("lcur")])
          stage(-0.2)
          P.op("act", lambda e, o=lc[0:64, :]: e.activation(out=o, in_=o, func=AF.Tanh), reads=[K2("lcur")], writes=[K2("lcur")])
          r_, k_, v_, z_ = cu[:, 0:256], cu[:, 256:512], cu[:, 512:768], cu[:, 768:1024]
          P.op("act", lambda e, o=B["vb"][:], v_=v_: e.copy(out=o, in_=v_), reads=[K2("cur")], writes=[K2("vb")])
          stage(1)
          mm(pP[0][:, 0:256], lc[0:64, :], w2a2[0:64, :], [K2("lcur"), "w2a2"], [("pP", 0)])
          mm(pP[1][:, 0:256], lc[64:128, :], w2a2[64:128, :], [K2("lcur"), "w2a2"], [("pP", 1)])
          dv(lambda e, o=B["lw"][:], s=pP[0][:, 0:256]: e.tensor_tensor(out=o, in0=s, in1=vec["rw_w0"][:], op=ALU.add),
             [("pP", 0), "rw_w0"], [K2("lw")])
          dv(lambda e, o=B["asig"][:], s=pP[1][:, 0:256]: e.tensor_tensor(out=o, in0=s, in1=vec["rw_a0"][:], op=ALU.add),
             [("pP", 1), "rw_a0"], [K2("asig")])
          P.op("act", lambda e, o=B["lw"][:]: e.activation(out=o, in_=o, func=AF.Sigmoid), reads=[K2("lw")], writes=[K2("lw")])
          P.op("act", lambda e, o=B["asig"][:]: e.activation(out=o, in_=o, func=AF.Sigmoid), reads=[K2("asig")], writes=[K2("asig")])
          dv(lambda e, o=B["lw"][:]: e.tensor_scalar(out=o, in0=o, scalar1=RW_DECAY_C, scalar2=None, op0=ALU.mult), [K2("lw")], [K2("lw")])
          stage(2)
          s4 = st4[b2]
          v3 = lambda ap: ap.rearrange("p (h j) -> p h j", h=4)
          dv(lambda e, o=B["kkn"][:], k_=k_: e.tensor_tensor(out=o, in0=k_, in1=vec["rw_kk"][:], op=ALU.mult), [K2("cur"), "rw_kk"], [K2("kkn")])
          dv(lambda e, o=B["tmp"][:], s=B["kkn"][:]: e.tensor_tensor(out=o, in0=s, in1=s, op=ALU.mult), [K2("kkn")], [K2("tmp")], eng="pool")
          dv(lambda e, o=s4[:, 0:4], s=v3(B["tmp"][:]): e.tensor_reduce(out=o, in_=s, axis=AX.X, op=ALU.add), [K2("tmp")], [K2("st4")])
          dv(lambda e, o=s4[:, 0:4]: e.tensor_scalar(out=o, in0=o, scalar1=1e-12, scalar2=None, op0=ALU.add), [K2("st4")], [K2("st4")])
          P.op("act", lambda e, o=s4[:, 0:4]: e.activation(out=o, in_=o, func=AF.Sqrt), reads=[K2("st4")], writes=[K2("st4")])
          dv(lambda e, o=s4[:, 0:4]: e.reciprocal(out=o, in_=o), [K2("st4")], [K2("st4")])
          dv(lambda e, o=v3(B["kkn"][:]), s=s4[:, 0:4].unsqueeze(2).to_broadcast([128, 4, 64]): e.tensor_tensor(out=o, in0=o, in1=s, op=ALU.mult),
             [K2("kkn"), K2("st4")], [K2("kkn")])
          dv(lambda e, o=B["tmp"][:], s=B["asig"][:]: e.scalar_tensor_tensor(out=o, in0=s, scalar=-1.0, in1=vec["rw_ka"][:], op0=ALU.add,
                                                                            op1=ALU.mult), [K2("asig"), "rw_ka"], [K2("tmp")])
          dv(lambda e, o=B["kp"][:], s=B["tmp"][:], k_=k_: e.scalar_tensor_tensor(out=o, in0=s, scalar=1.0, in1=k_, op0=ALU.add, op1=ALU.mult),
             [K2("tmp"), K2("cur")], [K2("kp")])
          dv(lambda e, o=B["aa"][:], s=B["kkn"][:]: e.tensor_scalar(out=o, in0=s, scalar1=-1.0, scalar2=None, op0=ALU.mult),
             [K2("kkn")], [K2("aa")], eng="pool")
          dv(lambda e, o=B["bb"][:], s=B["kkn"][:], a=B["asig"][:]: e.tensor_tensor(out=o, in0=s, in1=a, op=ALU.mult),
             [K2("kkn"), K2("asig")], [K2("bb")], eng="pool")
          dv(lambda e, o=B["tmp2"][:], s=B["kp"][:], r_=r_: e.tensor_tensor(out=o, in0=r_, in1=s, op=ALU.mult), [K2("cur"), K2("kp")], [K2("tmp2")])
          dv(lambda e, o=B["tmp2"][:]: e.tensor_tensor(out=o, in0=o, in1=vec["rw_rk"][:], op=ALU.mult), [K2("tmp2"), "rw_rk"], [K2("tmp2")])
          dv(lambda e, o=s4[:, 4:8], s=v3(B["tmp2"][:]): e.tensor_reduce(out=o, in_=s, axis=AX.X, op=ALU.add), [K2("tmp2")], [K2("st4")])
          dv(lambda e, o=v3(B["bon"][:]), s=v3(cu[:, 512:768]), c=s4[:, 4:8].unsqueeze(2).to_broadcast([128, 4, 64]):
             e.tensor_tensor(out=o, in0=s, in1=c, op=ALU.mult), [K2("cur"), K2("st4")], [K2("bon")])
          stage(3)
          mm(pP[0][:, 0:256], cst["mask_u"][:], B["lw"][:], ["mask_u", K2("lw")], [("pP", 0)])
          mm(pP[0][:, 256:512], cst["ones_bd"][:], B["lw"][:], ["ones_bd", K2("lw")], [("pP", 0)])
          evac(B["cum"][:], pP[0][:, 0:256], [("pP", 0)], [K2("cum")])
          evac(B["cumL"][:], pP[0][:, 256:512], [("pP", 0)], [K2("cumL")])
          for p in range(2):
              for c2 in range(2):
                  mm(pP[1][:, c2 * 2 + p:c2 * 2 + p + 1], B["lw"][:, p * 128:(p + 1) * 128], cst["ones_bd"][:, c2 * 64:c2 * 64 + 1],
                     [K2("lw"), "ones_bd"], [("pP", 1)])
          P.op("act", lambda e, o=gL[b2][:], s=pP[1][:, 0:4]: e.activation(out=o, in_=s, func=AF.Exp), reads=[("pP", 1)], writes=[K2("gL")])
          P.op("act", lambda e, o=B["e1"][:], s=B["cum"][:]: e.activation(out=o, in_=s, func=AF.Exp), reads=[K2("cum")], writes=[K2("e1")])
          dv(lambda e, o=B["rt"][:], s=B["e1"][:], r_=r_: e.tensor_tensor(out=o, in0=r_, in1=s, op=ALU.mult), [K2("cur"), K2("e1")], [K2("rt")])
          dv(lambda e, o=B["tmp"][:], s=B["cum"][:], l=B["lw"][:]: e.tensor_tensor(out=o, in0=s, in1=l, op=ALU.subtract),
             [K2("cum"), K2("lw")], [K2("tmp")], eng="pool")
          P.op("act", lambda e, o=B["tmp"][:]: e.activation(out=o, in_=o, func=AF.Exp), reads=[K2("tmp")], writes=[K2("tmp")])
          dv(lambda e, o=B["at"][:], s=B["aa"][:], t=B["tmp"][:]: e.tensor_tensor(out=o, in0=s, in1=t, op=ALU.mult),
             [K2("aa"), K2("tmp")], [K2("at")])
          P.op("act", lambda e, o=B["e1"][:], s=B["cum"][:]: e.activation(out=o, in_=s, func=AF.Exp, scale=-1.0),
               reads=[K2("cum"), K2("rt")], writes=[K2("e1")])
          dv(lambda e, o=B["bt"][:], s=B["bb"][:], t=B["e1"][:]: e.tensor_tensor(out=o, in0=s, in1=t, op=ALU.mult),
             [K2("bb"), K2("e1")], [K2("bt")])
          dv(lambda e, o=B["kt"][:], s=B["kp"][:], t=B["e1"][:]: e.tensor_tensor(out=o, in0=s, in1=t, op=ALU.mult),
             [K2("kp"), K2("e1")], [K2("kt")], eng="pool")
          dv(lambda e, o=B["tmp2"][:], s=B["cumL"][:], c=B["cum"][:]: e.tensor_tensor(out=o, in0=s, in1=c, op=ALU.subtract),
             [K2("cumL"), K2("cum")], [K2("tmp2")], eng="pool")
          P.op("act", lambda e, o=B["tmp2"][:]: e.activation(out=o, in_=o, func=AF.Exp), reads=[K2("tmp2")], writes=[K2("tmp2")])
          dv(lambda e, o=B["bh"][:], s=B["bb"][:], t=B["tmp2"][:]: e.tensor_tensor(out=o, in0=s, in1=t, op=ALU.mult),
             [K2("bb"), K2("tmp2")], [K2("bh")])
          dv(lambda e, o=B["kh"][:], s=B["kp"][:], t=B["tmp2"][:]: e.tensor_tensor(out=o, in0=s, in1=t, op=ALU.mult),
             [K2("kp"), K2("tmp2")], [K2("kh")], eng="pool")
          stage(4)
          for qi_, (nm_src, nm_dst) in enumerate((("at", "atT"), ("bt", "btT"), ("kt", "ktT"), ("rt", "rtT"))):
              for p in range(2):
                  P.op("pe", lambda e, o=pTb[:, (qi_ * 2 + p) * 128:(qi_ * 2 + p + 1) * 128], s=B[nm_src][:, p * 128:(p + 1) * 128]: e.transpose(
                      out=o, in_=s, identity=identb[:]), reads=[K2(nm_src), "identb"], writes=["pTb"])
          for qi_, (nm_src, nm_dst) in enumerate((("at", "atT"), ("bt", "btT"), ("kt", "ktT"), ("rt", "rtT"))):
              for p in range(2):
                  evac(TR[nm_dst][b2][p][:], pTb[:, (qi_ * 2 + p) * 128:(qi_ * 2 + p + 1) * 128], ["pTb"], [(nm_dst, b2, p)])
          marks.append(len(rec))
          slot = lambda h: (h % 2) * 2 + h // 2
          rk = lambda nm, h: (nm, b2, h // 2)
          opd = {}
          for h in range(4):
              p, r0 = h // 2, (h % 2) * 64
              opd[h] = {nm: TR[nm2][b2][p][r0:r0 + 64, :] for nm, nm2 in (("aT", "atT"), ("bT", "btT"), ("kT", "ktT"), ("rT", "rtT"))}
          jx, jy2 = nextA(), nextA()
          for r_, jb in ((0, jx), (1, jy2)):
              for p in range(2):
                  h = 2 * p + r_
                  o_ = opd[h]
                  mm(pA[jb][:, p * 128:(p + 1) * 128], o_["aT"], o_["bT"], [rk("atT", h), rk("btT", h)], [("pA", jb)])
                  mm(pA[jb][:, 256 + p * 128:256 + (p + 1) * 128], o_["bT"], o_["aT"], [rk("atT", h), rk("btT", h)], [("pA", jb)])
          for r_, jb in ((0, jx), (1, jy2)):
              sl_ = slice(2 * r_, 2 * r_ + 2)
              dv(lambda e, o=H["N"][:, sl_, :], s=pA[jb][:, 0:256].rearrange("p (a t) -> p a t", a=2), m=mask2["mask_sl"][:]:
                 e.tensor_tensor(out=o, in0=s, in1=m, op=ALU.mult), [("pA", jb), "mask_sl2"], [("N", r_)])
              dv(lambda e, o=H["NT"][:, sl_, :], s=pA[jb][:, 256:512].rearrange("p (a t) -> p a t", a=2), m=mask2["mask_su"][:]:
                 e.tensor_tensor(out=o, in0=s, in1=m, op=ALU.mult), [("pA", jb), "mask_su2"], [("NT", r_)])
          jz = nextA()
          jx2 = nextA()
          for r_, jb in ((0, jx2), (1, jz)):
              for p in range(2):
                  h = 2 * p + r_
                  o_ = opd[h]
                  mm(pA[jb][:, p * 128:(p + 1) * 128], o_["kT"], o_["aT"], [rk("ktT", h), rk("atT", h)], [("pA", jb)])
                  mm(pA[jb][:, 256 + p * 128:256 + (p + 1) * 128], o_["bT"], o_["rT"], [rk("btT", h), rk("rtT", h)], [("pA", jb)])
          for r_, jb in ((0, jx2), (1, jz)):
              sl_ = slice(2 * r_, 2 * r_ + 2)
              dv(lambda e, o=H["MakT"][:, sl_, :], s=pA[jb][:, 0:256].rearrange("p (a t) -> p a t", a=2), m=mask2["mask_su"][:]:
                 e.tensor_tensor(out=o, in0=s, in1=m, op=ALU.mult), [("pA", jb), "mask_su2"], [("MakT", r_)])
              dv(lambda e, o=HS["MrbT"][b2][:, sl_, :], s=pA[jb][:, 256:512].rearrange("p (a t) -> p a t", a=2), m=mask2["mask_u"][:]:
                 e.tensor_tensor(out=o, in0=s, in1=m, op=ALU.mult), [("pA", jb), "mask_u2"], [("MrbT", b2, r_)])
          jk0, jk1 = nextA(), nextA()
          for r_, jb in ((0, jk0), (1, jk1)):
              for p in range(2):
                  h = 2 * p + r_
                  o_ = opd[h]
                  mm(pA[jb][:, p * 128:(p + 1) * 128], o_["kT"], o_["rT"], [rk("ktT", h), rk("rtT", h)], [("pA", jb)])
          for r_, jb in ((0, jk0), (1, jk1)):
              sl_ = slice(2 * r_, 2 * r_ + 2)
              dv(lambda e, o=HS["MrkT"][b2][:, sl_, :], s=pA[jb][:, 0:256].rearrange("p (a t) -> p a t", a=2), m=mask2["mask_u"][:]:
                 e.tensor_tensor(out=o, in0=s, in1=m, op=ALU.mult), [("pA", jb), "mask_u2"], [("MrkT", b2, r_)])
          dv(lambda e, o=H["TTa"][:], s=H["NT"][:], i_=cst["ident_f"][:].unsqueeze(1).to_broadcast([128, 4, 128]):
             e.tensor_tensor(out=o, in0=s, in1=i_, op=ALU.add), [("NT", 0), ("NT", 1), "ident_f"], ["TTa"], eng="pool")
          cur_, curT_, nxt_, nxtT_ = "N", "NT", "Pa", "PaT"
          tc_, tn_ = "TTa", "TTb"
          kn = lambda nm: [(nm, 0), (nm, 1)] if nm in ("N", "NT") else [nm]
          for lvl in range(1, 6):
              ja = nextA()
              for sl in range(4):
                  mm(pA[ja][:, sl * 128:(sl + 1) * 128], H[curT_][:, sl, :], H[cur_][:, sl, :], kn(cur_) + kn(curT_), [("pA", ja)])
              evac(H[nxt_][:], pA[ja][:].rearrange("p (a t) -> p a t", a=4), [("pA", ja)], [nxt_])
              if lvl < 5:
                  jb = nextA()
                  for sl in range(4):
                      mm(pA[jb][:, sl * 128:(sl + 1) * 128], H[cur_][:, sl, :], H[curT_][:, sl, :], kn(cur_) + kn(curT_), [("pA", jb)])
                  evac(H[nxtT_][:], pA[jb][:].rearrange("p (a t) -> p a t", a=4), [("pA", jb)], [nxtT_])
              jc = nextA()
              for sl in range(4):
                  mm(pA[jc][:, sl * 128:(sl + 1) * 128], H[nxt_][:, sl, :], H[tc_][:, sl, :], [nxt_, tc_], [("pA", jc)])
              dv(lambda e, o=H[tn_][:], s=pA[jc][:].rearrange("p (a t) -> p a t", a=4), t=H[tc_][:]: e.tensor_tensor(out=o, in0=s, in1=t, op=ALU.add),
                 [("pA", jc), tc_], [tn_])
              if lvl == 1:
                  cur_, curT_, nxt_, nxtT_ = "Pa", "PaT", "Pb", "PbT"
              else:
                  cur_, curT_, nxt_, nxtT_ = nxt_, nxtT_, cur_, curT_
              tc_, tn_ = tn_, tc_
          TTn = tc_
          ja = nextA()
          for h in range(4):
              p, r0 = h // 2, (h % 2) * 64
              mm(pA[ja][r0:r0 + 64, p * 128:(p + 1) * 128], B["at"][:, h * 64:(h + 1) * 64], H[TTn][:, slot(h), :], [K2("at"), TTn], [("pA", ja)])
              mm(pA[ja][:, 256 + h * 64:256 + (h + 1) * 64], H["MakT"][:, slot(h), :], B["vb"][:, h * 64:(h + 1) * 64],
                 [("MakT", h % 2), K2("vb")], [("pA", ja)])
          evac(HS["P1T"][b2][:], pA[ja][:, 0:256].rearrange("p (a t) -> p a t", a=2), [("pA", ja)], [("P1T", b2)])
          evac(H["W2"][:], pA[ja][:, 256:512].rearrange("p (h i) -> p h i", h=4), [("pA", ja)], ["W2"])
          jb = nextA()
          for h in range(4):
              mm(pA[jb][:, h * 64:(h + 1) * 64], H[TTn][:, slot(h), :], H["W2"][:, h, :], [TTn, "W2"], [("pA", jb)])
          evac(HS["P2"][b2][:], pA[jb][:, 0:256].rearrange("p (h i) -> p h i", h=4), [("pA", jb)], [("P2", b2)])
          stage(6)
          marks.append(len(rec))
          for c2 in range(2):
              cs = slice(c2 * 64, (c2 + 1) * 64)
              jq = nextQ()
              for h in range(4):
                  mm(pQ[jq][cs, h * 64:(h + 1) * 64], HS["P1T"][b2][:, h // 2, cs], STb[h][:, :], [("P1T", b2), ("STb", h)], [("pQ", jq)])
              dv(lambda e, o=Usb[cs, :, :], s=pQ[jq][cs, 0:256].rearrange("p (h i) -> p h i", h=4), t=HS["P2"][b2][cs, :, :]:
                 e.tensor_tensor(out=o, in0=s, in1=t, op=ALU.add), [("pQ", jq), ("P2", b2)], ["Usb"])
              jy = nextQ()
              for h in range(4):
                  p = h // 2
                  yo = pQ[jy][cs, h * 64:(h + 1) * 64]
                  mm(yo, TR["rtT"][b2][p][:, cs], STb[h][:, :], [("rtT", b2, p), ("STb", h)], [("pQ", jy)], start=True, stop=False)
                  mm(yo, HS["MrkT"][b2][:, slot(h), cs], B["vb"][:, h * 64:(h + 1) * 64], [("MrkT", b2, h % 2), K2("vb")], [("pQ", jy)],
                     start=False, stop=False)
                  mm(yo, HS["MrbT"][b2][:, slot(h), cs], Usb[:, h, :], [("MrbT", b2, h % 2), "Usb"], [("pQ", jy)], start=False, stop=True)
              evac(B["yb"][cs, :], pQ[jy][cs, 0:256], [("pQ", jy)], [K2("yb")])
              js = nextQ()
              for h in range(4):
                  r0 = (h % 2) * 64
                  rows = slice(r0, r0 + 64)
                  so = pQ[js][rows, h * 64:(h + 1) * 64]
                  mm(so, B["kh"][cs, h * 64:(h + 1) * 64], B["vb"][cs, h * 64:(h + 1) * 64], [K2("kh"), K2("vb")], [("pQ", js)],
                     start=True, stop=False)
                  mm(so, B["bh"][cs, h * 64:(h + 1) * 64], Usb[cs, h, :], [K2("bh"), "Usb"], [("pQ", js)], start=False, stop=True)
              for h in range(4):
                  p, r0 = h // 2, (h % 2) * 64
                  rows = slice(r0, r0 + 64)
                  dv(lambda e, o=ST[h][rows, :], s=pQ[js][rows, h * 64:(h + 1) * 64], g=gL[b2][rows, c2 * 2 + p:c2 * 2 + p + 1]:
                     e.scalar_tensor_tensor(out=o, in0=o, scalar=g, in1=s, op0=ALU.mult, op1=ALU.add),
                     [("ST", h), ("pQ", js), K2("gL")], [("ST", h)])
                  P.op("act", lambda e, o=STb[h][rows, :], s=ST[h][rows, :]: e.copy(out=o, in_=s), reads=[("ST", h)], writes=[("STb", h)])
          stage(7)
          marks.append(len(rec))
          yb = B["yb"]
          dv(lambda e, o=s4[:, 8:12], s=v3(yb[:]): e.tensor_reduce(out=o, in_=s, axis=AX.X, op=ALU.add), [K2("yb")], [K2("st4")])
          dv(lambda e, o=B["tmp"][:], s=yb[:]: e.tensor_tensor(out=o, in0=s, in1=s, op=ALU.mult), [K2("yb")], [K2("tmp")], eng="pool")
          dv(lambda e, o=s4[:, 12:16], s=v3(B["tmp"][:]): e.tensor_reduce(out=o, in_=s, axis=AX.X, op=ALU.add), [K2("tmp")], [K2("st4")])
          dv(lambda e, o=s4[:, 8:16]: e.tensor_scalar(out=o, in0=o, scalar1=1.0 / 64, scalar2=None, op0=ALU.mult), [K2("st4")], [K2("st4")])
          dv(lambda e, o=s4[:, 0:4], m=s4[:, 8:12]: e.tensor_tensor(out=o, in0=m, in1=m, op=ALU.mult), [K2("st4")], [K2("st4")])
          dv(lambda e, o=s4[:, 12:16], m2=s4[:, 0:4]: e.tensor_tensor(out=o, in0=o, in1=m2, op=ALU.subtract), [K2("st4")], [K2("st4")])
          dv(lambda e, o=s4[:, 12:16]: e.tensor_scalar(out=o, in0=o, scalar1=GN_EPS, scalar2=None, op0=ALU.add), [K2("st4")], [K2("st4")])
          P.op("act", lambda e, o=s4[:, 12:16]: e.activation(out=o, in_=o, func=AF.Sqrt), reads=[K2("st4")], writes=[K2("st4")])
          dv(lambda e, o=s4[:, 12:16]: e.reciprocal(out=o, in_=o), [K2("st4")], [K2("st4")])
          dv(lambda e, o=v3(yb[:]), m=s4[:, 8:12].unsqueeze(2).to_broadcast([128, 4, 64]): e.tensor_tensor(out=o, in0=o, in1=m, op=ALU.subtract),
             [K2("yb"), K2("st4")], [K2("yb")])
          dv(lambda e, o=v3(yb[:]), r=s4[:, 12:16].unsqueeze(2).to_broadcast([128, 4, 64]): e.tensor_tensor(out=o, in0=o, in1=r, op=ALU.mult),
             [K2("yb"), K2("st4")], [K2("yb")])
          dv(lambda e, o=yb[:]: e.tensor_tensor(out=o, in0=o, in1=vec["rw_gng"][:], op=ALU.mult), [K2("yb"), "rw_gng"], [K2("yb")], eng="pool")
          dv(lambda e, o=yb[:]: e.tensor_tensor(out=o, in0=o, in1=vec["rw_gnb"][:], op=ALU.add), [K2("yb"), "rw_gnb"], [K2("yb")])
          dv(lambda e, o=yb[:], b_=B["bon"][:]: e.tensor_tensor(out=o, in0=o, in1=b_, op=ALU.add), [K2("yb"), K2("bon")], [K2("yb")], eng="pool")
          P.op("act", lambda e, o=B["tmp2"][:], z_=z_: e.activation(out=o, in_=z_, func=AF.Silu), reads=[K2("cur")], writes=[K2("tmp2")])
          dv(lambda e, o=yb[:], z=B["tmp2"][:]: e.tensor_tensor(out=o, in0=o, in1=z, op=ALU.mult), [K2("yb"), K2("tmp2")], [K2("yb")])
          P.dma("sp", lambda e, o=y_dst[t0:t0 + 128, :], s=yb[:]: e.dma_start(out=o, in_=s), reads=[K2("yb")], writes=[("y_rwkv", blk)])
      except _Stop:
          pass
    P.op, P.dma = real_op, real_dma
    if stagger:
        pipeline_merge(recs, 4)
    else:
        for rec, _ in recs:
            for (f, eng, fn, reads, writes) in rec:
                f(eng, fn, reads=reads, writes=writes)
    ph.close()


def phase_outproj(nc, y_tok, x_res, w_out, ident_d, x_new):
    ph = Phase(nc)
    P = ph.P
    D = D_MODEL
    KT = D // 128
    T = OWN
    TT = T // 128
    KQ = 8
    NKQ = KT // KQ
    NB = 512
    ident = ph.sb([128, 128], BF16, "ident")
    hT = ph.sb([128, KT, T], BF16, "hT")
    xt = [ph.sb([128, D], F32, "xt") for _ in range(2)]
    hb = [ph.sb([128, D], BF16, "hb") for _ in range(2)]
    wb = [ph.sb([128, KT, NB], BF16, "wb") for _ in range(2)]
    stg = [ph.sb([128, NB], F32, "stg") for _ in range(4)]
    rs = [ph.sb([128, NB], F32, "rs") for _ in range(4)]
    pT = [ph.ps([128, 1024], BF16, "pT") for _ in range(2)]
    pM = [ph.ps([128, 512], F32, "pM") for _ in range(6)]
    P.dma("sp", lambda e: e.dma_start(out=ident[:], in_=ident_d[:, :]), writes=["ident"])
    tc_ = 0
    for tt in range(TT):
        i = tt % 2
        P.dma("sp", lambda e, o=xt[i][:], s=y_tok[tt * 128:(tt + 1) * 128, :]: e.dma_start(out=o, in_=s), writes=[("xt", i)])
        P.op("act", lambda e, o=hb[i][:], s=xt[i][:]: e.copy(out=o, in_=s), reads=[("xt", i)], writes=[("hb", i)])
        for kq in range(NKQ):
            pb = tc_ % 2
            tc_ += 1
            for k8 in range(KQ):
                kt = kq * KQ + k8
                P.op("pe", lambda e, o=pT[pb][:, k8 * 128:(k8 + 1) * 128], s=hb[i][:, kt * 128:(kt + 1) * 128]: e.transpose(
                    out=o, in_=s, identity=ident[:]), reads=[("hb", i), "ident"], writes=[("pT", pb)])
            dst = hT[:, kq * KQ:(kq + 1) * KQ, tt * 128:(tt + 1) * 128]
            src = pT[pb][:].rearrange("p (k t) -> p k t", k=KQ)
            if kq % 2 == 0:
                P.op("dve", lambda e, dst=dst, src=src: e.tensor_copy(out=dst, in_=src), reads=[("pT", pb)], writes=[("hT", tt, kq)])
            else:
                P.op("act", lambda e, dst=dst, src=src: e.copy(out=dst, in_=src), reads=[("pT", pb)], writes=[("hT", tt, kq)])
    wv = w_out.rearrange("(kt p) n -> p kt n", p=128)
    mc = 0
    for cb in range(D // NB):
        c0 = cb * NB
        wi = cb % 2
        for kq in range(NKQ):
            P.dma("pool", lambda e, o=wb[wi][:, kq * KQ:(kq + 1) * KQ, :], s=wv[:, kq * KQ:(kq + 1) * KQ, c0:c0 + NB]: e.dma_start(
                out=o, in_=s), writes=[("wb", wi, kq)])
        for tt in range(TT):
            j = mc % 6
            s4 = mc % 4
            mc += 1
            for kt in range(KT):
                P.op("pe", lambda e, o=pM[j][:], l=hT[:, kt, tt * 128:(tt + 1) * 128], r=wb[wi][:, kt, :], kt=kt: e.matmul(
                    o, lhsT=l, rhs=r, start=(kt == 0), stop=(kt == KT - 1)),
                    reads=[("hT", tt, kt // KQ), ("wb", wi, kt // KQ)], writes=[("pM", j)])
            P.dma("sp", lambda e, o=rs[s4][:], s=x_res[tt * 128:(tt + 1) * 128, c0:c0 + NB]: e.dma_start(out=o, in_=s), writes=[("rs", s4)])
            P.op("dve", lambda e, o=stg[s4][:], a=pM[j][:], b_=rs[s4][:]: e.tensor_tensor(out=o, in0=a, in1=b_, op=ALU.add),
                 reads=[("pM", j), ("rs", s4)], writes=[("stg", s4)])
            P.dma("sp", lambda e, o=x_new[tt * 128:(tt + 1) * 128, c0:c0 + NB], s=stg[s4][:]: e.dma_start(out=o, in_=s),
                  reads=[("stg", s4)], writes=[("xn", tt, cb)])
    ph.close()


def phase_finalnorm(nc, x_in, g, out, ntiles=OWN // 128):
    ph = Phase(nc)
    P = ph.P
    D = D_MODEL
    gb = ph.sb([128, D], F32, "gb")
    xt = [ph.sb([128, D], F32, "xt") for _ in range(2)]
    junk = ph.sb([128, D], BF16, "junk")
    ss = ph.sb([128, ntiles], F32, "ss")
    P.dma("sp", lambda e: e.dma_start(out=gb[:], in_=g[0:1, :].partition_broadcast(128)), writes=["gb"])
    P.op("dve", lambda e: e.memset(ss[:], 0.0), writes=["ss"])
    for tt in range(ntiles):
        i = tt % 2
        P.dma("sp", lambda e, o=xt[i][:], s=x_in[tt * 128:(tt + 1) * 128, :]: e.dma_start(out=o, in_=s), writes=[("xt", i)])
        sc = ss[:, tt:tt + 1]
        P.op("act", lambda e, s=xt[i][:], sc=sc: e.activation(out=junk[:], in_=s, func=AF.Square, accum_out=sc),
             reads=[("xt", i), "ss"], writes=["junk", ("ssv", tt)])
        P.op("dve", lambda e, sc=sc: e.tensor_scalar(out=sc, in0=sc, scalar1=1.0 / D, scalar2=NORM_EPS, op0=ALU.mult, op1=ALU.add),
             reads=[("ssv", tt)], writes=[("ssv", tt)])
        P.op("act", lambda e, sc=sc: e.activation(out=sc, in_=sc, func=AF.Sqrt), reads=[("ssv", tt)], writes=[("ssv", tt)])
        P.op("dve", lambda e, sc=sc: e.reciprocal(out=sc, in_=sc), reads=[("ssv", tt)], writes=[("ssv", tt)])
        P.op("dve", lambda e, o=xt[i][:], sc=sc: e.scalar_tensor_tensor(out=o, in0=o, scalar=sc, in1=gb[:], op0=ALU.mult, op1=ALU.mult),
             reads=[("xt", i), ("ssv", tt), "gb"], writes=[("xt", i)])
        P.dma("sp", lambda e, o=out[tt * 128:(tt + 1) * 128, :], s=xt[i][:]: e.dma_start(out=o, in_=s), reads=[("xt", i)],
              writes=[("out", tt)])
    ph.close()


def own_tok(q):
    return ((4 * np.arange(8)[:, None] + q) * 128 + np.arange(128)[None, :]).reshape(-1)


def const_inputs():
    i = np.arange(128)
    sel4 = np.zeros((4, 4, 128), np.float32)
    for h in range(4):
        sel4[h, h, :] = 1.0
    same = (i[:, None] // 64) == (i[None, :] // 64)
    sl = ((i[None, :] < i[:, None]) & same).astype(np.float32)
    su = np.ascontiguousarray(sl.T)
    return {
        "ident": np.eye(128, dtype=ml_dtypes.bfloat16),
        "ident_f": np.eye(128, dtype=np.float32), "ident_b": np.eye(128, dtype=ml_dtypes.bfloat16),
        "tri_le": (i[:, None] <= i[None, :]).astype(np.float32), "ones_f": np.ones((128, 128), np.float32),
        "sel4": sel4, "negbig_lt": np.where(i[None, :] < i[:, None], NEG_BIG, 0.0).astype(np.float32),
        "ident30k_b": (30000.0 * np.eye(128)).astype(ml_dtypes.bfloat16),
        "iota512": np.arange(512, dtype=np.float32)[None, :].copy(),
        "pow2": (0.5 ** (np.arange(NBIS) + 1)).astype(np.float32)[None, :].copy(),
        "mask_sl": sl, "mask_su": su, "mask_u": su + np.eye(128, dtype=np.float32), "ones_bd": same.astype(np.float32),
    }


def layer_params(inp, l, q):
    c = np.ascontiguousarray
    p = {}
    p["qrel"] = (q * 128 + np.arange(128, dtype=np.float32))[:, None].copy()
    p["g"] = c(inp["norm_g"][l][None, :])
    p["mlp_ln_g"] = c(inp["mlp_ln_g"][l][None, :])
    p["mlp_ln_b"] = c(inp["mlp_ln_b"][l][None, :])
    p["mlp_wsT"] = c(inp["mlp_w_s"][l].transpose(2, 0, 1))
    p["mlp_bsT"] = c(inp["mlp_b_s"][l].T)
    cols = np.concatenate([np.arange(256 * q, 256 * (q + 1)), 1024 + np.arange(128 * q, 128 * (q + 1)),
                           1536 + np.arange(128 * q, 128 * (q + 1))])
    cw = inp["ssm_conv_w"][l][:, cols]
    cb = inp["ssm_conv_b"][l][cols]
    hs = slice(4 * q, 4 * q + 4)
    p["ssm_cw"] = c(cw.reshape(4, 4, 128).transpose(2, 1, 0))
    p["ssm_cb"] = c(cb.reshape(4, 128).T)
    p["ssm_dtb_t"] = c(np.tile(inp["ssm_dt_bias"][l][hs], 32)[None, :])
    p["ssm_alog_t"] = c(np.tile(inp["ssm_A_log"][l][hs], 32)[None, :])
    p["ssm_D"] = c(inp["ssm_D"][l][hs][None, :])
    p["ssm_ng"] = c(inp["ssm_norm_g"][l][256 * q:256 * (q + 1)][None, :])
    mu = inp["rwkv_mu"][l]
    hc = np.arange(256 * q, 256 * (q + 1))
    p["rw_mu_tm"] = c(np.concatenate([mu[1024 * k + hc] for k in range(4)])[None, :])
    p["rw_mu_fm"] = c(mu[4096:4224][:, None])
    p["rw_w2a2"] = c(np.concatenate([inp["rwkv_w2"][l][:, hc], inp["rwkv_a2"][l][:, hc]], axis=0))
    for nm, src in (("rw_w0", "rwkv_w0"), ("rw_a0", "rwkv_a0"), ("rw_kk", "rwkv_k_k"), ("rw_ka", "rwkv_k_a"), ("rw_rk", "rwkv_r_k"),
                    ("rw_gng", "rwkv_gn_g"), ("rw_gnb", "rwkv_gn_b")):
        p[nm] = c(inp[src][l].reshape(-1)[hc][None, :])
    return p


def _decl(nc, arrs):
    out = {}
    for n, a in arrs.items():
        dt = BF16 if a.dtype == ml_dtypes.bfloat16 else F32
        out[n] = nc.dram_tensor(n, list(a.shape), dt, kind="ExternalInput").ap()
    return out


def build_AB(sample_inputs):
    nc = bass.Bass("TRN2", target_bir_lowering=False)
    d = _decl(nc, sample_inputs)
    wts = {n: d["w_" + n] for n in GROUP_INFO}
    scr = make_scratch(nc)
    maskT = nc.dram_tensor("maskT", [128, NPAIR, 128], BF16).ap()
    y_own = nc.dram_tensor("y_own", [OWN, 2048], F32, kind="ExternalOutput").ap()
    y_all = nc.dram_tensor("y_all", [SEQ, 512], F32, kind="ExternalOutput").ap()
    phase_inproj(nc, d["x_all"], d["x_own"], d["g"], d["ident"], wts, scr)
    phase_indexer(nc, scr, d, maskT)
    phase_attn(nc, scr, d, maskT, y_own)
    phase_mlp(nc, scr, d, y_own)
    phase_ssm(nc, scr, d, y_all)
    phase_rwkv(nc, scr, d, y_all)
    return nc


def build_C(final):
    nc = bass.Bass("TRN2", target_bir_lowering=False)
    y_tok = nc.dram_tensor("y_tok", [OWN, D_MODEL], F32, kind="ExternalInput").ap()
    x_res = nc.dram_tensor("x_res", [OWN, D_MODEL], F32, kind="ExternalInput").ap()
    w_out = nc.dram_tensor("w_out", [D_MODEL, D_MODEL], F32, kind="ExternalInput").ap()
    ident = nc.dram_tensor("ident", [128, 128], BF16, kind="ExternalInput").ap()
    if final:
        g = nc.dram_tensor("gf", [1, D_MODEL], F32, kind="ExternalInput").ap()
        x_mid = nc.dram_tensor("x_mid", [OWN, D_MODEL], F32).ap()
        out = nc.dram_tensor("out", [OWN, D_MODEL], F32, kind="ExternalOutput").ap()
        phase_outproj(nc, y_tok, x_res, w_out, ident, x_mid)
        phase_finalnorm(nc, x_mid, g, out)
    else:
        out = nc.dram_tensor("out", [OWN, D_MODEL], F32, kind="ExternalOutput").ap()
        phase_outproj(nc, y_tok, x_res, w_out, ident, out)
    return nc


def kernel_unfused(**inp):
    inp = {k: np.asarray(v) for k, v in inp.items()}
    x = inp["x"]
    cst = const_inputs()
    otk = [own_tok(q) for q in range(NQ)]
    nc_ab = None
    for l in range(2):
        w_in = inp["w_in"][l]
        in_maps = []
        for c in range(NCORE):
            b, q = c // NQ, c % NQ
            m = dict(cst)
            m.update(layer_params(inp, l, q))
            m["x_all"] = np.ascontiguousarray(x[b])
            m["x_own"] = np.ascontiguousarray(x[b][otk[q]])
            for name, cols in col_groups(q).items():
                m["w_" + name] = np.ascontiguousarray(w_in[:, cols])
            in_maps.append(m)
        if nc_ab is None:
            nc_ab = build_AB(in_maps[0])
        res = run_bass_kernel_spmd(nc_ab, in_maps, core_ids=list(range(NCORE))).results
        in_maps_c = []
        for c in range(NCORE):
            b, q = c // NQ, c % NQ
            y_tok = np.empty((OWN, D_MODEL), np.float32)
            y_tok[:, 0:1024] = res[c]["y_own"][:, 0:1024]
            y_tok[:, 3072:4096] = res[c]["y_own"][:, 1024:2048]
            for q2 in range(NQ):
                ya = res[b * NQ + q2]["y_all"][otk[q]]
                y_tok[:, 1024 + 256 * q2:1024 + 256 * (q2 + 1)] = ya[:, 0:256]
                y_tok[:, 2048 + 256 * q2:2048 + 256 * (q2 + 1)] = ya[:, 256:512]
            m = {"y_tok": y_tok, "x_res": np.ascontiguousarray(x[b][otk[q]]), "w_out": np.ascontiguousarray(inp["w_out"][l]),
                 "ident": cst["ident"]}
            if l == 1:
                m["gf"] = np.ascontiguousarray(inp["final_norm_g"][None, :])
            in_maps_c.append(m)
        nc_c = build_C(final=(l == 1))
        resc = run_bass_kernel_spmd(nc_c, in_maps_c, core_ids=list(range(NCORE))).results
        xn = np.empty_like(x)
        for c in range(NCORE):
            b, q = c // NQ, c % NQ
            xn[b][otk[q]] = resc[c]["out"]
        x = xn
    return x


def fused_ginfo():
    gi = {"fmb_all": (1152, "fm", BF16, "all"), "tmb_all": (1024, "tm", BF16, "all"),
          "fmb_own": (2048, "fm", BF16, "all"), "tmf_own": (4112, "tm", F32, "all")}
    for q in range(NQ):
        gi[f"fmf_all{q}"] = (640, "fm", F32, "all")
        gi[f"tmf_all{q}"] = (1284, "tm", F32, "all")
    return gi


FUSED_BLOCKS = [(qb // 4 + 1, qb % 4, qb % 4 + 1) for qb in range(32)]
Q_KEYS = ("ssm_cw", "ssm_cb", "ssm_dtb_t", "ssm_alog_t", "ssm_D", "ssm_ng", "rw_mu_tm", "rw_mu_fm", "rw_w2a2", "rw_w0", "rw_a0",
          "rw_kk", "rw_ka", "rw_rk", "rw_gng", "rw_gnb")
L_KEYS = ("g", "mlp_ln_g", "mlp_ln_b", "mlp_wsT", "mlp_bsT")


def fused_inputs(inp, b):
    c = np.ascontiguousarray
    m = dict(const_inputs())
    m["qrel"] = c((np.arange(4, dtype=np.float32)[None, :] * 128 + np.arange(128, dtype=np.float32)[:, None]))
    m["x"] = c(inp["x"][b])
    m["gf"] = c(inp["final_norm_g"][None, :])
    for l in range(2):
        w_in = inp["w_in"][l]
        cg0 = col_groups(0)
        for name in ("fmb_all", "tmb_all", "fmb_own", "tmf_own"):
            m[f"w{l}_{name}"] = c(w_in[:, cg0[name]])
        for q in range(NQ):
            cg = col_groups(q)
            m[f"w{l}_fmf_all{q}"] = c(w_in[:, cg["fmf_all"]])
            m[f"w{l}_tmf_all{q}"] = c(w_in[:, cg["tmf_all"]])
            lp = layer_params(inp, l, q)
            for key in Q_KEYS:
                m[f"l{l}q{q}_{key}"] = lp[key]
            if q == 0:
                for key in L_KEYS:
                    m[f"l{l}_{key}"] = lp[key]
        m[f"wout{l}"] = c(inp["w_out"][l])
    return m


def build_fused(sample, upto=None, nlayers=2, skip=()):
    nc = bass.Bass("TRN2", target_bir_lowering=False)
    d = _decl(nc, sample)
    gi = fused_ginfo()
    scr = {}
    for name, (ncols, layout, dt, _) in gi.items():
        shape = [SEQ, ncols] if layout == "tm" else [ncols, SEQ]
        scr[name] = nc.dram_tensor("scr_" + name, shape, dt).ap()
    _, npairs = pair_offsets(FUSED_BLOCKS)
    maskT = nc.dram_tensor("maskT", [128, npairs, 128], BF16).ap()
    y_full = nc.dram_tensor("y_full", [SEQ, D_MODEL], F32).ap()
    xs = [d["x"], nc.dram_tensor("x1", [SEQ, D_MODEL], F32).ap(), nc.dram_tensor("x2", [SEQ, D_MODEL], F32).ap()]
    out = nc.dram_tensor("out", [SEQ, D_MODEL], F32, kind="ExternalOutput").ap()
    gnames = list(gi.keys())
    cnt = [0]

    def go():
        cnt[0] += 1
        return (upto is None or cnt[0] <= upto) and cnt[0] not in skip

    for l in range(nlayers):
        x_cur, x_nxt = xs[l], xs[l + 1]
        wts = {name: d[f"w{l}_{name}"] for name in gnames}
        plan = [(x_cur, p * 1024, [(n, p * 1024) for n in gnames]) for p in range(4)]
        if go():
            phase_inproj(nc, None, None, d[f"l{l}_g"], d["ident"], wts, scr, plan=plan, ginfo=gi)
        prm_l = dict(d)
        for key in L_KEYS:
            prm_l[key] = d[f"l{l}_{key}"]
        sc_own = {"fmb_own": scr["fmb_own"], "fmb_all": scr["fmb_all"], "tmf_own": scr["tmf_own"], "tmb_all": scr["tmb_all"]}
        if go():
            phase_indexer(nc, sc_own, prm_l, maskT, blocks=FUSED_BLOCKS)
        if go():
            phase_attn(nc, sc_own, prm_l, maskT, y_full, blocks=FUSED_BLOCKS)
        if go():
            phase_mlp(nc, sc_own, prm_l, y_full, nchunks=SEQ // 128, ycol0=3072)
        for q in range(NQ):
            prm_q = dict(d)
            for key in Q_KEYS:
                prm_q[key] = d[f"l{l}q{q}_{key}"]
            sc_q = {"fmf_all": scr[f"fmf_all{q}"], "tmf_all": scr[f"tmf_all{q}"]}
            if go():
                phase_ssm(nc, sc_q, prm_q, None, y_dst=y_full[:, 1024 + 256 * q:1024 + 256 * (q + 1)])
            if go():
                phase_rwkv(nc, sc_q, prm_q, None, y_dst=y_full[:, 2048 + 256 * q:2048 + 256 * (q + 1)])
        for p in range(4):
            rs_ = slice(p * 1024, (p + 1) * 1024)
            if go():
                phase_outproj(nc, y_full[rs_, :], x_cur[rs_, :], d[f"wout{l}"], d["ident"], x_nxt[rs_, :])
    if upto is None:
        phase_finalnorm(nc, xs[nlayers], d["gf"], out, ntiles=SEQ // 128)
    else:
        phase_finalnorm(nc, y_full, d["gf"], out, ntiles=SEQ // 128)
    return nc


def kernel_fused(**inp):
    inp = {k: np.asarray(v) for k, v in inp.items()}
    nb = inp["x"].shape[0]
    in_maps = [fused_inputs(inp, b) for b in range(nb)]
    nc = build_fused(in_maps[0])
    res = run_bass_kernel_spmd(nc, in_maps, core_ids=list(range(nb))).results
    return np.stack([res[b]["out"] for b in range(nb)], axis=0)


FUSED = True


def kernel(**inp):
    return kernel_fused(**inp) if FUSED else kernel_unfused(**inp)
```
